# Optimizing a Trainium2 kernel written in Bass

```python
import math
import jax, jax.numpy as jnp
from jax import lax
import numpy as np

D_MODEL = 4096
BATCH = 2
SEQ = 8192
DEPTH = 2

CTX_LEN = 256
GRID_W = 64

MLA_HEADS = 32
QK_NOPE = 128
QK_ROPE = 64
V_DIM = 128
Q_RANK = 768
KV_RANK = 512
MLA_WIDTH = MLA_HEADS * V_DIM
QK_DIM = QK_NOPE + QK_ROPE
ATTN_SCALE = 1.0 / math.sqrt(QK_DIM)
ROPE_BASE = 10000.0
Q_BLOCK = 128

SSM_INNER = 2 * D_MODEL
SSM_HEADDIM = 64
SSM_HEADS = SSM_INNER // SSM_HEADDIM
SSM_GROUPS = 8
SSM_HPG = SSM_HEADS // SSM_GROUPS
SSM_STATE = 128
CONV_K = 5
CONV_CH = SSM_INNER + 2 * SSM_GROUPS * SSM_STATE
CHUNK = 128

EPS = 1e-6

IN_WIDTH = (Q_RANK + KV_RANK + QK_ROPE + MLA_WIDTH + 2 * SSM_INNER
            + 2 * SSM_GROUPS * SSM_STATE + 2 * SSM_HEADS + 2 * D_MODEL)

kernel_name = "hybrid_mla_ssd_parallel_gated_dit"


def _split_in(p):
    sizes = (Q_RANK, KV_RANK, QK_ROPE, MLA_WIDTH, SSM_INNER, SSM_INNER,
             SSM_GROUPS * SSM_STATE, SSM_GROUPS * SSM_STATE, SSM_HEADS, SSM_HEADS,
             D_MODEL, D_MODEL)
    offs, acc = [], 0
    for s in sizes[:-1]:
        acc += s
        offs.append(acc)
    return jnp.split(p, offs, axis=-1)


def rms_norm(x, g):
    xf = x.astype(jnp.float32)
    y = xf * lax.rsqrt(jnp.mean(xf * xf, axis=-1, keepdims=True) + EPS)
    return (y * g.astype(jnp.float32)).astype(x.dtype)


def axial_angles(n_tokens):
    n_rows = n_tokens // GRID_W
    rows = jnp.broadcast_to(jnp.arange(n_rows, dtype=jnp.float32)[:, None], (n_rows, GRID_W)).reshape(-1)
    cols = jnp.broadcast_to(jnp.arange(GRID_W, dtype=jnp.float32)[None, :], (n_rows, GRID_W)).reshape(-1)
    n_freq = QK_ROPE // 4
    inv = ROPE_BASE ** (-jnp.arange(n_freq, dtype=jnp.float32) / n_freq)
    return jnp.concatenate([rows[:, None] * inv, cols[:, None] * inv], axis=-1)


def apply_rope(x, ang):
    x1, x2 = jnp.split(x, 2, axis=-1)
    cos = jnp.cos(ang).astype(x.dtype)
    sin = jnp.sin(ang).astype(x.dtype)
    return jnp.concatenate([x1 * cos - x2 * sin, x1 * sin + x2 * cos], axis=-1)


def mla_up(cq, ckv, g_q, w_uq, g_kv, w_ukv):
    b, n = cq.shape[:2]
    q = (rms_norm(cq, g_q) @ w_uq).reshape(b, n, MLA_HEADS, QK_DIM)
    kv = (rms_norm(ckv, g_kv) @ w_ukv).reshape(b, n, MLA_HEADS, QK_NOPE + V_DIM)
    return q[..., :QK_NOPE], q[..., QK_NOPE:], kv[..., :QK_NOPE], kv[..., QK_NOPE:]


def attend(q, k, v):
    s = jnp.einsum('bqhd,bkhd->bhqk', q, k).astype(jnp.float32) * ATTN_SCALE
    p = jax.nn.softmax(s, axis=-1).astype(v.dtype)
    return jnp.einsum('bhqk,bkhd->bqhd', p, v)


def attend_blocked(q, k, v):
    b, n, h, d = q.shape
    qb = jnp.moveaxis(q.reshape(b, n // Q_BLOCK, Q_BLOCK, h, d), 1, 0)
    ob = lax.map(lambda qblk: attend(qblk, k, v), qb)
    return jnp.moveaxis(ob, 0, 1).reshape(b, n, h, v.shape[-1])


def dwconv_centred(u, w, bias):
    pad = (CONV_K - 1) // 2
    y = lax.conv_general_dilated(u, w[:, None, :].astype(u.dtype), window_strides=(1,),
                                 padding=[(pad, pad)], dimension_numbers=('NWC', 'WIO', 'NWC'),
                                 feature_group_count=u.shape[-1])
    return jax.nn.silu(y + bias.astype(u.dtype))


def ssd_scan(xh, dt, A, Bm, Cm, h0):
    b, n = xh.shape[:2]
    nc = n // CHUNK

    def chunks(t):
        return jnp.moveaxis(t.reshape(b, nc, CHUNK, *t.shape[2:]), 1, 0)

    idx = jnp.arange(CHUNK)
    lower = (idx[:, None] >= idx[None, :])[None, :, :, None, None]

    def step(h, inp):
        xc, dtc, bc, cc = inp
        acs = jnp.cumsum(dtc * A, axis=1)
        seg = acs[:, :, None] - acs[:, None, :]
        lmat = jnp.exp(jnp.where(lower, seg, -jnp.inf))
        xdt = xc * dtc[..., None]
        cb = jnp.einsum('blgn,bsgn->bgls', cc, bc)
        y_diag = jnp.einsum('bgls,blsgh,bsghp->blghp', cb, lmat, xdt)
        y_off = jnp.einsum('blgn,bghpn->blghp', cc, h) * jnp.exp(acs)[..., None]
        tail = jnp.exp(acs[:, -1:] - acs)
        h_new = (h * jnp.exp(acs[:, -1])[..., None, None]
                 + jnp.einsum('bsgn,bsghp->bghpn', bc, xdt * tail[..., None]))
        return h_new, y_diag + y_off

    h_fin, ys = lax.scan(step, h0, (chunks(xh), chunks(dt), chunks(Bm), chunks(Cm)))
    return jnp.moveaxis(ys, 0, 1).reshape(xh.shape), h_fin


def ssm_branch(z, xs, bs, cs, dtf_raw, dtb_raw, conv_w, conv_b, dt_bias_f, dt_bias_b,
               a_log_f, a_log_b, d_skip, g_ssm, h0f, h0b):
    b, n = z.shape[:2]
    xbc = dwconv_centred(jnp.concatenate([xs, bs, cs], axis=-1), conv_w, conv_b)
    xs = xbc[..., :SSM_INNER]
    bs = xbc[..., SSM_INNER:SSM_INNER + SSM_GROUPS * SSM_STATE]
    cs = xbc[..., SSM_INNER + SSM_GROUPS * SSM_STATE:]
    f32 = jnp.float32
    xh = xs.astype(f32).reshape(b, n, SSM_GROUPS, SSM_HPG, SSM_HEADDIM)
    bm = bs.astype(f32).reshape(b, n, SSM_GROUPS, SSM_STATE)
    cm = cs.astype(f32).reshape(b, n, SSM_GROUPS, SSM_STATE)
    dt_f = jax.nn.softplus(dtf_raw.astype(f32) + dt_bias_f.astype(f32)).reshape(b, n, SSM_GROUPS, SSM_HPG)
    dt_b = jax.nn.softplus(dtb_raw.astype(f32) + dt_bias_b.astype(f32)).reshape(b, n, SSM_GROUPS, SSM_HPG)
    a_f = -jnp.exp(a_log_f.astype(f32)).reshape(SSM_GROUPS, SSM_HPG)
    a_b = -jnp.exp(a_log_b.astype(f32)).reshape(SSM_GROUPS, SSM_HPG)
    y_f, hf = ssd_scan(xh, dt_f, a_f, bm, cm, h0f)
    flip = lambda t: jnp.flip(t, axis=1)
    y_b, hb = ssd_scan(flip(xh), flip(dt_b), a_b, flip(bm), flip(cm), h0b)
    y = y_f + flip(y_b) + xh * d_skip.astype(f32).reshape(SSM_GROUPS, SSM_HPG)[..., None]
    y = y.reshape(b, n, SSM_INNER).astype(z.dtype)
    return rms_norm(y * jax.nn.silu(z), g_ssm), hf, hb


def setup_inputs(seed: int = 0) -> dict:
    key = jax.random.key(seed)
    ks = jax.random.split(key, 32)
    f32 = jnp.float32

    def nrm(k, shape, scale):
        return jax.random.normal(k, shape, f32) * scale

    def gain(k, shape):
        return 1.0 + 0.1 * jax.random.normal(k, shape, f32)

    def dt_bias(k):
        u = jax.random.uniform(k, (DEPTH, SSM_HEADS), f32)
        dt0 = jnp.exp(u * (math.log(0.1) - math.log(0.001)) + math.log(0.001))
        return dt0 + jnp.log(-jnp.expm1(-dt0))

    def a_log(k):
        return jnp.log(jax.random.uniform(k, (DEPTH, SSM_HEADS), f32, 1.0, 16.0))

    return {
        "x": nrm(ks[0], (BATCH, SEQ, D_MODEL), 1.0),
        "c": nrm(ks[1], (BATCH, D_MODEL), 1.0),
        "ctx": nrm(ks[2], (BATCH, CTX_LEN, D_MODEL), 1.0),
        "c_ctx": nrm(ks[3], (D_MODEL,), 1.0),
        "w_ada": nrm(ks[4], (DEPTH, D_MODEL, 3 * D_MODEL), 0.5 * D_MODEL ** -0.5),
        "b_ada": nrm(ks[5], (DEPTH, 3 * D_MODEL), 0.02),
        "g_pre": gain(ks[6], (DEPTH, D_MODEL)),
        "w_in": nrm(ks[7], (DEPTH, D_MODEL, IN_WIDTH), D_MODEL ** -0.5),
        "g_q": gain(ks[8], (DEPTH, Q_RANK)),
        "w_uq": nrm(ks[9], (DEPTH, Q_RANK, MLA_HEADS * QK_DIM), Q_RANK ** -0.5),
        "g_kv": gain(ks[10], (DEPTH, KV_RANK)),
        "w_ukv": nrm(ks[11], (DEPTH, KV_RANK, MLA_HEADS * (QK_NOPE + V_DIM)), KV_RANK ** -0.5),
        "conv_w": nrm(ks[12], (DEPTH, CONV_K, CONV_CH), CONV_K ** -0.5),
        "conv_b": nrm(ks[13], (DEPTH, CONV_CH), 0.02),
        "dt_bias_f": dt_bias(ks[14]),
        "dt_bias_b": dt_bias(ks[15]),
        "a_log_f": a_log(ks[16]),
        "a_log_b": a_log(ks[17]),
        "d_skip": gain(ks[18], (DEPTH, SSM_HEADS)),
        "g_ssm": gain(ks[19], (DEPTH, SSM_INNER)),
        "w_proj_a": nrm(ks[20], (DEPTH, MLA_WIDTH, D_MODEL), MLA_WIDTH ** -0.5),
        "w_proj_b": nrm(ks[21], (DEPTH, SSM_INNER, D_MODEL), SSM_INNER ** -0.5),
        "w_out": nrm(ks[22], (DEPTH, D_MODEL, D_MODEL), D_MODEL ** -0.5),
        "g_final": gain(ks[23], (D_MODEL,)),
    }


def reference(x, c, ctx, c_ctx, w_ada, b_ada, g_pre, w_in, g_q, w_uq, g_kv, w_ukv,
              conv_w, conv_b, dt_bias_f, dt_bias_b, a_log_f, a_log_b, d_skip, g_ssm,
              w_proj_a, w_proj_b, w_out, g_final):
    b, n = x.shape[:2]
    n_ctx = ctx.shape[1]
    ang = axial_angles(n)
    h, h_ctx = x, ctx
    for i in range(DEPTH):
        need_ctx_out = i < DEPTH - 1
        mod = jax.nn.silu(c) @ w_ada[i] + b_ada[i]
        shift, scale, gate = [m[:, None, :] for m in jnp.split(mod, 3, axis=-1)]
        mod_c = jax.nn.silu(c_ctx) @ w_ada[i] + b_ada[i]
        shift_c, scale_c, gate_c = jnp.split(mod_c, 3, axis=-1)

        u = rms_norm(h, g_pre[i]) * (1.0 + scale) + shift
        u_c = rms_norm(h_ctx, g_pre[i]) * (1.0 + scale_c) + shift_c
        (cq, ckv, kpe, ga, z, xs, bs, cs, dtf, dtb, mg_a, mg_b) = _split_in(u @ w_in[i])
        (cq_c, ckv_c, kpe_c, ga_c, z_c, xs_c, bs_c, cs_c, dtf_c, dtb_c, mg_a_c, mg_b_c) = _split_in(u_c @ w_in[i])

        q_nope, q_pe, k_nope, v = mla_up(cq, ckv, g_q[i], w_uq[i], g_kv[i], w_ukv[i])
        q_pe = apply_rope(q_pe, ang[None, :, None, :])
        k_pe = apply_rope(kpe, ang[None])
        q_lat = jnp.concatenate([q_nope, q_pe], axis=-1)
        k_lat = jnp.concatenate([k_nope, jnp.broadcast_to(k_pe[:, :, None, :], (b, n, MLA_HEADS, QK_ROPE))], axis=-1)
        qn_c, qp_c, kn_c, v_c = mla_up(cq_c, ckv_c, g_q[i], w_uq[i], g_kv[i], w_ukv[i])
        k_ctx = jnp.concatenate([kn_c, jnp.broadcast_to(kpe_c[:, :, None, :], (b, n_ctx, MLA_HEADS, QK_ROPE))], axis=-1)
        k_all = jnp.concatenate([k_lat, k_ctx], axis=1)
        v_all = jnp.concatenate([v, v_c], axis=1)
        o_a = attend_blocked(q_lat, k_all, v_all).reshape(b, n, MLA_WIDTH)
        br_a = (o_a * jax.nn.silu(ga)) @ w_proj_a[i]

        h0 = jnp.zeros((b, SSM_GROUPS, SSM_HPG, SSM_HEADDIM, SSM_STATE), jnp.float32)
        ssm_p = (conv_w[i], conv_b[i], dt_bias_f[i], dt_bias_b[i], a_log_f[i], a_log_b[i], d_skip[i], g_ssm[i])
        y_c, hf_c, hb_c = ssm_branch(z_c, xs_c, bs_c, cs_c, dtf_c, dtb_c, *ssm_p, h0, h0)
        y_l, _, _ = ssm_branch(z, xs, bs, cs, dtf, dtb, *ssm_p, hf_c, hb_c)
        br_b = y_l @ w_proj_b[i]

        merged = jax.nn.sigmoid(mg_a) * br_a + jax.nn.sigmoid(mg_b) * br_b
        h_new = h + gate * (merged @ w_out[i])

        if need_ctx_out:
            q_c = jnp.concatenate([qn_c, qp_c], axis=-1)
            o_c = attend(q_c, k_ctx, v_c).reshape(b, n_ctx, MLA_WIDTH)
            br_a_c = (o_c * jax.nn.silu(ga_c)) @ w_proj_a[i]
            br_b_c = y_c @ w_proj_b[i]
            merged_c = jax.nn.sigmoid(mg_a_c) * br_a_c + jax.nn.sigmoid(mg_b_c) * br_b_c
            h_ctx = h_ctx + gate_c * (merged_c @ w_out[i])
        h = h_new
    return rms_norm(h, g_final)
```

```python
import numpy as np
import concourse.bass as bass
import concourse.mybir as mybir
from concourse.bass_utils import run_bass_kernel_spmd
from contextlib import ExitStack

F32 = mybir.dt.float32
BF16 = mybir.dt.bfloat16
AF = mybir.ActivationFunctionType
ALU = mybir.AluOpType
AX = mybir.AxisListType


class _Op:
    __slots__ = ("eng", "fn", "reads", "writes", "excl", "tag", "ndma", "deps", "ms", "cum")

    def __init__(self, eng, fn, reads, writes, tag=None, ndma=0, excl=()):
        self.eng, self.fn, self.reads, self.writes = eng, fn, reads, writes
        self.excl = tuple(excl)
        self.tag, self.ndma = tag, ndma
        self.deps = None
        self.ms = 0
        self.cum = 0


class Prog:
    ENGS = ("pe", "act", "dve", "pool", "sp")

    def __init__(self, nc):
        self.nc = nc
        self.ops = []
        self.es = ExitStack()
        self.last_w = {}
        self.last_x = {}
        self.readers = {}
        self.tagcnt = {}
        self.bar = {}
        self.last_eng = {}
        self.last_tag = {}
        self.scopes = []
        self.colltags = set()
        self.uid = 0
        self.slotmap = {}

    def sbuf(self, name, shape, dt):
        self.uid += 1
        return self.es.enter_context(self.nc.sbuf_tensor("sb%d_%s" % (self.uid, name), list(shape), dt))

    def psum(self, name, shape, dt=F32):
        self.uid += 1
        return self.es.enter_context(self.nc.psum_tensor("ps%d_%s" % (self.uid, name), list(shape), dt))

    def push(self):
        self.scopes.append(self.es)
        self.es = ExitStack()

    def pop(self):
        self.barrier()
        self.es.close()
        self.es = self.scopes.pop()

    def barrier(self):
        snap = set(self.last_eng.values()) | set(self.last_tag.values())
        for e in self.ENGS:
            self.bar[e] = set(snap) | self.bar.get(e, set())
        self.slotmap = {}

    def _record(self, op):
        deps = set()
        for k in op.reads:
            d = self.last_w.get(k)
            if d is not None:
                deps.add(d)
        for k in op.writes:
            d = self.last_w.get(k)
            if d is not None:
                deps.add(d)
            deps.update(self.readers.get(k, ()))
        idx = len(self.ops)
        b = self.bar.pop(op.eng, None)
        if b:
            deps.update(b)
        if op.tag is None:
            self.last_eng[op.eng] = idx
        else:
            self.last_tag[op.tag] = idx
        for k in op.excl:
            d = self.last_x.get(k)
            if d is not None and self.ops[d].eng != op.eng:
                deps.add(d)
            self.last_x[k] = idx
        for k in op.writes:
            self.last_w[k] = idx
            self.readers[k] = []
        for k in op.reads:
            if k in op.writes:
                continue
            self.readers.setdefault(k, []).append(idx)
        op.deps = deps
        self.ops.append(op)
        return idx

    def op(self, eng, fn, reads=(), writes=(), excl=()):
        return self._record(_Op(eng, fn, tuple(reads), tuple(writes), excl=excl))

    def dma(self, eng, tag, fn, reads=(), writes=(), n=1):
        slot = self.slotmap.get(tag)
        if slot is None:
            slot = len(self.slotmap)
            self.slotmap[tag] = slot
        slot = "s%d" % slot
        tk = ("__slot__", slot)
        o = _Op(eng, fn, tuple(reads), tuple(writes) + (tk,), tag=slot, ndma=n)
        self.tagcnt[slot] = self.tagcnt.get(slot, 0) + n
        o.cum = self.tagcnt[slot] * 16
        return self._record(o)

    def coll(self, fn, n, reads=(), writes=()):
        slot = "cc"
        self.colltags.add(slot)
        tk = ("__slot__", slot)
        o = _Op("pool", fn, tuple(reads), tuple(writes) + (tk,), tag=slot, ndma=n)
        self.tagcnt[slot] = self.tagcnt.get(slot, 0) + n
        o.cum = self.tagcnt[slot]
        return self._record(o)

    def emit(self):
        nc = self.nc
        ops = self.ops
        need = set()
        for o in ops:
            best = {}
            dmb = {}
            for d in o.deps:
                p = ops[d]
                if p.tag is not None:
                    if p.tag not in dmb or dmb[p.tag] < d:
                        dmb[p.tag] = d
                    continue
                if p.eng == "pe" and o.eng == "pe" and o.tag is None:
                    continue
                if p.eng not in best or best[p.eng] < d:
                    best[p.eng] = d
            o.deps = list(best.values()) + list(dmb.values())
            for d in best.values():
                need.add(d)
        CAP = 8000
        cnt = {e: 0 for e in self.ENGS}
        for i, o in enumerate(ops):
            if o.tag is None and i in need:
                cnt[o.eng] += 1
                o.ms = cnt[o.eng]
        sems = {}
        for e in self.ENGS:
            for g in range(cnt[e] // CAP + 1):
                sems[e, g] = self.es.enter_context(nc.semaphore("s_%s_%d" % (e, g)))
        tsems = {}
        DCAP = CAP // 16
        for t, c in self.tagcnt.items():
            if t in self.colltags:
                assert c < CAP
                tsems[t, 0] = self.es.enter_context(nc.semaphore("d_%s" % (str(t),)))
            else:
                for g in range(c // DCAP + 1):
                    tsems[t, g] = self.es.enter_context(nc.semaphore("d_%s_%d" % (str(t), g)))
        self.stats_nsem = len(sems) + len(tsems)

        def esem(p):
            g = (p.ms - 1) // CAP
            return sems[p.eng, g], p.ms - g * CAP, ("e", p.eng, g)

        def dsem(p):
            if p.tag in self.colltags:
                return tsems[p.tag, 0], p.cum, ("t", p.tag, 0)
            k = p.cum // 16
            g = (k - 1) // DCAP
            return tsems[p.tag, g], (k - g * DCAP) * 16, ("t", p.tag, g)

        def dsem_prev(p):
            if p.tag in self.colltags:
                return None
            k = p.cum // 16
            g = (k - 1) // DCAP
            if g > 0 and (k - p.ndma) < g * DCAP:
                return tsems[p.tag, g - 1], DCAP * 16, ("t", p.tag, g - 1)
            return None
        per = {e: [] for e in self.ENGS}
        for o in ops:
            per[o.eng].append(o)
        self.stats = {e: len(per[e]) for e in self.ENGS}
        self.stats["milestones"] = dict(cnt)
        self.stats["tags"] = len(tsems)

        def run(e, eng):
            seen = {}
            nw = 0
            for o in per[e]:
                for d in o.deps:
                    p = ops[d]
                    if p.tag is not None:
                        pv = dsem_prev(p)
                        if pv is not None and seen.get(pv[2], 0) < pv[1]:
                            seen[pv[2]] = pv[1]
                            eng.wait_ge(pv[0], pv[1])
                            nw += 1
                        s, v, key = dsem(p)
                    else:
                        s, v, key = esem(p)
                    if seen.get(key, 0) >= v:
                        continue
                    seen[key] = v
                    eng.wait_ge(s, v)
                    nw += 1
                r = o.fn(eng)
                if o.tag is not None:
                    assert len(r) == o.ndma, (len(r), o.ndma, o.tag)
                    if o.tag in self.colltags:
                        for ins in r:
                            ins.then_inc(tsems[o.tag, 0], 1)
                    else:
                        k0 = o.cum // 16 - o.ndma
                        for q, ins in enumerate(r):
                            ins.then_inc(tsems[o.tag, (k0 + q) // DCAP], 16)
                elif o.ms:
                    r.then_inc(esem(o)[0], 1)
            if e == "sp":
                for t, c in self.tagcnt.items():
                    if t in self.colltags:
                        if seen.get(("t", t, 0), 0) < c:
                            eng.wait_ge(tsems[t, 0], c)
                        continue
                    for g in range(c // DCAP + 1):
                        v = min(c - g * DCAP, DCAP) * 16
                        if v > 0 and seen.get(("t", t, g), 0) < v:
                            eng.wait_ge(tsems[t, g], v)
            self.stats["waits_" + e] = nw

        with nc.Block() as block:
            @block.sync
            def _(sync):
                run("sp", sync)

            @block.tensor
            def _(tensor):
                run("pe", tensor)

            @block.vector
            def _(vector):
                run("dve", vector)

            @block.scalar
            def _(scalar):
                run("act", scalar)

            @block.gpsimd
            def _(gpsimd):
                run("pool", gpsimd)
        self.es.close()


def dma_pieces(e, dst, pieces, pat=None, **kw):
    out = []
    for (ap, p0, r) in pieces:
        src = ap.rearrange(pat, **kw) if pat else ap
        out.append(e.dma_start(out=dst[p0:p0 + r], in_=src))
    return out


import numpy as np

D = 4096
Q0 = 0; KV0 = 768; KPE0 = 1280; GA0 = 1344; Z0 = 5440; XS0 = 13632; B0 = 21824; C0 = 22848
DTF0 = 23872; DTB0 = 24000; MGA0 = 24128; MGB0 = 28224


def cols_for_core(j):
    r = np.arange
    kpe = np.concatenate([r(KPE0, KPE0 + 64), r(KPE0 + 32, KPE0 + 64), r(KPE0, KPE0 + 32)])
    dt = np.concatenate([r(DTF0 + 32 * j, DTF0 + 32 * j + 32), r(DTB0 + 32 * j, DTB0 + 32 * j + 32), -np.ones(64, np.int64)])
    cols = np.concatenate([
        r(Q0, Q0 + 768), r(KV0, KV0 + 512), kpe,
        r(GA0 + 1024 * j, GA0 + 1024 * (j + 1)),
        r(Z0 + 2048 * j, Z0 + 2048 * (j + 1)),
        r(XS0 + 2048 * j, XS0 + 2048 * (j + 1)),
        r(B0 + 256 * j, B0 + 256 * (j + 1)),
        r(C0 + 256 * j, C0 + 256 * (j + 1)),
        dt,
        r(MGA0 + 1024 * j, MGA0 + 1024 * (j + 1)),
        r(MGB0 + 1024 * j, MGB0 + 1024 * (j + 1)),
    ])
    assert cols.shape[0] == 72 * 128
    return cols


def chunked_weight(w, cols, width=128):
    K = w.shape[0]
    sel = w[:, np.maximum(cols, 0)]
    if (cols < 0).any():
        sel[:, cols < 0] = 0.0
    nchunk = cols.shape[0] // width
    a = sel.reshape(K // 128, 128, nchunk, width).transpose(2, 1, 0, 3)
    return np.ascontiguousarray(a).reshape(nchunk, 128, (K // 128) * width)


def colvec(v):
    return np.ascontiguousarray(v.reshape(-1, 128).T)


def prep_A(inp, i, b, j):
    c2 = np.stack([inp["c"][b], inp["c_ctx"]], axis=1)
    c2T = np.ascontiguousarray(c2.reshape(32, 128, 2).transpose(1, 0, 2))
    wada = inp["w_ada"][i][:, :2 * D]
    return {
        "c2T": c2T,
        "wadal": chunked_weight(wada, np.arange(2 * D)),
        "badaT": colvec(inp["b_ada"][i][:2 * D]),
        "gpreT": colvec(inp["g_pre"][i]),
        "identf": np.eye(128, dtype=np.float32),
        "wl": chunked_weight(inp["w_in"][i], cols_for_core(j)),
    }


def rope_tables(nt, nctx=256):
    n = nt - nctx
    f32 = np.float32
    rows = (np.arange(n) // 64).astype(f32)
    cols = (np.arange(n) % 64).astype(f32)
    inv = (f32(10000.0) ** (-np.arange(16, dtype=f32) / f32(16))).astype(f32)
    ang = np.concatenate([rows[:, None] * inv, cols[:, None] * inv], axis=-1).astype(f32)
    c = np.cos(ang).astype(f32).T
    s = np.sin(ang).astype(f32).T
    cos2 = np.ones((64, nt), f32)
    sin2 = np.zeros((64, nt), f32)
    cos2[0:32, nctx:] = c
    cos2[32:64, nctx:] = c
    sin2[0:32, nctx:] = -s
    sin2[32:64, nctx:] = s
    return cos2, sin2


def prep_B(inp, i, j, nt):
    hq = np.arange(8 * j * 192, (8 * j + 8) * 192)
    sw = np.concatenate([np.concatenate([np.arange(h * 192 + 160, h * 192 + 192), np.arange(h * 192 + 128, h * 192 + 160)])
                         for h in range(8 * j, 8 * j + 8)])
    hkv = np.arange(8 * j * 256, (8 * j + 8) * 256)
    cos2, sin2 = rope_tables(nt)
    return {
        "gqT": colvec(inp["g_q"][i]), "gkvT": colvec(inp["g_kv"][i]),
        "wuq": chunked_weight(inp["w_uq"][i], hq, width=1536)[0],
        "wuqsw": chunked_weight(inp["w_uq"][i], sw, width=512)[0],
        "wukv": chunked_weight(inp["w_ukv"][i], hkv, width=2048)[0],
        "cos2": cos2, "sin2": sin2, "onesf": np.ones((128, 128), np.float32),
    }


def prep_S(inp, i, j):
    ch = np.concatenate([np.arange(2048 * j, 2048 * (j + 1)), 8192 + np.arange(256 * j, 256 * (j + 1)),
                         9216 + np.arange(256 * j, 256 * (j + 1))])
    cw = inp["conv_w"][i][:, ch]
    convw = np.ascontiguousarray(cw.reshape(5, 20, 128).transpose(2, 1, 0))
    convb = colvec(inp["conv_b"][i][ch])
    hs = slice(32 * j, 32 * (j + 1))
    dtb = np.concatenate([inp["dt_bias_f"][i][hs], inp["dt_bias_b"][i][hs]])
    al = np.concatenate([inp["a_log_f"][i][hs], inp["a_log_b"][i][hs]])
    r = np.arange(128)
    f32 = np.float32
    return {
        "convw": convw, "convb": convb,
        "dtbias2": np.concatenate([dtb, dtb])[:, None].astype(f32),
        "alog2": np.concatenate([np.zeros(64, f32), al])[:, None].astype(f32),
        "TL": (r[:, None] <= r[None, :]).astype(f32), "TG": (r[:, None] >= r[None, :]).astype(f32),
        "SG": (r[:, None] > r[None, :]).astype(f32), "SL": (r[:, None] < r[None, :]).astype(f32),
        "gssmT": colvec(inp["g_ssm"][i][2048 * j:2048 * (j + 1)]),
        "dskbc": np.ascontiguousarray(np.broadcast_to(np.repeat(inp["d_skip"][i][hs], 64)[None, :], (128, 2048))).astype(f32),
    }


D = 4096
KC = D // 128
NCTX = 256
EPS = 1e-6

SEG = [("cq", 6, "id"), ("ckv", 4, "id"), ("kpe", 1, "id"), ("ga", 8, "silu"), ("z", 16, "silu"),
       ("xs", 16, "id"), ("B", 2, "id"), ("C", 2, "id"), ("dt", 1, "id"), ("mga", 8, "sig"), ("mgb", 8, "sig")]
SEG_OFF = {}
_o = 0
for _n, _c, _f in SEG:
    SEG_OFF[_n] = _o
    _o += _c
NCH = _o
NPT = SEG_OFF["mga"]


def token_blocks(nt):
    blks = [(0, NCTX)]
    t = NCTX
    while t < nt:
        blks.append((t, min(512, nt - t)))
        t += 512
    return blks


def phase_A(P, nc, io, blocks, chunks, wada_fn):
    P.push()
    identf = P.sbuf("identf", [128, 128], F32)
    P.dma("sp", "identf", lambda e: [e.dma_start(out=identf[:], in_=io["identf"])], writes=["identf"])
    c2 = P.sbuf("c2", [128, KC, 2], F32)
    sc = P.sbuf("sc", [128, KC, 2], F32)
    bada = P.sbuf("bada", [128, 64], F32)
    gpre = P.sbuf("gpre", [128, KC], F32)
    modT = P.sbuf("modT", [128, 64, 2], F32)
    gmul = P.sbuf("gmul", [128, KC, 2], F32)
    P.dma("sp", "c2", lambda e: [e.dma_start(out=c2[:], in_=io["c2T"])], writes=["c2"])
    P.dma("sp", "bada", lambda e: [e.dma_start(out=bada[:], in_=io["badaT"])], writes=["bada"])
    P.dma("sp", "gpre", lambda e: [e.dma_start(out=gpre[:], in_=io["gpreT"])], writes=["gpre"])
    P.op("act", lambda e: e.activation(out=sc[:], in_=c2[:], func=AF.Silu), reads=["c2"], writes=["sc"])
    P.push()
    wad = [P.sbuf("wad%d" % i, [128, KC, 128], F32) for i in range(2)]
    pm = P.psum("pm", [128, 512], F32)
    for oc in range(64):
        wb = wad[oc % 2]
        wp_ = wada_fn(oc)
        P.dma("sp", "wad%d" % (oc % 2), lambda e, wb=wb, wp_=wp_: dma_pieces(e, wb, wp_, "p (k n) -> p k n", k=KC),
              writes=["wad%d" % (oc % 2)], n=len(wp_))
        for k in range(KC):
            P.op("pe", lambda e, wb=wb, k=k, oc=oc: e.matmul(pm[:, 2 * (oc % 8):2 * (oc % 8) + 2], wb[:, k, :], sc[:, k, :],
                                                             start=(k == 0), stop=(k == KC - 1)),
                 reads=["wad%d" % (oc % 2), "sc"], excl=["b_pm"])
        P.op("dve", lambda e, oc=oc: e.tensor_scalar(out=modT[:, oc, :], in0=pm[:, 2 * (oc % 8):2 * (oc % 8) + 2],
                                                     scalar1=bada[:, oc:oc + 1], scalar2=None, op0=ALU.add),
             reads=["bada"], writes=["modT"], excl=["b_pm"])
    P.pop()
    for n in range(2):
        P.op("dve", lambda e, n=n: e.scalar_tensor_tensor(out=gmul[:, :, n], in0=modT[:, 32:64, n], scalar=1.0, in1=gpre[:],
                                                          op0=ALU.add, op1=ALU.mult),
             reads=["modT", "gpre"], writes=["gmul"])

    xt = [P.sbuf("xt%d" % i, [128, D], F32) for i in range(2)]
    xn = [P.sbuf("xn%d" % i, [128, D], F32) for i in range(2)]
    junk = P.sbuf("junk", [128, D], BF16)
    st = [P.sbuf("st%d" % i, [128, 4], F32) for i in range(2)]
    uT = [P.sbuf("uT%d" % i, [128, KC, 512], BF16) for i in range(2)]
    wbuf = [P.sbuf("wbuf%d" % i, [128, KC, 128], BF16) for i in range(3)]
    stage = [P.sbuf("stage%d" % i, [128, 512], F32) for i in range(3)]
    ptr = [P.psum("ptr%d" % i, [128, 512], F32) for i in range(2)]
    pmm = [P.psum("pmm%d" % i, [128, 512], F32) for i in range(2)]
    ti = 0
    wi = 0
    si = 0
    for bi, blk in enumerate(blocks):
        n = blk["n"]
        mcol = blk["mcol"]
        ub = uT[bi % 2]
        ukey = "uT%d" % (bi % 2)
        ukeys = []
        c0 = 0
        for tt, pieces in enumerate(blk["tiles"]):
            rows = sum(p[2] for p in pieces)
            x_ = xt[ti % 2]
            xn_ = xn[ti % 2]
            st_ = st[ti % 2]
            xk, xnk, stk = "xt%d" % (ti % 2), "xn%d" % (ti % 2), "st%d" % (ti % 2)
            P.dma("sp", xk, lambda e, x_=x_, pieces=pieces: [e.dma_start(out=x_[p0:p0 + r, :], in_=src) for (src, p0, r) in pieces],
                  writes=[xk], n=len(pieces))
            P.op("act", lambda e, x_=x_, st_=st_, rows=rows: e.activation(out=junk[0:rows, :], in_=x_[0:rows, :], func=AF.Square, accum_out=st_[0:rows, 0:1]),
                 reads=[xk], writes=["junk", stk])
            P.op("dve", lambda e, st_=st_, rows=rows: e.tensor_scalar(out=st_[0:rows, 1:2], in0=st_[0:rows, 0:1], scalar1=1.0 / D, scalar2=EPS,
                                                                      op0=ALU.mult, op1=ALU.add), reads=[stk], writes=[stk + "b"])
            P.op("act", lambda e, st_=st_, rows=rows: e.activation(out=st_[0:rows, 3:4], in_=st_[0:rows, 1:2], func=AF.Sqrt), reads=[stk + "b"], writes=[stk + "d"])
            P.op("dve", lambda e, st_=st_, rows=rows: e.reciprocal(out=st_[0:rows, 2:3], in_=st_[0:rows, 3:4]), reads=[stk + "d"], writes=[stk + "c"])
            P.op("dve", lambda e, x_=x_, xn_=xn_, st_=st_, rows=rows: e.tensor_scalar(out=xn_[0:rows, :], in0=x_[0:rows, :], scalar1=st_[0:rows, 2:3], scalar2=None,
                                                                                      op0=ALU.mult), reads=[xk, stk + "c"], writes=[xnk])
            for q in range(KC // 4):
                pb = ptr[q % 2]
                pk = "b_ptr%d" % (q % 2)
                for c4 in range(4):
                    c = q * 4 + c4
                    P.op("pe", lambda e, pb=pb, c4=c4, c=c, xn_=xn_, rows=rows: e.transpose(pb[:, c4 * 128:c4 * 128 + rows], xn_[0:rows, c * 128:(c + 1) * 128], identf[0:rows, 0:rows]),
                         reads=[xnk, "identf"], excl=[pk])
                for c4 in range(4):
                    c = q * 4 + c4
                    P.op("act", lambda e, pb=pb, c4=c4, c=c, ub=ub, c0=c0, mcol=mcol, rows=rows: e.activation(
                        out=ub[:, c, c0:c0 + rows], in_=pb[:, c4 * 128:c4 * 128 + rows], func=AF.Identity,
                        scale=gmul[:, c, mcol:mcol + 1], bias=modT[:, c, mcol:mcol + 1]),
                        reads=["gmul", "modT"], writes=[(ukey, tt)], excl=[pk])
            ukeys.append((ukey, tt))
            c0 += rows
            ti += 1
        assert c0 == n
        for cc, (w_ap, fn, dst_fn) in enumerate(chunks):
            wb = wbuf[wi % 3]
            wk = "wbuf%d" % (wi % 3)
            P.dma("pool", wk, lambda e, wb=wb, w_ap=w_ap: dma_pieces(e, wb, w_ap, "p (k n) -> p k n", k=KC),
                  writes=[wk], n=len(w_ap))
            pb = pmm[wi % 2]
            pk = "b_pmm%d" % (wi % 2)
            for k in range(KC):
                P.op("pe", lambda e, pb=pb, wb=wb, k=k, ub=ub, n=n: e.matmul(pb[:, 0:n], wb[:, k, :], ub[:, k, 0:n],
                                                                             start=(k == 0), stop=(k == KC - 1)),
                     reads=[wk] + ukeys, excl=[pk])
            sg = stage[si % 3]
            sk = "stage%d" % (si % 3)
            P.op("act", lambda e, sg=sg, pb=pb, n=n, fn=fn: e.activation(out=sg[:, 0:n], in_=pb[:, 0:n], func=fn),
                 writes=[sk], excl=[pk])
            dst = dst_fn(bi)
            P.dma("sp", sk, lambda e, sg=sg, dst=dst, n=n: [e.dma_start(out=dst, in_=sg[:, 0:n])], reads=[sk])
            wi += 1
            si += 1
    P.pop()


import math

ATTN_SCALE = 1.0 / math.sqrt(192.0)
NH = 8


def phase_B1(P, nc, io, nt):
    P.push()
    pT = io["pT"]
    wuq = P.sbuf("wuq", [128, 6, NH, 192], BF16)
    wuqsw = P.sbuf("wuqsw", [128, 6, NH, 64], BF16)
    wukv = P.sbuf("wukv", [128, 4, NH, 256], BF16)
    gq = P.sbuf("gq", [128, 6], F32)
    gkv = P.sbuf("gkv", [128, 4], F32)
    onesf = P.sbuf("onesf", [128, 128], F32)
    P.dma("pool", "wuq", lambda e: [e.dma_start(out=wuq[:], in_=io["wuq"].rearrange("p (k h d) -> p k h d", k=6, h=NH))], writes=["wuq"])
    P.dma("pool", "wuqsw", lambda e: [e.dma_start(out=wuqsw[:], in_=io["wuqsw"].rearrange("p (k h d) -> p k h d", k=6, h=NH))], writes=["wuqsw"])
    P.dma("pool", "wukv", lambda e: [e.dma_start(out=wukv[:], in_=io["wukv"].rearrange("p (k h d) -> p k h d", k=4, h=NH))], writes=["wukv"])
    P.dma("sp", "gq", lambda e: [e.dma_start(out=gq[:], in_=io["gqT"])], writes=["gq"])
    P.dma("sp", "gkv", lambda e: [e.dma_start(out=gkv[:], in_=io["gkvT"])], writes=["gkv"])
    P.dma("sp", "onesf", lambda e: [e.dma_start(out=onesf[:], in_=io["onesf"])], writes=["onesf"])

    cq = P.sbuf("cq", [128, 6, 512], F32)
    ckv = P.sbuf("ckv", [128, 4, 512], F32)
    kpe2 = P.sbuf("kpe2", [64, 2, 512], F32)
    cos = P.sbuf("cos", [64, 512], F32)
    sin = P.sbuf("sin", [64, 512], F32)
    sq = P.sbuf("sq", [128, 6, 512], F32)
    rr = P.sbuf("rr", [128, 3, 512], F32)
    cqn = P.sbuf("cqn", [128, 6, 512], BF16)
    ckvn = P.sbuf("ckvn", [128, 4, 512], BF16)
    t1 = [P.sbuf("t1_%d" % i, [64, 512], F32) for i in range(2)]
    t2 = [P.sbuf("t2_%d" % i, [64, 512], F32) for i in range(2)]
    stg = [P.sbuf("stg%d" % i, [128, 512], BF16) for i in range(4)]
    pss = P.psum("pss", [128, 512], F32)
    pmm = [P.psum("pbm%d" % i, [128, 512], F32) for i in range(2)]
    ppe = P.psum("ppe", [128, 512], F32)
    psw = P.psum("psw", [128, 512], F32)
    cnt = {"mm": 0, "stg": 0, "t": 0}

    def nxt_bank():
        i = cnt["mm"] % 2
        cnt["mm"] += 1
        return pmm[i], "b_pbm%d" % i

    def nxt_stg():
        i = cnt["stg"] % 4
        cnt["stg"] += 1
        return stg[i], "stg%d" % i

    def rmsnorm(src, skey, nk, g, dst, dkey, n, dim):
        P.op("act", lambda e: e.activation(out=sq[:, 0:nk, 0:n], in_=src[:, 0:nk, 0:n], func=AF.Square), reads=[skey], writes=["sq"])
        for k in range(nk):
            P.op("pe", lambda e, k=k: e.matmul(pss[:, 0:n], onesf[:], sq[:, k, 0:n], start=(k == 0), stop=(k == nk - 1)),
                 reads=["sq", "onesf"], excl=["b_pss"])
        P.op("dve", lambda e: e.tensor_scalar(out=rr[:, 0, 0:n], in0=pss[:, 0:n], scalar1=1.0 / dim, scalar2=EPS, op0=ALU.mult, op1=ALU.add),
             writes=["rr0"], excl=["b_pss"])
        P.op("act", lambda e: e.activation(out=rr[:, 1, 0:n], in_=rr[:, 0, 0:n], func=AF.Sqrt), reads=["rr0"], writes=["rr1"])
        P.op("dve", lambda e: e.reciprocal(out=rr[:, 2, 0:n], in_=rr[:, 1, 0:n]), reads=["rr1"], writes=["rr2"])
        for k in range(nk):
            P.op("dve", lambda e, k=k: e.scalar_tensor_tensor(out=dst[:, k, 0:n], in0=src[:, k, 0:n], scalar=g[:, k:k + 1], in1=rr[:, 2, 0:n],
                                                              op0=ALU.mult, op1=ALU.mult),
                 reads=[skey, "rr2", "gq", "gkv"], writes=[dkey])

    def rope(pe_ap, sw_ap, n, excl, rd, dst, dkey):
        i = cnt["t"] % 2
        cnt["t"] += 1
        a, b = t1[i], t2[i]
        P.op("dve", lambda e: e.tensor_tensor(out=a[:, 0:n], in0=pe_ap, in1=cos[:, 0:n], op=ALU.mult), reads=["cos"] + rd, writes=["t1_%d" % i], excl=excl[0:1])
        P.op("dve", lambda e: e.tensor_tensor(out=b[:, 0:n], in0=sw_ap, in1=sin[:, 0:n], op=ALU.mult), reads=["sin"] + rd, writes=["t2_%d" % i], excl=excl[1:2])
        P.op("pool", lambda e: e.tensor_tensor(out=dst, in0=a[:, 0:n], in1=b[:, 0:n], op=ALU.add), reads=["t1_%d" % i, "t2_%d" % i], writes=[dkey])

    def do_block(t0, n):
        P.dma("sp", "cq", lambda e, t0=t0, n=n: [e.dma_start(out=cq[:, :, 0:n], in_=pT[0:768, t0:t0 + n].rearrange("(k p) t -> p k t", p=128))], writes=["cq"])
        P.dma("sp", "ckv", lambda e, t0=t0, n=n: [e.dma_start(out=ckv[:, :, 0:n], in_=pT[768:1280, t0:t0 + n].rearrange("(k p) t -> p k t", p=128))], writes=["ckv"])
        P.dma("sp", "kpe2", lambda e, t0=t0, n=n: [e.dma_start(out=kpe2[:, :, 0:n], in_=pT[1280:1408, t0:t0 + n].rearrange("(a p) t -> p a t", p=64))], writes=["kpe2"])
        P.dma("sp", "cos", lambda e, t0=t0, n=n: [e.dma_start(out=cos[:, 0:n], in_=io["cos2"][:, t0:t0 + n])], writes=["cos"])
        P.dma("sp", "sin", lambda e, t0=t0, n=n: [e.dma_start(out=sin[:, 0:n], in_=io["sin2"][:, t0:t0 + n])], writes=["sin"])
        rmsnorm(cq, "cq", 6, gq, cqn, "cqn", n, 768.0)
        rmsnorm(ckv, "ckv", 4, gkv, ckvn, "ckvn", n, 512.0)
        sg, sk = nxt_stg()
        rope(kpe2[:, 0, 0:n], kpe2[:, 1, 0:n], n, [], ["kpe2"], sg[0:64, 0:n], sk)
        P.dma("sp", sk, lambda e, sg=sg, t0=t0, n=n: [e.dma_start(out=io["kpeT"][:, t0:t0 + n], in_=sg[0:64, 0:n])], reads=[sk])
        for h in range(NH):
            pb, pk = nxt_bank()
            for k in range(6):
                P.op("pe", lambda e, pb=pb, k=k, h=h: e.matmul(pb[:, 0:n], wuq[:, k, h, 0:128], cqn[:, k, 0:n], start=(k == 0), stop=(k == 5)),
                     reads=["wuq", "cqn"], excl=[pk])
            sg, sk = nxt_stg()
            P.op("act", lambda e, sg=sg, pb=pb: e.activation(out=sg[:, 0:n], in_=pb[:, 0:n], func=AF.Identity), writes=[sk], excl=[pk])
            P.dma("sp", sk, lambda e, sg=sg, h=h, t0=t0, n=n: [e.dma_start(out=io["qT"][h, 0:128, t0:t0 + n], in_=sg[:, 0:n])], reads=[sk])
            for k in range(6):
                P.op("pe", lambda e, k=k, h=h: e.matmul(ppe[0:64, 0:n], wuq[:, k, h, 128:192], cqn[:, k, 0:n], start=(k == 0), stop=(k == 5)),
                     reads=["wuq", "cqn"], excl=["b_ppe"])
            for k in range(6):
                P.op("pe", lambda e, k=k, h=h: e.matmul(psw[0:64, 0:n], wuqsw[:, k, h, :], cqn[:, k, 0:n], start=(k == 0), stop=(k == 5)),
                     reads=["wuqsw", "cqn"], excl=["b_psw"])
            sg, sk = nxt_stg()
            rope(ppe[0:64, 0:n], psw[0:64, 0:n], n, ["b_ppe", "b_psw"], [], sg[0:64, 0:n], sk)
            P.dma("sp", sk, lambda e, sg=sg, h=h, t0=t0, n=n: [e.dma_start(out=io["qT"][h, 128:192, t0:t0 + n], in_=sg[0:64, 0:n])], reads=[sk])
            pb, pk = nxt_bank()
            for k in range(4):
                P.op("pe", lambda e, pb=pb, k=k, h=h: e.matmul(pb[:, 0:n], wukv[:, k, h, 0:128], ckvn[:, k, 0:n], start=(k == 0), stop=(k == 3)),
                     reads=["wukv", "ckvn"], excl=[pk])
            sg, sk = nxt_stg()
            P.op("act", lambda e, sg=sg, pb=pb: e.activation(out=sg[:, 0:n], in_=pb[:, 0:n], func=AF.Identity), writes=[sk], excl=[pk])
            P.dma("sp", sk, lambda e, sg=sg, h=h, t0=t0, n=n: [e.dma_start(out=io["kT"][h, :, t0:t0 + n], in_=sg[:, 0:n])], reads=[sk])
        for tt in range(n // 128):
            tile = (t0 + tt * 128) // 128
            for half in range(2):
                pb, pk = nxt_bank()
                for k in range(4):
                    P.op("pe", lambda e, pb=pb, k=k, tt=tt, half=half: e.matmul(
                        pb[:, :].rearrange("p (h d) -> p h d", h=4), ckvn[:, k, tt * 128:(tt + 1) * 128], wukv[:, k, 4 * half:4 * half + 4, 128:256],
                        start=(k == 0), stop=(k == 3)), reads=["wukv", "ckvn"], excl=[pk])
                sg, sk = nxt_stg()
                P.op("act", lambda e, sg=sg, pb=pb: e.activation(out=sg[:, :], in_=pb[:, :], func=AF.Identity), writes=[sk], excl=[pk])
                P.dma("sp", sk, lambda e, sg=sg, half=half, tile=tile: [e.dma_start(
                    out=io["vs"][4 * half:4 * half + 4, :, tile, :].rearrange("h p d -> p h d"),
                    in_=sg[:, :].rearrange("p (h d) -> p h d", h=4))], reads=[sk])

    for (t0, n) in token_blocks(nt):
        do_block(t0, n)
    P.pop()


def phase_B2(P, nc, io, nt, heads=range(NH)):
    P.push()
    nkt = nt // 128
    KT = P.sbuf("KT", [128, nt], BF16)
    KP = P.sbuf("KP", [64, nt], BF16)
    V = P.sbuf("V", [128, nkt, 128], BF16)
    QN = P.sbuf("QN", [128, nt], BF16)
    QP = P.sbuf("QP", [64, nt], BF16)
    onesb = P.sbuf("onesb", [128, 128], BF16)
    pbuf = [P.sbuf("pbuf%d" % i, [128, 512], BF16) for i in range(3)]
    sga = [P.sbuf("sga%d" % i, [128, 512], F32) for i in range(2)]
    rec = [P.sbuf("rec%d" % i, [128, 512], F32) for i in range(2)]
    ot = [P.sbuf("ot%d" % i, [128, 512], F32) for i in range(2)]
    ogs = [P.sbuf("ogs%d" % i, [128, 512], BF16) for i in range(2)]
    S = [P.psum("S%d" % i, [128, 512], F32) for i in range(3)]
    O = [P.psum("O%d" % i, [128, 512], F32) for i in range(2)]
    Dn = [P.psum("Dn%d" % i, [128, 512], F32) for i in range(2)]
    P.op("pool", lambda e: e.memset(onesb[:], 1.0), writes=["onesb"])
    P.dma("sp", "KP", lambda e: [e.dma_start(out=KP[:], in_=io["kpeT"])], writes=["KP"])
    ga0 = SEG_OFF["ga"] * 128
    si = 0
    qb = 0
    for h in heads:
        P.dma("sp", "KT", lambda e, h=h: [e.dma_start(out=KT[:], in_=io["kT"][h])], writes=["KT"])
        P.dma("sp", "V", lambda e, h=h: [e.dma_start(out=V[:], in_=io["vs"][h])], writes=["V"])
        P.dma("sp", "QN", lambda e, h=h: [e.dma_start(out=QN[:], in_=io["qT"][h, 0:128, :])], writes=["QN"])
        P.dma("sp", "QP", lambda e, h=h: [e.dma_start(out=QP[:], in_=io["qT"][h, 128:192, :])], writes=["QP"])
        def do_qblock(h, t0, n, qb, si):
            tiles = list(range(NCTX // 128)) if t0 < NCTX else list(range(nkt))
            o_, ok = O[qb % 2], "b_O%d" % (qb % 2)
            d_, dk = Dn[qb % 2], "b_Dn%d" % (qb % 2)
            sg_, sgk = sga[qb % 2], "sga%d" % (qb % 2)
            P.dma("sp", sgk, lambda e, sg_=sg_, h=h, t0=t0, n=n: [e.dma_start(out=sg_[:, 0:n], in_=io["pT"][ga0 + h * 128:ga0 + (h + 1) * 128, t0:t0 + n])],
                  writes=[sgk])

            def qk(kt, slot):
                s_, sk = S[slot % 3], "b_S%d" % (slot % 3)
                P.op("pe", lambda e: e.matmul(s_[:, 0:n], KT[:, kt * 128:(kt + 1) * 128], QN[:, t0:t0 + n], start=True, stop=False),
                     reads=["KT", "QN"], excl=[sk])
                P.op("pe", lambda e: e.matmul(s_[:, 0:n], KP[:, kt * 128:(kt + 1) * 128], QP[:, t0:t0 + n], start=False, stop=True),
                     reads=["KP", "QP"], excl=[sk])
                pb, pk = pbuf[slot % 3], "pbuf%d" % (slot % 3)
                P.op("act", lambda e: e.activation(out=pb[:, 0:n], in_=s_[:, 0:n], func=AF.Exp, scale=ATTN_SCALE), writes=[pk], excl=[sk])

            def pv(kt, slot, first, last):
                pb, pk = pbuf[slot % 3], "pbuf%d" % (slot % 3)
                P.op("pe", lambda e: e.matmul(o_[:, 0:n], V[:, kt, :], pb[:, 0:n], start=first, stop=last), reads=["V", pk], excl=[ok])
                P.op("pe", lambda e: e.matmul(d_[:, 0:n], onesb[:], pb[:, 0:n], start=first, stop=last), reads=["onesb", pk], excl=[dk])

            qk(tiles[0], si)
            for idx, kt in enumerate(tiles):
                if idx + 1 < len(tiles):
                    qk(tiles[idx + 1], si + idx + 1)
                pv(kt, si + idx, idx == 0, idx == len(tiles) - 1)
            r_, rk = rec[qb % 2], "rec%d" % (qb % 2)
            t_, tk = ot[qb % 2], "ot%d" % (qb % 2)
            g_, gk = ogs[qb % 2], "ogs%d" % (qb % 2)
            P.op("dve", lambda e, r_=r_, d_=d_, n=n: e.reciprocal(out=r_[:, 0:n], in_=d_[:, 0:n]), writes=[rk], excl=[dk])
            P.op("dve", lambda e, r_=r_, o_=o_, t_=t_, n=n: e.tensor_tensor(out=t_[:, 0:n], in0=o_[:, 0:n], in1=r_[:, 0:n], op=ALU.mult),
                 reads=[rk], writes=[tk], excl=[ok])
            P.op("pool", lambda e, t_=t_, sg_=sg_, g_=g_, n=n: e.tensor_tensor(out=g_[:, 0:n], in0=t_[:, 0:n], in1=sg_[:, 0:n], op=ALU.mult),
                 reads=[tk, sgk], writes=[gk])
            P.dma("sp", gk, lambda e, g_=g_, h=h, t0=t0, n=n: [e.dma_start(out=io["ogT"][h * 128:(h + 1) * 128, t0:t0 + n], in_=g_[:, 0:n])], reads=[gk])
            return len(tiles)

        for (t0, n) in token_blocks(nt):
            si += do_qblock(h, t0, n, qb, si)
            qb += 1
    P.pop()


NG = 2
HPG = 16
HD = 64
GW = HPG * HD


def seq_blocks(nt):
    out = [(0, NCTX, 0, NCTX)]
    t = NCTX
    while t < nt:
        n = min(512, nt - t)
        out.append((t, n, NCTX, nt))
        t += n
    return out


def phase_S1(P, nc, io, nt):
    P.push()
    pT = io["pT"]
    identf = P.sbuf("identf1", [128, 128], F32)
    convw = P.sbuf("convw", [128, 20, 5], F32)
    convb = P.sbuf("convb", [128, 20], F32)
    dtb = P.sbuf("dtb", [128, 1], F32)
    alog = P.sbuf("alog", [128, 1], F32)
    mulc = P.sbuf("mulc", [128, 1], F32)
    P.dma("sp", "identf1", lambda e: [e.dma_start(out=identf[:], in_=io["identf"])], writes=["identf1"])
    P.dma("sp", "convw", lambda e: [e.dma_start(out=convw[:], in_=io["convw"])], writes=["convw"])
    P.dma("sp", "convb", lambda e: [e.dma_start(out=convb[:], in_=io["convb"])], writes=["convb"])
    P.dma("sp", "dtb", lambda e: [e.dma_start(out=dtb[:], in_=io["dtbias2"])], writes=["dtb"])
    P.dma("sp", "alog", lambda e: [e.dma_start(out=alog[:], in_=io["alog2"])], writes=["alog"])
    P.op("act", lambda e: e.activation(out=mulc[:], in_=alog[:], func=AF.Exp), reads=["alog"], writes=["mulc"])
    P.op("dve", lambda e: e.tensor_scalar(out=mulc[:], in0=mulc[:], scalar1=-1.0, scalar2=None, op0=ALU.mult), reads=["mulc"], writes=["mulc"])
    P.op("dve", lambda e: e.memset(mulc[0:64, :], 1.0), reads=["mulc"], writes=["mulc"])

    cin = [P.sbuf("cin%d" % i, [128, 516], F32) for i in range(3)]
    acc = [P.sbuf("acc%d" % i, [128, 512], F32) for i in range(2)]
    cv = [P.sbuf("cv%d" % i, [128, 512], F32) for i in range(8)]
    stf = [P.sbuf("stf%d" % i, [128, 512], F32) for i in range(3)]
    stb = [P.sbuf("stb%d" % i, [128, 512], BF16) for i in range(3)]
    sp_ = [P.sbuf("sp%d" % i, [128, 512], F32) for i in range(4)]
    pt = [P.psum("pt%d" % i, [128, 512], F32) for i in range(3)]
    c = {"cin": 0, "acc": 0, "cv": 0, "stf": 0, "stb": 0, "pt": 0}

    def rot(name, arr):
        i = c[name] % len(arr)
        c[name] += 1
        return arr[i], "%s%d" % (name, i)

    xs0 = SEG_OFF["xs"] * 128
    b0 = SEG_OFF["B"] * 128
    c0 = SEG_OFF["C"] * 128
    d0 = SEG_OFF["dt"] * 128

    def conv_chunk(row0, cc, t0, n, s0, s1):
        ci, cik = rot("cin", cin)
        lo = t0 - 2 if t0 > s0 else t0
        hi = t0 + n + 2 if t0 + n < s1 else t0 + n
        if lo == t0:
            P.op("pool", lambda e: e.memset(ci[:, 0:2], 0.0), writes=[cik])
        if hi == t0 + n:
            P.op("pool", lambda e: e.memset(ci[:, n + 2:n + 4], 0.0), writes=[cik])
        P.dma("sp", cik, lambda e: [e.dma_start(out=ci[:, lo - (t0 - 2):hi - (t0 - 2)], in_=pT[row0:row0 + 128, lo:hi])], reads=[cik], writes=[cik])
        a, ak = rot("acc", acc)
        P.op("dve", lambda e: e.tensor_scalar(out=a[:, 0:n], in0=ci[:, 0:n], scalar1=convw[:, cc, 0:1], scalar2=None, op0=ALU.mult),
             reads=[cik, "convw"], writes=[ak])
        for k in range(1, 5):
            P.op("dve", lambda e, k=k: e.scalar_tensor_tensor(out=a[:, 0:n], in0=ci[:, k:k + n], scalar=convw[:, cc, k:k + 1], in1=a[:, 0:n],
                                                              op0=ALU.mult, op1=ALU.add), reads=[cik, "convw", ak], writes=[ak])
        o, ok = rot("cv", cv)
        P.op("act", lambda e: e.activation(out=o[:, 0:n], in_=a[:, 0:n], func=AF.Silu, bias=convb[:, cc:cc + 1]), reads=[ak, "convb"], writes=[ok])
        return o, ok

    def do_block(t0, n, s0, s1):
        ntile = n // 128
        for cg in range(4):
            bufs = [conv_chunk(xs0 + (cg * 4 + q) * 128, cg * 4 + q, t0, n, s0, s1) for q in range(4)]
            for tt in range(ntile):
                pb, pk = rot("pt", pt)
                for q in range(4):
                    o, ok = bufs[q]
                    P.op("pe", lambda e, pb=pb, q=q, o=o, tt=tt: e.transpose(pb[:, q * 128:(q + 1) * 128], o[:, tt * 128:(tt + 1) * 128], identf[:]),
                         reads=[ok, "identf1"], excl=["b_" + pk])
                sg, sk = rot("stf", stf)
                P.op("act", lambda e, sg=sg, pb=pb: e.activation(out=sg[:], in_=pb[:], func=AF.Identity), writes=[sk], excl=["b_" + pk])
                r0 = t0 + tt * 128
                P.dma("sp", sk, lambda e, sg=sg, r0=r0, cg=cg: [e.dma_start(out=io["xs_tm"][r0:r0 + 128, cg * 512:(cg + 1) * 512], in_=sg[:])], reads=[sk])
        for g in range(NG):
            o, ok = conv_chunk(b0 + g * 128, 16 + g, t0, n, s0, s1)
            sg, sk = rot("stb", stb)
            P.op("act", lambda e, sg=sg, o=o: e.activation(out=sg[:, 0:n], in_=o[:, 0:n], func=AF.Identity), reads=[ok], writes=[sk])
            P.dma("sp", sk, lambda e, sg=sg, g=g: [e.dma_start(out=io["BT"][g, :, t0:t0 + n], in_=sg[:, 0:n])], reads=[sk])
            pb, pk = rot("pt", pt)
            for tt in range(ntile):
                P.op("pe", lambda e, pb=pb, o=o, tt=tt: e.transpose(pb[:, tt * 128:(tt + 1) * 128], o[:, tt * 128:(tt + 1) * 128], identf[:]),
                     reads=[ok, "identf1"], excl=["b_" + pk])
            sg2, sk2 = rot("stb", stb)
            P.op("act", lambda e, sg2=sg2, pb=pb: e.activation(out=sg2[:, 0:n], in_=pb[:, 0:n], func=AF.Identity), writes=[sk2], excl=["b_" + pk])
            P.dma("sp", sk2, lambda e, sg2=sg2, g=g: [e.dma_start(out=io["B_tm"][g, t0:t0 + n, :].rearrange("(tt p) m -> p tt m", p=128),
                                                                in_=sg2[:, 0:n].rearrange("p (tt m) -> p tt m", m=128))], reads=[sk2])
        for g in range(NG):
            o, ok = conv_chunk(c0 + g * 128, 18 + g, t0, n, s0, s1)
            sg, sk = rot("stb", stb)
            P.op("act", lambda e, sg=sg, o=o: e.activation(out=sg[:, 0:n], in_=o[:, 0:n], func=AF.Identity), reads=[ok], writes=[sk])
            P.dma("sp", sk, lambda e, sg=sg, g=g: [e.dma_start(out=io["CT"][g, :, t0:t0 + n], in_=sg[:, 0:n])], reads=[sk])
        xb, l1, l2, dd = sp_
        P.dma("sp", "sp0", lambda e: [e.dma_start(out=xb[0:64, 0:n], in_=pT[d0:d0 + 64, t0:t0 + n]),
                                      e.dma_start(out=xb[64:128, 0:n], in_=pT[d0:d0 + 64, t0:t0 + n])], writes=["sp0"], n=2)
        P.op("act", lambda e: e.activation(out=xb[:, 0:n], in_=xb[:, 0:n], func=AF.Identity, bias=dtb[:, 0:1]), reads=["sp0", "dtb"], writes=["sp0"])
        P.op("act", lambda e: e.activation(out=l1[:, 0:n], in_=xb[:, 0:n], func=AF.Abs), reads=["sp0"], writes=["sp1"])
        P.op("act", lambda e: e.activation(out=l1[:, 0:n], in_=l1[:, 0:n], func=AF.Exp, scale=-1.0), reads=["sp1"], writes=["sp1"])
        P.op("dve", lambda e: e.tensor_scalar(out=l1[:, 0:n], in0=l1[:, 0:n], scalar1=1.0, scalar2=None, op0=ALU.add), reads=["sp1"], writes=["sp1"])
        P.op("act", lambda e: e.activation(out=l2[:, 0:n], in_=l1[:, 0:n], func=AF.Ln), reads=["sp1"], writes=["sp2"])
        P.op("dve", lambda e: e.scalar_tensor_tensor(out=l2[:, 0:n], in0=xb[:, 0:n], scalar=0.0, in1=l2[:, 0:n], op0=ALU.max, op1=ALU.add),
             reads=["sp0", "sp2"], writes=["sp2"])
        P.op("dve", lambda e: e.tensor_scalar(out=dd[:, 0:n], in0=l2[:, 0:n], scalar1=mulc[:, 0:1], scalar2=None, op0=ALU.mult),
             reads=["sp2", "mulc"], writes=["sp3"])
        pb, pk = rot("pt", pt)
        for tt in range(ntile):
            P.op("pe", lambda e, pb=pb, tt=tt: e.transpose(pb[:, tt * 128:(tt + 1) * 128], dd[:, tt * 128:(tt + 1) * 128], identf[:]),
                 reads=["sp3", "identf1"], excl=["b_" + pk])
        sg, sk = rot("stf", stf)
        P.op("act", lambda e, sg=sg, pb=pb: e.activation(out=sg[:, 0:n], in_=pb[:, 0:n], func=AF.Identity), writes=[sk], excl=["b_" + pk])
        P.dma("sp", sk, lambda e, sg=sg: [e.dma_start(out=io["dtt"][t0:t0 + n, :].rearrange("(tt p) m -> p tt m", p=128),
                                                        in_=sg[:, 0:n].rearrange("p (tt m) -> p tt m", m=128))], reads=[sk])

    for (t0, n, s0, s1) in seq_blocks(nt):
        do_block(t0, n, s0, s1)
    P.pop()


def phase_S2(P, nc, io, nt, nsteps=None):
    P.push()
    nck = nt // 128
    nctx = NCTX // 128
    order = {0: list(range(nck)), 1: list(range(nctx - 1, -1, -1)) + list(range(nck - 1, nctx - 1, -1))}
    cm = {}
    for nm in ("TL", "TG", "SG", "SL", "onesf"):
        cm[nm] = P.sbuf("m_" + nm, [128, 128], F32)
        P.dma("sp", "m_" + nm, lambda e, nm=nm: [e.dma_start(out=cm[nm][:], in_=io[nm])], writes=["m_" + nm])
    TRI = {0: "TL", 1: "TG"}
    UU = {0: "SG", 1: "SL"}
    Sst = {}
    Sbf = {}
    for d in range(2):
        for g in range(NG):
            Sst[d, g] = P.sbuf("Sst%d%d" % (d, g), [128, GW], F32)
            Sbf[d, g] = P.sbuf("Sbf%d%d" % (d, g), [128, GW], BF16)
            P.op("pool", lambda e, d=d, g=g: e.memset(Sst[d, g][:], 0.0), writes=["Sst%d%d" % (d, g)])
            P.op("pool", lambda e, d=d, g=g: e.memset(Sbf[d, g][:], 0.0), writes=["Sbf%d%d" % (d, g)])
    NW = 2
    W = []
    for w in range(NW):
        W.append(dict(
            xs=P.sbuf("w%d_xs" % w, [128, GW], F32), xdt=P.sbuf("w%d_xdt" % w, [128, GW], BF16), xw=P.sbuf("w%d_xw" % w, [128, GW], BF16),
            R=P.sbuf("w%d_R" % w, [128, HPG, 128], F32), E=P.sbuf("w%d_E" % w, [128, HPG, 128], F32), MT=P.sbuf("w%d_MT" % w, [128, HPG, 128], BF16),
            mcb=P.sbuf("w%d_mcb" % w, [128, 128], F32), bt=P.sbuf("w%d_bt" % w, [128, 128], BF16), ct=P.sbuf("w%d_ct" % w, [128, 128], BF16),
            btm=P.sbuf("w%d_btm" % w, [128, 128], BF16), dtt=P.sbuf("w%d_dtt" % w, [128, 128], F32),
            ex3=P.sbuf("w%d_ex3" % w, [128, 48], F32), wv=P.sbuf("w%d_wv" % w, [128, 16], F32),
            ysb=P.sbuf("w%d_ysb" % w, [128, GW], F32), tmp=P.sbuf("w%d_tmp" % w, [128, GW], F32), yo=P.sbuf("w%d_yo" % w, [128, GW], F32),
        ))
    G = [P.psum("G%d" % i, [128, 512], F32) for i in range(2)]
    Y = [P.psum("Y%d" % i, [128, 512], F32) for i in range(2)]
    Z = [P.psum("Z%d" % i, [128, 512], F32) for i in range(2)]
    SM = P.psum("SM", [128, 512], F32)

    def step(si, d, g, ck):
        w = si % NW
        B = W[w]
        K = lambda s: "w%d_%s" % (w, s)
        tok = slice(ck * 128, (ck + 1) * 128)
        sk_st, sk_bf = "Sst%d%d" % (d, g), "Sbf%d%d" % (d, g)
        st, sb = Sst[d, g], Sbf[d, g]
        dcol = d * 32 + g * 16
        P.dma("sp", K("xs"), lambda e: [e.dma_start(out=B["xs"][:], in_=io["xs_tm"][tok, g * GW:(g + 1) * GW])], writes=[K("xs")])
        P.dma("sp", K("bt"), lambda e: [e.dma_start(out=B["bt"][:], in_=io["BT"][g, :, tok])], writes=[K("bt")])
        P.dma("sp", K("ct"), lambda e: [e.dma_start(out=B["ct"][:], in_=io["CT"][g, :, tok])], writes=[K("ct")])
        P.dma("sp", K("btm"), lambda e: [e.dma_start(out=B["btm"][:], in_=io["B_tm"][g, tok, :])], writes=[K("btm")])
        P.dma("sp", K("dtt"), lambda e: [e.dma_start(out=B["dtt"][:], in_=io["dtt"][tok, :])], writes=[K("dtt")])
        dt_ap = B["dtt"][:, dcol:dcol + 16]
        dta_ap = B["dtt"][:, 64 + dcol:64 + dcol + 16]
        tri, uu = cm[TRI[d]], cm[UU[d]]
        P.op("pool", lambda e: e.tensor_tensor(out=B["R"][:], in0=tri[:].unsqueeze(1).broadcast_to([128, HPG, 128]),
                                               in1=dta_ap.unsqueeze(2).broadcast_to([128, HPG, 128]), op=ALU.mult),
             reads=[K("dtt"), "m_" + TRI[d]], writes=[K("R")])
        P.op("pe", lambda e: e.matmul(SM[:, 0:128], B["bt"][:], B["ct"][:], start=True, stop=True), reads=[K("bt"), K("ct")], excl=["b_SM"])
        P.op("pe", lambda e: e.matmul(SM[:, 128:144], tri[:], dta_ap, start=True, stop=True), reads=[K("dtt"), "m_" + TRI[d]], excl=["b_SM"])
        P.op("pe", lambda e: e.matmul(SM[:, 144:160], uu[:], dta_ap, start=True, stop=True), reads=[K("dtt"), "m_" + UU[d]], excl=["b_SM"])
        P.op("pe", lambda e: e.matmul(SM[:, 160:176], cm["onesf"][:], dta_ap, start=True, stop=True), reads=[K("dtt"), "m_onesf"], excl=["b_SM"])
        P.op("dve", lambda e: e.tensor_tensor(out=B["mcb"][:], in0=SM[:, 0:128], in1=tri[:], op=ALU.mult), reads=["m_" + TRI[d]], writes=[K("mcb")], excl=["b_SM"])
        P.op("act", lambda e: e.activation(out=B["ex3"][:], in_=SM[:, 128:176], func=AF.Exp), writes=[K("ex3")], excl=["b_SM"])
        P.op("dve", lambda e: e.tensor_tensor(out=B["wv"][:], in0=B["ex3"][:, 16:32], in1=dt_ap, op=ALU.mult), reads=[K("ex3"), K("dtt")], writes=[K("wv")])
        for q in range(4):
            gb, gk = G[q % 2], "b_G%d" % (q % 2)
            P.op("pe", lambda e, gb=gb, q=q: e.matmul(gb[:, :].rearrange("p (h l) -> p h l", h=4), uu[:], B["R"][:, 4 * q:4 * q + 4, :], start=True, stop=True),
                 reads=[K("R"), "m_" + UU[d]], excl=[gk])
            P.op("act", lambda e, gb=gb, q=q: e.activation(out=B["E"][:, 4 * q:4 * q + 4, :], in_=gb[:, :].rearrange("p (h l) -> p h l", h=4), func=AF.Exp),
                 writes=[K("E")], excl=[gk])
        P.op("dve", lambda e: e.tensor_tensor(out=B["MT"][:], in0=B["E"][:], in1=B["mcb"][:].unsqueeze(1).broadcast_to([128, HPG, 128]), op=ALU.mult),
             reads=[K("E"), K("mcb")], writes=[K("MT")])
        xs3 = B["xs"][:, :].rearrange("p (h d) -> p h d", h=HPG)
        P.op("dve", lambda e: e.tensor_tensor(out=B["xdt"][:, :].rearrange("p (h d) -> p h d", h=HPG), in0=xs3,
                                              in1=dt_ap.unsqueeze(2).broadcast_to([128, HPG, HD]), op=ALU.mult),
             reads=[K("xs"), K("dtt")], writes=[K("xdt")])
        P.op("pool", lambda e: e.tensor_tensor(out=B["xw"][:, :].rearrange("p (h d) -> p h d", h=HPG), in0=xs3,
                                               in1=B["wv"][:].unsqueeze(2).broadcast_to([128, HPG, HD]), op=ALU.mult),
             reads=[K("xs"), K("wv")], writes=[K("xw")])
        for h in range(HPG):
            yb, yk = Y[h // 8], "b_Y%d" % (h // 8)
            P.op("pe", lambda e, yb=yb, h=h: e.matmul(yb[:, (h % 8) * HD:(h % 8 + 1) * HD], B["MT"][:, h, :], B["xdt"][:, h * HD:(h + 1) * HD], start=True, stop=True),
                 reads=[K("MT"), K("xdt")], excl=[yk])
        for hf in range(2):
            P.op("pe", lambda e, hf=hf: e.matmul(Z[hf][:, :], B["ct"][:], sb[:, hf * 512:(hf + 1) * 512], start=True, stop=True),
                 reads=[K("ct"), sk_bf], excl=["b_Z%d" % hf])
        for hf in range(2):
            P.op("act", lambda e, hf=hf: e.activation(out=B["ysb"][:, hf * 512:(hf + 1) * 512], in_=Y[hf][:, :], func=AF.Identity),
                 writes=[K("ysb") + str(hf)], excl=["b_Y%d" % hf])
            P.op("dve", lambda e, hf=hf: e.tensor_tensor(out=B["tmp"][:, hf * 512:(hf + 1) * 512].rearrange("p (h d) -> p h d", h=8),
                                                         in0=Z[hf][:, :].rearrange("p (h d) -> p h d", h=8),
                                                         in1=B["ex3"][:, 8 * hf:8 * hf + 8].unsqueeze(2).broadcast_to([128, 8, HD]), op=ALU.mult),
                 reads=[K("ex3")], writes=[K("tmp") + str(hf)], excl=["b_Z%d" % hf])
        P.op("pool", lambda e: e.tensor_tensor(out=B["yo"][:], in0=B["tmp"][:], in1=B["ysb"][:], op=ALU.add),
             reads=[K("tmp") + "0", K("tmp") + "1", K("ysb") + "0", K("ysb") + "1"], writes=[K("yo")])
        P.dma("pool", K("yo"), lambda e: [e.dma_start(out=io["yd"][d, tok, g * GW:(g + 1) * GW], in_=B["yo"][:])], reads=[K("yo")])
        for hf in range(2):
            P.op("pe", lambda e, hf=hf: e.matmul(Z[hf][:, :], B["btm"][:], B["xw"][:, hf * 512:(hf + 1) * 512], start=True, stop=True),
                 reads=[K("btm"), K("xw")], excl=["b_Z%d" % hf])
        P.op("dve", lambda e: e.tensor_tensor(out=st[:, :].rearrange("p (h d) -> p h d", h=HPG), in0=st[:, :].rearrange("p (h d) -> p h d", h=HPG),
                                              in1=B["ex3"][:, 32:48].unsqueeze(2).broadcast_to([128, HPG, HD]), op=ALU.mult),
             reads=[K("ex3"), sk_st], writes=[sk_st])
        for hf in range(2):
            P.op("dve", lambda e, hf=hf: e.tensor_tensor(out=st[:, hf * 512:(hf + 1) * 512], in0=st[:, hf * 512:(hf + 1) * 512], in1=Z[hf][:, :], op=ALU.add),
                 reads=[sk_st], writes=[sk_st], excl=["b_Z%d" % hf])
        P.op("act", lambda e: e.activation(out=sb[:], in_=st[:], func=AF.Identity), reads=[sk_st], writes=[sk_bf])

    si = 0
    ns = nck if nsteps is None else nsteps
    for i in range(ns):
        for d in range(2):
            for g in range(NG):
                step(si, d, g, order[d][i])
                si += 1
    P.pop()


def phase_S3(P, nc, io, nt):
    P.push()
    identf = P.sbuf("identf3", [128, 128], F32)
    onesf = P.sbuf("onesf3", [128, 128], F32)
    dsk = P.sbuf("dsk", [128, 2048], F32)
    P.dma("sp", "identf3", lambda e: [e.dma_start(out=identf[:], in_=io["identf"])], writes=["identf3"])
    P.dma("sp", "onesf3", lambda e: [e.dma_start(out=onesf[:], in_=io["onesf"])], writes=["onesf3"])
    P.dma("sp", "dsk", lambda e: [e.dma_start(out=dsk[:], in_=io["dskbc"])], writes=["dsk"])
    gssm = P.sbuf("gssm", [128, 16], F32)
    P.dma("sp", "gssm", lambda e: [e.dma_start(out=gssm[:], in_=io["gssmT"])], writes=["gssm"])
    yf = [P.sbuf("yf%d" % i, [128, 2048], F32) for i in range(2)]
    yb = [P.sbuf("yb%d" % i, [128, 2048], F32) for i in range(2)]
    xs = [P.sbuf("xs3_%d" % i, [128, 2048], F32) for i in range(2)]
    sz = [P.sbuf("sz%d" % i, [128, 16, 128], F32) for i in range(2)]
    yg = [P.sbuf("yg%d" % i, [128, 512], F32) for i in range(2)]
    sq = [P.sbuf("sq3_%d" % i, [128, 512], F32) for i in range(2)]
    ygs = [P.sbuf("ygs%d" % i, [128, 16, 512], BF16) for i in range(2)]
    ssr = P.sbuf("ssr", [1, 512], F32)
    pt = [P.psum("pt3_%d" % i, [128, 512], F32) for i in range(2)]
    pq = P.psum("pq", [128, 512], F32)
    z0 = SEG_OFF["z"] * 128
    nck = nt // 128
    qi = 0
    blocks = token_blocks(nt)
    for (t0, n) in blocks:
        bi = (t0 // 512) % 2 if t0 >= NCTX else 0
        gs, gsk = ygs[bi % 2], "ygs%d" % (bi % 2)
        for tt in range(n // 128):
            ck = (t0 + tt * 128) // 128
            i2 = ck % 2
            tok = slice(ck * 128, (ck + 1) * 128)
            a, b_, x_, s_ = yf[i2], yb[i2], xs[i2], sz[i2]
            ka, kb, kx, ks = "yf%d" % i2, "yb%d" % i2, "xs3_%d" % i2, "sz%d" % i2
            P.dma("sp", ka, lambda e, a=a, tok=tok: [e.dma_start(out=a[:], in_=io["yd"][0, tok, :])], writes=[ka])
            P.dma("sp", kb, lambda e, b_=b_, tok=tok: [e.dma_start(out=b_[:], in_=io["yd"][1, tok, :])], writes=[kb])
            P.dma("sp", kx, lambda e, x_=x_, tok=tok: [e.dma_start(out=x_[:], in_=io["xs_tm"][tok, :])], writes=[kx])
            P.dma("sp", ks, lambda e, s_=s_, tok=tok: [e.dma_start(out=s_[:], in_=io["pT"][z0:z0 + 2048, tok].rearrange("(c p) t -> p c t", p=128))], writes=[ks])
            P.op("dve", lambda e, a=a, b_=b_: e.tensor_tensor(out=a[:], in0=a[:], in1=b_[:], op=ALU.add), reads=[ka, kb], writes=[ka])
            P.op("pool", lambda e, x_=x_: e.tensor_tensor(out=x_[:], in0=x_[:], in1=dsk[:], op=ALU.mult), reads=[kx, "dsk"], writes=[kx])
            P.op("pool", lambda e, a=a, x_=x_: e.tensor_tensor(out=a[:], in0=a[:], in1=x_[:], op=ALU.add), reads=[ka, kx], writes=[ka])
            for q in range(4):
                pb, pk = pt[qi % 2], "b_pt3_%d" % (qi % 2)
                g_, gk = yg[qi % 2], "yg%d" % (qi % 2)
                s2, s2k = sq[qi % 2], "sq3_%d" % (qi % 2)
                qi += 1
                for c4 in range(4):
                    c = q * 4 + c4
                    P.op("pe", lambda e, pb=pb, a=a, c=c, c4=c4: e.transpose(pb[:, c4 * 128:(c4 + 1) * 128], a[:, c * 128:(c + 1) * 128], identf[:]),
                         reads=[ka, "identf3"], excl=[pk])
                P.op("dve", lambda e, pb=pb, g_=g_, s_=s_, q=q: e.tensor_tensor(out=g_[:, :].rearrange("p (c t) -> p c t", c=4),
                                                                              in0=pb[:, :].rearrange("p (c t) -> p c t", c=4),
                                                                              in1=s_[:, 4 * q:4 * q + 4, :], op=ALU.mult),
                     reads=[ks], writes=[gk], excl=[pk])
                for c4 in range(4):
                    c = q * 4 + c4
                    P.op("act", lambda e, g_=g_, gs=gs, c=c, c4=c4, tt=tt: e.activation(out=gs[:, c, tt * 128:(tt + 1) * 128],
                                                                                      in_=g_[:, c4 * 128:(c4 + 1) * 128], func=AF.Identity, scale=gssm[:, c:c + 1]),
                         reads=[gk, "gssm"], writes=[(gsk, tt, q, c4)])
                P.op("act", lambda e, g_=g_, s2=s2: e.activation(out=s2[:], in_=g_[:], func=AF.Square), reads=[gk], writes=[s2k])
                for c4 in range(4):
                    first = (q == 0 and c4 == 0)
                    last = (q == 3 and c4 == 3)
                    P.op("pe", lambda e, s2=s2, c4=c4, tt=tt, first=first, last=last: e.matmul(pq[0:1, tt * 128:(tt + 1) * 128], onesf[:, 0:1], s2[:, c4 * 128:(c4 + 1) * 128],
                                                                                             start=first, stop=last),
                         reads=[s2k, "onesf3"], excl=["b_pq"])
        P.op("dve", lambda e, n=n: e.tensor_copy(out=ssr[:, 0:n], in_=pq[0:1, 0:n]), writes=["ssr"], excl=["b_pq"])
        P.dma("sp", "ssr", lambda e, t0=t0, n=n: [e.dma_start(out=io["ssq"][:, t0:t0 + n], in_=ssr[:, 0:n])], reads=["ssr"])
        P.dma("sp", gsk, lambda e, gs=gs, t0=t0, n=n: [e.dma_start(out=io["ygT_parts"][hf][:, t0:t0 + n].rearrange("(c p) t -> p c t", p=128), in_=gs[:, 8 * hf:8 * hf + 8, 0:n]) for hf in range(2)],
              n=2, reads=[(gsk, tt, q, c4) for tt in range(n // 128) for q in range(4) for c4 in range(4)])
    P.pop()


TC = 2112
NLAT = 2048


def c_blocks(with_ctx):
    blks = [(t0, 256, 0) for t0 in range(0, NLAT, 256)]
    if with_ctx:
        blks.append((NLAT, 64, 1))
    return blks


def phase_Cmod(P, nc, io, wadag_fn):
    P.push()
    c2 = P.sbuf("cm_c2", [128, KC, 2], F32)
    sc = P.sbuf("cm_sc", [128, KC, 2], F32)
    scb = P.sbuf("cm_scb", [128, 2, KC, 128], F32)
    bb = P.sbuf("cm_bb", [128, D], F32)
    wg = [P.sbuf("cm_wg%d" % i, [128, KC, 256], F32) for i in range(2)]
    sg = [P.sbuf("cm_sg%d" % i, [128, 256], F32) for i in range(2)]
    pg = [P.psum("cm_pg%d" % i, [128, 512], F32) for i in range(2)]
    P.dma("sp", "cm_c2", lambda e: [e.dma_start(out=c2[:], in_=io["c2T"])], writes=["cm_c2"])
    P.dma("sp", "cm_bb", lambda e: [e.dma_start(out=bb[:], in_=io["badag"][0:1, :].broadcast_to([128, D]))], writes=["cm_bb"])
    P.op("act", lambda e: e.activation(out=sc[:], in_=c2[:], func=AF.Silu), reads=["cm_c2"], writes=["cm_sc"])
    for v in range(2):
        P.op("dve", lambda e, v=v: e.tensor_copy(out=scb[:, v], in_=sc[:, :, v].unsqueeze(2).broadcast_to([128, KC, 128])),
             reads=["cm_sc"], writes=["cm_scb"])
    si = 0
    for cb in range(16):
        w_, wk = wg[cb % 2], "cm_wg%d" % (cb % 2)
        wp_ = wadag_fn(cb)
        P.dma("sp", wk, lambda e, w_=w_, wp_=wp_: dma_pieces(e, w_, wp_, "p (k n) -> p k n", k=KC), writes=[wk], n=len(wp_))
        for v in range(2):
            for k in range(KC):
                P.op("pe", lambda e, v=v, k=k, w_=w_: e.matmul(pg[v][:, 0:256], scb[:, v, k, :], w_[:, k, :], start=(k == 0), stop=(k == KC - 1)),
                     reads=[wk, "cm_scb"], excl=["b_cm_pg%d" % v])
            s_, sk = sg[si % 2], "cm_sg%d" % (si % 2)
            si += 1
            P.op("dve", lambda e, v=v, s_=s_, cb=cb: e.tensor_tensor(out=s_[:], in0=pg[v][:, 0:256], in1=bb[:, cb * 256:(cb + 1) * 256], op=ALU.add),
                 reads=["cm_bb"], writes=[sk], excl=["b_cm_pg%d" % v])
            P.dma("sp", sk, lambda e, v=v, s_=s_, cb=cb: [e.dma_start(out=io["gsc"][v, :, cb * 256:(cb + 1) * 256], in_=s_[:])], reads=[sk])
    P.pop()


def phase_C(P, nc, io, wpa_fn, wpb_fn, wout_fn, with_ctx, final, dyn):
    P.push()
    mrg = P.sbuf("c_mrg", [128, KC, 256], BF16)
    ones4 = P.sbuf("c_ones4", [4, 128], F32)
    P.op("pool", lambda e: e.memset(ones4[:], 1.0), writes=["c_ones4"])
    for (t0, n, var) in c_blocks(with_ctx):
        P.push()
        ogb = P.sbuf("c_ogb", [128, KC, 256], BF16)
        ygb = P.sbuf("c_ygb", [128, 64, 256], BF16)
        ss4 = P.sbuf("c_ss4", [4, 256], F32)
        rs = P.sbuf("c_rs", [128, 3, 256], F32)
        wa = [P.sbuf("c_wa%d" % i, [128, KC, 128], BF16) for i in range(2)]
        wb = [P.sbuf("c_wb%d" % i, [128, 64, 128], BF16) for i in range(2)]
        sga = [P.sbuf("c_sga%d" % i, [128, 256], F32) for i in range(2)]
        sgb = [P.sbuf("c_sgb%d" % i, [128, 256], F32) for i in range(2)]
        m1 = [P.sbuf("c_m1%d" % i, [128, 256], F32) for i in range(2)]
        m2 = [P.sbuf("c_m2%d" % i, [128, 256], F32) for i in range(2)]
        pa = [P.psum("c_pa%d" % i, [128, 512], F32) for i in range(2)]
        pb = [P.psum("c_pb%d" % i, [128, 512], F32) for i in range(2)]
        pr = P.psum("c_pr", [128, 512], F32)

        def cols(ap, t0=t0, n=n):
            return ap[:, t0:t0 + n]

        P.dma("sp", "c_ogb", lambda e, n=n, cols=cols: [e.dma_start(out=ogb[:, :, 0:n], in_=cols(io["oga2"]).rearrange("(k p) t -> p k t", p=128))], writes=["c_ogb"])
        P.dma("sp", "c_ygb", lambda e, n=n, cols=cols: [e.dma_start(out=ygb[:, 32 * q:32 * q + 32, 0:n],
                                                                   in_=cols(io["yga_parts"][q]).rearrange("(k p) t -> p k t", p=128))
                                                       for q in range(2)], writes=["c_ygb"], n=2)
        P.dma("sp", "c_ss4", lambda e, n=n, cols=cols: [e.dma_start(out=ss4[:, 0:n], in_=cols(io["ssqa"]))], writes=["c_ss4"])
        P.op("pe", lambda e, n=n: e.matmul(pr[:, 0:n], ones4[:], ss4[:, 0:n], start=True, stop=True), reads=["c_ones4", "c_ss4"], excl=["b_c_pr"])
        P.op("dve", lambda e, n=n: e.tensor_scalar(out=rs[:, 0, 0:n], in0=pr[:, 0:n], scalar1=1.0 / 8192.0, scalar2=EPS, op0=ALU.mult, op1=ALU.add),
             writes=["c_rs0"], excl=["b_c_pr"])
        P.op("act", lambda e, n=n: e.activation(out=rs[:, 1, 0:n], in_=rs[:, 0, 0:n], func=AF.Sqrt), reads=["c_rs0"], writes=["c_rs1"])
        P.op("dve", lambda e, n=n: e.reciprocal(out=rs[:, 2, 0:n], in_=rs[:, 1, 0:n]), reads=["c_rs1"], writes=["c_rs2"])
        for oc in range(KC):
            i2 = oc % 2
            wpa_, wpb_ = wpa_fn(oc), wpb_fn(oc)
            P.dma("pool", "c_wa%d" % i2, lambda e, i2=i2, wpa_=wpa_: dma_pieces(e, wa[i2], wpa_, "p (k n) -> p k n", k=KC), writes=["c_wa%d" % i2], n=len(wpa_))
            P.dma("pool", "c_wb%d" % i2, lambda e, i2=i2, wpb_=wpb_: dma_pieces(e, wb[i2], wpb_, "p (k n) -> p k n", k=64), writes=["c_wb%d" % i2], n=len(wpb_))
            P.dma("sp", "c_sga%d" % i2, lambda e, oc=oc, i2=i2, t0=t0, n=n: [e.dma_start(out=sga[i2][:, 0:n], in_=io["sgT"][oc * 128:(oc + 1) * 128, t0:t0 + n])], writes=["c_sga%d" % i2])
            P.dma("sp", "c_sgb%d" % i2, lambda e, oc=oc, i2=i2, t0=t0, n=n: [e.dma_start(out=sgb[i2][:, 0:n], in_=io["sgT"][D + oc * 128:D + (oc + 1) * 128, t0:t0 + n])], writes=["c_sgb%d" % i2])
            for k in range(KC):
                P.op("pe", lambda e, k=k, i2=i2, n=n: e.matmul(pa[i2][:, 0:n], wa[i2][:, k, :], ogb[:, k, 0:n], start=(k == 0), stop=(k == KC - 1)),
                     reads=["c_wa%d" % i2, "c_ogb"], excl=["b_c_pa%d" % i2])
            for k in range(64):
                P.op("pe", lambda e, k=k, i2=i2, n=n: e.matmul(pb[i2][:, 0:n], wb[i2][:, k, :], ygb[:, k, 0:n], start=(k == 0), stop=(k == 63)),
                     reads=["c_wb%d" % i2, "c_ygb"], excl=["b_c_pb%d" % i2])
            P.op("dve", lambda e, i2=i2, n=n: e.tensor_tensor(out=m1[i2][:, 0:n], in0=pa[i2][:, 0:n], in1=sga[i2][:, 0:n], op=ALU.mult),
                 reads=["c_sga%d" % i2], writes=["c_m1%d" % i2], excl=["b_c_pa%d" % i2])
            P.op("dve", lambda e, i2=i2, n=n: e.tensor_tensor(out=m2[i2][:, 0:n], in0=pb[i2][:, 0:n], in1=rs[:, 2, 0:n], op=ALU.mult),
                 reads=["c_rs2"], writes=["c_m2%d" % i2], excl=["b_c_pb%d" % i2])
            P.op("pool", lambda e, i2=i2, n=n: e.tensor_tensor(out=m2[i2][:, 0:n], in0=m2[i2][:, 0:n], in1=sgb[i2][:, 0:n], op=ALU.mult),
                 reads=["c_sgb%d" % i2, "c_m2%d" % i2], writes=["c_m2%d" % i2])
            P.op("pool", lambda e, i2=i2, n=n, oc=oc: e.tensor_tensor(out=mrg[:, oc, 0:n], in0=m1[i2][:, 0:n], in1=m2[i2][:, 0:n], op=ALU.add),
                 reads=["c_m1%d" % i2, "c_m2%d" % i2], writes=["c_mrg"])
        P.pop()
        P.push()
        wo = [P.sbuf("c_wo%d" % i, [128, KC, 512], BF16) for i in range(2)]
        hrow = [P.sbuf("c_hrow%d" % i, [128, D], F32) for i in range(2)]
        gb = P.sbuf("c_gb", [128, D], F32)
        ht = [P.sbuf("c_ht%d" % i, [128, 512], F32) for i in range(2)]
        tmp = [P.sbuf("c_tmp%d" % i, [128, 512], F32) for i in range(2)]
        po = [P.psum("c_po%d" % i, [128, 512], F32) for i in range(2)]
        P.dma("sp", "c_gb", lambda e, var=var: [e.dma_start(out=gb[:], in_=io["gsc"][var])], writes=["c_gb"])
        if final:
            junk = P.sbuf("c_junk", [128, D], BF16)
            gf = P.sbuf("c_gf", [128, D], F32)
            st = P.sbuf("c_st", [128, 8], F32)
            P.dma("sp", "c_gf", lambda e: [e.dma_start(out=gf[:], in_=io["gfin"][0:1, :].broadcast_to([128, D]))], writes=["c_gf"])
        tiles = [(0, 128), (128, 128)] if n == 256 else [(0, n)]
        ci = 0
        for ob in range(8):
            w_, wk = wo[ob % 2], "c_wo%d" % (ob % 2)
            wo_ = wout_fn(ob)
            P.dma("pool", wk, lambda e, w_=w_, wo_=wo_: dma_pieces(e, w_, wo_, "p (k n) -> p k n", k=KC), writes=[wk], n=len(wo_))
            for ti, (c0, rows) in enumerate(tiles):
                p_, pk = po[ci % 2], "b_c_po%d" % (ci % 2)
                h_, hk = ht[ci % 2], "c_ht%d" % (ci % 2)
                t_, tk = tmp[ci % 2], "c_tmp%d" % (ci % 2)
                ci += 1
                P.dma("sp", hk, lambda e, h_=h_, t0=t0, c0=c0, rows=rows, ob=ob: [e.dma_start(out=h_[0:rows, :], in_=io["hin_rows"](t0 + c0, rows)[:, ob * 512:(ob + 1) * 512])], writes=[hk])
                for k in range(KC):
                    P.op("pe", lambda e, p_=p_, k=k, c0=c0, rows=rows, w_=w_: e.matmul(p_[0:rows, :], mrg[:, k, c0:c0 + rows], w_[:, k, :], start=(k == 0), stop=(k == KC - 1)),
                         reads=["c_mrg", wk], excl=[pk])
                P.op("dve", lambda e, p_=p_, t_=t_, rows=rows, ob=ob: e.tensor_tensor(out=t_[0:rows, :], in0=p_[0:rows, :], in1=gb[0:rows, ob * 512:(ob + 1) * 512], op=ALU.mult),
                     reads=["c_gb"], writes=[tk], excl=[pk])
                P.op("pool", lambda e, t_=t_, h_=h_, rows=rows, ob=ob, ti=ti: e.tensor_tensor(out=hrow[ti][0:rows, ob * 512:(ob + 1) * 512], in0=t_[0:rows, :], in1=h_[0:rows, :], op=ALU.add),
                     reads=[tk, hk], writes=[("c_hrow", ti, ob)])
        for ti, (c0, rows) in enumerate(tiles):
            hkeys = [("c_hrow", ti, ob) for ob in range(8)]
            if not final:
                P.dma("sp", "c_hst%d" % ti, lambda e, ti=ti, t0=t0, c0=c0, rows=rows: [e.dma_start(out=io["hout_rows"](t0 + c0, rows), in_=hrow[ti][0:rows, :])], reads=hkeys)
            else:
                assert var == 0
                P.op("act", lambda e, ti=ti: e.activation(out=junk[:], in_=hrow[ti][:], func=AF.Square, accum_out=st[:, 4 * ti:4 * ti + 1]), reads=hkeys, writes=["c_junk", "c_st%d" % ti])
                P.op("dve", lambda e, ti=ti: e.tensor_scalar(out=st[:, 4 * ti + 1:4 * ti + 2], in0=st[:, 4 * ti:4 * ti + 1], scalar1=1.0 / D, scalar2=EPS, op0=ALU.mult, op1=ALU.add),
                     reads=["c_st%d" % ti], writes=["c_st%db" % ti])
                P.op("act", lambda e, ti=ti: e.activation(out=st[:, 4 * ti + 3:4 * ti + 4], in_=st[:, 4 * ti + 1:4 * ti + 2], func=AF.Sqrt), reads=["c_st%db" % ti], writes=["c_st%dd" % ti])
                P.op("dve", lambda e, ti=ti: e.reciprocal(out=st[:, 4 * ti + 2:4 * ti + 3], in_=st[:, 4 * ti + 3:4 * ti + 4]), reads=["c_st%dd" % ti], writes=["c_st%dc" % ti])
                P.op("dve", lambda e, ti=ti: e.scalar_tensor_tensor(out=hrow[ti][:], in0=hrow[ti][:], scalar=st[:, 4 * ti + 2:4 * ti + 3], in1=gf[:], op0=ALU.mult, op1=ALU.mult),
                     reads=hkeys + ["c_st%dc" % ti, "c_gf"], writes=hkeys)
                P.dma("sp", "c_hst%d" % ti, lambda e, ti=ti, t0=t0, c0=c0: [e.dma_start(out=io["hnorm"][t0 + c0:t0 + c0 + 128, :], in_=hrow[ti][:])], reads=hkeys)
        P.pop()
    P.pop()


I32 = mybir.dt.int32
NT = 8448
DEPTH = 2
G8 = [list(range(8))]
G4 = [[0, 1, 2, 3], [4, 5, 6, 7]]
G2 = [[0, 4], [1, 5], [2, 6], [3, 7]]
FUNC = {"id": AF.Identity, "silu": AF.Silu, "sig": AF.Sigmoid}

LAYER_IN = {
    "wadal_sh": ([16, 128, 4096], F32), "badaT": ([128, 64], F32), "gpreT": ([128, 32], F32),
    "wl_sh": ([56, 128, 4096], F32), "wmg_sh": ([16, 128, 4096], F32),
    "gqT": ([128, 6], F32), "gkvT": ([128, 4], F32), "wuq": ([128, 9216], F32), "wuqsw": ([128, 3072], F32), "wukv": ([128, 8192], F32),
    "convw": ([128, 20, 5], F32), "convb": ([128, 20], F32), "dtbias2": ([128, 1], F32), "alog2": ([128, 1], F32),
    "dskbc": ([128, 2048], F32), "gssmT": ([128, 16], F32),
    "wadag_sh": ([4, 128, 8192], F32), "badag": ([1, 4096], F32),
    "wpa_sh": ([8, 128, 4096], F32), "wpb_sh": ([8, 128, 8192], F32), "wout_sh": ([2, 128, 16384], F32),
}
CONST_IN = {
    "xin_sh": ([TC, D], F32), "offs": ([1, 2], I32), "c2T": ([128, 32, 2], F32),
    "identf": ([128, 128], F32), "onesf": ([128, 128], F32), "TL": ([128, 128], F32), "TG": ([128, 128], F32),
    "SG": ([128, 128], F32), "SL": ([128, 128], F32), "cos2": ([64, NT], F32), "sin2": ([64, NT], F32), "gfin": ([1, D], F32),
}


def build_program(nlayers=DEPTH, probe=None):
    nc = bass.Bass("TRN2", target_bir_lowering=False)
    ext = {}
    for k, (shp, dt) in CONST_IN.items():
        ext[k] = nc.dram_tensor(k, shp, dt, kind="ExternalInput").ap()
    for i in range(nlayers):
        for k, (shp, dt) in LAYER_IN.items():
            ext["%s_%d" % (k, i)] = nc.dram_tensor("%s_%d" % (k, i), shp, dt, kind="ExternalInput").ap()
    hnorm = nc.dram_tensor("hnorm", [NLAT, D], F32, kind="ExternalOutput").ap()
    P = Prog(nc)

    def dram(name, shape, dt=F32):
        return nc.dram_tensor(name, list(shape), dt)

    dyn = {}
    rl = P.es.enter_context(nc.sync.register("r_lat"))
    rc = P.es.enter_context(nc.sync.register("r_ctx"))

    def ldregs(e):
        e.reg_load(rl, ext["offs"][0:1, 0:1])
        ins = e.reg_load(rc, ext["offs"][0:1, 1:2])
        dyn["lat"] = e.snap(rl)
        dyn["ctx"] = e.snap(rc)
        return ins
    P.op("sp", ldregs)

    class Gathered:
        def __init__(self, out, CR, W, unit):
            self.out, self.CR, self.W, self.unit = out, CR, W, unit

        def rows(self, rank, r0, n):
            res = []
            r = r0
            while r < r0 + n:
                c, i = r // self.CR, r % self.CR
                k = min(self.CR - i, r0 + n - r)
                res.append((self.out.ap()[c, rank * self.CR + i:rank * self.CR + i + k, :], r - r0, k))
                r += k
            return res

        def at(self, rank, a):
            return self.rows(rank, a * self.unit, self.unit)

    def gather_from(bnc, R, ncol, dt, name, esize):
        CR = 1
        while CR * 2 * ncol * esize <= (1 << 20) and R % (CR * 2) == 0:
            CR *= 2
        nch = R // CR
        out = dram("gat_%s" % name, [nch, 4 * CR, ncol], dt)
        return out, CR, nch

    def gather(src_ap, shape, dt, name):
        unit = shape[1] if len(shape) == 3 else 1
        ncol = shape[-1]
        R = shape[0] * unit
        esize = 4 if dt == F32 else 2
        bnc = dram("bnc_%s" % name, [R, ncol], dt)
        if len(shape) == 3:
            pieces = [(bnc.ap()[a * unit:(a + 1) * unit, :], src_ap[a]) for a in range(shape[0])]
        else:
            step = 264
            pieces = [(bnc.ap()[r0:min(r0 + step, R), :], src_ap[r0:min(r0 + step, R), :]) for r0 in range(0, R, step)]
        P.dma("sp", "bnc", lambda e, pieces=pieces: [e.dma_start(out=o_, in_=i_) for (o_, i_) in pieces],
              writes=[("bnc", name)], n=len(pieces))
        out, CR, nch = gather_from(bnc, R, ncol, dt, name, esize)
        do_gather(bnc, out, CR, nch, [("bnc", name)], [("gat", name)])
        return Gathered(out, CR, 4, unit)

    def do_gather(bnc, out, CR, nch, reads, writes):
        P.coll(lambda e: [e.collective_compute("AllGather", ALU.bypass, replica_groups=G4, ins=[bnc.ap()[c * CR:(c + 1) * CR, :]], outs=[out.ap()[c]])
                          for c in range(nch)], nch, reads=reads, writes=writes)

    xg = gather(ext["xin_sh"], [TC, D], F32, "x0")
    W_ = []
    for i in range(nlayers):
        w = {}
        w["wadal"] = gather(ext["wadal_sh_%d" % i], [16, 128, 4096], F32, "wadal%d" % i)
        w["wmg"] = gather(ext["wmg_sh_%d" % i], [16, 128, 4096], F32, "wmg%d" % i)
        w["wadag"] = gather(ext["wadag_sh_%d" % i], [4, 128, 8192], F32, "wadag%d" % i)
        w["wpa"] = gather(ext["wpa_sh_%d" % i], [8, 128, 4096], F32, "wpa%d" % i)
        w["wpb"] = gather(ext["wpb_sh_%d" % i], [8, 128, 8192], F32, "wpb%d" % i)
        w["wout"] = gather(ext["wout_sh_%d" % i], [2, 128, 16384], F32, "wout%d" % i)
        W_.append(w)
    P.barrier()
    if probe == "gathers":
        f0 = lambda pcs: pcs[0][0]
        pieces = [(hnorm[8 * r_:8 * r_ + 8, :], f0(xg.rows(r_, 0, 8))) for r_ in range(4)]
        pieces += [(hnorm[32:40, :], f0(xg.rows(2, NLAT + 8, 8)))]
        w = W_[0]
        pieces += [(hnorm[40:48, :], ext["wl_sh_0"][31, 0:8, :]), (hnorm[48:56, :], ext["wl_sh_0"][27, 8:16, :]),
                   (hnorm[56:64, :], f0(w["wadal"].at(2, 10))[0:8, :]), (hnorm[64:72, :], f0(w["wmg"].at(3, 15))[0:8, :]),
                   (hnorm[72:80, :], f0(w["wadag"].at(1, 3))[0:8, 0:4096]), (hnorm[80:88, :], f0(w["wpa"].at(3, 3))[0:8, :]),
                   (hnorm[88:96, :], f0(w["wpb"].at(1, 1))[0:8, 4096:8192]), (hnorm[96:104, :], f0(w["wout"].at(2, 0))[0:8, 0:4096])]
        P.dma("sp", "probe", lambda e: [e.dma_start(out=o_, in_=i_) for (o_, i_) in pieces], n=len(pieces))
        P.emit()
        return nc, P

    pT = dram("pT", [NPT * 128, NT]).ap()
    sc = {
        "qT": dram("qT", [8, 192, NT], BF16).ap(), "kT": dram("kT", [8, 128, NT], BF16).ap(), "kpeT": dram("kpeT", [64, NT], BF16).ap(),
        "vs": dram("vs", [8, 128, NT // 128, 128], BF16).ap(),
        "xs_tm": dram("xs_tm", [NT, 2048]).ap(), "B_tm": dram("B_tm", [2, NT, 128], BF16).ap(), "BT": dram("BT", [2, 128, NT], BF16).ap(),
        "CT": dram("CT", [2, 128, NT], BF16).ap(), "dtt": dram("dtt", [NT, 128]).ap(), "yd": dram("yd", [2, NT, 2048]).ap(),
        "sgT": dram("sgT", [2 * D, TC]).ap(), "gsc": dram("gsc", [2, 128, D]).ap(),
    }
    ogb_ = dram("og_b", [1024, NT], BF16)
    ygb_ = [dram("yg_b%d" % q, [1024, NT], BF16) for q in range(2)]
    ssb_ = dram("ss_b", [1, NT])
    oga, ogCR, ognch = gather_from(ogb_, 1024, NT, BF16, "og", 2)
    ygg = [gather_from(ygb_[q], 1024, NT, BF16, "yg%d" % q, 2) for q in range(2)]
    ssa, ssCR, ssnch = gather_from(ssb_, 1, NT, F32, "ss", 4)
    og_own = dram("og_own", [4 * 1024, TC], BF16)
    yg_own = [dram("yg_own%d" % q, [4 * 1024, TC], BF16) for q in range(2)]
    ss_own = dram("ss_own", [4, TC])
    hout = dram("hout", [TC, D])
    xg1o, hCR, hnch = gather_from(hout, TC, D, F32, "h1", 4)
    xg1 = Gathered(xg1o, hCR, 4, 1)

    def hout_rows(r0, n):
        return hout.ap()[r0:r0 + n, :]

    for i in range(nlayers):
        w = W_[i]
        L = lambda k, i=i: ext["%s_%d" % (k, i)]
        xga = xg if i == 0 else xg1
        own_rows = (lambda r0, n: ext["xin_sh"][r0:r0 + n, :]) if i == 0 else hout_rows
        modio = {"c2T": ext["c2T"], "badaT": L("badaT"), "gpreT": L("gpreT"), "identf": ext["identf"]}
        wada_fn = lambda oc, w=w: w["wadal"].at(oc // 16, oc % 16)
        blocks = [dict(n=256, mcol=1, tiles=[[(a_, o_, k_) for (a_, o_, k_) in xga.rows(2 * t, NLAT, 64)] + [(a_, 64 + o_, k_) for (a_, o_, k_) in xga.rows(2 * t + 1, NLAT, 64)] for t in range(2)])]
        for bl in range(16):
            tiles = []
            for tt in range(4):
                l0 = bl * 512 + tt * 128
                tiles.append(xga.rows(l0 // NLAT, l0 % NLAT, 128))
            blocks.append(dict(n=512, mcol=0, tiles=tiles))
        t0s = [0] + [256 + 512 * b_ for b_ in range(16)]
        chunks = []
        cc = 0
        for name, cnt, f in SEG:
            if name in ("mga", "mgb"):
                continue
            for q in range(cnt):
                w_ap = [(L("wl_sh")[cc], 0, 128)]
                chunks.append((w_ap, FUNC[f], (lambda bi, cc=cc: pT[cc * 128:(cc + 1) * 128, t0s[bi]:t0s[bi] + blocks[bi]["n"]])))
                cc += 1
        assert cc == NPT
        phase_A(P, nc, modio, blocks, chunks, wada_fn)
        P.barrier()
        iob = {"pT": pT, "gqT": L("gqT"), "gkvT": L("gkvT"), "wuq": L("wuq"), "wuqsw": L("wuqsw"), "wukv": L("wukv"),
               "cos2": ext["cos2"], "sin2": ext["sin2"], "onesf": ext["onesf"], "ogT": ogb_.ap()}
        iob.update(sc)
        phase_B1(P, nc, iob, NT)
        P.barrier()
        phase_B2(P, nc, iob, NT)
        P.barrier()
        ios = {"pT": pT, "convw": L("convw"), "convb": L("convb"), "dtbias2": L("dtbias2"), "alog2": L("alog2"), "identf": ext["identf"],
               "onesf": ext["onesf"], "TL": ext["TL"], "TG": ext["TG"], "SG": ext["SG"], "SL": ext["SL"], "dskbc": L("dskbc"), "gssmT": L("gssmT"),
               "ygT_parts": [t_.ap() for t_ in ygb_], "ssq": ssb_.ap()}
        ios.update(sc)
        phase_S1(P, nc, ios, NT)
        P.barrier()
        phase_S2(P, nc, ios, NT)
        P.barrier()
        phase_S3(P, nc, ios, NT)
        P.barrier()
        do_gather(ogb_, oga, ogCR, ognch, [], [])
        for q in range(2):
            do_gather(ygb_[q], ygg[q][0], ygg[q][1], ygg[q][2], [], [])
        do_gather(ssb_, ssa, ssCR, ssnch, [], [])
        mblocks = []
        for bl in range(4):
            mblocks.append(dict(n=512, mcol=0, tiles=[[(own_rows(bl * 512 + tt * 128, 128), 0, 128)] for tt in range(4)]))
        mt0 = [0, 512, 1024, 1536]
        if i < nlayers - 1:
            mblocks.append(dict(n=64, mcol=1, tiles=[[(own_rows(NLAT, 64), 0, 64)]]))
            mt0.append(NLAT)
        mchunks = []
        for cc in range(64):
            mchunks.append((w["wmg"].at(cc // 16, cc % 16), AF.Sigmoid,
                            (lambda bi, cc=cc: sc["sgT"][cc * 128:(cc + 1) * 128, mt0[bi]:mt0[bi] + mblocks[bi]["n"]])))
        phase_A(P, nc, modio, mblocks, mchunks, wada_fn)
        P.barrier()
        def own_cols(dst, src, CR, nch):
            sv = src.ap().rearrange("c q t -> (c q) t")
            dv = dst.ap()
            P.dma("sp", "owncp", lambda e: [e.dma_start(out=dv[:, 0:NLAT], in_=sv[:, bass.ds(dyn["lat"], NLAT)]),
                                            e.dma_start(out=dv[:, NLAT:TC], in_=sv[:, bass.ds(dyn["ctx"], 64)])], n=2)
        own_cols(og_own, oga, ogCR, ognch)
        for q in range(2):
            own_cols(yg_own[q], ygg[q][0], ygg[q][1], ygg[q][2])
        own_cols(ss_own, ssa, ssCR, ssnch)
        P.barrier()
        ioc = {"c2T": ext["c2T"], "badag": L("badag"), "gsc": sc["gsc"], "sgT": sc["sgT"], "hin_rows": own_rows, "gfin": ext["gfin"],
               "oga2": og_own.ap(), "yga_parts": [t_.ap() for t_ in yg_own],
               "ssqa": ss_own.ap(), "hout_rows": hout_rows, "hnorm": hnorm}
        phase_Cmod(P, nc, ioc, lambda cb, w=w: w["wadag"].at(cb // 4, cb % 4))
        P.barrier()
        final = (i == nlayers - 1)
        phase_C(P, nc, ioc, lambda oc, w=w: w["wpa"].at(oc // 8, oc % 8), lambda oc, w=w: w["wpb"].at(oc // 8, oc % 8),
                lambda ob, w=w: w["wout"].at(ob // 2, ob % 2), with_ctx=not final, final=final, dyn=dyn)
        P.barrier()
        if not final:
            do_gather(hout, xg1o, hCR, hnch, [], [])
            P.barrier()
    P.emit()
    return nc, P


def prep_core(inp, b, j, nlayers=DEPTH):
    r = 4 * b + j
    f32 = np.float32
    xin = np.concatenate([inp["x"][b][NLAT * j:NLAT * (j + 1)], inp["ctx"][b][64 * j:64 * (j + 1)]], axis=0)
    c2 = np.stack([inp["c"][b], inp["c_ctx"]], axis=1)
    cos2, sin2 = rope_tables(NT)
    rr = np.arange(128)
    m = {
        "xin_sh": np.ascontiguousarray(xin), "offs": np.array([[256 + NLAT * j, 64 * j]], np.int32),
        "c2T": np.ascontiguousarray(c2.reshape(32, 128, 2).transpose(1, 0, 2)),
        "identf": np.eye(128, dtype=f32), "onesf": np.ones((128, 128), f32),
        "TL": (rr[:, None] <= rr[None, :]).astype(f32), "TG": (rr[:, None] >= rr[None, :]).astype(f32),
        "SG": (rr[:, None] > rr[None, :]).astype(f32), "SL": (rr[:, None] < rr[None, :]).astype(f32),
        "cos2": cos2, "sin2": sin2, "gfin": inp["g_final"][None, :].astype(f32),
    }
    return m


def _gathered_perm():
    rho = np.arange(4096)
    c, r, ii = rho // 128, (rho % 128) // 32, rho % 32
    pa = r * 1024 + c * 32 + ii
    pb = np.concatenate([r * 2048 + q * 1024 + c * 32 + ii for q in range(2)])
    return pa, pb


PERM_A, PERM_B = _gathered_perm()


def prep_layer_shared(inp, i):
    wada = inp["w_ada"][i]
    out = {
        "wadal": chunked_weight(wada[:, :2 * D], np.arange(2 * D)),
        "wmg": chunked_weight(inp["w_in"][i], np.arange(MGA0, MGA0 + 2 * D)),
        "wadag": chunked_weight(wada[:, 2 * D:], np.arange(D), width=256),
        "wpa": chunked_weight(inp["w_proj_a"][i][PERM_A], np.arange(D)),
        "wpb": chunked_weight(inp["w_proj_b"][i][PERM_B], np.arange(D)),
        "wout": chunked_weight(inp["w_out"][i], np.arange(D), width=512),
    }
    return out


def prep_layer_core(inp, i, b, j, shared, wl_cache):
    r = 4 * b + j
    m = {}
    m["wadal_sh"] = shared["wadal"][16 * j:16 * j + 16]
    m["wmg_sh"] = shared["wmg"][16 * j:16 * j + 16]
    m["wadag_sh"] = shared["wadag"][4 * j:4 * j + 4]
    m["wpa_sh"] = shared["wpa"][8 * j:8 * j + 8]
    m["wpb_sh"] = shared["wpb"][8 * j:8 * j + 8]
    m["wout_sh"] = shared["wout"][2 * j:2 * j + 2]
    if j not in wl_cache:
        wl_cache[j] = chunked_weight(inp["w_in"][i], cols_for_core(j)[:NPT * 128])
    m["wl_sh"] = wl_cache[j]
    m["badaT"] = colvec(inp["b_ada"][i][:2 * D])
    m["gpreT"] = colvec(inp["g_pre"][i])
    m["badag"] = np.ascontiguousarray(inp["b_ada"][i][None, 2 * D:])
    pb = prep_B(inp, i, j, NT)
    for k in ("gqT", "gkvT", "wuq", "wuqsw", "wukv"):
        m[k] = pb[k]
    ps = prep_S(inp, i, j)
    for k in ("convw", "convb", "dtbias2", "alog2", "dskbc", "gssmT"):
        m[k] = ps[k]
    return m


def make_in_maps(inp, nlayers=DEPTH):
    maps = [prep_core(inp, r // 4, r % 4, nlayers) for r in range(8)]
    for i in range(nlayers):
        shared = prep_layer_shared(inp, i)
        wl_cache = {}
        for r in range(8):
            lm = prep_layer_core(inp, i, r // 4, r % 4, shared, wl_cache)
            for k, v in lm.items():
                maps[r]["%s_%d" % (k, i)] = np.ascontiguousarray(v, dtype=np.float32)
    return maps


_CACHE = {}


def kernel(**inputs):
    inp = {k: np.asarray(v) for k, v in inputs.items()}
    if "nc" not in _CACHE:
        _CACHE["nc"] = build_program()[0]
    nc = _CACHE["nc"]
    maps = make_in_maps(inp)
    res = run_bass_kernel_spmd(nc, maps, core_ids=list(range(8)))
    out = np.empty((2, 8192, D), np.float32)
    for r in range(8):
        b, j = r // 4, r % 4
        out[b, NLAT * j:NLAT * (j + 1)] = res.results[r]["hnorm"]
    return out
```

```python
import numpy as np
import concourse.bass as bass
import concourse.mybir as mybir
from concourse.bass_utils import run_bass_kernel_spmd
from contextlib import ExitStack

F32 = mybir.dt.float32
BF16 = mybir.dt.bfloat16
AF = mybir.ActivationFunctionType
ALU = mybir.AluOpType
AX = mybir.AxisListType


class _Op:
    __slots__ = ("eng", "fn", "reads", "writes", "excl", "tag", "ndma", "deps", "ms", "cum")

    def __init__(self, eng, fn, reads, writes, tag=None, ndma=0, excl=()):
        self.eng, self.fn, self.reads, self.writes = eng, fn, reads, writes
        self.excl = tuple(excl)
        self.tag, self.ndma = tag, ndma
        self.deps = None
        self.ms = 0
        self.cum = 0


class Prog:
    ENGS = ("pe", "act", "dve", "pool", "sp")

    def __init__(self, nc):
        self.nc = nc
        self.ops = []
        self.es = ExitStack()
        self.last_w = {}
        self.last_x = {}
        self.readers = {}
        self.tagcnt = {}
        self.bar = {}
        self.last_eng = {}
        self.last_tag = {}
        self.scopes = []
        self.colltags = set()
        self.uid = 0
        self.slotmap = {}
        self.detached = set()

    def sbuf(self, name, shape, dt):
        self.uid += 1
        return self.es.enter_context(self.nc.sbuf_tensor("sb%d_%s" % (self.uid, name), list(shape), dt))

    def psum(self, name, shape, dt=F32):
        self.uid += 1
        return self.es.enter_context(self.nc.psum_tensor("ps%d_%s" % (self.uid, name), list(shape), dt))

    def push(self):
        self.scopes.append(self.es)
        self.es = ExitStack()

    def pop(self):
        self.barrier()
        self.es.close()
        self.es = self.scopes.pop()

    def barrier(self):
        snap = set(self.last_eng.values()) | set(v for k, v in self.last_tag.items() if k not in self.detached)
        for e in self.ENGS:
            self.bar[e] = set(snap) | self.bar.get(e, set())
        self.slotmap = {}

    def _record(self, op):
        deps = set()
        for k in op.reads:
            d = self.last_w.get(k)
            if d is not None:
                deps.add(d)
        for k in op.writes:
            d = self.last_w.get(k)
            if d is not None:
                deps.add(d)
            deps.update(self.readers.get(k, ()))
        idx = len(self.ops)
        b = self.bar.pop(op.eng, None)
        if b:
            deps.update(b)
        if op.tag is None:
            self.last_eng[op.eng] = idx
        else:
            self.last_tag[op.tag] = idx
        for k in op.excl:
            d = self.last_x.get(k)
            if d is not None and self.ops[d].eng != op.eng:
                deps.add(d)
            self.last_x[k] = idx
        for k in op.writes:
            self.last_w[k] = idx
            self.readers[k] = []
        for k in op.reads:
            if k in op.writes:
                continue
            self.readers.setdefault(k, []).append(idx)
        op.deps = deps
        self.ops.append(op)
        return idx

    def op(self, eng, fn, reads=(), writes=(), excl=()):
        return self._record(_Op(eng, fn, tuple(reads), tuple(writes), excl=excl))

    def dma(self, eng, tag, fn, reads=(), writes=(), n=1):
        slot = self.slotmap.get(tag)
        if slot is None:
            slot = len(self.slotmap)
            self.slotmap[tag] = slot
        slot = "s%d" % slot
        tk = ("__slot__", slot)
        o = _Op(eng, fn, tuple(reads), tuple(writes) + (tk,), tag=slot, ndma=n)
        self.tagcnt[slot] = self.tagcnt.get(slot, 0) + n
        o.cum = self.tagcnt[slot] * 16
        return self._record(o)

    def detach(self, slot):
        self.detached.add(slot)

    def join(self, slot):
        self.detached.discard(slot)
        d = self.last_tag.get(slot)
        if d is not None:
            for e in self.ENGS:
                self.bar[e] = self.bar.get(e, set()) | {d}

    def coll(self, fn, n, reads=(), writes=(), slot="cc"):
        self.colltags.add(slot)
        tk = ("__slot__", slot)
        o = _Op("pool", fn, tuple(reads), tuple(writes) + (tk,), tag=slot, ndma=n)
        self.tagcnt[slot] = self.tagcnt.get(slot, 0) + n
        o.cum = self.tagcnt[slot]
        return self._record(o)

    def emit(self):
        nc = self.nc
        ops = self.ops
        need = set()
        for o in ops:
            best = {}
            dmb = {}
            for d in o.deps:
                p = ops[d]
                if p.tag is not None:
                    if p.tag not in dmb or dmb[p.tag] < d:
                        dmb[p.tag] = d
                    continue
                if p.eng == "pe" and o.eng == "pe" and o.tag is None:
                    continue
                if p.eng not in best or best[p.eng] < d:
                    best[p.eng] = d
            o.deps = list(best.values()) + list(dmb.values())
            for d in best.values():
                need.add(d)
        CAP = 8000
        cnt = {e: 0 for e in self.ENGS}
        for i, o in enumerate(ops):
            if o.tag is None and i in need:
                cnt[o.eng] += 1
                o.ms = cnt[o.eng]
        sems = {}
        for e in self.ENGS:
            for g in range(cnt[e] // CAP + 1):
                sems[e, g] = self.es.enter_context(nc.semaphore("s_%s_%d" % (e, g)))
        tsems = {}
        DCAP = CAP // 16
        for t, c in self.tagcnt.items():
            if t in self.colltags:
                assert c < CAP
                tsems[t, 0] = self.es.enter_context(nc.semaphore("d_%s" % (str(t),)))
            else:
                for g in range(c // DCAP + 1):
                    tsems[t, g] = self.es.enter_context(nc.semaphore("d_%s_%d" % (str(t), g)))
        self.stats_nsem = len(sems) + len(tsems)

        def esem(p):
            g = (p.ms - 1) // CAP
            return sems[p.eng, g], p.ms - g * CAP, ("e", p.eng, g)

        def dsem(p):
            if p.tag in self.colltags:
                return tsems[p.tag, 0], p.cum, ("t", p.tag, 0)
            k = p.cum // 16
            g = (k - 1) // DCAP
            return tsems[p.tag, g], (k - g * DCAP) * 16, ("t", p.tag, g)

        def dsem_prev(p):
            if p.tag in self.colltags:
                return None
            k = p.cum // 16
            g = (k - 1) // DCAP
            if g > 0 and (k - p.ndma) < g * DCAP:
                return tsems[p.tag, g - 1], DCAP * 16, ("t", p.tag, g - 1)
            return None
        per = {e: [] for e in self.ENGS}
        for o in ops:
            per[o.eng].append(o)
        self.stats = {e: len(per[e]) for e in self.ENGS}
        self.stats["milestones"] = dict(cnt)
        self.stats["tags"] = len(tsems)

        def run(e, eng):
            seen = {}
            nw = 0
            for o in per[e]:
                for d in o.deps:
                    p = ops[d]
                    if p.tag is not None:
                        pv = dsem_prev(p)
                        if pv is not None and seen.get(pv[2], 0) < pv[1]:
                            seen[pv[2]] = pv[1]
                            eng.wait_ge(pv[0], pv[1])
                            nw += 1
                        s, v, key = dsem(p)
                    else:
                        s, v, key = esem(p)
                    if seen.get(key, 0) >= v:
                        continue
                    seen[key] = v
                    eng.wait_ge(s, v)
                    nw += 1
                r = o.fn(eng)
                if o.tag is not None:
                    assert len(r) == o.ndma, (len(r), o.ndma, o.tag)
                    if o.tag in self.colltags:
                        for ins in r:
                            ins.then_inc(tsems[o.tag, 0], 1)
                    else:
                        k0 = o.cum // 16 - o.ndma
                        for q, ins in enumerate(r):
                            ins.then_inc(tsems[o.tag, (k0 + q) // DCAP], 16)
                elif o.ms:
                    r.then_inc(esem(o)[0], 1)
            if e == "sp":
                for t, c in self.tagcnt.items():
                    if t in self.colltags:
                        if seen.get(("t", t, 0), 0) < c:
                            eng.wait_ge(tsems[t, 0], c)
                        continue
                    for g in range(c // DCAP + 1):
                        v = min(c - g * DCAP, DCAP) * 16
                        if v > 0 and seen.get(("t", t, g), 0) < v:
                            eng.wait_ge(tsems[t, g], v)
            self.stats["waits_" + e] = nw

        with nc.Block() as block:
            @block.sync
            def _(sync):
                run("sp", sync)

            @block.tensor
            def _(tensor):
                run("pe", tensor)

            @block.vector
            def _(vector):
                run("dve", vector)

            @block.scalar
            def _(scalar):
                run("act", scalar)

            @block.gpsimd
            def _(gpsimd):
                run("pool", gpsimd)
        self.es.close()


def dma_pieces(e, dst, pieces, pat=None, **kw):
    out = []
    for (ap, p0, r) in pieces:
        src = ap.rearrange(pat, **kw) if pat else ap
        out.append(e.dma_start(out=dst[p0:p0 + r], in_=src))
    return out


import numpy as np

D = 4096
Q0 = 0; KV0 = 768; KPE0 = 1280; GA0 = 1344; Z0 = 5440; XS0 = 13632; B0 = 21824; C0 = 22848
DTF0 = 23872; DTB0 = 24000; MGA0 = 24128; MGB0 = 28224


def cols_for_core(j):
    r = np.arange
    kpe = np.concatenate([r(KPE0, KPE0 + 64), r(KPE0 + 32, KPE0 + 64), r(KPE0, KPE0 + 32)])
    dt = np.concatenate([r(DTF0 + 32 * j, DTF0 + 32 * j + 32), r(DTB0 + 32 * j, DTB0 + 32 * j + 32), -np.ones(64, np.int64)])
    cols = np.concatenate([
        r(Q0, Q0 + 768), r(KV0, KV0 + 512), kpe,
        r(GA0 + 1024 * j, GA0 + 1024 * (j + 1)),
        r(Z0 + 2048 * j, Z0 + 2048 * (j + 1)),
        r(XS0 + 2048 * j, XS0 + 2048 * (j + 1)),
        r(B0 + 256 * j, B0 + 256 * (j + 1)),
        r(C0 + 256 * j, C0 + 256 * (j + 1)),
        dt,
        r(MGA0 + 1024 * j, MGA0 + 1024 * (j + 1)),
        r(MGB0 + 1024 * j, MGB0 + 1024 * (j + 1)),
    ])
    assert cols.shape[0] == 72 * 128
    return cols


def chunked_weight(w, cols, width=128):
    K = w.shape[0]
    sel = w[:, np.maximum(cols, 0)]
    if (cols < 0).any():
        sel[:, cols < 0] = 0.0
    nchunk = cols.shape[0] // width
    a = sel.reshape(K // 128, 128, nchunk, width).transpose(2, 1, 0, 3)
    return np.ascontiguousarray(a).reshape(nchunk, 128, (K // 128) * width)


def colvec(v):
    return np.ascontiguousarray(v.reshape(-1, 128).T)


def prep_A(inp, i, b, j):
    c2 = np.stack([inp["c"][b], inp["c_ctx"]], axis=1)
    c2T = np.ascontiguousarray(c2.reshape(32, 128, 2).transpose(1, 0, 2))
    wada = inp["w_ada"][i][:, :2 * D]
    return {
        "c2T": c2T,
        "wadal": chunked_weight(wada, np.arange(2 * D)),
        "badaT": colvec(inp["b_ada"][i][:2 * D]),
        "gpreT": colvec(inp["g_pre"][i]),
        "identf": np.eye(128, dtype=np.float32),
        "wl": chunked_weight(inp["w_in"][i], cols_for_core(j)),
    }


def rope_tables(nt, nctx=256):
    n = nt - nctx
    f32 = np.float32
    rows = (np.arange(n) // 64).astype(f32)
    cols = (np.arange(n) % 64).astype(f32)
    inv = (f32(10000.0) ** (-np.arange(16, dtype=f32) / f32(16))).astype(f32)
    ang = np.concatenate([rows[:, None] * inv, cols[:, None] * inv], axis=-1).astype(f32)
    c = np.cos(ang).astype(f32).T
    s = np.sin(ang).astype(f32).T
    cos2 = np.ones((64, nt), f32)
    sin2 = np.zeros((64, nt), f32)
    cos2[0:32, nctx:] = c
    cos2[32:64, nctx:] = c
    sin2[0:32, nctx:] = -s
    sin2[32:64, nctx:] = s
    return cos2, sin2


def prep_B(inp, i, j, nt):
    hq = np.arange(8 * j * 192, (8 * j + 8) * 192)
    sw = np.concatenate([np.concatenate([np.arange(h * 192 + 160, h * 192 + 192), np.arange(h * 192 + 128, h * 192 + 160)])
                         for h in range(8 * j, 8 * j + 8)])
    hkv = np.arange(8 * j * 256, (8 * j + 8) * 256)
    cos2, sin2 = rope_tables(nt)
    return {
        "gqT": colvec(inp["g_q"][i]), "gkvT": colvec(inp["g_kv"][i]),
        "wuq": chunked_weight(inp["w_uq"][i], hq, width=1536)[0],
        "wuqsw": chunked_weight(inp["w_uq"][i], sw, width=512)[0],
        "wukv": chunked_weight(inp["w_ukv"][i], hkv, width=2048)[0],
        "cos2": cos2, "sin2": sin2, "onesf": np.ones((128, 128), np.float32),
    }


def prep_S(inp, i, j):
    ch = np.concatenate([np.arange(2048 * j, 2048 * (j + 1)), 8192 + np.arange(256 * j, 256 * (j + 1)),
                         9216 + np.arange(256 * j, 256 * (j + 1))])
    cw = inp["conv_w"][i][:, ch]
    convw = np.ascontiguousarray(cw.reshape(5, 20, 128).transpose(2, 1, 0))
    convb = colvec(inp["conv_b"][i][ch])
    hs = slice(32 * j, 32 * (j + 1))
    dtb = np.concatenate([inp["dt_bias_f"][i][hs], inp["dt_bias_b"][i][hs]])
    al = np.concatenate([inp["a_log_f"][i][hs], inp["a_log_b"][i][hs]])
    r = np.arange(128)
    f32 = np.float32
    return {
        "convw": convw, "convb": convb,
        "dtbias2": np.concatenate([dtb, dtb])[:, None].astype(f32),
        "alog2": np.concatenate([np.zeros(64, f32), al])[:, None].astype(f32),
        "TL": (r[:, None] <= r[None, :]).astype(f32), "TG": (r[:, None] >= r[None, :]).astype(f32),
        "SG": (r[:, None] > r[None, :]).astype(f32), "SL": (r[:, None] < r[None, :]).astype(f32),
        "gssmT": colvec(inp["g_ssm"][i][2048 * j:2048 * (j + 1)]),
        "dskbc": np.ascontiguousarray(np.broadcast_to(np.repeat(inp["d_skip"][i][hs], 64)[None, :], (128, 2048))).astype(f32),
    }


D = 4096
KC = D // 128
NCTX = 256
EPS = 1e-6

SEG = [("cq", 6, "id"), ("ckv", 4, "id"), ("kpe", 1, "id"), ("ga", 8, "silu"), ("z", 16, "silu"),
       ("xs", 16, "id"), ("B", 2, "id"), ("C", 2, "id"), ("dt", 1, "id"), ("mga", 8, "sig"), ("mgb", 8, "sig")]
SEG_OFF = {}
_o = 0
for _n, _c, _f in SEG:
    SEG_OFF[_n] = _o
    _o += _c
NCH = _o
NPT = SEG_OFF["mga"]


def token_blocks(nt):
    blks = [(0, NCTX)]
    t = NCTX
    while t < nt:
        blks.append((t, min(512, nt - t)))
        t += 512
    return blks


def phase_A(P, nc, io, blocks, chunks, wada_fn):
    P.push()
    identf = P.sbuf("identf", [128, 128], F32)
    P.dma("sp", "identf", lambda e: [e.dma_start(out=identf[:], in_=io["identf"])], writes=["identf"])
    c2 = P.sbuf("c2", [128, KC, 2], F32)
    sc = P.sbuf("sc", [128, KC, 2], F32)
    bada = P.sbuf("bada", [128, 64], F32)
    gpre = P.sbuf("gpre", [128, KC], F32)
    modT = P.sbuf("modT", [128, 64, 2], F32)
    gmul = P.sbuf("gmul", [128, KC, 2], F32)
    P.dma("sp", "c2", lambda e: [e.dma_start(out=c2[:], in_=io["c2T"])], writes=["c2"])
    P.dma("sp", "bada", lambda e: [e.dma_start(out=bada[:], in_=io["badaT"])], writes=["bada"])
    P.dma("sp", "gpre", lambda e: [e.dma_start(out=gpre[:], in_=io["gpreT"])], writes=["gpre"])
    P.op("act", lambda e: e.activation(out=sc[:], in_=c2[:], func=AF.Silu), reads=["c2"], writes=["sc"])
    P.push()
    wad = [P.sbuf("wad%d" % i, [128, KC, 128], F32) for i in range(2)]
    pm = P.psum("pm", [128, 512], F32)
    for oc in range(64):
        wb = wad[oc % 2]
        wp_ = wada_fn(oc)
        P.dma("sp", "wad%d" % (oc % 2), lambda e, wb=wb, wp_=wp_: dma_pieces(e, wb, wp_, "p (k n) -> p k n", k=KC),
              writes=["wad%d" % (oc % 2)], n=len(wp_))
        for k in range(KC):
            P.op("pe", lambda e, wb=wb, k=k, oc=oc: e.matmul(pm[:, 2 * (oc % 8):2 * (oc % 8) + 2], wb[:, k, :], sc[:, k, :],
                                                             start=(k == 0), stop=(k == KC - 1)),
                 reads=["wad%d" % (oc % 2), "sc"], excl=["b_pm"])
        P.op("dve", lambda e, oc=oc: e.tensor_scalar(out=modT[:, oc, :], in0=pm[:, 2 * (oc % 8):2 * (oc % 8) + 2],
                                                     scalar1=bada[:, oc:oc + 1], scalar2=None, op0=ALU.add),
             reads=["bada"], writes=["modT"], excl=["b_pm"])
    P.pop()
    for n in range(2):
        P.op("dve", lambda e, n=n: e.scalar_tensor_tensor(out=gmul[:, :, n], in0=modT[:, 32:64, n], scalar=1.0, in1=gpre[:],
                                                          op0=ALU.add, op1=ALU.mult),
             reads=["modT", "gpre"], writes=["gmul"])

    xt = [P.sbuf("xt%d" % i, [128, D], F32) for i in range(2)]
    xn = [P.sbuf("xn%d" % i, [128, D], F32) for i in range(2)]
    junk = P.sbuf("junk", [128, D], BF16)
    st = [P.sbuf("st%d" % i, [128, 4], F32) for i in range(2)]
    uT = [P.sbuf("uT%d" % i, [128, KC, 512], BF16) for i in range(2)]
    wbuf = [P.sbuf("wbuf%d" % i, [128, KC, 128], BF16) for i in range(3)]
    stage = [P.sbuf("stage%d" % i, [128, 512], F32) for i in range(3)]
    ptr = [P.psum("ptr%d" % i, [128, 512], F32) for i in range(2)]
    pmm = [P.psum("pmm%d" % i, [128, 512], F32) for i in range(2)]
    ti = 0
    wi = 0
    si = 0
    for bi, blk in enumerate(blocks):
        n = blk["n"]
        mcol = blk["mcol"]
        ub = uT[bi % 2]
        ukey = "uT%d" % (bi % 2)
        ukeys = []
        c0 = 0
        for tt, pieces in enumerate(blk["tiles"]):
            rows = sum(p[2] for p in pieces)
            x_ = xt[ti % 2]
            xn_ = xn[ti % 2]
            st_ = st[ti % 2]
            xk, xnk, stk = "xt%d" % (ti % 2), "xn%d" % (ti % 2), "st%d" % (ti % 2)
            P.dma("sp", xk, lambda e, x_=x_, pieces=pieces: [e.dma_start(out=x_[p0:p0 + r, :], in_=src) for (src, p0, r) in pieces],
                  writes=[xk], n=len(pieces))
            P.op("act", lambda e, x_=x_, st_=st_, rows=rows: e.activation(out=junk[0:rows, :], in_=x_[0:rows, :], func=AF.Square, accum_out=st_[0:rows, 0:1]),
                 reads=[xk], writes=["junk", stk])
            P.op("dve", lambda e, st_=st_, rows=rows: e.tensor_scalar(out=st_[0:rows, 1:2], in0=st_[0:rows, 0:1], scalar1=1.0 / D, scalar2=EPS,
                                                                      op0=ALU.mult, op1=ALU.add), reads=[stk], writes=[stk + "b"])
            P.op("act", lambda e, st_=st_, rows=rows: e.activation(out=st_[0:rows, 3:4], in_=st_[0:rows, 1:2], func=AF.Sqrt), reads=[stk + "b"], writes=[stk + "d"])
            P.op("dve", lambda e, st_=st_, rows=rows: e.reciprocal(out=st_[0:rows, 2:3], in_=st_[0:rows, 3:4]), reads=[stk + "d"], writes=[stk + "c"])
            P.op("dve", lambda e, x_=x_, xn_=xn_, st_=st_, rows=rows: e.tensor_scalar(out=xn_[0:rows, :], in0=x_[0:rows, :], scalar1=st_[0:rows, 2:3], scalar2=None,
                                                                                      op0=ALU.mult), reads=[xk, stk + "c"], writes=[xnk])
            for q in range(KC // 4):
                pb = ptr[q % 2]
                pk = "b_ptr%d" % (q % 2)
                for c4 in range(4):
                    c = q * 4 + c4
                    P.op("pe", lambda e, pb=pb, c4=c4, c=c, xn_=xn_, rows=rows: e.transpose(pb[:, c4 * 128:c4 * 128 + rows], xn_[0:rows, c * 128:(c + 1) * 128], identf[0:rows, 0:rows]),
                         reads=[xnk, "identf"], excl=[pk])
                for c4 in range(4):
                    c = q * 4 + c4
                    P.op("act", lambda e, pb=pb, c4=c4, c=c, ub=ub, c0=c0, mcol=mcol, rows=rows: e.activation(
                        out=ub[:, c, c0:c0 + rows], in_=pb[:, c4 * 128:c4 * 128 + rows], func=AF.Identity,
                        scale=gmul[:, c, mcol:mcol + 1], bias=modT[:, c, mcol:mcol + 1]),
                        reads=["gmul", "modT"], writes=[(ukey, tt)], excl=[pk])
            ukeys.append((ukey, tt))
            c0 += rows
            ti += 1
        assert c0 == n
        for cc, (w_ap, fn, dst_fn) in enumerate(chunks):
            wb = wbuf[wi % 3]
            wk = "wbuf%d" % (wi % 3)
            P.dma("pool", wk, lambda e, wb=wb, w_ap=w_ap: dma_pieces(e, wb, w_ap, "p (k n) -> p k n", k=KC),
                  writes=[wk], n=len(w_ap))
            pb = pmm[wi % 2]
            pk = "b_pmm%d" % (wi % 2)
            for k in range(KC):
                P.op("pe", lambda e, pb=pb, wb=wb, k=k, ub=ub, n=n: e.matmul(pb[:, 0:n], wb[:, k, :], ub[:, k, 0:n],
                                                                             start=(k == 0), stop=(k == KC - 1)),
                     reads=[wk] + ukeys, excl=[pk])
            sg = stage[si % 3]
            sk = "stage%d" % (si % 3)
            P.op("act", lambda e, sg=sg, pb=pb, n=n, fn=fn: e.activation(out=sg[:, 0:n], in_=pb[:, 0:n], func=fn),
                 writes=[sk], excl=[pk])
            dst = dst_fn(bi)
            P.dma("sp", sk, lambda e, sg=sg, dst=dst, n=n: [e.dma_start(out=dst, in_=sg[:, 0:n])], reads=[sk])
            wi += 1
            si += 1
    P.pop()


import math

ATTN_SCALE = 1.0 / math.sqrt(192.0)
NH = 8


def phase_B1(P, nc, io, nt):
    P.push()
    pT = io["pT"]
    wuq = P.sbuf("wuq", [128, 6, NH, 192], BF16)
    wuqsw = P.sbuf("wuqsw", [128, 6, NH, 64], BF16)
    wukv = P.sbuf("wukv", [128, 4, NH, 256], BF16)
    gq = P.sbuf("gq", [128, 6], F32)
    gkv = P.sbuf("gkv", [128, 4], F32)
    onesf = P.sbuf("onesf", [128, 128], F32)
    P.dma("pool", "wuq", lambda e: [e.dma_start(out=wuq[:], in_=io["wuq"].rearrange("p (k h d) -> p k h d", k=6, h=NH))], writes=["wuq"])
    P.dma("pool", "wuqsw", lambda e: [e.dma_start(out=wuqsw[:], in_=io["wuqsw"].rearrange("p (k h d) -> p k h d", k=6, h=NH))], writes=["wuqsw"])
    P.dma("pool", "wukv", lambda e: [e.dma_start(out=wukv[:], in_=io["wukv"].rearrange("p (k h d) -> p k h d", k=4, h=NH))], writes=["wukv"])
    P.dma("sp", "gq", lambda e: [e.dma_start(out=gq[:], in_=io["gqT"])], writes=["gq"])
    P.dma("sp", "gkv", lambda e: [e.dma_start(out=gkv[:], in_=io["gkvT"])], writes=["gkv"])
    P.dma("sp", "onesf", lambda e: [e.dma_start(out=onesf[:], in_=io["onesf"])], writes=["onesf"])

    cq = P.sbuf("cq", [128, 6, 512], F32)
    ckv = P.sbuf("ckv", [128, 4, 512], F32)
    kpe2 = P.sbuf("kpe2", [64, 2, 512], F32)
    cos = P.sbuf("cos", [64, 512], F32)
    sin = P.sbuf("sin", [64, 512], F32)
    sq = P.sbuf("sq", [128, 6, 512], F32)
    rr = P.sbuf("rr", [128, 3, 512], F32)
    cqn = P.sbuf("cqn", [128, 6, 512], BF16)
    ckvn = P.sbuf("ckvn", [128, 4, 512], BF16)
    t1 = [P.sbuf("t1_%d" % i, [64, 512], F32) for i in range(2)]
    t2 = [P.sbuf("t2_%d" % i, [64, 512], F32) for i in range(2)]
    stg = [P.sbuf("stg%d" % i, [128, 512], BF16) for i in range(4)]
    pss = P.psum("pss", [128, 512], F32)
    pmm = [P.psum("pbm%d" % i, [128, 512], F32) for i in range(2)]
    ppe = P.psum("ppe", [128, 512], F32)
    psw = P.psum("psw", [128, 512], F32)
    cnt = {"mm": 0, "stg": 0, "t": 0}

    def nxt_bank():
        i = cnt["mm"] % 2
        cnt["mm"] += 1
        return pmm[i], "b_pbm%d" % i

    def nxt_stg():
        i = cnt["stg"] % 4
        cnt["stg"] += 1
        return stg[i], "stg%d" % i

    def rmsnorm(src, skey, nk, g, dst, dkey, n, dim):
        P.op("act", lambda e: e.activation(out=sq[:, 0:nk, 0:n], in_=src[:, 0:nk, 0:n], func=AF.Square), reads=[skey], writes=["sq"])
        for k in range(nk):
            P.op("pe", lambda e, k=k: e.matmul(pss[:, 0:n], onesf[:], sq[:, k, 0:n], start=(k == 0), stop=(k == nk - 1)),
                 reads=["sq", "onesf"], excl=["b_pss"])
        P.op("dve", lambda e: e.tensor_scalar(out=rr[:, 0, 0:n], in0=pss[:, 0:n], scalar1=1.0 / dim, scalar2=EPS, op0=ALU.mult, op1=ALU.add),
             writes=["rr0"], excl=["b_pss"])
        P.op("act", lambda e: e.activation(out=rr[:, 1, 0:n], in_=rr[:, 0, 0:n], func=AF.Sqrt), reads=["rr0"], writes=["rr1"])
        P.op("dve", lambda e: e.reciprocal(out=rr[:, 2, 0:n], in_=rr[:, 1, 0:n]), reads=["rr1"], writes=["rr2"])
        for k in range(nk):
            P.op("dve", lambda e, k=k: e.scalar_tensor_tensor(out=dst[:, k, 0:n], in0=src[:, k, 0:n], scalar=g[:, k:k + 1], in1=rr[:, 2, 0:n],
                                                              op0=ALU.mult, op1=ALU.mult),
                 reads=[skey, "rr2", "gq", "gkv"], writes=[dkey])

    def rope(pe_ap, sw_ap, n, excl, rd, dst, dkey):
        i = cnt["t"] % 2
        cnt["t"] += 1
        a, b = t1[i], t2[i]
        P.op("dve", lambda e: e.tensor_tensor(out=a[:, 0:n], in0=pe_ap, in1=cos[:, 0:n], op=ALU.mult), reads=["cos"] + rd, writes=["t1_%d" % i], excl=excl[0:1])
        P.op("dve", lambda e: e.tensor_tensor(out=b[:, 0:n], in0=sw_ap, in1=sin[:, 0:n], op=ALU.mult), reads=["sin"] + rd, writes=["t2_%d" % i], excl=excl[1:2])
        P.op("pool", lambda e: e.tensor_tensor(out=dst, in0=a[:, 0:n], in1=b[:, 0:n], op=ALU.add), reads=["t1_%d" % i, "t2_%d" % i], writes=[dkey])

    def do_block(t0, n):
        P.dma("sp", "cq", lambda e, t0=t0, n=n: [e.dma_start(out=cq[:, :, 0:n], in_=pT[0:768, t0:t0 + n].rearrange("(k p) t -> p k t", p=128))], writes=["cq"])
        P.dma("sp", "ckv", lambda e, t0=t0, n=n: [e.dma_start(out=ckv[:, :, 0:n], in_=pT[768:1280, t0:t0 + n].rearrange("(k p) t -> p k t", p=128))], writes=["ckv"])
        P.dma("sp", "kpe2", lambda e, t0=t0, n=n: [e.dma_start(out=kpe2[:, :, 0:n], in_=pT[1280:1408, t0:t0 + n].rearrange("(a p) t -> p a t", p=64))], writes=["kpe2"])
        P.dma("sp", "cos", lambda e, t0=t0, n=n: [e.dma_start(out=cos[:, 0:n], in_=io["cos2"][:, t0:t0 + n])], writes=["cos"])
        P.dma("sp", "sin", lambda e, t0=t0, n=n: [e.dma_start(out=sin[:, 0:n], in_=io["sin2"][:, t0:t0 + n])], writes=["sin"])
        rmsnorm(cq, "cq", 6, gq, cqn, "cqn", n, 768.0)
        rmsnorm(ckv, "ckv", 4, gkv, ckvn, "ckvn", n, 512.0)
        sg, sk = nxt_stg()
        rope(kpe2[:, 0, 0:n], kpe2[:, 1, 0:n], n, [], ["kpe2"], sg[0:64, 0:n], sk)
        P.dma("sp", sk, lambda e, sg=sg, t0=t0, n=n: [e.dma_start(out=io["kpeT"][:, t0:t0 + n], in_=sg[0:64, 0:n])], reads=[sk])
        for h in range(NH):
            pb, pk = nxt_bank()
            for k in range(6):
                P.op("pe", lambda e, pb=pb, k=k, h=h: e.matmul(pb[:, 0:n], wuq[:, k, h, 0:128], cqn[:, k, 0:n], start=(k == 0), stop=(k == 5)),
                     reads=["wuq", "cqn"], excl=[pk])
            sg, sk = nxt_stg()
            P.op("act", lambda e, sg=sg, pb=pb: e.activation(out=sg[:, 0:n], in_=pb[:, 0:n], func=AF.Identity), writes=[sk], excl=[pk])
            P.dma("sp", sk, lambda e, sg=sg, h=h, t0=t0, n=n: [e.dma_start(out=io["qT"][h, 0:128, t0:t0 + n], in_=sg[:, 0:n])], reads=[sk])
            for k in range(6):
                P.op("pe", lambda e, k=k, h=h: e.matmul(ppe[0:64, 0:n], wuq[:, k, h, 128:192], cqn[:, k, 0:n], start=(k == 0), stop=(k == 5)),
                     reads=["wuq", "cqn"], excl=["b_ppe"])
            for k in range(6):
                P.op("pe", lambda e, k=k, h=h: e.matmul(psw[0:64, 0:n], wuqsw[:, k, h, :], cqn[:, k, 0:n], start=(k == 0), stop=(k == 5)),
                     reads=["wuqsw", "cqn"], excl=["b_psw"])
            sg, sk = nxt_stg()
            rope(ppe[0:64, 0:n], psw[0:64, 0:n], n, ["b_ppe", "b_psw"], [], sg[0:64, 0:n], sk)
            P.dma("sp", sk, lambda e, sg=sg, h=h, t0=t0, n=n: [e.dma_start(out=io["qT"][h, 128:192, t0:t0 + n], in_=sg[0:64, 0:n])], reads=[sk])
            pb, pk = nxt_bank()
            for k in range(4):
                P.op("pe", lambda e, pb=pb, k=k, h=h: e.matmul(pb[:, 0:n], wukv[:, k, h, 0:128], ckvn[:, k, 0:n], start=(k == 0), stop=(k == 3)),
                     reads=["wukv", "ckvn"], excl=[pk])
            sg, sk = nxt_stg()
            P.op("act", lambda e, sg=sg, pb=pb: e.activation(out=sg[:, 0:n], in_=pb[:, 0:n], func=AF.Identity), writes=[sk], excl=[pk])
            P.dma("sp", sk, lambda e, sg=sg, h=h, t0=t0, n=n: [e.dma_start(out=io["kT"][h, :, t0:t0 + n], in_=sg[:, 0:n])], reads=[sk])
        for tt in range(n // 128):
            tile = (t0 + tt * 128) // 128
            for half in range(2):
                pb, pk = nxt_bank()
                for k in range(4):
                    P.op("pe", lambda e, pb=pb, k=k, tt=tt, half=half: e.matmul(
                        pb[:, :].rearrange("p (h d) -> p h d", h=4), ckvn[:, k, tt * 128:(tt + 1) * 128], wukv[:, k, 4 * half:4 * half + 4, 128:256],
                        start=(k == 0), stop=(k == 3)), reads=["wukv", "ckvn"], excl=[pk])
                sg, sk = nxt_stg()
                P.op("act", lambda e, sg=sg, pb=pb: e.activation(out=sg[:, :], in_=pb[:, :], func=AF.Identity), writes=[sk], excl=[pk])
                P.dma("sp", sk, lambda e, sg=sg, half=half, tile=tile: [e.dma_start(
                    out=io["vs"][4 * half:4 * half + 4, :, tile, :].rearrange("h p d -> p h d"),
                    in_=sg[:, :].rearrange("p (h d) -> p h d", h=4))], reads=[sk])

    for (t0, n) in token_blocks(nt):
        do_block(t0, n)
    P.pop()


def phase_B2(P, nc, io, nt, heads=range(NH)):
    P.push()
    nkt = nt // 128
    KT = P.sbuf("KT", [128, nt], BF16)
    KP = P.sbuf("KP", [64, nt], BF16)
    V = P.sbuf("V", [128, nkt, 128], BF16)
    QN = P.sbuf("QN", [128, nt], BF16)
    QP = P.sbuf("QP", [64, nt], BF16)
    onesb = P.sbuf("onesb", [128, 128], BF16)
    pbuf = [P.sbuf("pbuf%d" % i, [128, 512], BF16) for i in range(3)]
    sga = [P.sbuf("sga%d" % i, [128, 512], F32) for i in range(2)]
    rec = [P.sbuf("rec%d" % i, [128, 512], F32) for i in range(2)]
    ot = [P.sbuf("ot%d" % i, [128, 512], F32) for i in range(2)]
    ogs = [P.sbuf("ogs%d" % i, [128, 512], BF16) for i in range(2)]
    S = [P.psum("S%d" % i, [128, 512], F32) for i in range(3)]
    O = [P.psum("O%d" % i, [128, 512], F32) for i in range(2)]
    Dn = [P.psum("Dn%d" % i, [128, 512], F32) for i in range(2)]
    P.op("pool", lambda e: e.memset(onesb[:], 1.0), writes=["onesb"])
    P.dma("sp", "KP", lambda e: [e.dma_start(out=KP[:], in_=io["kpeT"])], writes=["KP"])
    ga0 = SEG_OFF["ga"] * 128
    si = 0
    qb = 0
    for h in heads:
        P.dma("sp", "KT", lambda e, h=h: [e.dma_start(out=KT[:], in_=io["kT"][h])], writes=["KT"])
        P.dma("sp", "V", lambda e, h=h: [e.dma_start(out=V[:], in_=io["vs"][h])], writes=["V"])
        P.dma("sp", "QN", lambda e, h=h: [e.dma_start(out=QN[:], in_=io["qT"][h, 0:128, :])], writes=["QN"])
        P.dma("sp", "QP", lambda e, h=h: [e.dma_start(out=QP[:], in_=io["qT"][h, 128:192, :])], writes=["QP"])
        def do_qblock(h, t0, n, qb, si):
            tiles = list(range(NCTX // 128)) if t0 < NCTX else list(range(nkt))
            o_, ok = O[qb % 2], "b_O%d" % (qb % 2)
            d_, dk = Dn[qb % 2], "b_Dn%d" % (qb % 2)
            sg_, sgk = sga[qb % 2], "sga%d" % (qb % 2)
            P.dma("sp", sgk, lambda e, sg_=sg_, h=h, t0=t0, n=n: [e.dma_start(out=sg_[:, 0:n], in_=io["pT"][ga0 + h * 128:ga0 + (h + 1) * 128, t0:t0 + n])],
                  writes=[sgk])

            def qk(kt, slot):
                s_, sk = S[slot % 3], "b_S%d" % (slot % 3)
                P.op("pe", lambda e: e.matmul(s_[:, 0:n], KT[:, kt * 128:(kt + 1) * 128], QN[:, t0:t0 + n], start=True, stop=False),
                     reads=["KT", "QN"], excl=[sk])
                P.op("pe", lambda e: e.matmul(s_[:, 0:n], KP[:, kt * 128:(kt + 1) * 128], QP[:, t0:t0 + n], start=False, stop=True),
                     reads=["KP", "QP"], excl=[sk])
                pb, pk = pbuf[slot % 3], "pbuf%d" % (slot % 3)
                P.op("act", lambda e: e.activation(out=pb[:, 0:n], in_=s_[:, 0:n], func=AF.Exp, scale=ATTN_SCALE), writes=[pk], excl=[sk])

            def pv(kt, slot, first, last):
                pb, pk = pbuf[slot % 3], "pbuf%d" % (slot % 3)
                P.op("pe", lambda e: e.matmul(o_[:, 0:n], V[:, kt, :], pb[:, 0:n], start=first, stop=last), reads=["V", pk], excl=[ok])
                P.op("pe", lambda e: e.matmul(d_[:, 0:n], onesb[:], pb[:, 0:n], start=first, stop=last), reads=["onesb", pk], excl=[dk])

            qk(tiles[0], si)
            for idx, kt in enumerate(tiles):
                if idx + 1 < len(tiles):
                    qk(tiles[idx + 1], si + idx + 1)
                pv(kt, si + idx, idx == 0, idx == len(tiles) - 1)
            r_, rk = rec[qb % 2], "rec%d" % (qb % 2)
            t_, tk = ot[qb % 2], "ot%d" % (qb % 2)
            g_, gk = ogs[qb % 2], "ogs%d" % (qb % 2)
            P.op("dve", lambda e, r_=r_, d_=d_, n=n: e.reciprocal(out=r_[:, 0:n], in_=d_[:, 0:n]), writes=[rk], excl=[dk])
            P.op("dve", lambda e, r_=r_, o_=o_, t_=t_, n=n: e.tensor_tensor(out=t_[:, 0:n], in0=o_[:, 0:n], in1=r_[:, 0:n], op=ALU.mult),
                 reads=[rk], writes=[tk], excl=[ok])
            P.op("pool", lambda e, t_=t_, sg_=sg_, g_=g_, n=n: e.tensor_tensor(out=g_[:, 0:n], in0=t_[:, 0:n], in1=sg_[:, 0:n], op=ALU.mult),
                 reads=[tk, sgk], writes=[gk])
            P.dma("sp", gk, lambda e, g_=g_, h=h, t0=t0, n=n: [e.dma_start(out=io["ogT"][h * 128:(h + 1) * 128, t0:t0 + n], in_=g_[:, 0:n])], reads=[gk])
            return len(tiles)

        for (t0, n) in token_blocks(nt):
            si += do_qblock(h, t0, n, qb, si)
            qb += 1
    P.pop()


NG = 2
HPG = 16
HD = 64
GW = HPG * HD


def seq_blocks(nt):
    out = [(0, NCTX, 0, NCTX)]
    t = NCTX
    while t < nt:
        n = min(512, nt - t)
        out.append((t, n, NCTX, nt))
        t += n
    return out


def phase_S1(P, nc, io, nt):
    P.push()
    pT = io["pT"]
    identf = P.sbuf("identf1", [128, 128], F32)
    convw = P.sbuf("convw", [128, 20, 5], F32)
    convb = P.sbuf("convb", [128, 20], F32)
    dtb = P.sbuf("dtb", [128, 1], F32)
    alog = P.sbuf("alog", [128, 1], F32)
    mulc = P.sbuf("mulc", [128, 1], F32)
    P.dma("sp", "identf1", lambda e: [e.dma_start(out=identf[:], in_=io["identf"])], writes=["identf1"])
    P.dma("sp", "convw", lambda e: [e.dma_start(out=convw[:], in_=io["convw"])], writes=["convw"])
    P.dma("sp", "convb", lambda e: [e.dma_start(out=convb[:], in_=io["convb"])], writes=["convb"])
    P.dma("sp", "dtb", lambda e: [e.dma_start(out=dtb[:], in_=io["dtbias2"])], writes=["dtb"])
    P.dma("sp", "alog", lambda e: [e.dma_start(out=alog[:], in_=io["alog2"])], writes=["alog"])
    P.op("act", lambda e: e.activation(out=mulc[:], in_=alog[:], func=AF.Exp), reads=["alog"], writes=["mulc"])
    P.op("dve", lambda e: e.tensor_scalar(out=mulc[:], in0=mulc[:], scalar1=-1.0, scalar2=None, op0=ALU.mult), reads=["mulc"], writes=["mulc"])
    P.op("dve", lambda e: e.memset(mulc[0:64, :], 1.0), reads=["mulc"], writes=["mulc"])

    cin = [P.sbuf("cin%d" % i, [128, 516], F32) for i in range(3)]
    acc = [P.sbuf("acc%d" % i, [128, 512], F32) for i in range(2)]
    cv = [P.sbuf("cv%d" % i, [128, 512], F32) for i in range(8)]
    stf = [P.sbuf("stf%d" % i, [128, 512], F32) for i in range(3)]
    stb = [P.sbuf("stb%d" % i, [128, 512], BF16) for i in range(3)]
    sp_ = [P.sbuf("sp%d" % i, [128, 512], F32) for i in range(4)]
    pt = [P.psum("pt%d" % i, [128, 512], F32) for i in range(3)]
    c = {"cin": 0, "acc": 0, "cv": 0, "stf": 0, "stb": 0, "pt": 0}

    def rot(name, arr):
        i = c[name] % len(arr)
        c[name] += 1
        return arr[i], "%s%d" % (name, i)

    xs0 = SEG_OFF["xs"] * 128
    b0 = SEG_OFF["B"] * 128
    c0 = SEG_OFF["C"] * 128
    d0 = SEG_OFF["dt"] * 128

    def conv_chunk(row0, cc, t0, n, s0, s1):
        ci, cik = rot("cin", cin)
        lo = t0 - 2 if t0 > s0 else t0
        hi = t0 + n + 2 if t0 + n < s1 else t0 + n
        if lo == t0:
            P.op("pool", lambda e: e.memset(ci[:, 0:2], 0.0), writes=[cik])
        if hi == t0 + n:
            P.op("pool", lambda e: e.memset(ci[:, n + 2:n + 4], 0.0), writes=[cik])
        P.dma("sp", cik, lambda e: [e.dma_start(out=ci[:, lo - (t0 - 2):hi - (t0 - 2)], in_=pT[row0:row0 + 128, lo:hi])], reads=[cik], writes=[cik])
        a, ak = rot("acc", acc)
        P.op("dve", lambda e: e.tensor_scalar(out=a[:, 0:n], in0=ci[:, 0:n], scalar1=convw[:, cc, 0:1], scalar2=None, op0=ALU.mult),
             reads=[cik, "convw"], writes=[ak])
        for k in range(1, 5):
            P.op("dve", lambda e, k=k: e.scalar_tensor_tensor(out=a[:, 0:n], in0=ci[:, k:k + n], scalar=convw[:, cc, k:k + 1], in1=a[:, 0:n],
                                                              op0=ALU.mult, op1=ALU.add), reads=[cik, "convw", ak], writes=[ak])
        o, ok = rot("cv", cv)
        P.op("act", lambda e: e.activation(out=o[:, 0:n], in_=a[:, 0:n], func=AF.Silu, bias=convb[:, cc:cc + 1]), reads=[ak, "convb"], writes=[ok])
        return o, ok

    def do_block(t0, n, s0, s1):
        ntile = n // 128
        for cg in range(4):
            bufs = [conv_chunk(xs0 + (cg * 4 + q) * 128, cg * 4 + q, t0, n, s0, s1) for q in range(4)]
            for tt in range(ntile):
                pb, pk = rot("pt", pt)
                for q in range(4):
                    o, ok = bufs[q]
                    P.op("pe", lambda e, pb=pb, q=q, o=o, tt=tt: e.transpose(pb[:, q * 128:(q + 1) * 128], o[:, tt * 128:(tt + 1) * 128], identf[:]),
                         reads=[ok, "identf1"], excl=["b_" + pk])
                sg, sk = rot("stf", stf)
                P.op("act", lambda e, sg=sg, pb=pb: e.activation(out=sg[:], in_=pb[:], func=AF.Identity), writes=[sk], excl=["b_" + pk])
                r0 = t0 + tt * 128
                P.dma("sp", sk, lambda e, sg=sg, r0=r0, cg=cg: [e.dma_start(out=io["xs_tm"][r0:r0 + 128, cg * 512:(cg + 1) * 512], in_=sg[:])], reads=[sk])
        for g in range(NG):
            o, ok = conv_chunk(b0 + g * 128, 16 + g, t0, n, s0, s1)
            sg, sk = rot("stb", stb)
            P.op("act", lambda e, sg=sg, o=o: e.activation(out=sg[:, 0:n], in_=o[:, 0:n], func=AF.Identity), reads=[ok], writes=[sk])
            P.dma("sp", sk, lambda e, sg=sg, g=g: [e.dma_start(out=io["BT"][g, :, t0:t0 + n], in_=sg[:, 0:n])], reads=[sk])
            pb, pk = rot("pt", pt)
            for tt in range(ntile):
                P.op("pe", lambda e, pb=pb, o=o, tt=tt: e.transpose(pb[:, tt * 128:(tt + 1) * 128], o[:, tt * 128:(tt + 1) * 128], identf[:]),
                     reads=[ok, "identf1"], excl=["b_" + pk])
            sg2, sk2 = rot("stb", stb)
            P.op("act", lambda e, sg2=sg2, pb=pb: e.activation(out=sg2[:, 0:n], in_=pb[:, 0:n], func=AF.Identity), writes=[sk2], excl=["b_" + pk])
            P.dma("sp", sk2, lambda e, sg2=sg2, g=g: [e.dma_start(out=io["B_tm"][g, t0:t0 + n, :].rearrange("(tt p) m -> p tt m", p=128),
                                                                in_=sg2[:, 0:n].rearrange("p (tt m) -> p tt m", m=128))], reads=[sk2])
        for g in range(NG):
            o, ok = conv_chunk(c0 + g * 128, 18 + g, t0, n, s0, s1)
            sg, sk = rot("stb", stb)
            P.op("act", lambda e, sg=sg, o=o: e.activation(out=sg[:, 0:n], in_=o[:, 0:n], func=AF.Identity), reads=[ok], writes=[sk])
            P.dma("sp", sk, lambda e, sg=sg, g=g: [e.dma_start(out=io["CT"][g, :, t0:t0 + n], in_=sg[:, 0:n])], reads=[sk])
        xb, l1, l2, dd = sp_
        P.dma("sp", "sp0", lambda e: [e.dma_start(out=xb[0:64, 0:n], in_=pT[d0:d0 + 64, t0:t0 + n]),
                                      e.dma_start(out=xb[64:128, 0:n], in_=pT[d0:d0 + 64, t0:t0 + n])], writes=["sp0"], n=2)
        P.op("act", lambda e: e.activation(out=xb[:, 0:n], in_=xb[:, 0:n], func=AF.Identity, bias=dtb[:, 0:1]), reads=["sp0", "dtb"], writes=["sp0"])
        P.op("act", lambda e: e.activation(out=l1[:, 0:n], in_=xb[:, 0:n], func=AF.Abs), reads=["sp0"], writes=["sp1"])
        P.op("act", lambda e: e.activation(out=l1[:, 0:n], in_=l1[:, 0:n], func=AF.Exp, scale=-1.0), reads=["sp1"], writes=["sp1"])
        P.op("dve", lambda e: e.tensor_scalar(out=l1[:, 0:n], in0=l1[:, 0:n], scalar1=1.0, scalar2=None, op0=ALU.add), reads=["sp1"], writes=["sp1"])
        P.op("act", lambda e: e.activation(out=l2[:, 0:n], in_=l1[:, 0:n], func=AF.Ln), reads=["sp1"], writes=["sp2"])
        P.op("dve", lambda e: e.scalar_tensor_tensor(out=l2[:, 0:n], in0=xb[:, 0:n], scalar=0.0, in1=l2[:, 0:n], op0=ALU.max, op1=ALU.add),
             reads=["sp0", "sp2"], writes=["sp2"])
        P.op("dve", lambda e: e.tensor_scalar(out=dd[:, 0:n], in0=l2[:, 0:n], scalar1=mulc[:, 0:1], scalar2=None, op0=ALU.mult),
             reads=["sp2", "mulc"], writes=["sp3"])
        pb, pk = rot("pt", pt)
        for tt in range(ntile):
            P.op("pe", lambda e, pb=pb, tt=tt: e.transpose(pb[:, tt * 128:(tt + 1) * 128], dd[:, tt * 128:(tt + 1) * 128], identf[:]),
                 reads=["sp3", "identf1"], excl=["b_" + pk])
        sg, sk = rot("stf", stf)
        P.op("act", lambda e, sg=sg, pb=pb: e.activation(out=sg[:, 0:n], in_=pb[:, 0:n], func=AF.Identity), writes=[sk], excl=["b_" + pk])
        P.dma("sp", sk, lambda e, sg=sg: [e.dma_start(out=io["dtt"][t0:t0 + n, :].rearrange("(tt p) m -> p tt m", p=128),
                                                        in_=sg[:, 0:n].rearrange("p (tt m) -> p tt m", m=128))], reads=[sk])

    for (t0, n, s0, s1) in seq_blocks(nt):
        do_block(t0, n, s0, s1)
    P.pop()


def phase_S2(P, nc, io, nt, nsteps=None):
    P.push()
    nck = nt // 128
    nctx = NCTX // 128
    order = {0: list(range(nck)), 1: list(range(nctx - 1, -1, -1)) + list(range(nck - 1, nctx - 1, -1))}
    cm = {}
    for nm in ("TL", "TG", "SG", "SL", "onesf"):
        cm[nm] = P.sbuf("m_" + nm, [128, 128], F32)
        P.dma("sp", "m_" + nm, lambda e, nm=nm: [e.dma_start(out=cm[nm][:], in_=io[nm])], writes=["m_" + nm])
    TRI = {0: "TL", 1: "TG"}
    UU = {0: "SG", 1: "SL"}
    Sst = {}
    Sbf = {}
    for d in range(2):
        for g in range(NG):
            Sst[d, g] = P.sbuf("Sst%d%d" % (d, g), [128, GW], F32)
            Sbf[d, g] = P.sbuf("Sbf%d%d" % (d, g), [128, GW], BF16)
            P.op("pool", lambda e, d=d, g=g: e.memset(Sst[d, g][:], 0.0), writes=["Sst%d%d" % (d, g)])
            P.op("pool", lambda e, d=d, g=g: e.memset(Sbf[d, g][:], 0.0), writes=["Sbf%d%d" % (d, g)])
    NW = 2
    W = []
    for w in range(NW):
        W.append(dict(
            xs=P.sbuf("w%d_xs" % w, [128, GW], F32), xdt=P.sbuf("w%d_xdt" % w, [128, GW], BF16), xw=P.sbuf("w%d_xw" % w, [128, GW], BF16),
            R=P.sbuf("w%d_R" % w, [128, HPG, 128], F32), E=P.sbuf("w%d_E" % w, [128, HPG, 128], F32), MT=P.sbuf("w%d_MT" % w, [128, HPG, 128], BF16),
            mcb=P.sbuf("w%d_mcb" % w, [128, 128], F32), bt=P.sbuf("w%d_bt" % w, [128, 128], BF16), ct=P.sbuf("w%d_ct" % w, [128, 128], BF16),
            btm=P.sbuf("w%d_btm" % w, [128, 128], BF16), dtt=P.sbuf("w%d_dtt" % w, [128, 128], F32),
            ex3=P.sbuf("w%d_ex3" % w, [128, 48], F32), wv=P.sbuf("w%d_wv" % w, [128, 16], F32),
            ysb=P.sbuf("w%d_ysb" % w, [128, GW], F32), tmp=P.sbuf("w%d_tmp" % w, [128, GW], F32), yo=P.sbuf("w%d_yo" % w, [128, GW], F32),
        ))
    G = [P.psum("G%d" % i, [128, 512], F32) for i in range(2)]
    Y = [P.psum("Y%d" % i, [128, 512], F32) for i in range(2)]
    Z = [P.psum("Z%d" % i, [128, 512], F32) for i in range(2)]
    SM = P.psum("SM", [128, 512], F32)

    def step(si, d, g, ck):
        w = si % NW
        B = W[w]
        K = lambda s: "w%d_%s" % (w, s)
        tok = slice(ck * 128, (ck + 1) * 128)
        sk_st, sk_bf = "Sst%d%d" % (d, g), "Sbf%d%d" % (d, g)
        st, sb = Sst[d, g], Sbf[d, g]
        dcol = d * 32 + g * 16
        P.dma("sp", K("xs"), lambda e: [e.dma_start(out=B["xs"][:], in_=io["xs_tm"][tok, g * GW:(g + 1) * GW])], writes=[K("xs")])
        P.dma("sp", K("bt"), lambda e: [e.dma_start(out=B["bt"][:], in_=io["BT"][g, :, tok])], writes=[K("bt")])
        P.dma("sp", K("ct"), lambda e: [e.dma_start(out=B["ct"][:], in_=io["CT"][g, :, tok])], writes=[K("ct")])
        P.dma("sp", K("btm"), lambda e: [e.dma_start(out=B["btm"][:], in_=io["B_tm"][g, tok, :])], writes=[K("btm")])
        P.dma("sp", K("dtt"), lambda e: [e.dma_start(out=B["dtt"][:], in_=io["dtt"][tok, :])], writes=[K("dtt")])
        dt_ap = B["dtt"][:, dcol:dcol + 16]
        dta_ap = B["dtt"][:, 64 + dcol:64 + dcol + 16]
        tri, uu = cm[TRI[d]], cm[UU[d]]
        P.op("pool", lambda e: e.tensor_tensor(out=B["R"][:], in0=tri[:].unsqueeze(1).broadcast_to([128, HPG, 128]),
                                               in1=dta_ap.unsqueeze(2).broadcast_to([128, HPG, 128]), op=ALU.mult),
             reads=[K("dtt"), "m_" + TRI[d]], writes=[K("R")])
        P.op("pe", lambda e: e.matmul(SM[:, 0:128], B["bt"][:], B["ct"][:], start=True, stop=True), reads=[K("bt"), K("ct")], excl=["b_SM"])
        P.op("pe", lambda e: e.matmul(SM[:, 128:144], tri[:], dta_ap, start=True, stop=True), reads=[K("dtt"), "m_" + TRI[d]], excl=["b_SM"])
        P.op("pe", lambda e: e.matmul(SM[:, 144:160], uu[:], dta_ap, start=True, stop=True), reads=[K("dtt"), "m_" + UU[d]], excl=["b_SM"])
        P.op("pe", lambda e: e.matmul(SM[:, 160:176], cm["onesf"][:], dta_ap, start=True, stop=True), reads=[K("dtt"), "m_onesf"], excl=["b_SM"])
        P.op("dve", lambda e: e.tensor_tensor(out=B["mcb"][:], in0=SM[:, 0:128], in1=tri[:], op=ALU.mult), reads=["m_" + TRI[d]], writes=[K("mcb")], excl=["b_SM"])
        P.op("act", lambda e: e.activation(out=B["ex3"][:], in_=SM[:, 128:176], func=AF.Exp), writes=[K("ex3")], excl=["b_SM"])
        P.op("dve", lambda e: e.tensor_tensor(out=B["wv"][:], in0=B["ex3"][:, 16:32], in1=dt_ap, op=ALU.mult), reads=[K("ex3"), K("dtt")], writes=[K("wv")])
        for q in range(4):
            gb, gk = G[q % 2], "b_G%d" % (q % 2)
            P.op("pe", lambda e, gb=gb, q=q: e.matmul(gb[:, :].rearrange("p (h l) -> p h l", h=4), uu[:], B["R"][:, 4 * q:4 * q + 4, :], start=True, stop=True),
                 reads=[K("R"), "m_" + UU[d]], excl=[gk])
            P.op("act", lambda e, gb=gb, q=q: e.activation(out=B["E"][:, 4 * q:4 * q + 4, :], in_=gb[:, :].rearrange("p (h l) -> p h l", h=4), func=AF.Exp),
                 writes=[K("E")], excl=[gk])
        P.op("dve", lambda e: e.tensor_tensor(out=B["MT"][:], in0=B["E"][:], in1=B["mcb"][:].unsqueeze(1).broadcast_to([128, HPG, 128]), op=ALU.mult),
             reads=[K("E"), K("mcb")], writes=[K("MT")])
        xs3 = B["xs"][:, :].rearrange("p (h d) -> p h d", h=HPG)
        P.op("dve", lambda e: e.tensor_tensor(out=B["xdt"][:, :].rearrange("p (h d) -> p h d", h=HPG), in0=xs3,
                                              in1=dt_ap.unsqueeze(2).broadcast_to([128, HPG, HD]), op=ALU.mult),
             reads=[K("xs"), K("dtt")], writes=[K("xdt")])
        P.op("pool", lambda e: e.tensor_tensor(out=B["xw"][:, :].rearrange("p (h d) -> p h d", h=HPG), in0=xs3,
                                               in1=B["wv"][:].unsqueeze(2).broadcast_to([128, HPG, HD]), op=ALU.mult),
             reads=[K("xs"), K("wv")], writes=[K("xw")])
        for h in range(HPG):
            yb, yk = Y[h // 8], "b_Y%d" % (h // 8)
            P.op("pe", lambda e, yb=yb, h=h: e.matmul(yb[:, (h % 8) * HD:(h % 8 + 1) * HD], B["MT"][:, h, :], B["xdt"][:, h * HD:(h + 1) * HD], start=True, stop=True),
                 reads=[K("MT"), K("xdt")], excl=[yk])
        for hf in range(2):
            P.op("pe", lambda e, hf=hf: e.matmul(Z[hf][:, :], B["ct"][:], sb[:, hf * 512:(hf + 1) * 512], start=True, stop=True),
                 reads=[K("ct"), sk_bf], excl=["b_Z%d" % hf])
        for hf in range(2):
            P.op("act", lambda e, hf=hf: e.activation(out=B["ysb"][:, hf * 512:(hf + 1) * 512], in_=Y[hf][:, :], func=AF.Identity),
                 writes=[K("ysb") + str(hf)], excl=["b_Y%d" % hf])
            P.op("dve", lambda e, hf=hf: e.tensor_tensor(out=B["tmp"][:, hf * 512:(hf + 1) * 512].rearrange("p (h d) -> p h d", h=8),
                                                         in0=Z[hf][:, :].rearrange("p (h d) -> p h d", h=8),
                                                         in1=B["ex3"][:, 8 * hf:8 * hf + 8].unsqueeze(2).broadcast_to([128, 8, HD]), op=ALU.mult),
                 reads=[K("ex3")], writes=[K("tmp") + str(hf)], excl=["b_Z%d" % hf])
        P.op("pool", lambda e: e.tensor_tensor(out=B["yo"][:], in0=B["tmp"][:], in1=B["ysb"][:], op=ALU.add),
             reads=[K("tmp") + "0", K("tmp") + "1", K("ysb") + "0", K("ysb") + "1"], writes=[K("yo")])
        P.dma("pool", K("yo"), lambda e: [e.dma_start(out=io["yd"][d, tok, g * GW:(g + 1) * GW], in_=B["yo"][:])], reads=[K("yo")])
        for hf in range(2):
            P.op("pe", lambda e, hf=hf: e.matmul(Z[hf][:, :], B["btm"][:], B["xw"][:, hf * 512:(hf + 1) * 512], start=True, stop=True),
                 reads=[K("btm"), K("xw")], excl=["b_Z%d" % hf])
        P.op("dve", lambda e: e.tensor_tensor(out=st[:, :].rearrange("p (h d) -> p h d", h=HPG), in0=st[:, :].rearrange("p (h d) -> p h d", h=HPG),
                                              in1=B["ex3"][:, 32:48].unsqueeze(2).broadcast_to([128, HPG, HD]), op=ALU.mult),
             reads=[K("ex3"), sk_st], writes=[sk_st])
        for hf in range(2):
            P.op("dve", lambda e, hf=hf: e.tensor_tensor(out=st[:, hf * 512:(hf + 1) * 512], in0=st[:, hf * 512:(hf + 1) * 512], in1=Z[hf][:, :], op=ALU.add),
                 reads=[sk_st], writes=[sk_st], excl=["b_Z%d" % hf])
        P.op("act", lambda e: e.activation(out=sb[:], in_=st[:], func=AF.Identity), reads=[sk_st], writes=[sk_bf])

    si = 0
    ns = nck if nsteps is None else nsteps
    for i in range(ns):
        for d in range(2):
            for g in range(NG):
                step(si, d, g, order[d][i])
                si += 1
    P.pop()


def phase_S3(P, nc, io, nt):
    P.push()
    identf = P.sbuf("identf3", [128, 128], F32)
    onesf = P.sbuf("onesf3", [128, 128], F32)
    dsk = P.sbuf("dsk", [128, 2048], F32)
    P.dma("sp", "identf3", lambda e: [e.dma_start(out=identf[:], in_=io["identf"])], writes=["identf3"])
    P.dma("sp", "onesf3", lambda e: [e.dma_start(out=onesf[:], in_=io["onesf"])], writes=["onesf3"])
    P.dma("sp", "dsk", lambda e: [e.dma_start(out=dsk[:], in_=io["dskbc"])], writes=["dsk"])
    gssm = P.sbuf("gssm", [128, 16], F32)
    P.dma("sp", "gssm", lambda e: [e.dma_start(out=gssm[:], in_=io["gssmT"])], writes=["gssm"])
    yf = [P.sbuf("yf%d" % i, [128, 2048], F32) for i in range(2)]
    yb = [P.sbuf("yb%d" % i, [128, 2048], F32) for i in range(2)]
    xs = [P.sbuf("xs3_%d" % i, [128, 2048], F32) for i in range(2)]
    sz = [P.sbuf("sz%d" % i, [128, 16, 128], F32) for i in range(2)]
    yg = [P.sbuf("yg%d" % i, [128, 512], F32) for i in range(2)]
    sq = [P.sbuf("sq3_%d" % i, [128, 512], F32) for i in range(2)]
    ygs = [P.sbuf("ygs%d" % i, [128, 16, 512], BF16) for i in range(2)]
    ssr = P.sbuf("ssr", [1, 512], F32)
    pt = [P.psum("pt3_%d" % i, [128, 512], F32) for i in range(2)]
    pq = P.psum("pq", [128, 512], F32)
    z0 = SEG_OFF["z"] * 128
    nck = nt // 128
    qi = 0
    blocks = token_blocks(nt)
    for (t0, n) in blocks:
        bi = (t0 // 512) % 2 if t0 >= NCTX else 0
        gs, gsk = ygs[bi % 2], "ygs%d" % (bi % 2)
        for tt in range(n // 128):
            ck = (t0 + tt * 128) // 128
            i2 = ck % 2
            tok = slice(ck * 128, (ck + 1) * 128)
            a, b_, x_, s_ = yf[i2], yb[i2], xs[i2], sz[i2]
            ka, kb, kx, ks = "yf%d" % i2, "yb%d" % i2, "xs3_%d" % i2, "sz%d" % i2
            P.dma("sp", ka, lambda e, a=a, tok=tok: [e.dma_start(out=a[:], in_=io["yd"][0, tok, :])], writes=[ka])
            P.dma("sp", kb, lambda e, b_=b_, tok=tok: [e.dma_start(out=b_[:], in_=io["yd"][1, tok, :])], writes=[kb])
            P.dma("sp", kx, lambda e, x_=x_, tok=tok: [e.dma_start(out=x_[:], in_=io["xs_tm"][tok, :])], writes=[kx])
            P.dma("sp", ks, lambda e, s_=s_, tok=tok: [e.dma_start(out=s_[:], in_=io["pT"][z0:z0 + 2048, tok].rearrange("(c p) t -> p c t", p=128))], writes=[ks])
            P.op("dve", lambda e, a=a, b_=b_: e.tensor_tensor(out=a[:], in0=a[:], in1=b_[:], op=ALU.add), reads=[ka, kb], writes=[ka])
            P.op("pool", lambda e, x_=x_: e.tensor_tensor(out=x_[:], in0=x_[:], in1=dsk[:], op=ALU.mult), reads=[kx, "dsk"], writes=[kx])
            P.op("pool", lambda e, a=a, x_=x_: e.tensor_tensor(out=a[:], in0=a[:], in1=x_[:], op=ALU.add), reads=[ka, kx], writes=[ka])
            for q in range(4):
                pb, pk = pt[qi % 2], "b_pt3_%d" % (qi % 2)
                g_, gk = yg[qi % 2], "yg%d" % (qi % 2)
                s2, s2k = sq[qi % 2], "sq3_%d" % (qi % 2)
                qi += 1
                for c4 in range(4):
                    c = q * 4 + c4
                    P.op("pe", lambda e, pb=pb, a=a, c=c, c4=c4: e.transpose(pb[:, c4 * 128:(c4 + 1) * 128], a[:, c * 128:(c + 1) * 128], identf[:]),
                         reads=[ka, "identf3"], excl=[pk])
                P.op("dve", lambda e, pb=pb, g_=g_, s_=s_, q=q: e.tensor_tensor(out=g_[:, :].rearrange("p (c t) -> p c t", c=4),
                                                                              in0=pb[:, :].rearrange("p (c t) -> p c t", c=4),
                                                                              in1=s_[:, 4 * q:4 * q + 4, :], op=ALU.mult),
                     reads=[ks], writes=[gk], excl=[pk])
                for c4 in range(4):
                    c = q * 4 + c4
                    P.op("act", lambda e, g_=g_, gs=gs, c=c, c4=c4, tt=tt: e.activation(out=gs[:, c, tt * 128:(tt + 1) * 128],
                                                                                      in_=g_[:, c4 * 128:(c4 + 1) * 128], func=AF.Identity, scale=gssm[:, c:c + 1]),
                         reads=[gk, "gssm"], writes=[(gsk, tt, q, c4)])
                P.op("act", lambda e, g_=g_, s2=s2: e.activation(out=s2[:], in_=g_[:], func=AF.Square), reads=[gk], writes=[s2k])
                for c4 in range(4):
                    first = (q == 0 and c4 == 0)
                    last = (q == 3 and c4 == 3)
                    P.op("pe", lambda e, s2=s2, c4=c4, tt=tt, first=first, last=last: e.matmul(pq[0:1, tt * 128:(tt + 1) * 128], onesf[:, 0:1], s2[:, c4 * 128:(c4 + 1) * 128],
                                                                                             start=first, stop=last),
                         reads=[s2k, "onesf3"], excl=["b_pq"])
        P.op("dve", lambda e, n=n: e.tensor_copy(out=ssr[:, 0:n], in_=pq[0:1, 0:n]), writes=["ssr"], excl=["b_pq"])
        P.dma("sp", "ssr", lambda e, t0=t0, n=n: [e.dma_start(out=io["ssq"][:, t0:t0 + n], in_=ssr[:, 0:n])], reads=["ssr"])
        P.dma("sp", gsk, lambda e, gs=gs, t0=t0, n=n: [e.dma_start(out=io["ygT_parts"][hf][:, t0:t0 + n].rearrange("(c p) t -> p c t", p=128), in_=gs[:, 8 * hf:8 * hf + 8, 0:n]) for hf in range(2)],
              n=2, reads=[(gsk, tt, q, c4) for tt in range(n // 128) for q in range(4) for c4 in range(4)])
    P.pop()


TC = 2112
NLAT = 2048


def c_blocks(with_ctx):
    blks = [(t0, 256, 0) for t0 in range(0, NLAT, 256)]
    if with_ctx:
        blks.append((NLAT, 64, 1))
    return blks


def phase_Cmod(P, nc, io, wadag_fn):
    P.push()
    c2 = P.sbuf("cm_c2", [128, KC, 2], F32)
    sc = P.sbuf("cm_sc", [128, KC, 2], F32)
    scb = P.sbuf("cm_scb", [128, 2, KC, 128], F32)
    bb = P.sbuf("cm_bb", [128, D], F32)
    wg = [P.sbuf("cm_wg%d" % i, [128, KC, 256], F32) for i in range(2)]
    sg = [P.sbuf("cm_sg%d" % i, [128, 256], F32) for i in range(2)]
    pg = [P.psum("cm_pg%d" % i, [128, 512], F32) for i in range(2)]
    P.dma("sp", "cm_c2", lambda e: [e.dma_start(out=c2[:], in_=io["c2T"])], writes=["cm_c2"])
    P.dma("sp", "cm_bb", lambda e: [e.dma_start(out=bb[:], in_=io["badag"][0:1, :].broadcast_to([128, D]))], writes=["cm_bb"])
    P.op("act", lambda e: e.activation(out=sc[:], in_=c2[:], func=AF.Silu), reads=["cm_c2"], writes=["cm_sc"])
    for v in range(2):
        P.op("dve", lambda e, v=v: e.tensor_copy(out=scb[:, v], in_=sc[:, :, v].unsqueeze(2).broadcast_to([128, KC, 128])),
             reads=["cm_sc"], writes=["cm_scb"])
    si = 0
    for cb in range(16):
        w_, wk = wg[cb % 2], "cm_wg%d" % (cb % 2)
        wp_ = wadag_fn(cb)
        P.dma("sp", wk, lambda e, w_=w_, wp_=wp_: dma_pieces(e, w_, wp_, "p (k n) -> p k n", k=KC), writes=[wk], n=len(wp_))
        for v in range(2):
            for k in range(KC):
                P.op("pe", lambda e, v=v, k=k, w_=w_: e.matmul(pg[v][:, 0:256], scb[:, v, k, :], w_[:, k, :], start=(k == 0), stop=(k == KC - 1)),
                     reads=[wk, "cm_scb"], excl=["b_cm_pg%d" % v])
            s_, sk = sg[si % 2], "cm_sg%d" % (si % 2)
            si += 1
            P.op("dve", lambda e, v=v, s_=s_, cb=cb: e.tensor_tensor(out=s_[:], in0=pg[v][:, 0:256], in1=bb[:, cb * 256:(cb + 1) * 256], op=ALU.add),
                 reads=["cm_bb"], writes=[sk], excl=["b_cm_pg%d" % v])
            P.dma("sp", sk, lambda e, v=v, s_=s_, cb=cb: [e.dma_start(out=io["gsc"][v, :, cb * 256:(cb + 1) * 256], in_=s_[:])], reads=[sk])
    P.pop()


def phase_C(P, nc, io, wpa_fn, wpb_fn, wout_fn, with_ctx, final, dyn):
    P.push()
    mrg = P.sbuf("c_mrg", [128, KC, 256], BF16)
    ones4 = P.sbuf("c_ones4", [4, 128], F32)
    P.op("pool", lambda e: e.memset(ones4[:], 1.0), writes=["c_ones4"])
    for (t0, n, var) in c_blocks(with_ctx):
        P.push()
        ogb = P.sbuf("c_ogb", [128, KC, 256], BF16)
        ygb = P.sbuf("c_ygb", [128, 64, 256], BF16)
        ss4 = P.sbuf("c_ss4", [4, 256], F32)
        rs = P.sbuf("c_rs", [128, 3, 256], F32)
        wa = [P.sbuf("c_wa%d" % i, [128, KC, 128], BF16) for i in range(2)]
        wb = [P.sbuf("c_wb%d" % i, [128, 64, 128], BF16) for i in range(2)]
        sga = [P.sbuf("c_sga%d" % i, [128, 256], F32) for i in range(2)]
        sgb = [P.sbuf("c_sgb%d" % i, [128, 256], F32) for i in range(2)]
        m1 = [P.sbuf("c_m1%d" % i, [128, 256], F32) for i in range(2)]
        m2 = [P.sbuf("c_m2%d" % i, [128, 256], F32) for i in range(2)]
        pa = [P.psum("c_pa%d" % i, [128, 512], F32) for i in range(2)]
        pb = [P.psum("c_pb%d" % i, [128, 512], F32) for i in range(2)]
        pr = P.psum("c_pr", [128, 512], F32)

        def cols(ap, t0=t0, n=n):
            return ap[:, t0:t0 + n]

        P.dma("sp", "c_ogb", lambda e, n=n, cols=cols: [e.dma_start(out=ogb[:, :, 0:n], in_=cols(io["oga2"]).rearrange("(k p) t -> p k t", p=128))], writes=["c_ogb"])
        P.dma("sp", "c_ygb", lambda e, n=n, cols=cols: [e.dma_start(out=ygb[:, 32 * q:32 * q + 32, 0:n],
                                                                   in_=cols(io["yga_parts"][q]).rearrange("(k p) t -> p k t", p=128))
                                                       for q in range(2)], writes=["c_ygb"], n=2)
        P.dma("sp", "c_ss4", lambda e, n=n, cols=cols: [e.dma_start(out=ss4[:, 0:n], in_=cols(io["ssqa"]))], writes=["c_ss4"])
        P.op("pe", lambda e, n=n: e.matmul(pr[:, 0:n], ones4[:], ss4[:, 0:n], start=True, stop=True), reads=["c_ones4", "c_ss4"], excl=["b_c_pr"])
        P.op("dve", lambda e, n=n: e.tensor_scalar(out=rs[:, 0, 0:n], in0=pr[:, 0:n], scalar1=1.0 / 8192.0, scalar2=EPS, op0=ALU.mult, op1=ALU.add),
             writes=["c_rs0"], excl=["b_c_pr"])
        P.op("act", lambda e, n=n: e.activation(out=rs[:, 1, 0:n], in_=rs[:, 0, 0:n], func=AF.Sqrt), reads=["c_rs0"], writes=["c_rs1"])
        P.op("dve", lambda e, n=n: e.reciprocal(out=rs[:, 2, 0:n], in_=rs[:, 1, 0:n]), reads=["c_rs1"], writes=["c_rs2"])
        for oc in range(KC):
            i2 = oc % 2
            wpa_, wpb_ = wpa_fn(oc), wpb_fn(oc)
            P.dma("pool", "c_wa%d" % i2, lambda e, i2=i2, wpa_=wpa_: dma_pieces(e, wa[i2], wpa_, "p (k n) -> p k n", k=KC), writes=["c_wa%d" % i2], n=len(wpa_))
            P.dma("pool", "c_wb%d" % i2, lambda e, i2=i2, wpb_=wpb_: dma_pieces(e, wb[i2], wpb_, "p (k n) -> p k n", k=64), writes=["c_wb%d" % i2], n=len(wpb_))
            P.dma("sp", "c_sga%d" % i2, lambda e, oc=oc, i2=i2, t0=t0, n=n: [e.dma_start(out=sga[i2][:, 0:n], in_=io["sgT"][oc * 128:(oc + 1) * 128, t0:t0 + n])], writes=["c_sga%d" % i2])
            P.dma("sp", "c_sgb%d" % i2, lambda e, oc=oc, i2=i2, t0=t0, n=n: [e.dma_start(out=sgb[i2][:, 0:n], in_=io["sgT"][D + oc * 128:D + (oc + 1) * 128, t0:t0 + n])], writes=["c_sgb%d" % i2])
            for k in range(KC):
                P.op("pe", lambda e, k=k, i2=i2, n=n: e.matmul(pa[i2][:, 0:n], wa[i2][:, k, :], ogb[:, k, 0:n], start=(k == 0), stop=(k == KC - 1)),
                     reads=["c_wa%d" % i2, "c_ogb"], excl=["b_c_pa%d" % i2])
            for k in range(64):
                P.op("pe", lambda e, k=k, i2=i2, n=n: e.matmul(pb[i2][:, 0:n], wb[i2][:, k, :], ygb[:, k, 0:n], start=(k == 0), stop=(k == 63)),
                     reads=["c_wb%d" % i2, "c_ygb"], excl=["b_c_pb%d" % i2])
            P.op("dve", lambda e, i2=i2, n=n: e.tensor_tensor(out=m1[i2][:, 0:n], in0=pa[i2][:, 0:n], in1=sga[i2][:, 0:n], op=ALU.mult),
                 reads=["c_sga%d" % i2], writes=["c_m1%d" % i2], excl=["b_c_pa%d" % i2])
            P.op("dve", lambda e, i2=i2, n=n: e.tensor_tensor(out=m2[i2][:, 0:n], in0=pb[i2][:, 0:n], in1=rs[:, 2, 0:n], op=ALU.mult),
                 reads=["c_rs2"], writes=["c_m2%d" % i2], excl=["b_c_pb%d" % i2])
            P.op("pool", lambda e, i2=i2, n=n: e.tensor_tensor(out=m2[i2][:, 0:n], in0=m2[i2][:, 0:n], in1=sgb[i2][:, 0:n], op=ALU.mult),
                 reads=["c_sgb%d" % i2, "c_m2%d" % i2], writes=["c_m2%d" % i2])
            P.op("pool", lambda e, i2=i2, n=n, oc=oc: e.tensor_tensor(out=mrg[:, oc, 0:n], in0=m1[i2][:, 0:n], in1=m2[i2][:, 0:n], op=ALU.add),
                 reads=["c_m1%d" % i2, "c_m2%d" % i2], writes=["c_mrg"])
        P.pop()
        P.push()
        wo = [P.sbuf("c_wo%d" % i, [128, KC, 512], BF16) for i in range(2)]
        hrow = [P.sbuf("c_hrow%d" % i, [128, D], F32) for i in range(2)]
        gb = P.sbuf("c_gb", [128, D], F32)
        ht = [P.sbuf("c_ht%d" % i, [128, 512], F32) for i in range(2)]
        tmp = [P.sbuf("c_tmp%d" % i, [128, 512], F32) for i in range(2)]
        po = [P.psum("c_po%d" % i, [128, 512], F32) for i in range(2)]
        P.dma("sp", "c_gb", lambda e, var=var: [e.dma_start(out=gb[:], in_=io["gsc"][var])], writes=["c_gb"])
        if final:
            junk = P.sbuf("c_junk", [128, D], BF16)
            gf = P.sbuf("c_gf", [128, D], F32)
            st = P.sbuf("c_st", [128, 8], F32)
            P.dma("sp", "c_gf", lambda e: [e.dma_start(out=gf[:], in_=io["gfin"][0:1, :].broadcast_to([128, D]))], writes=["c_gf"])
        tiles = [(0, 128), (128, 128)] if n == 256 else [(0, n)]
        ci = 0
        for ob in range(8):
            w_, wk = wo[ob % 2], "c_wo%d" % (ob % 2)
            wo_ = wout_fn(ob)
            P.dma("pool", wk, lambda e, w_=w_, wo_=wo_: dma_pieces(e, w_, wo_, "p (k n) -> p k n", k=KC), writes=[wk], n=len(wo_))
            for ti, (c0, rows) in enumerate(tiles):
                p_, pk = po[ci % 2], "b_c_po%d" % (ci % 2)
                h_, hk = ht[ci % 2], "c_ht%d" % (ci % 2)
                t_, tk = tmp[ci % 2], "c_tmp%d" % (ci % 2)
                ci += 1
                P.dma("sp", hk, lambda e, h_=h_, t0=t0, c0=c0, rows=rows, ob=ob: [e.dma_start(out=h_[0:rows, :], in_=io["hin_rows"](t0 + c0, rows)[:, ob * 512:(ob + 1) * 512])], writes=[hk])
                for k in range(KC):
                    P.op("pe", lambda e, p_=p_, k=k, c0=c0, rows=rows, w_=w_: e.matmul(p_[0:rows, :], mrg[:, k, c0:c0 + rows], w_[:, k, :], start=(k == 0), stop=(k == KC - 1)),
                         reads=["c_mrg", wk], excl=[pk])
                P.op("dve", lambda e, p_=p_, t_=t_, rows=rows, ob=ob: e.tensor_tensor(out=t_[0:rows, :], in0=p_[0:rows, :], in1=gb[0:rows, ob * 512:(ob + 1) * 512], op=ALU.mult),
                     reads=["c_gb"], writes=[tk], excl=[pk])
                P.op("pool", lambda e, t_=t_, h_=h_, rows=rows, ob=ob, ti=ti: e.tensor_tensor(out=hrow[ti][0:rows, ob * 512:(ob + 1) * 512], in0=t_[0:rows, :], in1=h_[0:rows, :], op=ALU.add),
                     reads=[tk, hk], writes=[("c_hrow", ti, ob)])
        for ti, (c0, rows) in enumerate(tiles):
            hkeys = [("c_hrow", ti, ob) for ob in range(8)]
            if not final:
                P.dma("sp", "c_hst%d" % ti, lambda e, ti=ti, t0=t0, c0=c0, rows=rows: [e.dma_start(out=io["hout_rows"](t0 + c0, rows), in_=hrow[ti][0:rows, :])], reads=hkeys)
            else:
                assert var == 0
                P.op("act", lambda e, ti=ti: e.activation(out=junk[:], in_=hrow[ti][:], func=AF.Square, accum_out=st[:, 4 * ti:4 * ti + 1]), reads=hkeys, writes=["c_junk", "c_st%d" % ti])
                P.op("dve", lambda e, ti=ti: e.tensor_scalar(out=st[:, 4 * ti + 1:4 * ti + 2], in0=st[:, 4 * ti:4 * ti + 1], scalar1=1.0 / D, scalar2=EPS, op0=ALU.mult, op1=ALU.add),
                     reads=["c_st%d" % ti], writes=["c_st%db" % ti])
                P.op("act", lambda e, ti=ti: e.activation(out=st[:, 4 * ti + 3:4 * ti + 4], in_=st[:, 4 * ti + 1:4 * ti + 2], func=AF.Sqrt), reads=["c_st%db" % ti], writes=["c_st%dd" % ti])
                P.op("dve", lambda e, ti=ti: e.reciprocal(out=st[:, 4 * ti + 2:4 * ti + 3], in_=st[:, 4 * ti + 3:4 * ti + 4]), reads=["c_st%dd" % ti], writes=["c_st%dc" % ti])
                P.op("dve", lambda e, ti=ti: e.scalar_tensor_tensor(out=hrow[ti][:], in0=hrow[ti][:], scalar=st[:, 4 * ti + 2:4 * ti + 3], in1=gf[:], op0=ALU.mult, op1=ALU.mult),
                     reads=hkeys + ["c_st%dc" % ti, "c_gf"], writes=hkeys)
                P.dma("sp", "c_hst%d" % ti, lambda e, ti=ti, t0=t0, c0=c0: [e.dma_start(out=io["hnorm"][t0 + c0:t0 + c0 + 128, :], in_=hrow[ti][:])], reads=hkeys)
        P.pop()
    P.pop()


I32 = mybir.dt.int32
NT = 8448
DEPTH = 2
G8 = [list(range(8))]
G4 = [[0, 1, 2, 3], [4, 5, 6, 7]]
G2 = [[0, 4], [1, 5], [2, 6], [3, 7]]
FUNC = {"id": AF.Identity, "silu": AF.Silu, "sig": AF.Sigmoid}

LAYER_IN = {
    "wadal_sh": ([16, 128, 4096], F32), "badaT": ([128, 64], F32), "gpreT": ([128, 32], F32),
    "wl_sh": ([56, 128, 4096], F32), "wmg_sh": ([16, 128, 4096], F32),
    "gqT": ([128, 6], F32), "gkvT": ([128, 4], F32), "wuq": ([128, 9216], F32), "wuqsw": ([128, 3072], F32), "wukv": ([128, 8192], F32),
    "convw": ([128, 20, 5], F32), "convb": ([128, 20], F32), "dtbias2": ([128, 1], F32), "alog2": ([128, 1], F32),
    "dskbc": ([128, 2048], F32), "gssmT": ([128, 16], F32),
    "wadag_sh": ([4, 128, 8192], F32), "badag": ([1, 4096], F32),
    "wpa_sh": ([8, 128, 4096], F32), "wpb_sh": ([8, 128, 8192], F32), "wout_sh": ([2, 128, 16384], F32),
}
CONST_IN = {
    "xin_sh": ([TC, D], F32), "offs": ([1, 2], I32), "c2T": ([128, 32, 2], F32),
    "identf": ([128, 128], F32), "onesf": ([128, 128], F32), "TL": ([128, 128], F32), "TG": ([128, 128], F32),
    "SG": ([128, 128], F32), "SL": ([128, 128], F32), "cos2": ([64, NT], F32), "sin2": ([64, NT], F32), "gfin": ([1, D], F32),
}


def build_program(nlayers=DEPTH, probe=None):
    nc = bass.Bass("TRN2", target_bir_lowering=False)
    ext = {}
    for k, (shp, dt) in CONST_IN.items():
        ext[k] = nc.dram_tensor(k, shp, dt, kind="ExternalInput").ap()
    for i in range(nlayers):
        for k, (shp, dt) in LAYER_IN.items():
            ext["%s_%d" % (k, i)] = nc.dram_tensor("%s_%d" % (k, i), shp, dt, kind="ExternalInput").ap()
    hnorm = nc.dram_tensor("hnorm", [NLAT, D], F32, kind="ExternalOutput").ap()
    P = Prog(nc)

    def dram(name, shape, dt=F32):
        return nc.dram_tensor(name, list(shape), dt)

    dyn = {}
    rl = P.es.enter_context(nc.sync.register("r_lat"))
    rc = P.es.enter_context(nc.sync.register("r_ctx"))

    def ldregs(e):
        e.reg_load(rl, ext["offs"][0:1, 0:1])
        ins = e.reg_load(rc, ext["offs"][0:1, 1:2])
        dyn["lat"] = e.snap(rl)
        dyn["ctx"] = e.snap(rc)
        return ins
    P.op("sp", ldregs)

    class Gathered:
        def __init__(self, out, CR, W, unit):
            self.out, self.CR, self.W, self.unit = out, CR, W, unit

        def rows(self, rank, r0, n):
            res = []
            r = r0
            while r < r0 + n:
                c, i = r // self.CR, r % self.CR
                k = min(self.CR - i, r0 + n - r)
                res.append((self.out.ap()[c, rank * self.CR + i:rank * self.CR + i + k, :], r - r0, k))
                r += k
            return res

        def at(self, rank, a):
            return self.rows(rank, a * self.unit, self.unit)

    def gather_from(bnc, R, ncol, dt, name, esize):
        CR = 1
        while CR * 2 * ncol * esize <= (1 << 20) and R % (CR * 2) == 0:
            CR *= 2
        nch = R // CR
        out = dram("gat_%s" % name, [nch, 4 * CR, ncol], dt)
        return out, CR, nch

    def gather(src_ap, shape, dt, name, batch):
        unit = shape[1] if len(shape) == 3 else 1
        ncol = shape[-1]
        R = shape[0] * unit
        esize = 4 if dt == F32 else 2
        bnc = dram("bnc_%s" % name, [R, ncol], dt)
        if len(shape) == 3:
            pieces = [(bnc.ap()[a * unit:(a + 1) * unit, :], src_ap[a]) for a in range(shape[0])]
        else:
            step = 264
            pieces = [(bnc.ap()[r0:min(r0 + step, R), :], src_ap[r0:min(r0 + step, R), :]) for r0 in range(0, R, step)]
        P.dma("sp", "bnc", lambda e, pieces=pieces: [e.dma_start(out=o_, in_=i_) for (o_, i_) in pieces],
              writes=[("bnc", name)], n=len(pieces))
        out, CR, nch = gather_from(bnc, R, ncol, dt, name, esize)
        batch.append((bnc, out, CR, nch, ("bnc", name)))
        return Gathered(out, CR, 4, unit)

    def flush(batch, slot):
        items = list(batch)
        tot = sum(it[3] for it in items)
        P.coll(lambda e: [e.collective_compute("AllGather", ALU.bypass, replica_groups=G4, ins=[b_.ap()[c * CR:(c + 1) * CR, :]], outs=[o_.ap()[c]])
                          for (b_, o_, CR, nch, _k) in items for c in range(nch)], tot, reads=[it[4] for it in items], writes=[], slot=slot)
        del batch[:]

    def do_gather(bnc, out, CR, nch, reads, writes):
        P.coll(lambda e: [e.collective_compute("AllGather", ALU.bypass, replica_groups=G4, ins=[bnc.ap()[c * CR:(c + 1) * CR, :]], outs=[out.ap()[c]])
                          for c in range(nch)], nch, reads=reads, writes=writes)

    def precast(g, name):
        shp = g.out.ap().shape
        o2 = dram("bf_%s" % name, list(shp), BF16)
        P.dma("pool", "precast", lambda e: [e.dma_start(out=o2.ap()[c], in_=g.out.ap()[c]) for c in range(shp[0])], n=shp[0])
        return Gathered(o2, g.CR, g.W, g.unit)

    b0_ = []
    xg = gather(ext["xin_sh"], [TC, D], F32, "x0", b0_)
    W_ = [dict() for _ in range(nlayers)]
    W_[0]["wadal"] = gather(ext["wadal_sh_0"], [16, 128, 4096], F32, "wadal0", b0_)
    flush(b0_, "cc")
    for i in range(nlayers):
        w = W_[i]
        bt = []
        if i > 0:
            w["wadal"] = gather(ext["wadal_sh_%d" % i], [16, 128, 4096], F32, "wadal%d" % i, bt)
        w["wmg"] = gather(ext["wmg_sh_%d" % i], [16, 128, 4096], F32, "wmg%d" % i, bt)
        w["wadag"] = gather(ext["wadag_sh_%d" % i], [4, 128, 8192], F32, "wadag%d" % i, bt)
        w["wpa"] = gather(ext["wpa_sh_%d" % i], [8, 128, 4096], F32, "wpa%d" % i, bt)
        w["wpb"] = gather(ext["wpb_sh_%d" % i], [8, 128, 8192], F32, "wpb%d" % i, bt)
        w["wout"] = gather(ext["wout_sh_%d" % i], [2, 128, 16384], F32, "wout%d" % i, bt)
        P.detach("ccw%d" % i)
        flush(bt, "ccw%d" % i)
    P.barrier()
    if probe == "gathers":
        f0 = lambda pcs: pcs[0][0]
        pieces = [(hnorm[8 * r_:8 * r_ + 8, :], f0(xg.rows(r_, 0, 8))) for r_ in range(4)]
        pieces += [(hnorm[32:40, :], f0(xg.rows(2, NLAT + 8, 8)))]
        w = W_[0]
        pieces += [(hnorm[40:48, :], ext["wl_sh_0"][31, 0:8, :]), (hnorm[48:56, :], ext["wl_sh_0"][27, 8:16, :]),
                   (hnorm[56:64, :], f0(w["wadal"].at(2, 10))[0:8, :]), (hnorm[64:72, :], f0(w["wmg"].at(3, 15))[0:8, :]),
                   (hnorm[72:80, :], f0(w["wadag"].at(1, 3))[0:8, 0:4096]), (hnorm[80:88, :], f0(w["wpa"].at(3, 3))[0:8, :]),
                   (hnorm[88:96, :], f0(w["wpb"].at(1, 1))[0:8, 4096:8192]), (hnorm[96:104, :], f0(w["wout"].at(2, 0))[0:8, 0:4096])]
        P.dma("sp", "probe", lambda e: [e.dma_start(out=o_, in_=i_) for (o_, i_) in pieces], n=len(pieces))
        P.emit()
        return nc, P

    pT = dram("pT", [NPT * 128, NT]).ap()
    sc = {
        "qT": dram("qT", [8, 192, NT], BF16).ap(), "kT": dram("kT", [8, 128, NT], BF16).ap(), "kpeT": dram("kpeT", [64, NT], BF16).ap(),
        "vs": dram("vs", [8, 128, NT // 128, 128], BF16).ap(),
        "xs_tm": dram("xs_tm", [NT, 2048]).ap(), "B_tm": dram("B_tm", [2, NT, 128], BF16).ap(), "BT": dram("BT", [2, 128, NT], BF16).ap(),
        "CT": dram("CT", [2, 128, NT], BF16).ap(), "dtt": dram("dtt", [NT, 128]).ap(), "yd": dram("yd", [2, NT, 2048]).ap(),
        "sgT": dram("sgT", [2 * D, TC]).ap(), "gsc": dram("gsc", [2, 128, D]).ap(),
    }
    ogb_ = dram("og_b", [1024, NT], BF16)
    ygb_ = [dram("yg_b%d" % q, [1024, NT], BF16) for q in range(2)]
    ssb_ = dram("ss_b", [1, NT])
    oga, ogCR, ognch = gather_from(ogb_, 1024, NT, BF16, "og", 2)
    ygg = [gather_from(ygb_[q], 1024, NT, BF16, "yg%d" % q, 2) for q in range(2)]
    ssa, ssCR, ssnch = gather_from(ssb_, 1, NT, F32, "ss", 4)
    og_own = dram("og_own", [4 * 1024, TC], BF16)
    yg_own = [dram("yg_own%d" % q, [4 * 1024, TC], BF16) for q in range(2)]
    ss_own = dram("ss_own", [4, TC])
    hout = dram("hout", [TC, D])
    xg1o, hCR, hnch = gather_from(hout, TC, D, F32, "h1", 4)
    xg1 = Gathered(xg1o, hCR, 4, 1)

    def hout_rows(r0, n):
        return hout.ap()[r0:r0 + n, :]

    for i in range(nlayers):
        w = W_[i]
        L = lambda k, i=i: ext["%s_%d" % (k, i)]
        xga = xg if i == 0 else xg1
        own_rows = (lambda r0, n: ext["xin_sh"][r0:r0 + n, :]) if i == 0 else hout_rows
        modio = {"c2T": ext["c2T"], "badaT": L("badaT"), "gpreT": L("gpreT"), "identf": ext["identf"]}
        if i > 0:
            P.join("ccw%d" % i)
            P.barrier()
        wlb = dram("wlbf_%d" % i, [NPT * 128, 4096], BF16)
        P.dma("pool", "precast", lambda e, i=i, wlb=wlb: [e.dma_start(out=wlb.ap()[c * 128:(c + 1) * 128, :], in_=ext["wl_sh_%d" % i][c]) for c in range(NPT)], n=NPT)
        P.barrier()
        wada_fn = lambda oc, w=w: w["wadal"].at(oc // 16, oc % 16)
        blocks = [dict(n=256, mcol=1, tiles=[[(a_, o_, k_) for (a_, o_, k_) in xga.rows(2 * t, NLAT, 64)] + [(a_, 64 + o_, k_) for (a_, o_, k_) in xga.rows(2 * t + 1, NLAT, 64)] for t in range(2)])]
        for bl in range(16):
            tiles = []
            for tt in range(4):
                l0 = bl * 512 + tt * 128
                tiles.append(xga.rows(l0 // NLAT, l0 % NLAT, 128))
            blocks.append(dict(n=512, mcol=0, tiles=tiles))
        t0s = [0] + [256 + 512 * b_ for b_ in range(16)]
        chunks = []
        cc = 0
        for name, cnt, f in SEG:
            if name in ("mga", "mgb"):
                continue
            for q in range(cnt):
                w_ap = [(wlb.ap()[cc * 128:(cc + 1) * 128, :], 0, 128)]
                chunks.append((w_ap, FUNC[f], (lambda bi, cc=cc: pT[cc * 128:(cc + 1) * 128, t0s[bi]:t0s[bi] + blocks[bi]["n"]])))
                cc += 1
        assert cc == NPT
        phase_A(P, nc, modio, blocks, chunks, wada_fn)
        P.barrier()
        iob = {"pT": pT, "gqT": L("gqT"), "gkvT": L("gkvT"), "wuq": L("wuq"), "wuqsw": L("wuqsw"), "wukv": L("wukv"),
               "cos2": ext["cos2"], "sin2": ext["sin2"], "onesf": ext["onesf"], "ogT": ogb_.ap()}
        iob.update(sc)
        phase_B1(P, nc, iob, NT)
        P.barrier()
        phase_B2(P, nc, iob, NT)
        P.barrier()
        ios = {"pT": pT, "convw": L("convw"), "convb": L("convb"), "dtbias2": L("dtbias2"), "alog2": L("alog2"), "identf": ext["identf"],
               "onesf": ext["onesf"], "TL": ext["TL"], "TG": ext["TG"], "SG": ext["SG"], "SL": ext["SL"], "dskbc": L("dskbc"), "gssmT": L("gssmT"),
               "ygT_parts": [t_.ap() for t_ in ygb_], "ssq": ssb_.ap()}
        ios.update(sc)
        phase_S1(P, nc, ios, NT)
        P.barrier()
        phase_S2(P, nc, ios, NT)
        P.barrier()
        phase_S3(P, nc, ios, NT)
        P.barrier()
        do_gather(ogb_, oga, ogCR, ognch, [], [])
        for q in range(2):
            do_gather(ygb_[q], ygg[q][0], ygg[q][1], ygg[q][2], [], [])
        do_gather(ssb_, ssa, ssCR, ssnch, [], [])
        if i == 0:
            P.join("ccw0")
            P.barrier()
        wmg_b = precast(w["wmg"], "wmg%d" % i)
        wpa_b = precast(w["wpa"], "wpa%d" % i)
        wpb_b = precast(w["wpb"], "wpb%d" % i)
        wout_b = precast(w["wout"], "wout%d" % i)
        P.barrier()
        mblocks = []
        for bl in range(4):
            mblocks.append(dict(n=512, mcol=0, tiles=[[(own_rows(bl * 512 + tt * 128, 128), 0, 128)] for tt in range(4)]))
        mt0 = [0, 512, 1024, 1536]
        if i < nlayers - 1:
            mblocks.append(dict(n=64, mcol=1, tiles=[[(own_rows(NLAT, 64), 0, 64)]]))
            mt0.append(NLAT)
        mchunks = []
        for cc in range(64):
            mchunks.append((wmg_b.at(cc // 16, cc % 16), AF.Sigmoid,
                            (lambda bi, cc=cc: sc["sgT"][cc * 128:(cc + 1) * 128, mt0[bi]:mt0[bi] + mblocks[bi]["n"]])))
        phase_A(P, nc, modio, mblocks, mchunks, wada_fn)
        P.barrier()
        def own_cols(dst, src, CR, nch):
            sv = src.ap().rearrange("c q t -> (c q) t")
            dv = dst.ap()
            P.dma("sp", "owncp", lambda e: [e.dma_start(out=dv[:, 0:NLAT], in_=sv[:, bass.ds(dyn["lat"], NLAT)]),
                                            e.dma_start(out=dv[:, NLAT:TC], in_=sv[:, bass.ds(dyn["ctx"], 64)])], n=2)
        own_cols(og_own, oga, ogCR, ognch)
        for q in range(2):
            own_cols(yg_own[q], ygg[q][0], ygg[q][1], ygg[q][2])
        own_cols(ss_own, ssa, ssCR, ssnch)
        P.barrier()
        ioc = {"c2T": ext["c2T"], "badag": L("badag"), "gsc": sc["gsc"], "sgT": sc["sgT"], "hin_rows": own_rows, "gfin": ext["gfin"],
               "oga2": og_own.ap(), "yga_parts": [t_.ap() for t_ in yg_own],
               "ssqa": ss_own.ap(), "hout_rows": hout_rows, "hnorm": hnorm}
        phase_Cmod(P, nc, ioc, lambda cb, w=w: w["wadag"].at(cb // 4, cb % 4))
        P.barrier()
        final = (i == nlayers - 1)
        phase_C(P, nc, ioc, lambda oc, g_=wpa_b: g_.at(oc // 8, oc % 8), lambda oc, g_=wpb_b: g_.at(oc // 8, oc % 8),
                lambda ob, g_=wout_b: g_.at(ob // 2, ob % 2), with_ctx=not final, final=final, dyn=dyn)
        P.barrier()
        if not final:
            do_gather(hout, xg1o, hCR, hnch, [], [])
            P.barrier()
    P.emit()
    return nc, P


def prep_core(inp, b, j, nlayers=DEPTH):
    r = 4 * b + j
    f32 = np.float32
    xin = np.concatenate([inp["x"][b][NLAT * j:NLAT * (j + 1)], inp["ctx"][b][64 * j:64 * (j + 1)]], axis=0)
    c2 = np.stack([inp["c"][b], inp["c_ctx"]], axis=1)
    cos2, sin2 = rope_tables(NT)
    rr = np.arange(128)
    m = {
        "xin_sh": np.ascontiguousarray(xin), "offs": np.array([[256 + NLAT * j, 64 * j]], np.int32),
        "c2T": np.ascontiguousarray(c2.reshape(32, 128, 2).transpose(1, 0, 2)),
        "identf": np.eye(128, dtype=f32), "onesf": np.ones((128, 128), f32),
        "TL": (rr[:, None] <= rr[None, :]).astype(f32), "TG": (rr[:, None] >= rr[None, :]).astype(f32),
        "SG": (rr[:, None] > rr[None, :]).astype(f32), "SL": (rr[:, None] < rr[None, :]).astype(f32),
        "cos2": cos2, "sin2": sin2, "gfin": inp["g_final"][None, :].astype(f32),
    }
    return m


def _gathered_perm():
    rho = np.arange(4096)
    c, r, ii = rho // 128, (rho % 128) // 32, rho % 32
    pa = r * 1024 + c * 32 + ii
    pb = np.concatenate([r * 2048 + q * 1024 + c * 32 + ii for q in range(2)])
    return pa, pb


PERM_A, PERM_B = _gathered_perm()


def prep_layer_shared(inp, i):
    wada = inp["w_ada"][i]
    out = {
        "wadal": chunked_weight(wada[:, :2 * D], np.arange(2 * D)),
        "wmg": chunked_weight(inp["w_in"][i], np.arange(MGA0, MGA0 + 2 * D)),
        "wadag": chunked_weight(wada[:, 2 * D:], np.arange(D), width=256),
        "wpa": chunked_weight(inp["w_proj_a"][i][PERM_A], np.arange(D)),
        "wpb": chunked_weight(inp["w_proj_b"][i][PERM_B], np.arange(D)),
        "wout": chunked_weight(inp["w_out"][i], np.arange(D), width=512),
    }
    return out


def prep_layer_core(inp, i, b, j, shared, wl_cache):
    r = 4 * b + j
    m = {}
    m["wadal_sh"] = shared["wadal"][16 * j:16 * j + 16]
    m["wmg_sh"] = shared["wmg"][16 * j:16 * j + 16]
    m["wadag_sh"] = shared["wadag"][4 * j:4 * j + 4]
    m["wpa_sh"] = shared["wpa"][8 * j:8 * j + 8]
    m["wpb_sh"] = shared["wpb"][8 * j:8 * j + 8]
    m["wout_sh"] = shared["wout"][2 * j:2 * j + 2]
    if j not in wl_cache:
        wl_cache[j] = chunked_weight(inp["w_in"][i], cols_for_core(j)[:NPT * 128])
    m["wl_sh"] = wl_cache[j]
    m["badaT"] = colvec(inp["b_ada"][i][:2 * D])
    m["gpreT"] = colvec(inp["g_pre"][i])
    m["badag"] = np.ascontiguousarray(inp["b_ada"][i][None, 2 * D:])
    pb = prep_B(inp, i, j, NT)
    for k in ("gqT", "gkvT", "wuq", "wuqsw", "wukv"):
        m[k] = pb[k]
    ps = prep_S(inp, i, j)
    for k in ("convw", "convb", "dtbias2", "alog2", "dskbc", "gssmT"):
        m[k] = ps[k]
    return m


def make_in_maps(inp, nlayers=DEPTH):
    maps = [prep_core(inp, r // 4, r % 4, nlayers) for r in range(8)]
    for i in range(nlayers):
        shared = prep_layer_shared(inp, i)
        wl_cache = {}
        for r in range(8):
            lm = prep_layer_core(inp, i, r // 4, r % 4, shared, wl_cache)
            for k, v in lm.items():
                maps[r]["%s_%d" % (k, i)] = np.ascontiguousarray(v, dtype=np.float32)
    return maps


_CACHE = {}


def kernel(**inputs):
    inp = {k: np.asarray(v) for k, v in inputs.items()}
    if "nc" not in _CACHE:
        _CACHE["nc"] = build_program()[0]
    nc = _CACHE["nc"]
    maps = make_in_maps(inp)
    res = run_bass_kernel_spmd(nc, maps, core_ids=list(range(8)))
    out = np.empty((2, 8192, D), np.float32)
    for r in range(8):
        b, j = r // 4, r % 4
        out[b, NLAT * j:NLAT * (j + 1)] = res.results[r]["hnorm"]
    return out
```

```python
import numpy as np
import concourse.bass as bass
import concourse.mybir as mybir
from concourse.bass_utils import run_bass_kernel_spmd
from contextlib import ExitStack

F32 = mybir.dt.float32
BF16 = mybir.dt.bfloat16
AF = mybir.ActivationFunctionType
ALU = mybir.AluOpType
AX = mybir.AxisListType


class _Op:
    __slots__ = ("eng", "fn", "reads", "writes", "excl", "tag", "ndma", "deps", "ms", "cum")

    def __init__(self, eng, fn, reads, writes, tag=None, ndma=0, excl=()):
        self.eng, self.fn, self.reads, self.writes = eng, fn, reads, writes
        self.excl = tuple(excl)
        self.tag, self.ndma = tag, ndma
        self.deps = None
        self.ms = 0
        self.cum = 0


class Prog:
    ENGS = ("pe", "act", "dve", "pool", "sp")

    def __init__(self, nc):
        self.nc = nc
        self.ops = []
        self.es = ExitStack()
        self.last_w = {}
        self.last_x = {}
        self.readers = {}
        self.tagcnt = {}
        self.bar = {}
        self.last_eng = {}
        self.last_tag = {}
        self.scopes = []
        self.colltags = set()
        self.uid = 0
        self.slotmap = {}
        self.detached = set()

    def sbuf(self, name, shape, dt):
        self.uid += 1
        return self.es.enter_context(self.nc.sbuf_tensor("sb%d_%s" % (self.uid, name), list(shape), dt))

    def psum(self, name, shape, dt=F32):
        self.uid += 1
        return self.es.enter_context(self.nc.psum_tensor("ps%d_%s" % (self.uid, name), list(shape), dt))

    def push(self):
        self.scopes.append(self.es)
        self.es = ExitStack()

    def pop(self):
        self.barrier()
        self.es.close()
        self.es = self.scopes.pop()

    def barrier(self):
        snap = set(self.last_eng.values()) | set(v for k, v in self.last_tag.items() if k not in self.detached)
        for e in self.ENGS:
            self.bar[e] = set(snap) | self.bar.get(e, set())
        self.slotmap = {}

    def _record(self, op):
        deps = set()
        for k in op.reads:
            d = self.last_w.get(k)
            if d is not None:
                deps.add(d)
        for k in op.writes:
            d = self.last_w.get(k)
            if d is not None:
                deps.add(d)
            deps.update(self.readers.get(k, ()))
        idx = len(self.ops)
        b = self.bar.pop(op.eng, None)
        if b:
            deps.update(b)
        if op.tag is None:
            self.last_eng[op.eng] = idx
        else:
            self.last_tag[op.tag] = idx
        for k in op.excl:
            d = self.last_x.get(k)
            if d is not None and self.ops[d].eng != op.eng:
                deps.add(d)
            self.last_x[k] = idx
        for k in op.writes:
            self.last_w[k] = idx
            self.readers[k] = []
        for k in op.reads:
            if k in op.writes:
                continue
            self.readers.setdefault(k, []).append(idx)
        op.deps = deps
        self.ops.append(op)
        return idx

    def op(self, eng, fn, reads=(), writes=(), excl=()):
        return self._record(_Op(eng, fn, tuple(reads), tuple(writes), excl=excl))

    def dma(self, eng, tag, fn, reads=(), writes=(), n=1):
        slot = self.slotmap.get(tag)
        if slot is None:
            slot = len(self.slotmap)
            self.slotmap[tag] = slot
        slot = "s%d" % slot
        tk = ("__slot__", slot)
        o = _Op(eng, fn, tuple(reads), tuple(writes) + (tk,), tag=slot, ndma=n)
        self.tagcnt[slot] = self.tagcnt.get(slot, 0) + n
        o.cum = self.tagcnt[slot] * 16
        return self._record(o)

    def detach(self, slot):
        self.detached.add(slot)

    def join(self, slot):
        self.detached.discard(slot)
        d = self.last_tag.get(slot)
        if d is not None:
            for e in self.ENGS:
                self.bar[e] = self.bar.get(e, set()) | {d}

    def coll(self, fn, n, reads=(), writes=(), slot="cc"):
        self.colltags.add(slot)
        tk = ("__slot__", slot)
        o = _Op("pool", fn, tuple(reads), tuple(writes) + (tk,), tag=slot, ndma=n)
        self.tagcnt[slot] = self.tagcnt.get(slot, 0) + n
        o.cum = self.tagcnt[slot]
        return self._record(o)

    def emit(self):
        nc = self.nc
        ops = self.ops
        need = set()
        for o in ops:
            best = {}
            dmb = {}
            for d in o.deps:
                p = ops[d]
                if p.tag is not None:
                    if p.tag not in dmb or dmb[p.tag] < d:
                        dmb[p.tag] = d
                    continue
                if p.eng == "pe" and o.eng == "pe" and o.tag is None:
                    continue
                if p.eng not in best or best[p.eng] < d:
                    best[p.eng] = d
            o.deps = list(best.values()) + list(dmb.values())
            for d in best.values():
                need.add(d)
        CAP = 8000
        cnt = {e: 0 for e in self.ENGS}
        for i, o in enumerate(ops):
            if o.tag is None and i in need:
                cnt[o.eng] += 1
                o.ms = cnt[o.eng]
        sems = {}
        for e in self.ENGS:
            for g in range(cnt[e] // CAP + 1):
                sems[e, g] = self.es.enter_context(nc.semaphore("s_%s_%d" % (e, g)))
        tsems = {}
        DCAP = CAP // 16
        for t, c in self.tagcnt.items():
            if t in self.colltags:
                assert c < CAP
                tsems[t, 0] = self.es.enter_context(nc.semaphore("d_%s" % (str(t),)))
            else:
                for g in range(c // DCAP + 1):
                    tsems[t, g] = self.es.enter_context(nc.semaphore("d_%s_%d" % (str(t), g)))
        self.stats_nsem = len(sems) + len(tsems)

        def esem(p):
            g = (p.ms - 1) // CAP
            return sems[p.eng, g], p.ms - g * CAP, ("e", p.eng, g)

        def dsem(p):
            if p.tag in self.colltags:
                return tsems[p.tag, 0], p.cum, ("t", p.tag, 0)
            k = p.cum // 16
            g = (k - 1) // DCAP
            return tsems[p.tag, g], (k - g * DCAP) * 16, ("t", p.tag, g)

        def dsem_prev(p):
            if p.tag in self.colltags:
                return None
            k = p.cum // 16
            g = (k - 1) // DCAP
            if g > 0 and (k - p.ndma) < g * DCAP:
                return tsems[p.tag, g - 1], DCAP * 16, ("t", p.tag, g - 1)
            return None
        per = {e: [] for e in self.ENGS}
        for o in ops:
            per[o.eng].append(o)
        self.stats = {e: len(per[e]) for e in self.ENGS}
        self.stats["milestones"] = dict(cnt)
        self.stats["tags"] = len(tsems)

        def run(e, eng):
            seen = {}
            nw = 0
            for o in per[e]:
                for d in o.deps:
                    p = ops[d]
                    if p.tag is not None:
                        pv = dsem_prev(p)
                        if pv is not None and seen.get(pv[2], 0) < pv[1]:
                            seen[pv[2]] = pv[1]
                            eng.wait_ge(pv[0], pv[1])
                            nw += 1
                        s, v, key = dsem(p)
                    else:
                        s, v, key = esem(p)
                    if seen.get(key, 0) >= v:
                        continue
                    seen[key] = v
                    eng.wait_ge(s, v)
                    nw += 1
                r = o.fn(eng)
                if o.tag is not None:
                    assert len(r) == o.ndma, (len(r), o.ndma, o.tag)
                    if o.tag in self.colltags:
                        for ins in r:
                            ins.then_inc(tsems[o.tag, 0], 1)
                    else:
                        k0 = o.cum // 16 - o.ndma
                        for q, ins in enumerate(r):
                            ins.then_inc(tsems[o.tag, (k0 + q) // DCAP], 16)
                elif o.ms:
                    r.then_inc(esem(o)[0], 1)
            if e == "sp":
                for t, c in self.tagcnt.items():
                    if t in self.colltags:
                        if seen.get(("t", t, 0), 0) < c:
                            eng.wait_ge(tsems[t, 0], c)
                        continue
                    for g in range(c // DCAP + 1):
                        v = min(c - g * DCAP, DCAP) * 16
                        if v > 0 and seen.get(("t", t, g), 0) < v:
                            eng.wait_ge(tsems[t, g], v)
            self.stats["waits_" + e] = nw

        with nc.Block() as block:
            @block.sync
            def _(sync):
                run("sp", sync)

            @block.tensor
            def _(tensor):
                run("pe", tensor)

            @block.vector
            def _(vector):
                run("dve", vector)

            @block.scalar
            def _(scalar):
                run("act", scalar)

            @block.gpsimd
            def _(gpsimd):
                run("pool", gpsimd)
        self.es.close()


def dma_pieces(e, dst, pieces, pat=None, **kw):
    out = []
    for (ap, p0, r) in pieces:
        src = ap.rearrange(pat, **kw) if pat else ap
        out.append(e.dma_start(out=dst[p0:p0 + r], in_=src))
    return out


import numpy as np

D = 4096
Q0 = 0; KV0 = 768; KPE0 = 1280; GA0 = 1344; Z0 = 5440; XS0 = 13632; B0 = 21824; C0 = 22848
DTF0 = 23872; DTB0 = 24000; MGA0 = 24128; MGB0 = 28224


def cols_for_core(j):
    r = np.arange
    kpe = np.concatenate([r(KPE0, KPE0 + 64), r(KPE0 + 32, KPE0 + 64), r(KPE0, KPE0 + 32)])
    dt = np.concatenate([r(DTF0 + 32 * j, DTF0 + 32 * j + 32), r(DTB0 + 32 * j, DTB0 + 32 * j + 32), -np.ones(64, np.int64)])
    cols = np.concatenate([
        r(Q0, Q0 + 768), r(KV0, KV0 + 512), kpe,
        r(GA0 + 1024 * j, GA0 + 1024 * (j + 1)),
        r(Z0 + 2048 * j, Z0 + 2048 * (j + 1)),
        r(XS0 + 2048 * j, XS0 + 2048 * (j + 1)),
        r(B0 + 256 * j, B0 + 256 * (j + 1)),
        r(C0 + 256 * j, C0 + 256 * (j + 1)),
        dt,
        r(MGA0 + 1024 * j, MGA0 + 1024 * (j + 1)),
        r(MGB0 + 1024 * j, MGB0 + 1024 * (j + 1)),
    ])
    assert cols.shape[0] == 72 * 128
    return cols


def chunked_weight(w, cols, width=128):
    K = w.shape[0]
    sel = w[:, np.maximum(cols, 0)]
    if (cols < 0).any():
        sel[:, cols < 0] = 0.0
    nchunk = cols.shape[0] // width
    a = sel.reshape(K // 128, 128, nchunk, width).transpose(2, 1, 0, 3)
    return np.ascontiguousarray(a).reshape(nchunk, 128, (K // 128) * width)


def colvec(v):
    return np.ascontiguousarray(v.reshape(-1, 128).T)


def prep_A(inp, i, b, j):
    c2 = np.stack([inp["c"][b], inp["c_ctx"]], axis=1)
    c2T = np.ascontiguousarray(c2.reshape(32, 128, 2).transpose(1, 0, 2))
    wada = inp["w_ada"][i][:, :2 * D]
    return {
        "c2T": c2T,
        "wadal": chunked_weight(wada, np.arange(2 * D)),
        "badaT": colvec(inp["b_ada"][i][:2 * D]),
        "gpreT": colvec(inp["g_pre"][i]),
        "identf": np.eye(128, dtype=np.float32),
        "wl": chunked_weight(inp["w_in"][i], cols_for_core(j)),
    }


def rope_tables(nt, nctx=256):
    n = nt - nctx
    f32 = np.float32
    rows = (np.arange(n) // 64).astype(f32)
    cols = (np.arange(n) % 64).astype(f32)
    inv = (f32(10000.0) ** (-np.arange(16, dtype=f32) / f32(16))).astype(f32)
    ang = np.concatenate([rows[:, None] * inv, cols[:, None] * inv], axis=-1).astype(f32)
    c = np.cos(ang).astype(f32).T
    s = np.sin(ang).astype(f32).T
    cos2 = np.ones((64, nt), f32)
    sin2 = np.zeros((64, nt), f32)
    cos2[0:32, nctx:] = c
    cos2[32:64, nctx:] = c
    sin2[0:32, nctx:] = -s
    sin2[32:64, nctx:] = s
    return cos2, sin2


def prep_B(inp, i, j, nt):
    hq = np.arange(8 * j * 192, (8 * j + 8) * 192)
    sw = np.concatenate([np.concatenate([np.arange(h * 192 + 160, h * 192 + 192), np.arange(h * 192 + 128, h * 192 + 160)])
                         for h in range(8 * j, 8 * j + 8)])
    hkv = np.arange(8 * j * 256, (8 * j + 8) * 256)
    cos2, sin2 = rope_tables(nt)
    return {
        "gqT": colvec(inp["g_q"][i]), "gkvT": colvec(inp["g_kv"][i]),
        "wuq": chunked_weight(inp["w_uq"][i], hq, width=1536)[0],
        "wuqsw": chunked_weight(inp["w_uq"][i], sw, width=512)[0],
        "wukv": chunked_weight(inp["w_ukv"][i], hkv, width=2048)[0],
        "cos2": cos2, "sin2": sin2, "onesf": np.ones((128, 128), np.float32),
    }


def prep_S(inp, i, j):
    ch = np.concatenate([np.arange(2048 * j, 2048 * (j + 1)), 8192 + np.arange(256 * j, 256 * (j + 1)),
                         9216 + np.arange(256 * j, 256 * (j + 1))])
    cw = inp["conv_w"][i][:, ch]
    convw = np.ascontiguousarray(cw.reshape(5, 20, 128).transpose(2, 1, 0))
    convb = colvec(inp["conv_b"][i][ch])
    hs = slice(32 * j, 32 * (j + 1))
    dtb = np.concatenate([inp["dt_bias_f"][i][hs], inp["dt_bias_b"][i][hs]])
    al = np.concatenate([inp["a_log_f"][i][hs], inp["a_log_b"][i][hs]])
    r = np.arange(128)
    f32 = np.float32
    return {
        "convw": convw, "convb": convb,
        "dtbias2": np.concatenate([dtb, dtb])[:, None].astype(f32),
        "alog2": np.concatenate([np.zeros(64, f32), al])[:, None].astype(f32),
        "TL": (r[:, None] <= r[None, :]).astype(f32), "TG": (r[:, None] >= r[None, :]).astype(f32),
        "SG": (r[:, None] > r[None, :]).astype(f32), "SL": (r[:, None] < r[None, :]).astype(f32),
        "gssmT": colvec(inp["g_ssm"][i][2048 * j:2048 * (j + 1)]),
        "dskbc": np.ascontiguousarray(np.broadcast_to(np.repeat(inp["d_skip"][i][hs], 64)[None, :], (128, 2048))).astype(f32),
    }


D = 4096
KC = D // 128
NCTX = 256
EPS = 1e-6

SEG = [("cq", 6, "id"), ("ckv", 4, "id"), ("kpe", 1, "id"), ("ga", 8, "silu"), ("z", 16, "silu"),
       ("xs", 16, "id"), ("B", 2, "id"), ("C", 2, "id"), ("dt", 1, "id"), ("mga", 8, "sig"), ("mgb", 8, "sig")]
SEG_OFF = {}
_o = 0
for _n, _c, _f in SEG:
    SEG_OFF[_n] = _o
    _o += _c
NCH = _o
NPT = SEG_OFF["mga"]


def token_blocks(nt):
    blks = [(0, NCTX)]
    t = NCTX
    while t < nt:
        blks.append((t, min(512, nt - t)))
        t += 512
    return blks


def phase_A(P, nc, io, blocks, chunks, wada_fn):
    P.push()
    identf = P.sbuf("identf", [128, 128], F32)
    P.dma("sp", "identf", lambda e: [e.dma_start(out=identf[:], in_=io["identf"])], writes=["identf"])
    c2 = P.sbuf("c2", [128, KC, 2], F32)
    sc = P.sbuf("sc", [128, KC, 2], F32)
    bada = P.sbuf("bada", [128, 64], F32)
    gpre = P.sbuf("gpre", [128, KC], F32)
    modT = P.sbuf("modT", [128, 64, 2], F32)
    gmul = P.sbuf("gmul", [128, KC, 2], F32)
    P.dma("sp", "c2", lambda e: [e.dma_start(out=c2[:], in_=io["c2T"])], writes=["c2"])
    P.dma("sp", "bada", lambda e: [e.dma_start(out=bada[:], in_=io["badaT"])], writes=["bada"])
    P.dma("sp", "gpre", lambda e: [e.dma_start(out=gpre[:], in_=io["gpreT"])], writes=["gpre"])
    P.op("act", lambda e: e.activation(out=sc[:], in_=c2[:], func=AF.Silu), reads=["c2"], writes=["sc"])
    P.push()
    wad = [P.sbuf("wad%d" % i, [128, KC, 128], F32) for i in range(2)]
    pm = P.psum("pm", [128, 512], F32)
    for oc in range(64):
        wb = wad[oc % 2]
        wp_ = wada_fn(oc)
        P.dma("sp", "wad%d" % (oc % 2), lambda e, wb=wb, wp_=wp_: dma_pieces(e, wb, wp_, "p (k n) -> p k n", k=KC),
              writes=["wad%d" % (oc % 2)], n=len(wp_))
        for k in range(KC):
            P.op("pe", lambda e, wb=wb, k=k, oc=oc: e.matmul(pm[:, 2 * (oc % 8):2 * (oc % 8) + 2], wb[:, k, :], sc[:, k, :],
                                                             start=(k == 0), stop=(k == KC - 1)),
                 reads=["wad%d" % (oc % 2), "sc"], excl=["b_pm"])
        P.op("dve", lambda e, oc=oc: e.tensor_scalar(out=modT[:, oc, :], in0=pm[:, 2 * (oc % 8):2 * (oc % 8) + 2],
                                                     scalar1=bada[:, oc:oc + 1], scalar2=None, op0=ALU.add),
             reads=["bada"], writes=["modT"], excl=["b_pm"])
    P.pop()
    for n in range(2):
        P.op("dve", lambda e, n=n: e.scalar_tensor_tensor(out=gmul[:, :, n], in0=modT[:, 32:64, n], scalar=1.0, in1=gpre[:],
                                                          op0=ALU.add, op1=ALU.mult),
             reads=["modT", "gpre"], writes=["gmul"])

    xt = [P.sbuf("xt%d" % i, [128, D], F32) for i in range(2)]
    xn = [P.sbuf("xn%d" % i, [128, D], F32) for i in range(2)]
    junk = P.sbuf("junk", [128, D], BF16)
    st = [P.sbuf("st%d" % i, [128, 4], F32) for i in range(2)]
    uT = [P.sbuf("uT%d" % i, [128, KC, 512], BF16) for i in range(2)]
    wbuf = [P.sbuf("wbuf%d" % i, [128, KC, 128], BF16) for i in range(3)]
    stage = [P.sbuf("stage%d" % i, [128, 512], F32) for i in range(3)]
    ptr = [P.psum("ptr%d" % i, [128, 512], F32) for i in range(2)]
    pmm = [P.psum("pmm%d" % i, [128, 512], F32) for i in range(2)]
    ti = 0
    wi = 0
    si = 0
    for bi, blk in enumerate(blocks):
        n = blk["n"]
        mcol = blk["mcol"]
        ub = uT[bi % 2]
        ukey = "uT%d" % (bi % 2)
        ukeys = []
        c0 = 0
        for tt, pieces in enumerate(blk["tiles"]):
            rows = sum(p[2] for p in pieces)
            x_ = xt[ti % 2]
            xn_ = xn[ti % 2]
            st_ = st[ti % 2]
            xk, xnk, stk = "xt%d" % (ti % 2), "xn%d" % (ti % 2), "st%d" % (ti % 2)
            P.dma("sp", xk, lambda e, x_=x_, pieces=pieces: [e.dma_start(out=x_[p0:p0 + r, :], in_=src) for (src, p0, r) in pieces],
                  writes=[xk], n=len(pieces))
            P.op("act", lambda e, x_=x_, st_=st_, rows=rows: e.activation(out=junk[0:rows, :], in_=x_[0:rows, :], func=AF.Square, accum_out=st_[0:rows, 0:1]),
                 reads=[xk], writes=["junk", stk])
            P.op("dve", lambda e, st_=st_, rows=rows: e.tensor_scalar(out=st_[0:rows, 1:2], in0=st_[0:rows, 0:1], scalar1=1.0 / D, scalar2=EPS,
                                                                      op0=ALU.mult, op1=ALU.add), reads=[stk], writes=[stk + "b"])
            P.op("act", lambda e, st_=st_, rows=rows: e.activation(out=st_[0:rows, 3:4], in_=st_[0:rows, 1:2], func=AF.Sqrt), reads=[stk + "b"], writes=[stk + "d"])
            P.op("dve", lambda e, st_=st_, rows=rows: e.reciprocal(out=st_[0:rows, 2:3], in_=st_[0:rows, 3:4]), reads=[stk + "d"], writes=[stk + "c"])
            P.op("dve", lambda e, x_=x_, xn_=xn_, st_=st_, rows=rows: e.tensor_scalar(out=xn_[0:rows, :], in0=x_[0:rows, :], scalar1=st_[0:rows, 2:3], scalar2=None,
                                                                                      op0=ALU.mult), reads=[xk, stk + "c"], writes=[xnk])
            for q in range(KC // 4):
                pb = ptr[q % 2]
                pk = "b_ptr%d" % (q % 2)
                for c4 in range(4):
                    c = q * 4 + c4
                    P.op("pe", lambda e, pb=pb, c4=c4, c=c, xn_=xn_, rows=rows: e.transpose(pb[:, c4 * 128:c4 * 128 + rows], xn_[0:rows, c * 128:(c + 1) * 128], identf[0:rows, 0:rows]),
                         reads=[xnk, "identf"], excl=[pk])
                for c4 in range(4):
                    c = q * 4 + c4
                    P.op("act", lambda e, pb=pb, c4=c4, c=c, ub=ub, c0=c0, mcol=mcol, rows=rows: e.activation(
                        out=ub[:, c, c0:c0 + rows], in_=pb[:, c4 * 128:c4 * 128 + rows], func=AF.Identity,
                        scale=gmul[:, c, mcol:mcol + 1], bias=modT[:, c, mcol:mcol + 1]),
                        reads=["gmul", "modT"], writes=[(ukey, tt)], excl=[pk])
            ukeys.append((ukey, tt))
            c0 += rows
            ti += 1
        assert c0 == n
        for cc, (w_ap, fn, dst_fn) in enumerate(chunks):
            wb = wbuf[wi % 3]
            wk = "wbuf%d" % (wi % 3)
            P.dma("pool", wk, lambda e, wb=wb, w_ap=w_ap: dma_pieces(e, wb, w_ap, "p (k n) -> p k n", k=KC),
                  writes=[wk], n=len(w_ap))
            pb = pmm[wi % 2]
            pk = "b_pmm%d" % (wi % 2)
            for k in range(KC):
                P.op("pe", lambda e, pb=pb, wb=wb, k=k, ub=ub, n=n: e.matmul(pb[:, 0:n], wb[:, k, :], ub[:, k, 0:n],
                                                                             start=(k == 0), stop=(k == KC - 1)),
                     reads=[wk] + ukeys, excl=[pk])
            sg = stage[si % 3]
            sk = "stage%d" % (si % 3)
            P.op("act", lambda e, sg=sg, pb=pb, n=n, fn=fn: e.activation(out=sg[:, 0:n], in_=pb[:, 0:n], func=fn),
                 writes=[sk], excl=[pk])
            dst = dst_fn(bi)
            P.dma("sp", sk, lambda e, sg=sg, dst=dst, n=n: [e.dma_start(out=dst, in_=sg[:, 0:n])], reads=[sk])
            wi += 1
            si += 1
    P.pop()


import math

ATTN_SCALE = 1.0 / math.sqrt(192.0)
NH = 8


def phase_B1(P, nc, io, nt):
    P.push()
    pT = io["pT"]
    wuq = P.sbuf("wuq", [128, 6, NH, 192], BF16)
    wuqsw = P.sbuf("wuqsw", [128, 6, NH, 64], BF16)
    wukv = P.sbuf("wukv", [128, 4, NH, 256], BF16)
    gq = P.sbuf("gq", [128, 6], F32)
    gkv = P.sbuf("gkv", [128, 4], F32)
    onesf = P.sbuf("onesf", [128, 128], F32)
    P.dma("pool", "wuq", lambda e: [e.dma_start(out=wuq[:], in_=io["wuq"].rearrange("p (k h d) -> p k h d", k=6, h=NH))], writes=["wuq"])
    P.dma("pool", "wuqsw", lambda e: [e.dma_start(out=wuqsw[:], in_=io["wuqsw"].rearrange("p (k h d) -> p k h d", k=6, h=NH))], writes=["wuqsw"])
    P.dma("pool", "wukv", lambda e: [e.dma_start(out=wukv[:], in_=io["wukv"].rearrange("p (k h d) -> p k h d", k=4, h=NH))], writes=["wukv"])
    P.dma("sp", "gq", lambda e: [e.dma_start(out=gq[:], in_=io["gqT"])], writes=["gq"])
    P.dma("sp", "gkv", lambda e: [e.dma_start(out=gkv[:], in_=io["gkvT"])], writes=["gkv"])
    P.dma("sp", "onesf", lambda e: [e.dma_start(out=onesf[:], in_=io["onesf"])], writes=["onesf"])

    cq = P.sbuf("cq", [128, 6, 512], F32)
    ckv = P.sbuf("ckv", [128, 4, 512], F32)
    kpe2 = P.sbuf("kpe2", [64, 2, 512], F32)
    cos = P.sbuf("cos", [64, 512], F32)
    sin = P.sbuf("sin", [64, 512], F32)
    sq = P.sbuf("sq", [128, 6, 512], F32)
    rr = P.sbuf("rr", [128, 3, 512], F32)
    cqn = P.sbuf("cqn", [128, 6, 512], BF16)
    ckvn = P.sbuf("ckvn", [128, 4, 512], BF16)
    t1 = [P.sbuf("t1_%d" % i, [64, 512], F32) for i in range(2)]
    t2 = [P.sbuf("t2_%d" % i, [64, 512], F32) for i in range(2)]
    stg = [P.sbuf("stg%d" % i, [128, 512], BF16) for i in range(4)]
    pss = P.psum("pss", [128, 512], F32)
    pmm = [P.psum("pbm%d" % i, [128, 512], F32) for i in range(2)]
    ppe = P.psum("ppe", [128, 512], F32)
    psw = P.psum("psw", [128, 512], F32)
    cnt = {"mm": 0, "stg": 0, "t": 0}

    def nxt_bank():
        i = cnt["mm"] % 2
        cnt["mm"] += 1
        return pmm[i], "b_pbm%d" % i

    def nxt_stg():
        i = cnt["stg"] % 4
        cnt["stg"] += 1
        return stg[i], "stg%d" % i

    def rmsnorm(src, skey, nk, g, dst, dkey, n, dim):
        P.op("act", lambda e: e.activation(out=sq[:, 0:nk, 0:n], in_=src[:, 0:nk, 0:n], func=AF.Square), reads=[skey], writes=["sq"])
        for k in range(nk):
            P.op("pe", lambda e, k=k: e.matmul(pss[:, 0:n], onesf[:], sq[:, k, 0:n], start=(k == 0), stop=(k == nk - 1)),
                 reads=["sq", "onesf"], excl=["b_pss"])
        P.op("dve", lambda e: e.tensor_scalar(out=rr[:, 0, 0:n], in0=pss[:, 0:n], scalar1=1.0 / dim, scalar2=EPS, op0=ALU.mult, op1=ALU.add),
             writes=["rr0"], excl=["b_pss"])
        P.op("act", lambda e: e.activation(out=rr[:, 1, 0:n], in_=rr[:, 0, 0:n], func=AF.Sqrt), reads=["rr0"], writes=["rr1"])
        P.op("dve", lambda e: e.reciprocal(out=rr[:, 2, 0:n], in_=rr[:, 1, 0:n]), reads=["rr1"], writes=["rr2"])
        for k in range(nk):
            P.op("dve", lambda e, k=k: e.scalar_tensor_tensor(out=dst[:, k, 0:n], in0=src[:, k, 0:n], scalar=g[:, k:k + 1], in1=rr[:, 2, 0:n],
                                                              op0=ALU.mult, op1=ALU.mult),
                 reads=[skey, "rr2", "gq", "gkv"], writes=[dkey])

    def rope(pe_ap, sw_ap, n, excl, rd, dst, dkey):
        i = cnt["t"] % 2
        cnt["t"] += 1
        a, b = t1[i], t2[i]
        P.op("dve", lambda e: e.tensor_tensor(out=a[:, 0:n], in0=pe_ap, in1=cos[:, 0:n], op=ALU.mult), reads=["cos"] + rd, writes=["t1_%d" % i], excl=excl[0:1])
        P.op("dve", lambda e: e.tensor_tensor(out=b[:, 0:n], in0=sw_ap, in1=sin[:, 0:n], op=ALU.mult), reads=["sin"] + rd, writes=["t2_%d" % i], excl=excl[1:2])
        P.op("pool", lambda e: e.tensor_tensor(out=dst, in0=a[:, 0:n], in1=b[:, 0:n], op=ALU.add), reads=["t1_%d" % i, "t2_%d" % i], writes=[dkey])

    def do_block(t0, n):
        P.dma("sp", "cq", lambda e, t0=t0, n=n: [e.dma_start(out=cq[:, :, 0:n], in_=pT[0:768, t0:t0 + n].rearrange("(k p) t -> p k t", p=128))], writes=["cq"])
        P.dma("sp", "ckv", lambda e, t0=t0, n=n: [e.dma_start(out=ckv[:, :, 0:n], in_=pT[768:1280, t0:t0 + n].rearrange("(k p) t -> p k t", p=128))], writes=["ckv"])
        P.dma("sp", "kpe2", lambda e, t0=t0, n=n: [e.dma_start(out=kpe2[:, :, 0:n], in_=pT[1280:1408, t0:t0 + n].rearrange("(a p) t -> p a t", p=64))], writes=["kpe2"])
        P.dma("sp", "cos", lambda e, t0=t0, n=n: [e.dma_start(out=cos[:, 0:n], in_=io["cos2"][:, t0:t0 + n])], writes=["cos"])
        P.dma("sp", "sin", lambda e, t0=t0, n=n: [e.dma_start(out=sin[:, 0:n], in_=io["sin2"][:, t0:t0 + n])], writes=["sin"])
        rmsnorm(cq, "cq", 6, gq, cqn, "cqn", n, 768.0)
        rmsnorm(ckv, "ckv", 4, gkv, ckvn, "ckvn", n, 512.0)
        sg, sk = nxt_stg()
        rope(kpe2[:, 0, 0:n], kpe2[:, 1, 0:n], n, [], ["kpe2"], sg[0:64, 0:n], sk)
        P.dma("sp", sk, lambda e, sg=sg, t0=t0, n=n: [e.dma_start(out=io["kpeT"][:, t0:t0 + n], in_=sg[0:64, 0:n])], reads=[sk])
        for h in range(NH):
            pb, pk = nxt_bank()
            for k in range(6):
                P.op("pe", lambda e, pb=pb, k=k, h=h: e.matmul(pb[:, 0:n], wuq[:, k, h, 0:128], cqn[:, k, 0:n], start=(k == 0), stop=(k == 5)),
                     reads=["wuq", "cqn"], excl=[pk])
            sg, sk = nxt_stg()
            P.op("act", lambda e, sg=sg, pb=pb: e.activation(out=sg[:, 0:n], in_=pb[:, 0:n], func=AF.Identity), writes=[sk], excl=[pk])
            P.dma("sp", sk, lambda e, sg=sg, h=h, t0=t0, n=n: [e.dma_start(out=io["qT"][h, 0:128, t0:t0 + n], in_=sg[:, 0:n])], reads=[sk])
            for k in range(6):
                P.op("pe", lambda e, k=k, h=h: e.matmul(ppe[0:64, 0:n], wuq[:, k, h, 128:192], cqn[:, k, 0:n], start=(k == 0), stop=(k == 5)),
                     reads=["wuq", "cqn"], excl=["b_ppe"])
            for k in range(6):
                P.op("pe", lambda e, k=k, h=h: e.matmul(psw[0:64, 0:n], wuqsw[:, k, h, :], cqn[:, k, 0:n], start=(k == 0), stop=(k == 5)),
                     reads=["wuqsw", "cqn"], excl=["b_psw"])
            sg, sk = nxt_stg()
            rope(ppe[0:64, 0:n], psw[0:64, 0:n], n, ["b_ppe", "b_psw"], [], sg[0:64, 0:n], sk)
            P.dma("sp", sk, lambda e, sg=sg, h=h, t0=t0, n=n: [e.dma_start(out=io["qT"][h, 128:192, t0:t0 + n], in_=sg[0:64, 0:n])], reads=[sk])
            pb, pk = nxt_bank()
            for k in range(4):
                P.op("pe", lambda e, pb=pb, k=k, h=h: e.matmul(pb[:, 0:n], wukv[:, k, h, 0:128], ckvn[:, k, 0:n], start=(k == 0), stop=(k == 3)),
                     reads=["wukv", "ckvn"], excl=[pk])
            sg, sk = nxt_stg()
            P.op("act", lambda e, sg=sg, pb=pb: e.activation(out=sg[:, 0:n], in_=pb[:, 0:n], func=AF.Identity), writes=[sk], excl=[pk])
            P.dma("sp", sk, lambda e, sg=sg, h=h, t0=t0, n=n: [e.dma_start(out=io["kT"][h, :, t0:t0 + n], in_=sg[:, 0:n])], reads=[sk])
        for tt in range(n // 128):
            tile = (t0 + tt * 128) // 128
            for half in range(2):
                pb, pk = nxt_bank()
                for k in range(4):
                    P.op("pe", lambda e, pb=pb, k=k, tt=tt, half=half: e.matmul(
                        pb[:, :].rearrange("p (h d) -> p h d", h=4), ckvn[:, k, tt * 128:(tt + 1) * 128], wukv[:, k, 4 * half:4 * half + 4, 128:256],
                        start=(k == 0), stop=(k == 3)), reads=["wukv", "ckvn"], excl=[pk])
                sg, sk = nxt_stg()
                P.op("act", lambda e, sg=sg, pb=pb: e.activation(out=sg[:, :], in_=pb[:, :], func=AF.Identity), writes=[sk], excl=[pk])
                P.dma("sp", sk, lambda e, sg=sg, half=half, tile=tile: [e.dma_start(
                    out=io["vs"][4 * half:4 * half + 4, :, tile, :].rearrange("h p d -> p h d"),
                    in_=sg[:, :].rearrange("p (h d) -> p h d", h=4))], reads=[sk])

    for (t0, n) in token_blocks(nt):
        do_block(t0, n)
    P.pop()


def phase_B2(P, nc, io, nt, heads=range(NH)):
    P.push()
    nkt = nt // 128
    KT = P.sbuf("KT", [128, nt], BF16)
    KP = P.sbuf("KP", [64, nt], BF16)
    V = P.sbuf("V", [128, nkt, 128], BF16)
    QN = P.sbuf("QN", [128, nt], BF16)
    QP = P.sbuf("QP", [64, nt], BF16)
    onesb = P.sbuf("onesb", [128, 128], BF16)
    pbuf = [P.sbuf("pbuf%d" % i, [128, 512], BF16) for i in range(3)]
    sga = [P.sbuf("sga%d" % i, [128, 512], F32) for i in range(2)]
    rec = [P.sbuf("rec%d" % i, [128, 512], F32) for i in range(2)]
    ot = [P.sbuf("ot%d" % i, [128, 512], F32) for i in range(2)]
    ogs = [P.sbuf("ogs%d" % i, [128, 512], BF16) for i in range(2)]
    S = [P.psum("S%d" % i, [128, 512], F32) for i in range(3)]
    O = [P.psum("O%d" % i, [128, 512], F32) for i in range(2)]
    Dn = [P.psum("Dn%d" % i, [128, 512], F32) for i in range(2)]
    P.op("pool", lambda e: e.memset(onesb[:], 1.0), writes=["onesb"])
    P.dma("sp", "KP", lambda e: [e.dma_start(out=KP[:], in_=io["kpeT"])], writes=["KP"])
    ga0 = SEG_OFF["ga"] * 128
    si = 0
    qb = 0
    for h in heads:
        P.dma("sp", "KT", lambda e, h=h: [e.dma_start(out=KT[:], in_=io["kT"][h])], writes=["KT"])
        P.dma("sp", "V", lambda e, h=h: [e.dma_start(out=V[:], in_=io["vs"][h])], writes=["V"])
        P.dma("sp", "QN", lambda e, h=h: [e.dma_start(out=QN[:], in_=io["qT"][h, 0:128, :])], writes=["QN"])
        P.dma("sp", "QP", lambda e, h=h: [e.dma_start(out=QP[:], in_=io["qT"][h, 128:192, :])], writes=["QP"])
        def do_qblock(h, t0, n, qb, si):
            tiles = list(range(NCTX // 128)) if t0 < NCTX else list(range(nkt))
            o_, ok = O[qb % 2], "b_O%d" % (qb % 2)
            d_, dk = Dn[qb % 2], "b_Dn%d" % (qb % 2)
            sg_, sgk = sga[qb % 2], "sga%d" % (qb % 2)
            P.dma("sp", sgk, lambda e, sg_=sg_, h=h, t0=t0, n=n: [e.dma_start(out=sg_[:, 0:n], in_=io["pT"][ga0 + h * 128:ga0 + (h + 1) * 128, t0:t0 + n])],
                  writes=[sgk])

            def qk(kt, slot):
                s_, sk = S[slot % 3], "b_S%d" % (slot % 3)
                P.op("pe", lambda e: e.matmul(s_[:, 0:n], KT[:, kt * 128:(kt + 1) * 128], QN[:, t0:t0 + n], start=True, stop=False),
                     reads=["KT", "QN"], excl=[sk])
                P.op("pe", lambda e: e.matmul(s_[:, 0:n], KP[:, kt * 128:(kt + 1) * 128], QP[:, t0:t0 + n], start=False, stop=True),
                     reads=["KP", "QP"], excl=[sk])
                pb, pk = pbuf[slot % 3], "pbuf%d" % (slot % 3)
                P.op("act", lambda e: e.activation(out=pb[:, 0:n], in_=s_[:, 0:n], func=AF.Exp, scale=ATTN_SCALE), writes=[pk], excl=[sk])

            def pv(kt, slot, first, last):
                pb, pk = pbuf[slot % 3], "pbuf%d" % (slot % 3)
                P.op("pe", lambda e: e.matmul(o_[:, 0:n], V[:, kt, :], pb[:, 0:n], start=first, stop=last), reads=["V", pk], excl=[ok])
                P.op("pe", lambda e: e.matmul(d_[:, 0:n], onesb[:], pb[:, 0:n], start=first, stop=last), reads=["onesb", pk], excl=[dk])

            qk(tiles[0], si)
            for idx, kt in enumerate(tiles):
                if idx + 1 < len(tiles):
                    qk(tiles[idx + 1], si + idx + 1)
                pv(kt, si + idx, idx == 0, idx == len(tiles) - 1)
            r_, rk = rec[qb % 2], "rec%d" % (qb % 2)
            t_, tk = ot[qb % 2], "ot%d" % (qb % 2)
            g_, gk = ogs[qb % 2], "ogs%d" % (qb % 2)
            P.op("dve", lambda e, r_=r_, d_=d_, n=n: e.reciprocal(out=r_[:, 0:n], in_=d_[:, 0:n]), writes=[rk], excl=[dk])
            P.op("dve", lambda e, r_=r_, o_=o_, t_=t_, n=n: e.tensor_tensor(out=t_[:, 0:n], in0=o_[:, 0:n], in1=r_[:, 0:n], op=ALU.mult),
                 reads=[rk], writes=[tk], excl=[ok])
            P.op("pool", lambda e, t_=t_, sg_=sg_, g_=g_, n=n: e.tensor_tensor(out=g_[:, 0:n], in0=t_[:, 0:n], in1=sg_[:, 0:n], op=ALU.mult),
                 reads=[tk, sgk], writes=[gk])
            P.dma("sp", gk, lambda e, g_=g_, h=h, t0=t0, n=n: [e.dma_start(out=io["ogT"][h * 128:(h + 1) * 128, t0:t0 + n], in_=g_[:, 0:n])], reads=[gk])
            return len(tiles)

        for (t0, n) in token_blocks(nt):
            si += do_qblock(h, t0, n, qb, si)
            qb += 1
    P.pop()


NG = 2
HPG = 16
HD = 64
GW = HPG * HD


def seq_blocks(nt):
    out = [(0, NCTX, 0, NCTX)]
    t = NCTX
    while t < nt:
        n = min(512, nt - t)
        out.append((t, n, NCTX, nt))
        t += n
    return out


def phase_S1(P, nc, io, nt):
    P.push()
    pT = io["pT"]
    identf = P.sbuf("identf1", [128, 128], F32)
    convw = P.sbuf("convw", [128, 20, 5], F32)
    convb = P.sbuf("convb", [128, 20], F32)
    dtb = P.sbuf("dtb", [128, 1], F32)
    alog = P.sbuf("alog", [128, 1], F32)
    mulc = P.sbuf("mulc", [128, 1], F32)
    P.dma("sp", "identf1", lambda e: [e.dma_start(out=identf[:], in_=io["identf"])], writes=["identf1"])
    P.dma("sp", "convw", lambda e: [e.dma_start(out=convw[:], in_=io["convw"])], writes=["convw"])
    P.dma("sp", "convb", lambda e: [e.dma_start(out=convb[:], in_=io["convb"])], writes=["convb"])
    P.dma("sp", "dtb", lambda e: [e.dma_start(out=dtb[:], in_=io["dtbias2"])], writes=["dtb"])
    P.dma("sp", "alog", lambda e: [e.dma_start(out=alog[:], in_=io["alog2"])], writes=["alog"])
    P.op("act", lambda e: e.activation(out=mulc[:], in_=alog[:], func=AF.Exp), reads=["alog"], writes=["mulc"])
    P.op("dve", lambda e: e.tensor_scalar(out=mulc[:], in0=mulc[:], scalar1=-1.0, scalar2=None, op0=ALU.mult), reads=["mulc"], writes=["mulc"])
    P.op("dve", lambda e: e.memset(mulc[0:64, :], 1.0), reads=["mulc"], writes=["mulc"])

    cin = [P.sbuf("cin%d" % i, [128, 516], F32) for i in range(3)]
    acc = [P.sbuf("acc%d" % i, [128, 512], F32) for i in range(2)]
    cv = [P.sbuf("cv%d" % i, [128, 512], F32) for i in range(8)]
    stf = [P.sbuf("stf%d" % i, [128, 512], F32) for i in range(3)]
    stb = [P.sbuf("stb%d" % i, [128, 512], BF16) for i in range(3)]
    sp_ = [P.sbuf("sp%d" % i, [128, 512], F32) for i in range(4)]
    pt = [P.psum("pt%d" % i, [128, 512], F32) for i in range(3)]
    c = {"cin": 0, "acc": 0, "cv": 0, "stf": 0, "stb": 0, "pt": 0}

    def rot(name, arr):
        i = c[name] % len(arr)
        c[name] += 1
        return arr[i], "%s%d" % (name, i)

    xs0 = SEG_OFF["xs"] * 128
    b0 = SEG_OFF["B"] * 128
    c0 = SEG_OFF["C"] * 128
    d0 = SEG_OFF["dt"] * 128

    def conv_chunk(row0, cc, t0, n, s0, s1):
        ci, cik = rot("cin", cin)
        lo = t0 - 2 if t0 > s0 else t0
        hi = t0 + n + 2 if t0 + n < s1 else t0 + n
        if lo == t0:
            P.op("pool", lambda e: e.memset(ci[:, 0:2], 0.0), writes=[cik])
        if hi == t0 + n:
            P.op("pool", lambda e: e.memset(ci[:, n + 2:n + 4], 0.0), writes=[cik])
        P.dma("sp", cik, lambda e: [e.dma_start(out=ci[:, lo - (t0 - 2):hi - (t0 - 2)], in_=pT[row0:row0 + 128, lo:hi])], reads=[cik], writes=[cik])
        a, ak = rot("acc", acc)
        P.op("dve", lambda e: e.tensor_scalar(out=a[:, 0:n], in0=ci[:, 0:n], scalar1=convw[:, cc, 0:1], scalar2=None, op0=ALU.mult),
             reads=[cik, "convw"], writes=[ak])
        for k in range(1, 5):
            P.op("dve", lambda e, k=k: e.scalar_tensor_tensor(out=a[:, 0:n], in0=ci[:, k:k + n], scalar=convw[:, cc, k:k + 1], in1=a[:, 0:n],
                                                              op0=ALU.mult, op1=ALU.add), reads=[cik, "convw", ak], writes=[ak])
        o, ok = rot("cv", cv)
        P.op("act", lambda e: e.activation(out=o[:, 0:n], in_=a[:, 0:n], func=AF.Silu, bias=convb[:, cc:cc + 1]), reads=[ak, "convb"], writes=[ok])
        return o, ok

    def do_block(t0, n, s0, s1):
        ntile = n // 128
        for cg in range(4):
            bufs = [conv_chunk(xs0 + (cg * 4 + q) * 128, cg * 4 + q, t0, n, s0, s1) for q in range(4)]
            for tt in range(ntile):
                pb, pk = rot("pt", pt)
                for q in range(4):
                    o, ok = bufs[q]
                    P.op("pe", lambda e, pb=pb, q=q, o=o, tt=tt: e.transpose(pb[:, q * 128:(q + 1) * 128], o[:, tt * 128:(tt + 1) * 128], identf[:]),
                         reads=[ok, "identf1"], excl=["b_" + pk])
                sg, sk = rot("stf", stf)
                P.op("act", lambda e, sg=sg, pb=pb: e.activation(out=sg[:], in_=pb[:], func=AF.Identity), writes=[sk], excl=["b_" + pk])
                r0 = t0 + tt * 128
                P.dma("sp", sk, lambda e, sg=sg, r0=r0, cg=cg: [e.dma_start(out=io["xs_tm"][r0:r0 + 128, cg * 512:(cg + 1) * 512], in_=sg[:])], reads=[sk])
        for g in range(NG):
            o, ok = conv_chunk(b0 + g * 128, 16 + g, t0, n, s0, s1)
            sg, sk = rot("stb", stb)
            P.op("act", lambda e, sg=sg, o=o: e.activation(out=sg[:, 0:n], in_=o[:, 0:n], func=AF.Identity), reads=[ok], writes=[sk])
            P.dma("sp", sk, lambda e, sg=sg, g=g: [e.dma_start(out=io["BT"][g, :, t0:t0 + n], in_=sg[:, 0:n])], reads=[sk])
            pb, pk = rot("pt", pt)
            for tt in range(ntile):
                P.op("pe", lambda e, pb=pb, o=o, tt=tt: e.transpose(pb[:, tt * 128:(tt + 1) * 128], o[:, tt * 128:(tt + 1) * 128], identf[:]),
                     reads=[ok, "identf1"], excl=["b_" + pk])
            sg2, sk2 = rot("stb", stb)
            P.op("act", lambda e, sg2=sg2, pb=pb: e.activation(out=sg2[:, 0:n], in_=pb[:, 0:n], func=AF.Identity), writes=[sk2], excl=["b_" + pk])
            P.dma("sp", sk2, lambda e, sg2=sg2, g=g: [e.dma_start(out=io["B_tm"][g, t0:t0 + n, :].rearrange("(tt p) m -> p tt m", p=128),
                                                                in_=sg2[:, 0:n].rearrange("p (tt m) -> p tt m", m=128))], reads=[sk2])
        for g in range(NG):
            o, ok = conv_chunk(c0 + g * 128, 18 + g, t0, n, s0, s1)
            sg, sk = rot("stb", stb)
            P.op("act", lambda e, sg=sg, o=o: e.activation(out=sg[:, 0:n], in_=o[:, 0:n], func=AF.Identity), reads=[ok], writes=[sk])
            P.dma("sp", sk, lambda e, sg=sg, g=g: [e.dma_start(out=io["CT"][g, :, t0:t0 + n], in_=sg[:, 0:n])], reads=[sk])
        xb, l1, l2, dd = sp_
        P.dma("sp", "sp0", lambda e: [e.dma_start(out=xb[0:64, 0:n], in_=pT[d0:d0 + 64, t0:t0 + n]),
                                      e.dma_start(out=xb[64:128, 0:n], in_=pT[d0:d0 + 64, t0:t0 + n])], writes=["sp0"], n=2)
        P.op("act", lambda e: e.activation(out=xb[:, 0:n], in_=xb[:, 0:n], func=AF.Identity, bias=dtb[:, 0:1]), reads=["sp0", "dtb"], writes=["sp0"])
        P.op("act", lambda e: e.activation(out=l1[:, 0:n], in_=xb[:, 0:n], func=AF.Abs), reads=["sp0"], writes=["sp1"])
        P.op("act", lambda e: e.activation(out=l1[:, 0:n], in_=l1[:, 0:n], func=AF.Exp, scale=-1.0), reads=["sp1"], writes=["sp1"])
        P.op("dve", lambda e: e.tensor_scalar(out=l1[:, 0:n], in0=l1[:, 0:n], scalar1=1.0, scalar2=None, op0=ALU.add), reads=["sp1"], writes=["sp1"])
        P.op("act", lambda e: e.activation(out=l2[:, 0:n], in_=l1[:, 0:n], func=AF.Ln), reads=["sp1"], writes=["sp2"])
        P.op("dve", lambda e: e.scalar_tensor_tensor(out=l2[:, 0:n], in0=xb[:, 0:n], scalar=0.0, in1=l2[:, 0:n], op0=ALU.max, op1=ALU.add),
             reads=["sp0", "sp2"], writes=["sp2"])
        P.op("dve", lambda e: e.tensor_scalar(out=dd[:, 0:n], in0=l2[:, 0:n], scalar1=mulc[:, 0:1], scalar2=None, op0=ALU.mult),
             reads=["sp2", "mulc"], writes=["sp3"])
        pb, pk = rot("pt", pt)
        for tt in range(ntile):
            P.op("pe", lambda e, pb=pb, tt=tt: e.transpose(pb[:, tt * 128:(tt + 1) * 128], dd[:, tt * 128:(tt + 1) * 128], identf[:]),
                 reads=["sp3", "identf1"], excl=["b_" + pk])
        sg, sk = rot("stf", stf)
        P.op("act", lambda e, sg=sg, pb=pb: e.activation(out=sg[:, 0:n], in_=pb[:, 0:n], func=AF.Identity), writes=[sk], excl=["b_" + pk])
        P.dma("sp", sk, lambda e, sg=sg: [e.dma_start(out=io["dtt"][t0:t0 + n, :].rearrange("(tt p) m -> p tt m", p=128),
                                                        in_=sg[:, 0:n].rearrange("p (tt m) -> p tt m", m=128))], reads=[sk])

    for (t0, n, s0, s1) in seq_blocks(nt):
        do_block(t0, n, s0, s1)
    P.pop()


def phase_S2(P, nc, io, nt, nsteps=None):
    P.push()
    nck = nt // 128
    nctx = NCTX // 128
    order = {0: list(range(nck)), 1: list(range(nctx - 1, -1, -1)) + list(range(nck - 1, nctx - 1, -1))}
    cm = {}
    for nm in ("TL", "TG", "SG", "SL", "onesf"):
        cm[nm] = P.sbuf("m_" + nm, [128, 128], F32)
        P.dma("sp", "m_" + nm, lambda e, nm=nm: [e.dma_start(out=cm[nm][:], in_=io[nm])], writes=["m_" + nm])
    TRI = {0: "TL", 1: "TG"}
    UU = {0: "SG", 1: "SL"}
    Sst = {}
    Sbf = {}
    for d in range(2):
        for g in range(NG):
            Sst[d, g] = P.sbuf("Sst%d%d" % (d, g), [128, GW], F32)
            Sbf[d, g] = P.sbuf("Sbf%d%d" % (d, g), [128, GW], BF16)
            P.op("pool", lambda e, d=d, g=g: e.memset(Sst[d, g][:], 0.0), writes=["Sst%d%d" % (d, g)])
            P.op("pool", lambda e, d=d, g=g: e.memset(Sbf[d, g][:], 0.0), writes=["Sbf%d%d" % (d, g)])
    NW = 3
    W = []
    for w in range(NW):
        W.append(dict(
            xs=P.sbuf("w%d_xs" % w, [128, GW], F32), xdt=P.sbuf("w%d_xdt" % w, [128, GW], BF16), xw=P.sbuf("w%d_xw" % w, [128, GW], BF16),
            R=P.sbuf("w%d_R" % w, [128, HPG, 128], F32), E=P.sbuf("w%d_E" % w, [128, HPG, 128], F32), MT=P.sbuf("w%d_MT" % w, [128, HPG, 128], BF16),
            mcb=P.sbuf("w%d_mcb" % w, [128, 128], F32), bt=P.sbuf("w%d_bt" % w, [128, 128], BF16), ct=P.sbuf("w%d_ct" % w, [128, 128], BF16),
            btm=P.sbuf("w%d_btm" % w, [128, 128], BF16), dtt=P.sbuf("w%d_dtt" % w, [128, 128], F32),
            ex3=P.sbuf("w%d_ex3" % w, [128, 48], F32), wv=P.sbuf("w%d_wv" % w, [128, 16], F32),
            ysb=P.sbuf("w%d_ysb" % w, [128, GW], F32), tmp=P.sbuf("w%d_tmp" % w, [128, GW], F32), yo=P.sbuf("w%d_yo" % w, [128, GW], F32),
        ))
    G = [P.psum("G%d" % i, [128, 512], F32) for i in range(2)]
    Y = [P.psum("Y%d" % i, [128, 512], F32) for i in range(2)]
    Z = [P.psum("Z%d" % i, [128, 512], F32) for i in range(2)]
    SM = P.psum("SM", [128, 512], F32)

    def step(si, d, g, ck):
        w = si % NW
        B = W[w]
        K = lambda s: "w%d_%s" % (w, s)
        tok = slice(ck * 128, (ck + 1) * 128)
        sk_st, sk_bf = "Sst%d%d" % (d, g), "Sbf%d%d" % (d, g)
        st, sb = Sst[d, g], Sbf[d, g]
        dcol = d * 32 + g * 16
        P.dma("sp", K("xs"), lambda e: [e.dma_start(out=B["xs"][:], in_=io["xs_tm"][tok, g * GW:(g + 1) * GW])], writes=[K("xs")])
        P.dma("sp", K("bt"), lambda e: [e.dma_start(out=B["bt"][:], in_=io["BT"][g, :, tok])], writes=[K("bt")])
        P.dma("sp", K("ct"), lambda e: [e.dma_start(out=B["ct"][:], in_=io["CT"][g, :, tok])], writes=[K("ct")])
        P.dma("sp", K("btm"), lambda e: [e.dma_start(out=B["btm"][:], in_=io["B_tm"][g, tok, :])], writes=[K("btm")])
        P.dma("sp", K("dtt"), lambda e: [e.dma_start(out=B["dtt"][:], in_=io["dtt"][tok, :])], writes=[K("dtt")])
        dt_ap = B["dtt"][:, dcol:dcol + 16]
        dta_ap = B["dtt"][:, 64 + dcol:64 + dcol + 16]
        tri, uu = cm[TRI[d]], cm[UU[d]]
        P.op("pool", lambda e: e.tensor_tensor(out=B["R"][:], in0=tri[:].unsqueeze(1).broadcast_to([128, HPG, 128]),
                                               in1=dta_ap.unsqueeze(2).broadcast_to([128, HPG, 128]), op=ALU.mult),
             reads=[K("dtt"), "m_" + TRI[d]], writes=[K("R")])
        P.op("pe", lambda e: e.matmul(SM[:, 0:128], B["bt"][:], B["ct"][:], start=True, stop=True), reads=[K("bt"), K("ct")], excl=["b_SM"])
        P.op("pe", lambda e: e.matmul(SM[:, 128:144], tri[:], dta_ap, start=True, stop=True), reads=[K("dtt"), "m_" + TRI[d]], excl=["b_SM"])
        P.op("pe", lambda e: e.matmul(SM[:, 144:160], uu[:], dta_ap, start=True, stop=True), reads=[K("dtt"), "m_" + UU[d]], excl=["b_SM"])
        P.op("pe", lambda e: e.matmul(SM[:, 160:176], cm["onesf"][:], dta_ap, start=True, stop=True), reads=[K("dtt"), "m_onesf"], excl=["b_SM"])
        P.op("dve", lambda e: e.tensor_tensor(out=B["mcb"][:], in0=SM[:, 0:128], in1=tri[:], op=ALU.mult), reads=["m_" + TRI[d]], writes=[K("mcb")], excl=["b_SM"])
        P.op("act", lambda e: e.activation(out=B["ex3"][:], in_=SM[:, 128:176], func=AF.Exp), writes=[K("ex3")], excl=["b_SM"])
        P.op("dve", lambda e: e.tensor_tensor(out=B["wv"][:], in0=B["ex3"][:, 16:32], in1=dt_ap, op=ALU.mult), reads=[K("ex3"), K("dtt")], writes=[K("wv")])
        for q in range(4):
            gb, gk = G[q % 2], "b_G%d" % (q % 2)
            P.op("pe", lambda e, gb=gb, q=q: e.matmul(gb[:, :].rearrange("p (h l) -> p h l", h=4), uu[:], B["R"][:, 4 * q:4 * q + 4, :], start=True, stop=True),
                 reads=[K("R"), "m_" + UU[d]], excl=[gk])
            P.op("act", lambda e, gb=gb, q=q: e.activation(out=B["E"][:, 4 * q:4 * q + 4, :], in_=gb[:, :].rearrange("p (h l) -> p h l", h=4), func=AF.Exp),
                 writes=[K("E")], excl=[gk])
        P.op("dve", lambda e: e.tensor_tensor(out=B["MT"][:], in0=B["E"][:], in1=B["mcb"][:].unsqueeze(1).broadcast_to([128, HPG, 128]), op=ALU.mult),
             reads=[K("E"), K("mcb")], writes=[K("MT")])
        xs3 = B["xs"][:, :].rearrange("p (h d) -> p h d", h=HPG)
        P.op("dve", lambda e: e.tensor_tensor(out=B["xdt"][:, :].rearrange("p (h d) -> p h d", h=HPG), in0=xs3,
                                              in1=dt_ap.unsqueeze(2).broadcast_to([128, HPG, HD]), op=ALU.mult),
             reads=[K("xs"), K("dtt")], writes=[K("xdt")])
        P.op("pool", lambda e: e.tensor_tensor(out=B["xw"][:, :].rearrange("p (h d) -> p h d", h=HPG), in0=xs3,
                                               in1=B["wv"][:].unsqueeze(2).broadcast_to([128, HPG, HD]), op=ALU.mult),
             reads=[K("xs"), K("wv")], writes=[K("xw")])
        for h in range(HPG):
            yb, yk = Y[h // 8], "b_Y%d" % (h // 8)
            P.op("pe", lambda e, yb=yb, h=h: e.matmul(yb[:, (h % 8) * HD:(h % 8 + 1) * HD], B["MT"][:, h, :], B["xdt"][:, h * HD:(h + 1) * HD], start=True, stop=True),
                 reads=[K("MT"), K("xdt")], excl=[yk])
        for hf in range(2):
            P.op("pe", lambda e, hf=hf: e.matmul(Z[hf][:, :], B["ct"][:], sb[:, hf * 512:(hf + 1) * 512], start=True, stop=True),
                 reads=[K("ct"), sk_bf], excl=["b_Z%d" % hf])
        for hf in range(2):
            P.op("act", lambda e, hf=hf: e.activation(out=B["ysb"][:, hf * 512:(hf + 1) * 512], in_=Y[hf][:, :], func=AF.Identity),
                 writes=[K("ysb") + str(hf)], excl=["b_Y%d" % hf])
            P.op("dve", lambda e, hf=hf: e.tensor_tensor(out=B["tmp"][:, hf * 512:(hf + 1) * 512].rearrange("p (h d) -> p h d", h=8),
                                                         in0=Z[hf][:, :].rearrange("p (h d) -> p h d", h=8),
                                                         in1=B["ex3"][:, 8 * hf:8 * hf + 8].unsqueeze(2).broadcast_to([128, 8, HD]), op=ALU.mult),
                 reads=[K("ex3")], writes=[K("tmp") + str(hf)], excl=["b_Z%d" % hf])
        P.op("pool", lambda e: e.tensor_tensor(out=B["yo"][:], in0=B["tmp"][:], in1=B["ysb"][:], op=ALU.add),
             reads=[K("tmp") + "0", K("tmp") + "1", K("ysb") + "0", K("ysb") + "1"], writes=[K("yo")])
        P.dma("pool", K("yo"), lambda e: [e.dma_start(out=io["yd"][d, tok, g * GW:(g + 1) * GW], in_=B["yo"][:])], reads=[K("yo")])
        for hf in range(2):
            P.op("pe", lambda e, hf=hf: e.matmul(Z[hf][:, :], B["btm"][:], B["xw"][:, hf * 512:(hf + 1) * 512], start=True, stop=True),
                 reads=[K("btm"), K("xw")], excl=["b_Z%d" % hf])
        P.op("dve", lambda e: e.tensor_tensor(out=st[:, :].rearrange("p (h d) -> p h d", h=HPG), in0=st[:, :].rearrange("p (h d) -> p h d", h=HPG),
                                              in1=B["ex3"][:, 32:48].unsqueeze(2).broadcast_to([128, HPG, HD]), op=ALU.mult),
             reads=[K("ex3"), sk_st], writes=[sk_st])
        for hf in range(2):
            P.op("dve", lambda e, hf=hf: e.tensor_tensor(out=st[:, hf * 512:(hf + 1) * 512], in0=st[:, hf * 512:(hf + 1) * 512], in1=Z[hf][:, :], op=ALU.add),
                 reads=[sk_st], writes=[sk_st], excl=["b_Z%d" % hf])
        P.op("act", lambda e: e.activation(out=sb[:], in_=st[:], func=AF.Identity), reads=[sk_st], writes=[sk_bf])

    si = 0
    ns = nck if nsteps is None else nsteps
    for i in range(ns):
        for d in range(2):
            for g in range(NG):
                step(si, d, g, order[d][i])
                si += 1
    P.pop()


def phase_S3(P, nc, io, nt):
    P.push()
    identf = P.sbuf("identf3", [128, 128], F32)
    onesf = P.sbuf("onesf3", [128, 128], F32)
    dsk = P.sbuf("dsk", [128, 2048], F32)
    P.dma("sp", "identf3", lambda e: [e.dma_start(out=identf[:], in_=io["identf"])], writes=["identf3"])
    P.dma("sp", "onesf3", lambda e: [e.dma_start(out=onesf[:], in_=io["onesf"])], writes=["onesf3"])
    P.dma("sp", "dsk", lambda e: [e.dma_start(out=dsk[:], in_=io["dskbc"])], writes=["dsk"])
    gssm = P.sbuf("gssm", [128, 16], F32)
    P.dma("sp", "gssm", lambda e: [e.dma_start(out=gssm[:], in_=io["gssmT"])], writes=["gssm"])
    yf = [P.sbuf("yf%d" % i, [128, 2048], F32) for i in range(2)]
    yb = [P.sbuf("yb%d" % i, [128, 2048], F32) for i in range(2)]
    xs = [P.sbuf("xs3_%d" % i, [128, 2048], F32) for i in range(2)]
    sz = [P.sbuf("sz%d" % i, [128, 16, 128], F32) for i in range(2)]
    yg = [P.sbuf("yg%d" % i, [128, 512], F32) for i in range(2)]
    sq = [P.sbuf("sq3_%d" % i, [128, 512], F32) for i in range(2)]
    ygs = [P.sbuf("ygs%d" % i, [128, 16, 512], BF16) for i in range(2)]
    ssr = P.sbuf("ssr", [1, 512], F32)
    pt = [P.psum("pt3_%d" % i, [128, 512], F32) for i in range(2)]
    pq = P.psum("pq", [128, 512], F32)
    z0 = SEG_OFF["z"] * 128
    nck = nt // 128
    qi = 0
    blocks = token_blocks(nt)
    for (t0, n) in blocks:
        bi = (t0 // 512) % 2 if t0 >= NCTX else 0
        gs, gsk = ygs[bi % 2], "ygs%d" % (bi % 2)
        for tt in range(n // 128):
            ck = (t0 + tt * 128) // 128
            i2 = ck % 2
            tok = slice(ck * 128, (ck + 1) * 128)
            a, b_, x_, s_ = yf[i2], yb[i2], xs[i2], sz[i2]
            ka, kb, kx, ks = "yf%d" % i2, "yb%d" % i2, "xs3_%d" % i2, "sz%d" % i2
            P.dma("sp", ka, lambda e, a=a, tok=tok: [e.dma_start(out=a[:], in_=io["yd"][0, tok, :])], writes=[ka])
            P.dma("sp", kb, lambda e, b_=b_, tok=tok: [e.dma_start(out=b_[:], in_=io["yd"][1, tok, :])], writes=[kb])
            P.dma("sp", kx, lambda e, x_=x_, tok=tok: [e.dma_start(out=x_[:], in_=io["xs_tm"][tok, :])], writes=[kx])
            P.dma("sp", ks, lambda e, s_=s_, tok=tok: [e.dma_start(out=s_[:], in_=io["pT"][z0:z0 + 2048, tok].rearrange("(c p) t -> p c t", p=128))], writes=[ks])
            P.op("dve", lambda e, a=a, b_=b_: e.tensor_tensor(out=a[:], in0=a[:], in1=b_[:], op=ALU.add), reads=[ka, kb], writes=[ka])
            P.op("pool", lambda e, x_=x_: e.tensor_tensor(out=x_[:], in0=x_[:], in1=dsk[:], op=ALU.mult), reads=[kx, "dsk"], writes=[kx])
            P.op("pool", lambda e, a=a, x_=x_: e.tensor_tensor(out=a[:], in0=a[:], in1=x_[:], op=ALU.add), reads=[ka, kx], writes=[ka])
            for q in range(4):
                pb, pk = pt[qi % 2], "b_pt3_%d" % (qi % 2)
                g_, gk = yg[qi % 2], "yg%d" % (qi % 2)
                s2, s2k = sq[qi % 2], "sq3_%d" % (qi % 2)
                qi += 1
                for c4 in range(4):
                    c = q * 4 + c4
                    P.op("pe", lambda e, pb=pb, a=a, c=c, c4=c4: e.transpose(pb[:, c4 * 128:(c4 + 1) * 128], a[:, c * 128:(c + 1) * 128], identf[:]),
                         reads=[ka, "identf3"], excl=[pk])
                P.op("dve", lambda e, pb=pb, g_=g_, s_=s_, q=q: e.tensor_tensor(out=g_[:, :].rearrange("p (c t) -> p c t", c=4),
                                                                              in0=pb[:, :].rearrange("p (c t) -> p c t", c=4),
                                                                              in1=s_[:, 4 * q:4 * q + 4, :], op=ALU.mult),
                     reads=[ks], writes=[gk], excl=[pk])
                for c4 in range(4):
                    c = q * 4 + c4
                    P.op("act", lambda e, g_=g_, gs=gs, c=c, c4=c4, tt=tt: e.activation(out=gs[:, c, tt * 128:(tt + 1) * 128],
                                                                                      in_=g_[:, c4 * 128:(c4 + 1) * 128], func=AF.Identity, scale=gssm[:, c:c + 1]),
                         reads=[gk, "gssm"], writes=[(gsk, tt, q, c4)])
                P.op("act", lambda e, g_=g_, s2=s2: e.activation(out=s2[:], in_=g_[:], func=AF.Square), reads=[gk], writes=[s2k])
                for c4 in range(4):
                    first = (q == 0 and c4 == 0)
                    last = (q == 3 and c4 == 3)
                    P.op("pe", lambda e, s2=s2, c4=c4, tt=tt, first=first, last=last: e.matmul(pq[0:1, tt * 128:(tt + 1) * 128], onesf[:, 0:1], s2[:, c4 * 128:(c4 + 1) * 128],
                                                                                             start=first, stop=last),
                         reads=[s2k, "onesf3"], excl=["b_pq"])
        P.op("dve", lambda e, n=n: e.tensor_copy(out=ssr[:, 0:n], in_=pq[0:1, 0:n]), writes=["ssr"], excl=["b_pq"])
        P.dma("sp", "ssr", lambda e, t0=t0, n=n: [e.dma_start(out=io["ssq"][:, t0:t0 + n], in_=ssr[:, 0:n])], reads=["ssr"])
        P.dma("sp", gsk, lambda e, gs=gs, t0=t0, n=n: [e.dma_start(out=io["ygT_parts"][hf][:, t0:t0 + n].rearrange("(c p) t -> p c t", p=128), in_=gs[:, 8 * hf:8 * hf + 8, 0:n]) for hf in range(2)],
              n=2, reads=[(gsk, tt, q, c4) for tt in range(n // 128) for q in range(4) for c4 in range(4)])
    P.pop()


TC = 2112
NLAT = 2048


def c_blocks(with_ctx):
    blks = [(t0, 256, 0) for t0 in range(0, NLAT, 256)]
    if with_ctx:
        blks.append((NLAT, 64, 1))
    return blks


def phase_Cmod(P, nc, io, wadag_fn):
    P.push()
    c2 = P.sbuf("cm_c2", [128, KC, 2], F32)
    sc = P.sbuf("cm_sc", [128, KC, 2], F32)
    scb = P.sbuf("cm_scb", [128, 2, KC, 128], F32)
    bb = P.sbuf("cm_bb", [128, D], F32)
    wg = [P.sbuf("cm_wg%d" % i, [128, KC, 256], F32) for i in range(2)]
    sg = [P.sbuf("cm_sg%d" % i, [128, 256], F32) for i in range(2)]
    pg = [P.psum("cm_pg%d" % i, [128, 512], F32) for i in range(2)]
    P.dma("sp", "cm_c2", lambda e: [e.dma_start(out=c2[:], in_=io["c2T"])], writes=["cm_c2"])
    P.dma("sp", "cm_bb", lambda e: [e.dma_start(out=bb[:], in_=io["badag"][0:1, :].broadcast_to([128, D]))], writes=["cm_bb"])
    P.op("act", lambda e: e.activation(out=sc[:], in_=c2[:], func=AF.Silu), reads=["cm_c2"], writes=["cm_sc"])
    for v in range(2):
        P.op("dve", lambda e, v=v: e.tensor_copy(out=scb[:, v], in_=sc[:, :, v].unsqueeze(2).broadcast_to([128, KC, 128])),
             reads=["cm_sc"], writes=["cm_scb"])
    si = 0
    for cb in range(16):
        w_, wk = wg[cb % 2], "cm_wg%d" % (cb % 2)
        wp_ = wadag_fn(cb)
        P.dma("sp", wk, lambda e, w_=w_, wp_=wp_: dma_pieces(e, w_, wp_, "p (k n) -> p k n", k=KC), writes=[wk], n=len(wp_))
        for v in range(2):
            for k in range(KC):
                P.op("pe", lambda e, v=v, k=k, w_=w_: e.matmul(pg[v][:, 0:256], scb[:, v, k, :], w_[:, k, :], start=(k == 0), stop=(k == KC - 1)),
                     reads=[wk, "cm_scb"], excl=["b_cm_pg%d" % v])
            s_, sk = sg[si % 2], "cm_sg%d" % (si % 2)
            si += 1
            P.op("dve", lambda e, v=v, s_=s_, cb=cb: e.tensor_tensor(out=s_[:], in0=pg[v][:, 0:256], in1=bb[:, cb * 256:(cb + 1) * 256], op=ALU.add),
                 reads=["cm_bb"], writes=[sk], excl=["b_cm_pg%d" % v])
            P.dma("sp", sk, lambda e, v=v, s_=s_, cb=cb: [e.dma_start(out=io["gsc"][v, :, cb * 256:(cb + 1) * 256], in_=s_[:])], reads=[sk])
    P.pop()


def phase_C(P, nc, io, wpa_fn, wpb_fn, wout_fn, with_ctx, final, dyn):
    P.push()
    mrg = P.sbuf("c_mrg", [128, KC, 256], BF16)
    ones4 = P.sbuf("c_ones4", [4, 128], F32)
    P.op("pool", lambda e: e.memset(ones4[:], 1.0), writes=["c_ones4"])
    for (t0, n, var) in c_blocks(with_ctx):
        P.push()
        ogb = P.sbuf("c_ogb", [128, KC, 256], BF16)
        ygb = P.sbuf("c_ygb", [128, 64, 256], BF16)
        ss4 = P.sbuf("c_ss4", [4, 256], F32)
        rs = P.sbuf("c_rs", [128, 3, 256], F32)
        wa = [P.sbuf("c_wa%d" % i, [128, KC, 128], BF16) for i in range(2)]
        wb = [P.sbuf("c_wb%d" % i, [128, 64, 128], BF16) for i in range(2)]
        sga = [P.sbuf("c_sga%d" % i, [128, 256], F32) for i in range(2)]
        sgb = [P.sbuf("c_sgb%d" % i, [128, 256], F32) for i in range(2)]
        m1 = [P.sbuf("c_m1%d" % i, [128, 256], F32) for i in range(2)]
        m2 = [P.sbuf("c_m2%d" % i, [128, 256], F32) for i in range(2)]
        pa = [P.psum("c_pa%d" % i, [128, 512], F32) for i in range(2)]
        pb = [P.psum("c_pb%d" % i, [128, 512], F32) for i in range(2)]
        pr = P.psum("c_pr", [128, 512], F32)

        def cols(ap, t0=t0, n=n):
            return ap[:, t0:t0 + n]

        P.dma("sp", "c_ogb", lambda e, n=n, cols=cols: [e.dma_start(out=ogb[:, :, 0:n], in_=cols(io["oga2"]).rearrange("(k p) t -> p k t", p=128))], writes=["c_ogb"])
        P.dma("sp", "c_ygb", lambda e, n=n, cols=cols: [e.dma_start(out=ygb[:, 32 * q:32 * q + 32, 0:n],
                                                                   in_=cols(io["yga_parts"][q]).rearrange("(k p) t -> p k t", p=128))
                                                       for q in range(2)], writes=["c_ygb"], n=2)
        P.dma("sp", "c_ss4", lambda e, n=n, cols=cols: [e.dma_start(out=ss4[:, 0:n], in_=cols(io["ssqa"]))], writes=["c_ss4"])
        P.op("pe", lambda e, n=n: e.matmul(pr[:, 0:n], ones4[:], ss4[:, 0:n], start=True, stop=True), reads=["c_ones4", "c_ss4"], excl=["b_c_pr"])
        P.op("dve", lambda e, n=n: e.tensor_scalar(out=rs[:, 0, 0:n], in0=pr[:, 0:n], scalar1=1.0 / 8192.0, scalar2=EPS, op0=ALU.mult, op1=ALU.add),
             writes=["c_rs0"], excl=["b_c_pr"])
        P.op("act", lambda e, n=n: e.activation(out=rs[:, 1, 0:n], in_=rs[:, 0, 0:n], func=AF.Sqrt), reads=["c_rs0"], writes=["c_rs1"])
        P.op("dve", lambda e, n=n: e.reciprocal(out=rs[:, 2, 0:n], in_=rs[:, 1, 0:n]), reads=["c_rs1"], writes=["c_rs2"])
        for oc in range(KC):
            i2 = oc % 2
            wpa_, wpb_ = wpa_fn(oc), wpb_fn(oc)
            P.dma("pool", "c_wa%d" % i2, lambda e, i2=i2, wpa_=wpa_: dma_pieces(e, wa[i2], wpa_, "p (k n) -> p k n", k=KC), writes=["c_wa%d" % i2], n=len(wpa_))
            P.dma("pool", "c_wb%d" % i2, lambda e, i2=i2, wpb_=wpb_: dma_pieces(e, wb[i2], wpb_, "p (k n) -> p k n", k=64), writes=["c_wb%d" % i2], n=len(wpb_))
            P.dma("sp", "c_sga%d" % i2, lambda e, oc=oc, i2=i2, t0=t0, n=n: [e.dma_start(out=sga[i2][:, 0:n], in_=io["sgT"][oc * 128:(oc + 1) * 128, t0:t0 + n])], writes=["c_sga%d" % i2])
            P.dma("sp", "c_sgb%d" % i2, lambda e, oc=oc, i2=i2, t0=t0, n=n: [e.dma_start(out=sgb[i2][:, 0:n], in_=io["sgT"][D + oc * 128:D + (oc + 1) * 128, t0:t0 + n])], writes=["c_sgb%d" % i2])
            for k in range(KC):
                P.op("pe", lambda e, k=k, i2=i2, n=n: e.matmul(pa[i2][:, 0:n], wa[i2][:, k, :], ogb[:, k, 0:n], start=(k == 0), stop=(k == KC - 1)),
                     reads=["c_wa%d" % i2, "c_ogb"], excl=["b_c_pa%d" % i2])
            for k in range(64):
                P.op("pe", lambda e, k=k, i2=i2, n=n: e.matmul(pb[i2][:, 0:n], wb[i2][:, k, :], ygb[:, k, 0:n], start=(k == 0), stop=(k == 63)),
                     reads=["c_wb%d" % i2, "c_ygb"], excl=["b_c_pb%d" % i2])
            P.op("dve", lambda e, i2=i2, n=n: e.tensor_tensor(out=m1[i2][:, 0:n], in0=pa[i2][:, 0:n], in1=sga[i2][:, 0:n], op=ALU.mult),
                 reads=["c_sga%d" % i2], writes=["c_m1%d" % i2], excl=["b_c_pa%d" % i2])
            P.op("dve", lambda e, i2=i2, n=n: e.tensor_tensor(out=m2[i2][:, 0:n], in0=pb[i2][:, 0:n], in1=rs[:, 2, 0:n], op=ALU.mult),
                 reads=["c_rs2"], writes=["c_m2%d" % i2], excl=["b_c_pb%d" % i2])
            P.op("pool", lambda e, i2=i2, n=n: e.tensor_tensor(out=m2[i2][:, 0:n], in0=m2[i2][:, 0:n], in1=sgb[i2][:, 0:n], op=ALU.mult),
                 reads=["c_sgb%d" % i2, "c_m2%d" % i2], writes=["c_m2%d" % i2])
            P.op("pool", lambda e, i2=i2, n=n, oc=oc: e.tensor_tensor(out=mrg[:, oc, 0:n], in0=m1[i2][:, 0:n], in1=m2[i2][:, 0:n], op=ALU.add),
                 reads=["c_m1%d" % i2, "c_m2%d" % i2], writes=["c_mrg"])
        P.pop()
        P.push()
        wo = [P.sbuf("c_wo%d" % i, [128, KC, 512], BF16) for i in range(2)]
        hrow = [P.sbuf("c_hrow%d" % i, [128, D], F32) for i in range(2)]
        gb = P.sbuf("c_gb", [128, D], F32)
        ht = [P.sbuf("c_ht%d" % i, [128, 512], F32) for i in range(2)]
        tmp = [P.sbuf("c_tmp%d" % i, [128, 512], F32) for i in range(2)]
        po = [P.psum("c_po%d" % i, [128, 512], F32) for i in range(2)]
        P.dma("sp", "c_gb", lambda e, var=var: [e.dma_start(out=gb[:], in_=io["gsc"][var])], writes=["c_gb"])
        if final:
            junk = P.sbuf("c_junk", [128, D], BF16)
            gf = P.sbuf("c_gf", [128, D], F32)
            st = P.sbuf("c_st", [128, 8], F32)
            P.dma("sp", "c_gf", lambda e: [e.dma_start(out=gf[:], in_=io["gfin"][0:1, :].broadcast_to([128, D]))], writes=["c_gf"])
        tiles = [(0, 128), (128, 128)] if n == 256 else [(0, n)]
        ci = 0
        for ob in range(8):
            w_, wk = wo[ob % 2], "c_wo%d" % (ob % 2)
            wo_ = wout_fn(ob)
            P.dma("pool", wk, lambda e, w_=w_, wo_=wo_: dma_pieces(e, w_, wo_, "p (k n) -> p k n", k=KC), writes=[wk], n=len(wo_))
            for ti, (c0, rows) in enumerate(tiles):
                p_, pk = po[ci % 2], "b_c_po%d" % (ci % 2)
                h_, hk = ht[ci % 2], "c_ht%d" % (ci % 2)
                t_, tk = tmp[ci % 2], "c_tmp%d" % (ci % 2)
                ci += 1
                P.dma("sp", hk, lambda e, h_=h_, t0=t0, c0=c0, rows=rows, ob=ob: [e.dma_start(out=h_[0:rows, :], in_=io["hin_rows"](t0 + c0, rows)[:, ob * 512:(ob + 1) * 512])], writes=[hk])
                for k in range(KC):
                    P.op("pe", lambda e, p_=p_, k=k, c0=c0, rows=rows, w_=w_: e.matmul(p_[0:rows, :], mrg[:, k, c0:c0 + rows], w_[:, k, :], start=(k == 0), stop=(k == KC - 1)),
                         reads=["c_mrg", wk], excl=[pk])
                P.op("dve", lambda e, p_=p_, t_=t_, rows=rows, ob=ob: e.tensor_tensor(out=t_[0:rows, :], in0=p_[0:rows, :], in1=gb[0:rows, ob * 512:(ob + 1) * 512], op=ALU.mult),
                     reads=["c_gb"], writes=[tk], excl=[pk])
                P.op("pool", lambda e, t_=t_, h_=h_, rows=rows, ob=ob, ti=ti: e.tensor_tensor(out=hrow[ti][0:rows, ob * 512:(ob + 1) * 512], in0=t_[0:rows, :], in1=h_[0:rows, :], op=ALU.add),
                     reads=[tk, hk], writes=[("c_hrow", ti, ob)])
        for ti, (c0, rows) in enumerate(tiles):
            hkeys = [("c_hrow", ti, ob) for ob in range(8)]
            if not final:
                P.dma("sp", "c_hst%d" % ti, lambda e, ti=ti, t0=t0, c0=c0, rows=rows: [e.dma_start(out=io["hout_rows"](t0 + c0, rows), in_=hrow[ti][0:rows, :])], reads=hkeys)
            else:
                assert var == 0
                P.op("act", lambda e, ti=ti: e.activation(out=junk[:], in_=hrow[ti][:], func=AF.Square, accum_out=st[:, 4 * ti:4 * ti + 1]), reads=hkeys, writes=["c_junk", "c_st%d" % ti])
                P.op("dve", lambda e, ti=ti: e.tensor_scalar(out=st[:, 4 * ti + 1:4 * ti + 2], in0=st[:, 4 * ti:4 * ti + 1], scalar1=1.0 / D, scalar2=EPS, op0=ALU.mult, op1=ALU.add),
                     reads=["c_st%d" % ti], writes=["c_st%db" % ti])
                P.op("act", lambda e, ti=ti: e.activation(out=st[:, 4 * ti + 3:4 * ti + 4], in_=st[:, 4 * ti + 1:4 * ti + 2], func=AF.Sqrt), reads=["c_st%db" % ti], writes=["c_st%dd" % ti])
                P.op("dve", lambda e, ti=ti: e.reciprocal(out=st[:, 4 * ti + 2:4 * ti + 3], in_=st[:, 4 * ti + 3:4 * ti + 4]), reads=["c_st%dd" % ti], writes=["c_st%dc" % ti])
                P.op("dve", lambda e, ti=ti: e.scalar_tensor_tensor(out=hrow[ti][:], in0=hrow[ti][:], scalar=st[:, 4 * ti + 2:4 * ti + 3], in1=gf[:], op0=ALU.mult, op1=ALU.mult),
                     reads=hkeys + ["c_st%dc" % ti, "c_gf"], writes=hkeys)
                P.dma("sp", "c_hst%d" % ti, lambda e, ti=ti, t0=t0, c0=c0: [e.dma_start(out=io["hnorm"][t0 + c0:t0 + c0 + 128, :], in_=hrow[ti][:])], reads=hkeys)
        P.pop()
    P.pop()


I32 = mybir.dt.int32
NT = 8448
DEPTH = 2
G8 = [list(range(8))]
G4 = [[0, 1, 2, 3], [4, 5, 6, 7]]
G2 = [[0, 4], [1, 5], [2, 6], [3, 7]]
FUNC = {"id": AF.Identity, "silu": AF.Silu, "sig": AF.Sigmoid}

LAYER_IN = {
    "wadal_sh": ([16, 128, 4096], F32), "badaT": ([128, 64], F32), "gpreT": ([128, 32], F32),
    "wl_sh": ([56, 128, 4096], F32), "wmg_sh": ([16, 128, 4096], F32),
    "gqT": ([128, 6], F32), "gkvT": ([128, 4], F32), "wuq": ([128, 9216], F32), "wuqsw": ([128, 3072], F32), "wukv": ([128, 8192], F32),
    "convw": ([128, 20, 5], F32), "convb": ([128, 20], F32), "dtbias2": ([128, 1], F32), "alog2": ([128, 1], F32),
    "dskbc": ([128, 2048], F32), "gssmT": ([128, 16], F32),
    "wadag_sh": ([4, 128, 8192], F32), "badag": ([1, 4096], F32),
    "wpa_sh": ([8, 128, 4096], F32), "wpb_sh": ([8, 128, 8192], F32), "wout_sh": ([2, 128, 16384], F32),
}
CONST_IN = {
    "xin_sh": ([TC, D], F32), "offs": ([1, 2], I32), "c2T": ([128, 32, 2], F32),
    "identf": ([128, 128], F32), "onesf": ([128, 128], F32), "TL": ([128, 128], F32), "TG": ([128, 128], F32),
    "SG": ([128, 128], F32), "SL": ([128, 128], F32), "cos2": ([64, NT], F32), "sin2": ([64, NT], F32), "gfin": ([1, D], F32),
}


def build_program(nlayers=DEPTH, probe=None):
    nc = bass.Bass("TRN2", target_bir_lowering=False)
    ext = {}
    for k, (shp, dt) in CONST_IN.items():
        ext[k] = nc.dram_tensor(k, shp, dt, kind="ExternalInput").ap()
    for i in range(nlayers):
        for k, (shp, dt) in LAYER_IN.items():
            ext["%s_%d" % (k, i)] = nc.dram_tensor("%s_%d" % (k, i), shp, dt, kind="ExternalInput").ap()
    hnorm = nc.dram_tensor("hnorm", [NLAT, D], F32, kind="ExternalOutput").ap()
    P = Prog(nc)

    def dram(name, shape, dt=F32):
        return nc.dram_tensor(name, list(shape), dt)

    dyn = {}
    rl = P.es.enter_context(nc.sync.register("r_lat"))
    rc = P.es.enter_context(nc.sync.register("r_ctx"))

    def ldregs(e):
        e.reg_load(rl, ext["offs"][0:1, 0:1])
        ins = e.reg_load(rc, ext["offs"][0:1, 1:2])
        dyn["lat"] = e.snap(rl)
        dyn["ctx"] = e.snap(rc)
        return ins
    P.op("sp", ldregs)

    class Gathered:
        def __init__(self, out, CR, W, unit):
            self.out, self.CR, self.W, self.unit = out, CR, W, unit

        def rows(self, rank, r0, n):
            res = []
            r = r0
            while r < r0 + n:
                c, i = r // self.CR, r % self.CR
                k = min(self.CR - i, r0 + n - r)
                res.append((self.out.ap()[c, rank * self.CR + i:rank * self.CR + i + k, :], r - r0, k))
                r += k
            return res

        def at(self, rank, a):
            return self.rows(rank, a * self.unit, self.unit)

    def gather_from(bnc, R, ncol, dt, name, esize):
        CR = 1
        while CR * 2 * ncol * esize <= (1 << 20) and R % (CR * 2) == 0:
            CR *= 2
        nch = R // CR
        out = dram("gat_%s" % name, [nch, 4 * CR, ncol], dt)
        return out, CR, nch

    def gather(src_ap, shape, dt, name, batch):
        unit = shape[1] if len(shape) == 3 else 1
        ncol = shape[-1]
        R = shape[0] * unit
        esize = 4 if dt == F32 else 2
        bnc = dram("bnc_%s" % name, [R, ncol], dt)
        if len(shape) == 3:
            pieces = [(bnc.ap()[a * unit:(a + 1) * unit, :], src_ap[a]) for a in range(shape[0])]
        else:
            step = 264
            pieces = [(bnc.ap()[r0:min(r0 + step, R), :], src_ap[r0:min(r0 + step, R), :]) for r0 in range(0, R, step)]
        P.dma("sp", "bnc", lambda e, pieces=pieces: [e.dma_start(out=o_, in_=i_) for (o_, i_) in pieces],
              writes=[("bnc", name)], n=len(pieces))
        out, CR, nch = gather_from(bnc, R, ncol, dt, name, esize)
        batch.append((bnc, out, CR, nch, ("bnc", name)))
        return Gathered(out, CR, 4, unit)

    def flush(batch, slot):
        items = list(batch)
        tot = sum(it[3] for it in items)
        P.coll(lambda e: [e.collective_compute("AllGather", ALU.bypass, replica_groups=G4, ins=[b_.ap()[c * CR:(c + 1) * CR, :]], outs=[o_.ap()[c]])
                          for (b_, o_, CR, nch, _k) in items for c in range(nch)], tot, reads=[it[4] for it in items], writes=[], slot=slot)
        del batch[:]

    def do_gather(bnc, out, CR, nch, reads, writes):
        P.coll(lambda e: [e.collective_compute("AllGather", ALU.bypass, replica_groups=G4, ins=[bnc.ap()[c * CR:(c + 1) * CR, :]], outs=[out.ap()[c]])
                          for c in range(nch)], nch, reads=reads, writes=writes)

    def precast(g, name):
        shp = g.out.ap().shape
        o2 = dram("bf_%s" % name, list(shp), BF16)
        P.dma("pool", "precast", lambda e: [e.dma_start(out=o2.ap()[c], in_=g.out.ap()[c]) for c in range(shp[0])], n=shp[0])
        return Gathered(o2, g.CR, g.W, g.unit)

    b0_ = []
    xg = gather(ext["xin_sh"], [TC, D], F32, "x0", b0_)
    W_ = [dict() for _ in range(nlayers)]
    W_[0]["wadal"] = gather(ext["wadal_sh_0"], [16, 128, 4096], F32, "wadal0", b0_)
    flush(b0_, "cc")
    for i in range(nlayers):
        w = W_[i]
        bt = []
        if i > 0:
            w["wadal"] = gather(ext["wadal_sh_%d" % i], [16, 128, 4096], F32, "wadal%d" % i, bt)
        w["wmg"] = gather(ext["wmg_sh_%d" % i], [16, 128, 4096], F32, "wmg%d" % i, bt)
        w["wadag"] = gather(ext["wadag_sh_%d" % i], [4, 128, 8192], F32, "wadag%d" % i, bt)
        w["wpa"] = gather(ext["wpa_sh_%d" % i], [8, 128, 4096], F32, "wpa%d" % i, bt)
        w["wpb"] = gather(ext["wpb_sh_%d" % i], [8, 128, 8192], F32, "wpb%d" % i, bt)
        w["wout"] = gather(ext["wout_sh_%d" % i], [2, 128, 16384], F32, "wout%d" % i, bt)
        P.detach("ccw%d" % i)
        flush(bt, "ccw%d" % i)
    P.barrier()
    if probe == "gathers":
        f0 = lambda pcs: pcs[0][0]
        pieces = [(hnorm[8 * r_:8 * r_ + 8, :], f0(xg.rows(r_, 0, 8))) for r_ in range(4)]
        pieces += [(hnorm[32:40, :], f0(xg.rows(2, NLAT + 8, 8)))]
        w = W_[0]
        pieces += [(hnorm[40:48, :], ext["wl_sh_0"][31, 0:8, :]), (hnorm[48:56, :], ext["wl_sh_0"][27, 8:16, :]),
                   (hnorm[56:64, :], f0(w["wadal"].at(2, 10))[0:8, :]), (hnorm[64:72, :], f0(w["wmg"].at(3, 15))[0:8, :]),
                   (hnorm[72:80, :], f0(w["wadag"].at(1, 3))[0:8, 0:4096]), (hnorm[80:88, :], f0(w["wpa"].at(3, 3))[0:8, :]),
                   (hnorm[88:96, :], f0(w["wpb"].at(1, 1))[0:8, 4096:8192]), (hnorm[96:104, :], f0(w["wout"].at(2, 0))[0:8, 0:4096])]
        P.dma("sp", "probe", lambda e: [e.dma_start(out=o_, in_=i_) for (o_, i_) in pieces], n=len(pieces))
        P.emit()
        return nc, P

    pT = dram("pT", [NPT * 128, NT]).ap()
    sc = {
        "qT": dram("qT", [8, 192, NT], BF16).ap(), "kT": dram("kT", [8, 128, NT], BF16).ap(), "kpeT": dram("kpeT", [64, NT], BF16).ap(),
        "vs": dram("vs", [8, 128, NT // 128, 128], BF16).ap(),
        "xs_tm": dram("xs_tm", [NT, 2048]).ap(), "B_tm": dram("B_tm", [2, NT, 128], BF16).ap(), "BT": dram("BT", [2, 128, NT], BF16).ap(),
        "CT": dram("CT", [2, 128, NT], BF16).ap(), "dtt": dram("dtt", [NT, 128]).ap(), "yd": dram("yd", [2, NT, 2048]).ap(),
        "sgT": dram("sgT", [2 * D, TC]).ap(), "gsc": dram("gsc", [2, 128, D]).ap(),
    }
    ogb_ = dram("og_b", [1024, NT], BF16)
    ygb_ = [dram("yg_b%d" % q, [1024, NT], BF16) for q in range(2)]
    ssb_ = dram("ss_b", [1, NT])
    oga, ogCR, ognch = gather_from(ogb_, 1024, NT, BF16, "og", 2)
    ygg = [gather_from(ygb_[q], 1024, NT, BF16, "yg%d" % q, 2) for q in range(2)]
    ssa, ssCR, ssnch = gather_from(ssb_, 1, NT, F32, "ss", 4)
    og_own = dram("og_own", [4 * 1024, TC], BF16)
    yg_own = [dram("yg_own%d" % q, [4 * 1024, TC], BF16) for q in range(2)]
    ss_own = dram("ss_own", [4, TC])
    hout = dram("hout", [TC, D])
    xg1o, hCR, hnch = gather_from(hout, TC, D, F32, "h1", 4)
    xg1 = Gathered(xg1o, hCR, 4, 1)

    def hout_rows(r0, n):
        return hout.ap()[r0:r0 + n, :]

    for i in range(nlayers):
        w = W_[i]
        L = lambda k, i=i: ext["%s_%d" % (k, i)]
        xga = xg if i == 0 else xg1
        own_rows = (lambda r0, n: ext["xin_sh"][r0:r0 + n, :]) if i == 0 else hout_rows
        modio = {"c2T": ext["c2T"], "badaT": L("badaT"), "gpreT": L("gpreT"), "identf": ext["identf"]}
        if i > 0:
            P.join("ccw%d" % i)
            P.barrier()
        wlb = dram("wlbf_%d" % i, [NPT * 128, 4096], BF16)
        P.dma("pool", "precast", lambda e, i=i, wlb=wlb: [e.dma_start(out=wlb.ap()[c * 128:(c + 1) * 128, :], in_=ext["wl_sh_%d" % i][c]) for c in range(NPT)], n=NPT)
        P.barrier()
        wada_fn = lambda oc, w=w: w["wadal"].at(oc // 16, oc % 16)
        blocks = [dict(n=256, mcol=1, tiles=[[(a_, o_, k_) for (a_, o_, k_) in xga.rows(2 * t, NLAT, 64)] + [(a_, 64 + o_, k_) for (a_, o_, k_) in xga.rows(2 * t + 1, NLAT, 64)] for t in range(2)])]
        for bl in range(16):
            tiles = []
            for tt in range(4):
                l0 = bl * 512 + tt * 128
                tiles.append(xga.rows(l0 // NLAT, l0 % NLAT, 128))
            blocks.append(dict(n=512, mcol=0, tiles=tiles))
        t0s = [0] + [256 + 512 * b_ for b_ in range(16)]
        chunks = []
        cc = 0
        for name, cnt, f in SEG:
            if name in ("mga", "mgb"):
                continue
            for q in range(cnt):
                w_ap = [(wlb.ap()[cc * 128:(cc + 1) * 128, :], 0, 128)]
                chunks.append((w_ap, FUNC[f], (lambda bi, cc=cc: pT[cc * 128:(cc + 1) * 128, t0s[bi]:t0s[bi] + blocks[bi]["n"]])))
                cc += 1
        assert cc == NPT
        phase_A(P, nc, modio, blocks, chunks, wada_fn)
        P.barrier()
        iob = {"pT": pT, "gqT": L("gqT"), "gkvT": L("gkvT"), "wuq": L("wuq"), "wuqsw": L("wuqsw"), "wukv": L("wukv"),
               "cos2": ext["cos2"], "sin2": ext["sin2"], "onesf": ext["onesf"], "ogT": ogb_.ap()}
        iob.update(sc)
        phase_B1(P, nc, iob, NT)
        P.barrier()
        phase_B2(P, nc, iob, NT)
        P.barrier()
        ios = {"pT": pT, "convw": L("convw"), "convb": L("convb"), "dtbias2": L("dtbias2"), "alog2": L("alog2"), "identf": ext["identf"],
               "onesf": ext["onesf"], "TL": ext["TL"], "TG": ext["TG"], "SG": ext["SG"], "SL": ext["SL"], "dskbc": L("dskbc"), "gssmT": L("gssmT"),
               "ygT_parts": [t_.ap() for t_ in ygb_], "ssq": ssb_.ap()}
        ios.update(sc)
        phase_S1(P, nc, ios, NT)
        P.barrier()
        phase_S2(P, nc, ios, NT)
        P.barrier()
        phase_S3(P, nc, ios, NT)
        P.barrier()
        xb_ = [(ogb_, oga, ogCR, ognch, ("x", "og"))] + [(ygb_[q], ygg[q][0], ygg[q][1], ygg[q][2], ("x", "yg%d" % q)) for q in range(2)]
        xb_.append((ssb_, ssa, ssCR, ssnch, ("x", "ss")))
        flush(xb_, "cc")
        if i == 0:
            P.join("ccw0")
            P.barrier()
        wmg_b = precast(w["wmg"], "wmg%d" % i)
        wpa_b = precast(w["wpa"], "wpa%d" % i)
        wpb_b = precast(w["wpb"], "wpb%d" % i)
        wout_b = precast(w["wout"], "wout%d" % i)
        P.barrier()
        mblocks = []
        for bl in range(4):
            mblocks.append(dict(n=512, mcol=0, tiles=[[(own_rows(bl * 512 + tt * 128, 128), 0, 128)] for tt in range(4)]))
        mt0 = [0, 512, 1024, 1536]
        if i < nlayers - 1:
            mblocks.append(dict(n=64, mcol=1, tiles=[[(own_rows(NLAT, 64), 0, 64)]]))
            mt0.append(NLAT)
        mchunks = []
        for cc in range(64):
            mchunks.append((wmg_b.at(cc // 16, cc % 16), AF.Sigmoid,
                            (lambda bi, cc=cc: sc["sgT"][cc * 128:(cc + 1) * 128, mt0[bi]:mt0[bi] + mblocks[bi]["n"]])))
        phase_A(P, nc, modio, mblocks, mchunks, wada_fn)
        P.barrier()
        def own_cols(dst, src, CR, nch):
            sv = src.ap().rearrange("c q t -> (c q) t")
            dv = dst.ap()
            P.dma("sp", "owncp", lambda e: [e.dma_start(out=dv[:, 0:NLAT], in_=sv[:, bass.ds(dyn["lat"], NLAT)]),
                                            e.dma_start(out=dv[:, NLAT:TC], in_=sv[:, bass.ds(dyn["ctx"], 64)])], n=2)
        own_cols(og_own, oga, ogCR, ognch)
        for q in range(2):
            own_cols(yg_own[q], ygg[q][0], ygg[q][1], ygg[q][2])
        own_cols(ss_own, ssa, ssCR, ssnch)
        P.barrier()
        ioc = {"c2T": ext["c2T"], "badag": L("badag"), "gsc": sc["gsc"], "sgT": sc["sgT"], "hin_rows": own_rows, "gfin": ext["gfin"],
               "oga2": og_own.ap(), "yga_parts": [t_.ap() for t_ in yg_own],
               "ssqa": ss_own.ap(), "hout_rows": hout_rows, "hnorm": hnorm}
        phase_Cmod(P, nc, ioc, lambda cb, w=w: w["wadag"].at(cb // 4, cb % 4))
        P.barrier()
        final = (i == nlayers - 1)
        phase_C(P, nc, ioc, lambda oc, g_=wpa_b: g_.at(oc // 8, oc % 8), lambda oc, g_=wpb_b: g_.at(oc // 8, oc % 8),
                lambda ob, g_=wout_b: g_.at(ob // 2, ob % 2), with_ctx=not final, final=final, dyn=dyn)
        P.barrier()
        if not final:
            do_gather(hout, xg1o, hCR, hnch, [], [])
            P.barrier()
    P.emit()
    return nc, P


def prep_core(inp, b, j, nlayers=DEPTH):
    r = 4 * b + j
    f32 = np.float32
    xin = np.concatenate([inp["x"][b][NLAT * j:NLAT * (j + 1)], inp["ctx"][b][64 * j:64 * (j + 1)]], axis=0)
    c2 = np.stack([inp["c"][b], inp["c_ctx"]], axis=1)
    cos2, sin2 = rope_tables(NT)
    rr = np.arange(128)
    m = {
        "xin_sh": np.ascontiguousarray(xin), "offs": np.array([[256 + NLAT * j, 64 * j]], np.int32),
        "c2T": np.ascontiguousarray(c2.reshape(32, 128, 2).transpose(1, 0, 2)),
        "identf": np.eye(128, dtype=f32), "onesf": np.ones((128, 128), f32),
        "TL": (rr[:, None] <= rr[None, :]).astype(f32), "TG": (rr[:, None] >= rr[None, :]).astype(f32),
        "SG": (rr[:, None] > rr[None, :]).astype(f32), "SL": (rr[:, None] < rr[None, :]).astype(f32),
        "cos2": cos2, "sin2": sin2, "gfin": inp["g_final"][None, :].astype(f32),
    }
    return m


def _gathered_perm():
    rho = np.arange(4096)
    c, r, ii = rho // 128, (rho % 128) // 32, rho % 32
    pa = r * 1024 + c * 32 + ii
    pb = np.concatenate([r * 2048 + q * 1024 + c * 32 + ii for q in range(2)])
    return pa, pb


PERM_A, PERM_B = _gathered_perm()


def prep_layer_shared(inp, i):
    wada = inp["w_ada"][i]
    out = {
        "wadal": chunked_weight(wada[:, :2 * D], np.arange(2 * D)),
        "wmg": chunked_weight(inp["w_in"][i], np.arange(MGA0, MGA0 + 2 * D)),
        "wadag": chunked_weight(wada[:, 2 * D:], np.arange(D), width=256),
        "wpa": chunked_weight(inp["w_proj_a"][i][PERM_A], np.arange(D)),
        "wpb": chunked_weight(inp["w_proj_b"][i][PERM_B], np.arange(D)),
        "wout": chunked_weight(inp["w_out"][i], np.arange(D), width=512),
    }
    return out


def prep_layer_core(inp, i, b, j, shared, wl_cache):
    r = 4 * b + j
    m = {}
    m["wadal_sh"] = shared["wadal"][16 * j:16 * j + 16]
    m["wmg_sh"] = shared["wmg"][16 * j:16 * j + 16]
    m["wadag_sh"] = shared["wadag"][4 * j:4 * j + 4]
    m["wpa_sh"] = shared["wpa"][8 * j:8 * j + 8]
    m["wpb_sh"] = shared["wpb"][8 * j:8 * j + 8]
    m["wout_sh"] = shared["wout"][2 * j:2 * j + 2]
    if j not in wl_cache:
        wl_cache[j] = chunked_weight(inp["w_in"][i], cols_for_core(j)[:NPT * 128])
    m["wl_sh"] = wl_cache[j]
    m["badaT"] = colvec(inp["b_ada"][i][:2 * D])
    m["gpreT"] = colvec(inp["g_pre"][i])
    m["badag"] = np.ascontiguousarray(inp["b_ada"][i][None, 2 * D:])
    pb = prep_B(inp, i, j, NT)
    for k in ("gqT", "gkvT", "wuq", "wuqsw", "wukv"):
        m[k] = pb[k]
    ps = prep_S(inp, i, j)
    for k in ("convw", "convb", "dtbias2", "alog2", "dskbc", "gssmT"):
        m[k] = ps[k]
    return m


def make_in_maps(inp, nlayers=DEPTH):
    maps = [prep_core(inp, r // 4, r % 4, nlayers) for r in range(8)]
    for i in range(nlayers):
        shared = prep_layer_shared(inp, i)
        wl_cache = {}
        for r in range(8):
            lm = prep_layer_core(inp, i, r // 4, r % 4, shared, wl_cache)
            for k, v in lm.items():
                maps[r]["%s_%d" % (k, i)] = np.ascontiguousarray(v, dtype=np.float32)
    return maps


_CACHE = {}


def kernel(**inputs):
    inp = {k: np.asarray(v) for k, v in inputs.items()}
    if "nc" not in _CACHE:
        _CACHE["nc"] = build_program()[0]
    nc = _CACHE["nc"]
    maps = make_in_maps(inp)
    res = run_bass_kernel_spmd(nc, maps, core_ids=list(range(8)))
    out = np.empty((2, 8192, D), np.float32)
    for r in range(8):
        b, j = r // 4, r % 4
        out[b, NLAT * j:NLAT * (j + 1)] = res.results[r]["hnorm"]
    return out
```

```python
import numpy as np
import concourse.bass as bass
import concourse.mybir as mybir
from concourse.bass_utils import run_bass_kernel_spmd
from contextlib import ExitStack

F32 = mybir.dt.float32
BF16 = mybir.dt.bfloat16
AF = mybir.ActivationFunctionType
ALU = mybir.AluOpType
AX = mybir.AxisListType


class _Op:
    __slots__ = ("eng", "fn", "reads", "writes", "excl", "tag", "ndma", "deps", "ms", "cum")

    def __init__(self, eng, fn, reads, writes, tag=None, ndma=0, excl=()):
        self.eng, self.fn, self.reads, self.writes = eng, fn, reads, writes
        self.excl = tuple(excl)
        self.tag, self.ndma = tag, ndma
        self.deps = None
        self.ms = 0
        self.cum = 0


class Prog:
    ENGS = ("pe", "act", "dve", "pool", "sp")

    def __init__(self, nc):
        self.nc = nc
        self.ops = []
        self.es = ExitStack()
        self.last_w = {}
        self.last_x = {}
        self.readers = {}
        self.tagcnt = {}
        self.bar = {}
        self.last_eng = {}
        self.last_tag = {}
        self.scopes = []
        self.colltags = set()
        self.uid = 0
        self.slotmap = {}
        self.detached = set()

    def sbuf(self, name, shape, dt):
        self.uid += 1
        return self.es.enter_context(self.nc.sbuf_tensor("sb%d_%s" % (self.uid, name), list(shape), dt))

    def psum(self, name, shape, dt=F32):
        self.uid += 1
        return self.es.enter_context(self.nc.psum_tensor("ps%d_%s" % (self.uid, name), list(shape), dt))

    def push(self):
        self.scopes.append(self.es)
        self.es = ExitStack()

    def pop(self):
        self.barrier()
        self.es.close()
        self.es = self.scopes.pop()

    def barrier(self):
        snap = set(self.last_eng.values()) | set(v for k, v in self.last_tag.items() if k not in self.detached)
        for e in self.ENGS:
            self.bar[e] = set(snap) | self.bar.get(e, set())
        self.slotmap = {}

    def _record(self, op):
        deps = set()
        for k in op.reads:
            d = self.last_w.get(k)
            if d is not None:
                deps.add(d)
        for k in op.writes:
            d = self.last_w.get(k)
            if d is not None:
                deps.add(d)
            deps.update(self.readers.get(k, ()))
        idx = len(self.ops)
        b = self.bar.pop(op.eng, None)
        if b:
            deps.update(b)
        if op.tag is None:
            self.last_eng[op.eng] = idx
        else:
            self.last_tag[op.tag] = idx
        for k in op.excl:
            d = self.last_x.get(k)
            if d is not None and self.ops[d].eng != op.eng:
                deps.add(d)
            self.last_x[k] = idx
        for k in op.writes:
            self.last_w[k] = idx
            self.readers[k] = []
        for k in op.reads:
            if k in op.writes:
                continue
            self.readers.setdefault(k, []).append(idx)
        op.deps = deps
        self.ops.append(op)
        return idx

    def op(self, eng, fn, reads=(), writes=(), excl=()):
        return self._record(_Op(eng, fn, tuple(reads), tuple(writes), excl=excl))

    def dma(self, eng, tag, fn, reads=(), writes=(), n=1):
        slot = self.slotmap.get(tag)
        if slot is None:
            slot = len(self.slotmap)
            self.slotmap[tag] = slot
        slot = "s%d" % slot
        tk = ("__slot__", slot)
        o = _Op(eng, fn, tuple(reads), tuple(writes) + (tk,), tag=slot, ndma=n)
        self.tagcnt[slot] = self.tagcnt.get(slot, 0) + n
        o.cum = self.tagcnt[slot] * 16
        return self._record(o)

    def detach(self, slot):
        self.detached.add(slot)

    def join(self, slot):
        self.detached.discard(slot)
        d = self.last_tag.get(slot)
        if d is not None:
            for e in self.ENGS:
                self.bar[e] = self.bar.get(e, set()) | {d}

    def coll(self, fn, n, reads=(), writes=(), slot="cc"):
        self.colltags.add(slot)
        tk = ("__slot__", slot)
        o = _Op("pool", fn, tuple(reads), tuple(writes) + (tk,), tag=slot, ndma=n)
        self.tagcnt[slot] = self.tagcnt.get(slot, 0) + n
        o.cum = self.tagcnt[slot]
        return self._record(o)

    def emit(self):
        nc = self.nc
        ops = self.ops
        need = set()
        for o in ops:
            best = {}
            dmb = {}
            for d in o.deps:
                p = ops[d]
                if p.tag is not None:
                    if p.tag not in dmb or dmb[p.tag] < d:
                        dmb[p.tag] = d
                    continue
                if p.eng == "pe" and o.eng == "pe" and o.tag is None:
                    continue
                if p.eng not in best or best[p.eng] < d:
                    best[p.eng] = d
            o.deps = list(best.values()) + list(dmb.values())
            for d in best.values():
                need.add(d)
        CAP = 8000
        cnt = {e: 0 for e in self.ENGS}
        for i, o in enumerate(ops):
            if o.tag is None and i in need:
                cnt[o.eng] += 1
                o.ms = cnt[o.eng]
        sems = {}
        for e in self.ENGS:
            for g in range(cnt[e] // CAP + 1):
                sems[e, g] = self.es.enter_context(nc.semaphore("s_%s_%d" % (e, g)))
        tsems = {}
        DCAP = CAP // 16
        for t, c in self.tagcnt.items():
            if t in self.colltags:
                assert c < CAP
                tsems[t, 0] = self.es.enter_context(nc.semaphore("d_%s" % (str(t),)))
            else:
                for g in range(c // DCAP + 1):
                    tsems[t, g] = self.es.enter_context(nc.semaphore("d_%s_%d" % (str(t), g)))
        self.stats_nsem = len(sems) + len(tsems)

        def esem(p):
            g = (p.ms - 1) // CAP
            return sems[p.eng, g], p.ms - g * CAP, ("e", p.eng, g)

        def dsem(p):
            if p.tag in self.colltags:
                return tsems[p.tag, 0], p.cum, ("t", p.tag, 0)
            k = p.cum // 16
            g = (k - 1) // DCAP
            return tsems[p.tag, g], (k - g * DCAP) * 16, ("t", p.tag, g)

        def dsem_prev(p):
            if p.tag in self.colltags:
                return None
            k = p.cum // 16
            g = (k - 1) // DCAP
            if g > 0 and (k - p.ndma) < g * DCAP:
                return tsems[p.tag, g - 1], DCAP * 16, ("t", p.tag, g - 1)
            return None
        per = {e: [] for e in self.ENGS}
        for o in ops:
            per[o.eng].append(o)
        self.stats = {e: len(per[e]) for e in self.ENGS}
        self.stats["milestones"] = dict(cnt)
        self.stats["tags"] = len(tsems)

        def run(e, eng):
            seen = {}
            nw = 0
            for o in per[e]:
                for d in o.deps:
                    p = ops[d]
                    if p.tag is not None:
                        pv = dsem_prev(p)
                        if pv is not None and seen.get(pv[2], 0) < pv[1]:
                            seen[pv[2]] = pv[1]
                            eng.wait_ge(pv[0], pv[1])
                            nw += 1
                        s, v, key = dsem(p)
                    else:
                        s, v, key = esem(p)
                    if seen.get(key, 0) >= v:
                        continue
                    seen[key] = v
                    eng.wait_ge(s, v)
                    nw += 1
                r = o.fn(eng)
                if o.tag is not None:
                    assert len(r) == o.ndma, (len(r), o.ndma, o.tag)
                    if o.tag in self.colltags:
                        for ins in r:
                            ins.then_inc(tsems[o.tag, 0], 1)
                    else:
                        k0 = o.cum // 16 - o.ndma
                        for q, ins in enumerate(r):
                            ins.then_inc(tsems[o.tag, (k0 + q) // DCAP], 16)
                elif o.ms:
                    r.then_inc(esem(o)[0], 1)
            if e == "sp":
                for t, c in self.tagcnt.items():
                    if t in self.colltags:
                        if seen.get(("t", t, 0), 0) < c:
                            eng.wait_ge(tsems[t, 0], c)
                        continue
                    for g in range(c // DCAP + 1):
                        v = min(c - g * DCAP, DCAP) * 16
                        if v > 0 and seen.get(("t", t, g), 0) < v:
                            eng.wait_ge(tsems[t, g], v)
            self.stats["waits_" + e] = nw

        with nc.Block() as block:
            @block.sync
            def _(sync):
                run("sp", sync)

            @block.tensor
            def _(tensor):
                run("pe", tensor)

            @block.vector
            def _(vector):
                run("dve", vector)

            @block.scalar
            def _(scalar):
                run("act", scalar)

            @block.gpsimd
            def _(gpsimd):
                run("pool", gpsimd)
        self.es.close()


def dma_pieces(e, dst, pieces, pat=None, **kw):
    out = []
    for (ap, p0, r) in pieces:
        src = ap.rearrange(pat, **kw) if pat else ap
        out.append(e.dma_start(out=dst[p0:p0 + r], in_=src))
    return out


import numpy as np

D = 4096
Q0 = 0; KV0 = 768; KPE0 = 1280; GA0 = 1344; Z0 = 5440; XS0 = 13632; B0 = 21824; C0 = 22848
DTF0 = 23872; DTB0 = 24000; MGA0 = 24128; MGB0 = 28224


def cols_for_core(j):
    r = np.arange
    kpe = np.concatenate([r(KPE0, KPE0 + 64), r(KPE0 + 32, KPE0 + 64), r(KPE0, KPE0 + 32)])
    dt = np.concatenate([r(DTF0 + 32 * j, DTF0 + 32 * j + 32), r(DTB0 + 32 * j, DTB0 + 32 * j + 32), -np.ones(64, np.int64)])
    cols = np.concatenate([
        r(Q0, Q0 + 768), r(KV0, KV0 + 512), kpe,
        r(GA0 + 1024 * j, GA0 + 1024 * (j + 1)),
        r(Z0 + 2048 * j, Z0 + 2048 * (j + 1)),
        r(XS0 + 2048 * j, XS0 + 2048 * (j + 1)),
        r(B0 + 256 * j, B0 + 256 * (j + 1)),
        r(C0 + 256 * j, C0 + 256 * (j + 1)),
        dt,
        r(MGA0 + 1024 * j, MGA0 + 1024 * (j + 1)),
        r(MGB0 + 1024 * j, MGB0 + 1024 * (j + 1)),
    ])
    assert cols.shape[0] == 72 * 128
    return cols


def chunked_weight(w, cols, width=128):
    K = w.shape[0]
    sel = w[:, np.maximum(cols, 0)]
    if (cols < 0).any():
        sel[:, cols < 0] = 0.0
    nchunk = cols.shape[0] // width
    a = sel.reshape(K // 128, 128, nchunk, width).transpose(2, 1, 0, 3)
    return np.ascontiguousarray(a).reshape(nchunk, 128, (K // 128) * width)


def colvec(v):
    return np.ascontiguousarray(v.reshape(-1, 128).T)


def prep_A(inp, i, b, j):
    c2 = np.stack([inp["c"][b], inp["c_ctx"]], axis=1)
    c2T = np.ascontiguousarray(c2.reshape(32, 128, 2).transpose(1, 0, 2))
    wada = inp["w_ada"][i][:, :2 * D]
    return {
        "c2T": c2T,
        "wadal": chunked_weight(wada, np.arange(2 * D)),
        "badaT": colvec(inp["b_ada"][i][:2 * D]),
        "gpreT": colvec(inp["g_pre"][i]),
        "identf": np.eye(128, dtype=np.float32),
        "wl": chunked_weight(inp["w_in"][i], cols_for_core(j)),
    }


def rope_tables(nt, nctx=256):
    n = nt - nctx
    f32 = np.float32
    rows = (np.arange(n) // 64).astype(f32)
    cols = (np.arange(n) % 64).astype(f32)
    inv = (f32(10000.0) ** (-np.arange(16, dtype=f32) / f32(16))).astype(f32)
    ang = np.concatenate([rows[:, None] * inv, cols[:, None] * inv], axis=-1).astype(f32)
    c = np.cos(ang).astype(f32).T
    s = np.sin(ang).astype(f32).T
    cos2 = np.ones((64, nt), f32)
    sin2 = np.zeros((64, nt), f32)
    cos2[0:32, nctx:] = c
    cos2[32:64, nctx:] = c
    sin2[0:32, nctx:] = -s
    sin2[32:64, nctx:] = s
    return cos2, sin2


def prep_B(inp, i, j, nt):
    hq = np.arange(8 * j * 192, (8 * j + 8) * 192)
    sw = np.concatenate([np.concatenate([np.arange(h * 192 + 160, h * 192 + 192), np.arange(h * 192 + 128, h * 192 + 160)])
                         for h in range(8 * j, 8 * j + 8)])
    hkv = np.arange(8 * j * 256, (8 * j + 8) * 256)
    cos2, sin2 = rope_tables(nt)
    return {
        "gqT": colvec(inp["g_q"][i]), "gkvT": colvec(inp["g_kv"][i]),
        "wuq": chunked_weight(inp["w_uq"][i], hq, width=1536)[0],
        "wuqsw": chunked_weight(inp["w_uq"][i], sw, width=512)[0],
        "wukv": chunked_weight(inp["w_ukv"][i], hkv, width=2048)[0],
        "cos2": cos2, "sin2": sin2, "onesf": np.ones((128, 128), np.float32),
    }


def prep_S(inp, i, j):
    ch = np.concatenate([np.arange(2048 * j, 2048 * (j + 1)), 8192 + np.arange(256 * j, 256 * (j + 1)),
                         9216 + np.arange(256 * j, 256 * (j + 1))])
    cw = inp["conv_w"][i][:, ch]
    convw = np.ascontiguousarray(cw.reshape(5, 20, 128).transpose(2, 1, 0))
    convb = colvec(inp["conv_b"][i][ch])
    hs = slice(32 * j, 32 * (j + 1))
    dtb = np.concatenate([inp["dt_bias_f"][i][hs], inp["dt_bias_b"][i][hs]])
    al = np.concatenate([inp["a_log_f"][i][hs], inp["a_log_b"][i][hs]])
    r = np.arange(128)
    f32 = np.float32
    return {
        "convw": convw, "convb": convb,
        "dtbias2": np.concatenate([dtb, dtb])[:, None].astype(f32),
        "alog2": np.concatenate([np.zeros(64, f32), al])[:, None].astype(f32),
        "TL": (r[:, None] <= r[None, :]).astype(f32), "TG": (r[:, None] >= r[None, :]).astype(f32),
        "SG": (r[:, None] > r[None, :]).astype(f32), "SL": (r[:, None] < r[None, :]).astype(f32),
        "gssmT": colvec(inp["g_ssm"][i][2048 * j:2048 * (j + 1)]),
        "dskbc": np.ascontiguousarray(np.broadcast_to(np.repeat(inp["d_skip"][i][hs], 64)[None, :], (128, 2048))).astype(f32),
    }


D = 4096
KC = D // 128
NCTX = 256
EPS = 1e-6

SEG = [("cq", 6, "id"), ("ckv", 4, "id"), ("kpe", 1, "id"), ("ga", 8, "silu"), ("z", 16, "silu"),
       ("xs", 16, "id"), ("B", 2, "id"), ("C", 2, "id"), ("dt", 1, "id"), ("mga", 8, "sig"), ("mgb", 8, "sig")]
SEG_OFF = {}
_o = 0
for _n, _c, _f in SEG:
    SEG_OFF[_n] = _o
    _o += _c
NCH = _o
NPT = SEG_OFF["mga"]


def token_blocks(nt):
    blks = [(0, NCTX)]
    t = NCTX
    while t < nt:
        blks.append((t, min(512, nt - t)))
        t += 512
    return blks


def phase_A(P, nc, io, blocks, chunks, wada_fn):
    P.push()
    identf = P.sbuf("identf", [128, 128], F32)
    P.dma("sp", "identf", lambda e: [e.dma_start(out=identf[:], in_=io["identf"])], writes=["identf"])
    c2 = P.sbuf("c2", [128, KC, 2], F32)
    sc = P.sbuf("sc", [128, KC, 2], F32)
    bada = P.sbuf("bada", [128, 64], F32)
    gpre = P.sbuf("gpre", [128, KC], F32)
    modT = P.sbuf("modT", [128, 64, 2], F32)
    gmul = P.sbuf("gmul", [128, KC, 2], F32)
    P.dma("sp", "c2", lambda e: [e.dma_start(out=c2[:], in_=io["c2T"])], writes=["c2"])
    P.dma("sp", "bada", lambda e: [e.dma_start(out=bada[:], in_=io["badaT"])], writes=["bada"])
    P.dma("sp", "gpre", lambda e: [e.dma_start(out=gpre[:], in_=io["gpreT"])], writes=["gpre"])
    P.op("act", lambda e: e.activation(out=sc[:], in_=c2[:], func=AF.Silu), reads=["c2"], writes=["sc"])
    P.push()
    wad = [P.sbuf("wad%d" % i, [128, KC, 128], F32) for i in range(2)]
    pm = P.psum("pm", [128, 512], F32)
    for oc in range(64):
        wb = wad[oc % 2]
        wp_ = wada_fn(oc)
        P.dma("sp", "wad%d" % (oc % 2), lambda e, wb=wb, wp_=wp_: dma_pieces(e, wb, wp_, "p (k n) -> p k n", k=KC),
              writes=["wad%d" % (oc % 2)], n=len(wp_))
        for k in range(KC):
            P.op("pe", lambda e, wb=wb, k=k, oc=oc: e.matmul(pm[:, 2 * (oc % 8):2 * (oc % 8) + 2], wb[:, k, :], sc[:, k, :],
                                                             start=(k == 0), stop=(k == KC - 1)),
                 reads=["wad%d" % (oc % 2), "sc"], excl=["b_pm"])
        P.op("dve", lambda e, oc=oc: e.tensor_scalar(out=modT[:, oc, :], in0=pm[:, 2 * (oc % 8):2 * (oc % 8) + 2],
                                                     scalar1=bada[:, oc:oc + 1], scalar2=None, op0=ALU.add),
             reads=["bada"], writes=["modT"], excl=["b_pm"])
    P.pop()
    for n in range(2):
        P.op("dve", lambda e, n=n: e.scalar_tensor_tensor(out=gmul[:, :, n], in0=modT[:, 32:64, n], scalar=1.0, in1=gpre[:],
                                                          op0=ALU.add, op1=ALU.mult),
             reads=["modT", "gpre"], writes=["gmul"])

    xt = [P.sbuf("xt%d" % i, [128, D], F32) for i in range(2)]
    xn = [P.sbuf("xn%d" % i, [128, D], F32) for i in range(2)]
    junk = P.sbuf("junk", [128, D], BF16)
    st = [P.sbuf("st%d" % i, [128, 4], F32) for i in range(2)]
    uT = [P.sbuf("uT%d" % i, [128, KC, 512], BF16) for i in range(2)]
    wbuf = [P.sbuf("wbuf%d" % i, [128, KC, 128], BF16) for i in range(3)]
    stage = [P.sbuf("stage%d" % i, [128, 512], F32) for i in range(3)]
    ptr = [P.psum("ptr%d" % i, [128, 512], F32) for i in range(2)]
    pmm = [P.psum("pmm%d" % i, [128, 512], F32) for i in range(2)]
    ti = 0
    wi = 0
    si = 0
    for bi, blk in enumerate(blocks):
        n = blk["n"]
        mcol = blk["mcol"]
        ub = uT[bi % 2]
        ukey = "uT%d" % (bi % 2)
        ukeys = []
        c0 = 0
        for tt, pieces in enumerate(blk["tiles"]):
            rows = sum(p[2] for p in pieces)
            x_ = xt[ti % 2]
            xn_ = xn[ti % 2]
            st_ = st[ti % 2]
            xk, xnk, stk = "xt%d" % (ti % 2), "xn%d" % (ti % 2), "st%d" % (ti % 2)
            P.dma("sp", xk, lambda e, x_=x_, pieces=pieces: [e.dma_start(out=x_[p0:p0 + r, :], in_=src) for (src, p0, r) in pieces],
                  writes=[xk], n=len(pieces))
            P.op("act", lambda e, x_=x_, st_=st_, rows=rows: e.activation(out=junk[0:rows, :], in_=x_[0:rows, :], func=AF.Square, accum_out=st_[0:rows, 0:1]),
                 reads=[xk], writes=["junk", stk])
            P.op("dve", lambda e, st_=st_, rows=rows: e.tensor_scalar(out=st_[0:rows, 1:2], in0=st_[0:rows, 0:1], scalar1=1.0 / D, scalar2=EPS,
                                                                      op0=ALU.mult, op1=ALU.add), reads=[stk], writes=[stk + "b"])
            P.op("act", lambda e, st_=st_, rows=rows: e.activation(out=st_[0:rows, 3:4], in_=st_[0:rows, 1:2], func=AF.Sqrt), reads=[stk + "b"], writes=[stk + "d"])
            P.op("dve", lambda e, st_=st_, rows=rows: e.reciprocal(out=st_[0:rows, 2:3], in_=st_[0:rows, 3:4]), reads=[stk + "d"], writes=[stk + "c"])
            P.op("dve", lambda e, x_=x_, xn_=xn_, st_=st_, rows=rows: e.tensor_scalar(out=xn_[0:rows, :], in0=x_[0:rows, :], scalar1=st_[0:rows, 2:3], scalar2=None,
                                                                                      op0=ALU.mult), reads=[xk, stk + "c"], writes=[xnk])
            for q in range(KC // 4):
                pb = ptr[q % 2]
                pk = "b_ptr%d" % (q % 2)
                for c4 in range(4):
                    c = q * 4 + c4
                    P.op("pe", lambda e, pb=pb, c4=c4, c=c, xn_=xn_, rows=rows: e.transpose(pb[:, c4 * 128:c4 * 128 + rows], xn_[0:rows, c * 128:(c + 1) * 128], identf[0:rows, 0:rows]),
                         reads=[xnk, "identf"], excl=[pk])
                for c4 in range(4):
                    c = q * 4 + c4
                    P.op("act", lambda e, pb=pb, c4=c4, c=c, ub=ub, c0=c0, mcol=mcol, rows=rows: e.activation(
                        out=ub[:, c, c0:c0 + rows], in_=pb[:, c4 * 128:c4 * 128 + rows], func=AF.Identity,
                        scale=gmul[:, c, mcol:mcol + 1], bias=modT[:, c, mcol:mcol + 1]),
                        reads=["gmul", "modT"], writes=[(ukey, tt)], excl=[pk])
            ukeys.append((ukey, tt))
            c0 += rows
            ti += 1
        assert c0 == n
        for cc, (w_ap, fn, dst_fn) in enumerate(chunks):
            wb = wbuf[wi % 3]
            wk = "wbuf%d" % (wi % 3)
            P.dma("pool", wk, lambda e, wb=wb, w_ap=w_ap: dma_pieces(e, wb, w_ap, "p (k n) -> p k n", k=KC),
                  writes=[wk], n=len(w_ap))
            pb = pmm[wi % 2]
            pk = "b_pmm%d" % (wi % 2)
            for k in range(KC):
                P.op("pe", lambda e, pb=pb, wb=wb, k=k, ub=ub, n=n: e.matmul(pb[:, 0:n], wb[:, k, :], ub[:, k, 0:n],
                                                                             start=(k == 0), stop=(k == KC - 1)),
                     reads=[wk] + ukeys, excl=[pk])
            sg = stage[si % 3]
            sk = "stage%d" % (si % 3)
            P.op("act", lambda e, sg=sg, pb=pb, n=n, fn=fn: e.activation(out=sg[:, 0:n], in_=pb[:, 0:n], func=fn),
                 writes=[sk], excl=[pk])
            dst = dst_fn(bi)
            P.dma("sp", sk, lambda e, sg=sg, dst=dst, n=n: [e.dma_start(out=dst, in_=sg[:, 0:n])], reads=[sk])
            wi += 1
            si += 1
    P.pop()


import math

ATTN_SCALE = 1.0 / math.sqrt(192.0)
NH = 8


def phase_B1(P, nc, io, nt):
    P.push()
    pT = io["pT"]
    wuq = P.sbuf("wuq", [128, 6, NH, 192], BF16)
    wuqsw = P.sbuf("wuqsw", [128, 6, NH, 64], BF16)
    wukv = P.sbuf("wukv", [128, 4, NH, 256], BF16)
    gq = P.sbuf("gq", [128, 6], F32)
    gkv = P.sbuf("gkv", [128, 4], F32)
    onesf = P.sbuf("onesf", [128, 128], F32)
    P.dma("pool", "wuq", lambda e: [e.dma_start(out=wuq[:], in_=io["wuq"].rearrange("p (k h d) -> p k h d", k=6, h=NH))], writes=["wuq"])
    P.dma("pool", "wuqsw", lambda e: [e.dma_start(out=wuqsw[:], in_=io["wuqsw"].rearrange("p (k h d) -> p k h d", k=6, h=NH))], writes=["wuqsw"])
    P.dma("pool", "wukv", lambda e: [e.dma_start(out=wukv[:], in_=io["wukv"].rearrange("p (k h d) -> p k h d", k=4, h=NH))], writes=["wukv"])
    P.dma("sp", "gq", lambda e: [e.dma_start(out=gq[:], in_=io["gqT"])], writes=["gq"])
    P.dma("sp", "gkv", lambda e: [e.dma_start(out=gkv[:], in_=io["gkvT"])], writes=["gkv"])
    P.dma("sp", "onesf", lambda e: [e.dma_start(out=onesf[:], in_=io["onesf"])], writes=["onesf"])

    cq = P.sbuf("cq", [128, 6, 512], F32)
    ckv = P.sbuf("ckv", [128, 4, 512], F32)
    kpe2 = P.sbuf("kpe2", [64, 2, 512], F32)
    cos = P.sbuf("cos", [64, 512], F32)
    sin = P.sbuf("sin", [64, 512], F32)
    sq = P.sbuf("sq", [128, 6, 512], F32)
    rr = P.sbuf("rr", [128, 3, 512], F32)
    cqn = P.sbuf("cqn", [128, 6, 512], BF16)
    ckvn = P.sbuf("ckvn", [128, 4, 512], BF16)
    t1 = [P.sbuf("t1_%d" % i, [64, 512], F32) for i in range(2)]
    t2 = [P.sbuf("t2_%d" % i, [64, 512], F32) for i in range(2)]
    stg = [P.sbuf("stg%d" % i, [128, 512], BF16) for i in range(4)]
    pss = P.psum("pss", [128, 512], F32)
    pmm = [P.psum("pbm%d" % i, [128, 512], F32) for i in range(2)]
    ppe = P.psum("ppe", [128, 512], F32)
    psw = P.psum("psw", [128, 512], F32)
    cnt = {"mm": 0, "stg": 0, "t": 0}

    def nxt_bank():
        i = cnt["mm"] % 2
        cnt["mm"] += 1
        return pmm[i], "b_pbm%d" % i

    def nxt_stg():
        i = cnt["stg"] % 4
        cnt["stg"] += 1
        return stg[i], "stg%d" % i

    def rmsnorm(src, skey, nk, g, dst, dkey, n, dim):
        P.op("act", lambda e: e.activation(out=sq[:, 0:nk, 0:n], in_=src[:, 0:nk, 0:n], func=AF.Square), reads=[skey], writes=["sq"])
        for k in range(nk):
            P.op("pe", lambda e, k=k: e.matmul(pss[:, 0:n], onesf[:], sq[:, k, 0:n], start=(k == 0), stop=(k == nk - 1)),
                 reads=["sq", "onesf"], excl=["b_pss"])
        P.op("dve", lambda e: e.tensor_scalar(out=rr[:, 0, 0:n], in0=pss[:, 0:n], scalar1=1.0 / dim, scalar2=EPS, op0=ALU.mult, op1=ALU.add),
             writes=["rr0"], excl=["b_pss"])
        P.op("act", lambda e: e.activation(out=rr[:, 1, 0:n], in_=rr[:, 0, 0:n], func=AF.Sqrt), reads=["rr0"], writes=["rr1"])
        P.op("dve", lambda e: e.reciprocal(out=rr[:, 2, 0:n], in_=rr[:, 1, 0:n]), reads=["rr1"], writes=["rr2"])
        for k in range(nk):
            P.op("dve", lambda e, k=k: e.scalar_tensor_tensor(out=dst[:, k, 0:n], in0=src[:, k, 0:n], scalar=g[:, k:k + 1], in1=rr[:, 2, 0:n],
                                                              op0=ALU.mult, op1=ALU.mult),
                 reads=[skey, "rr2", "gq", "gkv"], writes=[dkey])

    def rope(pe_ap, sw_ap, n, excl, rd, dst, dkey):
        i = cnt["t"] % 2
        cnt["t"] += 1
        a, b = t1[i], t2[i]
        P.op("dve", lambda e: e.tensor_tensor(out=a[:, 0:n], in0=pe_ap, in1=cos[:, 0:n], op=ALU.mult), reads=["cos"] + rd, writes=["t1_%d" % i], excl=excl[0:1])
        P.op("dve", lambda e: e.tensor_tensor(out=b[:, 0:n], in0=sw_ap, in1=sin[:, 0:n], op=ALU.mult), reads=["sin"] + rd, writes=["t2_%d" % i], excl=excl[1:2])
        P.op("pool", lambda e: e.tensor_tensor(out=dst, in0=a[:, 0:n], in1=b[:, 0:n], op=ALU.add), reads=["t1_%d" % i, "t2_%d" % i], writes=[dkey])

    def do_block(t0, n):
        P.dma("sp", "cq", lambda e, t0=t0, n=n: [e.dma_start(out=cq[:, :, 0:n], in_=pT[0:768, t0:t0 + n].rearrange("(k p) t -> p k t", p=128))], writes=["cq"])
        P.dma("sp", "ckv", lambda e, t0=t0, n=n: [e.dma_start(out=ckv[:, :, 0:n], in_=pT[768:1280, t0:t0 + n].rearrange("(k p) t -> p k t", p=128))], writes=["ckv"])
        P.dma("sp", "kpe2", lambda e, t0=t0, n=n: [e.dma_start(out=kpe2[:, :, 0:n], in_=pT[1280:1408, t0:t0 + n].rearrange("(a p) t -> p a t", p=64))], writes=["kpe2"])
        P.dma("sp", "cos", lambda e, t0=t0, n=n: [e.dma_start(out=cos[:, 0:n], in_=io["cos2"][:, t0:t0 + n])], writes=["cos"])
        P.dma("sp", "sin", lambda e, t0=t0, n=n: [e.dma_start(out=sin[:, 0:n], in_=io["sin2"][:, t0:t0 + n])], writes=["sin"])
        rmsnorm(cq, "cq", 6, gq, cqn, "cqn", n, 768.0)
        rmsnorm(ckv, "ckv", 4, gkv, ckvn, "ckvn", n, 512.0)
        sg, sk = nxt_stg()
        rope(kpe2[:, 0, 0:n], kpe2[:, 1, 0:n], n, [], ["kpe2"], sg[0:64, 0:n], sk)
        P.dma("sp", sk, lambda e, sg=sg, t0=t0, n=n: [e.dma_start(out=io["kpeT"][:, t0:t0 + n], in_=sg[0:64, 0:n])], reads=[sk])
        for h in range(NH):
            pb, pk = nxt_bank()
            for k in range(6):
                P.op("pe", lambda e, pb=pb, k=k, h=h: e.matmul(pb[:, 0:n], wuq[:, k, h, 0:128], cqn[:, k, 0:n], start=(k == 0), stop=(k == 5)),
                     reads=["wuq", "cqn"], excl=[pk])
            sg, sk = nxt_stg()
            P.op("act", lambda e, sg=sg, pb=pb: e.activation(out=sg[:, 0:n], in_=pb[:, 0:n], func=AF.Identity), writes=[sk], excl=[pk])
            P.dma("sp", sk, lambda e, sg=sg, h=h, t0=t0, n=n: [e.dma_start(out=io["qT"][h, 0:128, t0:t0 + n], in_=sg[:, 0:n])], reads=[sk])
            for k in range(6):
                P.op("pe", lambda e, k=k, h=h: e.matmul(ppe[0:64, 0:n], wuq[:, k, h, 128:192], cqn[:, k, 0:n], start=(k == 0), stop=(k == 5)),
                     reads=["wuq", "cqn"], excl=["b_ppe"])
            for k in range(6):
                P.op("pe", lambda e, k=k, h=h: e.matmul(psw[0:64, 0:n], wuqsw[:, k, h, :], cqn[:, k, 0:n], start=(k == 0), stop=(k == 5)),
                     reads=["wuqsw", "cqn"], excl=["b_psw"])
            sg, sk = nxt_stg()
            rope(ppe[0:64, 0:n], psw[0:64, 0:n], n, ["b_ppe", "b_psw"], [], sg[0:64, 0:n], sk)
            P.dma("sp", sk, lambda e, sg=sg, h=h, t0=t0, n=n: [e.dma_start(out=io["qT"][h, 128:192, t0:t0 + n], in_=sg[0:64, 0:n])], reads=[sk])
            pb, pk = nxt_bank()
            for k in range(4):
                P.op("pe", lambda e, pb=pb, k=k, h=h: e.matmul(pb[:, 0:n], wukv[:, k, h, 0:128], ckvn[:, k, 0:n], start=(k == 0), stop=(k == 3)),
                     reads=["wukv", "ckvn"], excl=[pk])
            sg, sk = nxt_stg()
            P.op("act", lambda e, sg=sg, pb=pb: e.activation(out=sg[:, 0:n], in_=pb[:, 0:n], func=AF.Identity), writes=[sk], excl=[pk])
            P.dma("sp", sk, lambda e, sg=sg, h=h, t0=t0, n=n: [e.dma_start(out=io["kT"][h, :, t0:t0 + n], in_=sg[:, 0:n])], reads=[sk])
        for tt in range(n // 128):
            tile = (t0 + tt * 128) // 128
            for half in range(2):
                pb, pk = nxt_bank()
                for k in range(4):
                    P.op("pe", lambda e, pb=pb, k=k, tt=tt, half=half: e.matmul(
                        pb[:, :].rearrange("p (h d) -> p h d", h=4), ckvn[:, k, tt * 128:(tt + 1) * 128], wukv[:, k, 4 * half:4 * half + 4, 128:256],
                        start=(k == 0), stop=(k == 3)), reads=["wukv", "ckvn"], excl=[pk])
                sg, sk = nxt_stg()
                P.op("act", lambda e, sg=sg, pb=pb: e.activation(out=sg[:, :], in_=pb[:, :], func=AF.Identity), writes=[sk], excl=[pk])
                P.dma("sp", sk, lambda e, sg=sg, half=half, tile=tile: [e.dma_start(
                    out=io["vs"][4 * half:4 * half + 4, :, tile, :].rearrange("h p d -> p h d"),
                    in_=sg[:, :].rearrange("p (h d) -> p h d", h=4))], reads=[sk])

    for (t0, n) in token_blocks(nt):
        do_block(t0, n)
    P.pop()


def phase_B2(P, nc, io, nt, heads=range(NH)):
    P.push()
    nkt = nt // 128
    KT = P.sbuf("KT", [128, nt], BF16)
    KP = P.sbuf("KP", [64, nt], BF16)
    V = P.sbuf("V", [128, nkt, 128], BF16)
    QN = P.sbuf("QN", [128, nt], BF16)
    QP = P.sbuf("QP", [64, nt], BF16)
    onesb = P.sbuf("onesb", [128, 128], BF16)
    pbuf = [P.sbuf("pbuf%d" % i, [128, 512], BF16) for i in range(4)]
    sga = [P.sbuf("sga%d" % i, [128, 512], F32) for i in range(2)]
    rec = [P.sbuf("rec%d" % i, [128, 512], F32) for i in range(2)]
    ot = [P.sbuf("ot%d" % i, [128, 512], F32) for i in range(2)]
    ogs = [P.sbuf("ogs%d" % i, [128, 512], BF16) for i in range(2)]
    S = [P.psum("S%d" % i, [128, 512], F32) for i in range(4)]
    O = [P.psum("O%d" % i, [128, 512], F32) for i in range(2)]
    Dn = [P.psum("Dn%d" % i, [128, 512], F32) for i in range(2)]
    P.op("pool", lambda e: e.memset(onesb[:], 1.0), writes=["onesb"])
    P.dma("sp", "KP", lambda e: [e.dma_start(out=KP[:], in_=io["kpeT"])], writes=["KP"])
    ga0 = SEG_OFF["ga"] * 128
    si = 0
    qb = 0
    for h in heads:
        P.dma("sp", "KT", lambda e, h=h: [e.dma_start(out=KT[:], in_=io["kT"][h])], writes=["KT"])
        P.dma("sp", "V", lambda e, h=h: [e.dma_start(out=V[:], in_=io["vs"][h])], writes=["V"])
        P.dma("sp", "QN", lambda e, h=h: [e.dma_start(out=QN[:], in_=io["qT"][h, 0:128, :])], writes=["QN"])
        P.dma("sp", "QP", lambda e, h=h: [e.dma_start(out=QP[:], in_=io["qT"][h, 128:192, :])], writes=["QP"])
        def do_qblock(h, t0, n, qb, si):
            tiles = list(range(NCTX // 128)) if t0 < NCTX else list(range(nkt))
            o_, ok = O[qb % 2], "b_O%d" % (qb % 2)
            d_, dk = Dn[qb % 2], "b_Dn%d" % (qb % 2)
            sg_, sgk = sga[qb % 2], "sga%d" % (qb % 2)
            P.dma("sp", sgk, lambda e, sg_=sg_, h=h, t0=t0, n=n: [e.dma_start(out=sg_[:, 0:n], in_=io["pT"][ga0 + h * 128:ga0 + (h + 1) * 128, t0:t0 + n])],
                  writes=[sgk])

            def qk(kt, slot):
                s_, sk = S[slot % 4], "b_S%d" % (slot % 4)
                P.op("pe", lambda e: e.matmul(s_[:, 0:n], KT[:, kt * 128:(kt + 1) * 128], QN[:, t0:t0 + n], start=True, stop=False),
                     reads=["KT", "QN"], excl=[sk])
                P.op("pe", lambda e: e.matmul(s_[:, 0:n], KP[:, kt * 128:(kt + 1) * 128], QP[:, t0:t0 + n], start=False, stop=True),
                     reads=["KP", "QP"], excl=[sk])
                pb, pk = pbuf[slot % 4], "pbuf%d" % (slot % 4)
                P.op("act", lambda e: e.activation(out=pb[:, 0:n], in_=s_[:, 0:n], func=AF.Exp, scale=ATTN_SCALE), writes=[pk], excl=[sk])

            def pv(kt, slot, first, last):
                pb, pk = pbuf[slot % 4], "pbuf%d" % (slot % 4)
                P.op("pe", lambda e: e.matmul(o_[:, 0:n], V[:, kt, :], pb[:, 0:n], start=first, stop=last), reads=["V", pk], excl=[ok])
                P.op("pe", lambda e: e.matmul(d_[:, 0:n], onesb[:], pb[:, 0:n], start=first, stop=last), reads=["onesb", pk], excl=[dk])

            qk(tiles[0], si)
            if len(tiles) > 1:
                qk(tiles[1], si + 1)
            for idx, kt in enumerate(tiles):
                if idx + 2 < len(tiles):
                    qk(tiles[idx + 2], si + idx + 2)
                pv(kt, si + idx, idx == 0, idx == len(tiles) - 1)
            r_, rk = rec[qb % 2], "rec%d" % (qb % 2)
            t_, tk = ot[qb % 2], "ot%d" % (qb % 2)
            g_, gk = ogs[qb % 2], "ogs%d" % (qb % 2)
            P.op("dve", lambda e, r_=r_, d_=d_, n=n: e.reciprocal(out=r_[:, 0:n], in_=d_[:, 0:n]), writes=[rk], excl=[dk])
            P.op("dve", lambda e, r_=r_, o_=o_, t_=t_, n=n: e.tensor_tensor(out=t_[:, 0:n], in0=o_[:, 0:n], in1=r_[:, 0:n], op=ALU.mult),
                 reads=[rk], writes=[tk], excl=[ok])
            P.op("pool", lambda e, t_=t_, sg_=sg_, g_=g_, n=n: e.tensor_tensor(out=g_[:, 0:n], in0=t_[:, 0:n], in1=sg_[:, 0:n], op=ALU.mult),
                 reads=[tk, sgk], writes=[gk])
            P.dma("sp", gk, lambda e, g_=g_, h=h, t0=t0, n=n: [e.dma_start(out=io["ogT"][h * 128:(h + 1) * 128, t0:t0 + n], in_=g_[:, 0:n])], reads=[gk])
            return len(tiles)

        for (t0, n) in token_blocks(nt):
            si += do_qblock(h, t0, n, qb, si)
            qb += 1
    P.pop()


NG = 2
HPG = 16
HD = 64
GW = HPG * HD


def seq_blocks(nt):
    out = [(0, NCTX, 0, NCTX)]
    t = NCTX
    while t < nt:
        n = min(512, nt - t)
        out.append((t, n, NCTX, nt))
        t += n
    return out


def phase_S1(P, nc, io, nt):
    P.push()
    pT = io["pT"]
    identf = P.sbuf("identf1", [128, 128], F32)
    convw = P.sbuf("convw", [128, 20, 5], F32)
    convb = P.sbuf("convb", [128, 20], F32)
    dtb = P.sbuf("dtb", [128, 1], F32)
    alog = P.sbuf("alog", [128, 1], F32)
    mulc = P.sbuf("mulc", [128, 1], F32)
    P.dma("sp", "identf1", lambda e: [e.dma_start(out=identf[:], in_=io["identf"])], writes=["identf1"])
    P.dma("sp", "convw", lambda e: [e.dma_start(out=convw[:], in_=io["convw"])], writes=["convw"])
    P.dma("sp", "convb", lambda e: [e.dma_start(out=convb[:], in_=io["convb"])], writes=["convb"])
    P.dma("sp", "dtb", lambda e: [e.dma_start(out=dtb[:], in_=io["dtbias2"])], writes=["dtb"])
    P.dma("sp", "alog", lambda e: [e.dma_start(out=alog[:], in_=io["alog2"])], writes=["alog"])
    P.op("act", lambda e: e.activation(out=mulc[:], in_=alog[:], func=AF.Exp), reads=["alog"], writes=["mulc"])
    P.op("dve", lambda e: e.tensor_scalar(out=mulc[:], in0=mulc[:], scalar1=-1.0, scalar2=None, op0=ALU.mult), reads=["mulc"], writes=["mulc"])
    P.op("dve", lambda e: e.memset(mulc[0:64, :], 1.0), reads=["mulc"], writes=["mulc"])

    cin = [P.sbuf("cin%d" % i, [128, 516], F32) for i in range(3)]
    acc = [P.sbuf("acc%d" % i, [128, 512], F32) for i in range(2)]
    cv = [P.sbuf("cv%d" % i, [128, 512], F32) for i in range(8)]
    stf = [P.sbuf("stf%d" % i, [128, 512], F32) for i in range(3)]
    stb = [P.sbuf("stb%d" % i, [128, 512], BF16) for i in range(3)]
    sp_ = [P.sbuf("sp%d" % i, [128, 512], F32) for i in range(4)]
    pt = [P.psum("pt%d" % i, [128, 512], F32) for i in range(3)]
    c = {"cin": 0, "acc": 0, "cv": 0, "stf": 0, "stb": 0, "pt": 0}

    def rot(name, arr):
        i = c[name] % len(arr)
        c[name] += 1
        return arr[i], "%s%d" % (name, i)

    xs0 = SEG_OFF["xs"] * 128
    b0 = SEG_OFF["B"] * 128
    c0 = SEG_OFF["C"] * 128
    d0 = SEG_OFF["dt"] * 128

    def conv_chunk(row0, cc, t0, n, s0, s1):
        ci, cik = rot("cin", cin)
        lo = t0 - 2 if t0 > s0 else t0
        hi = t0 + n + 2 if t0 + n < s1 else t0 + n
        if lo == t0:
            P.op("pool", lambda e: e.memset(ci[:, 0:2], 0.0), writes=[cik])
        if hi == t0 + n:
            P.op("pool", lambda e: e.memset(ci[:, n + 2:n + 4], 0.0), writes=[cik])
        P.dma("sp", cik, lambda e: [e.dma_start(out=ci[:, lo - (t0 - 2):hi - (t0 - 2)], in_=pT[row0:row0 + 128, lo:hi])], reads=[cik], writes=[cik])
        a, ak = rot("acc", acc)
        P.op("dve", lambda e: e.tensor_scalar(out=a[:, 0:n], in0=ci[:, 0:n], scalar1=convw[:, cc, 0:1], scalar2=None, op0=ALU.mult),
             reads=[cik, "convw"], writes=[ak])
        for k in range(1, 5):
            P.op("dve", lambda e, k=k: e.scalar_tensor_tensor(out=a[:, 0:n], in0=ci[:, k:k + n], scalar=convw[:, cc, k:k + 1], in1=a[:, 0:n],
                                                              op0=ALU.mult, op1=ALU.add), reads=[cik, "convw", ak], writes=[ak])
        o, ok = rot("cv", cv)
        P.op("act", lambda e: e.activation(out=o[:, 0:n], in_=a[:, 0:n], func=AF.Silu, bias=convb[:, cc:cc + 1]), reads=[ak, "convb"], writes=[ok])
        return o, ok

    def do_block(t0, n, s0, s1):
        ntile = n // 128
        for cg in range(4):
            bufs = [conv_chunk(xs0 + (cg * 4 + q) * 128, cg * 4 + q, t0, n, s0, s1) for q in range(4)]
            for tt in range(ntile):
                pb, pk = rot("pt", pt)
                for q in range(4):
                    o, ok = bufs[q]
                    P.op("pe", lambda e, pb=pb, q=q, o=o, tt=tt: e.transpose(pb[:, q * 128:(q + 1) * 128], o[:, tt * 128:(tt + 1) * 128], identf[:]),
                         reads=[ok, "identf1"], excl=["b_" + pk])
                sg, sk = rot("stf", stf)
                P.op("act", lambda e, sg=sg, pb=pb: e.activation(out=sg[:], in_=pb[:], func=AF.Identity), writes=[sk], excl=["b_" + pk])
                r0 = t0 + tt * 128
                P.dma("sp", sk, lambda e, sg=sg, r0=r0, cg=cg: [e.dma_start(out=io["xs_tm"][r0:r0 + 128, cg * 512:(cg + 1) * 512], in_=sg[:])], reads=[sk])
        for g in range(NG):
            o, ok = conv_chunk(b0 + g * 128, 16 + g, t0, n, s0, s1)
            sg, sk = rot("stb", stb)
            P.op("act", lambda e, sg=sg, o=o: e.activation(out=sg[:, 0:n], in_=o[:, 0:n], func=AF.Identity), reads=[ok], writes=[sk])
            P.dma("sp", sk, lambda e, sg=sg, g=g: [e.dma_start(out=io["BT"][g, :, t0:t0 + n], in_=sg[:, 0:n])], reads=[sk])
            pb, pk = rot("pt", pt)
            for tt in range(ntile):
                P.op("pe", lambda e, pb=pb, o=o, tt=tt: e.transpose(pb[:, tt * 128:(tt + 1) * 128], o[:, tt * 128:(tt + 1) * 128], identf[:]),
                     reads=[ok, "identf1"], excl=["b_" + pk])
            sg2, sk2 = rot("stb", stb)
            P.op("act", lambda e, sg2=sg2, pb=pb: e.activation(out=sg2[:, 0:n], in_=pb[:, 0:n], func=AF.Identity), writes=[sk2], excl=["b_" + pk])
            P.dma("sp", sk2, lambda e, sg2=sg2, g=g: [e.dma_start(out=io["B_tm"][g, t0:t0 + n, :].rearrange("(tt p) m -> p tt m", p=128),
                                                                in_=sg2[:, 0:n].rearrange("p (tt m) -> p tt m", m=128))], reads=[sk2])
        for g in range(NG):
            o, ok = conv_chunk(c0 + g * 128, 18 + g, t0, n, s0, s1)
            sg, sk = rot("stb", stb)
            P.op("act", lambda e, sg=sg, o=o: e.activation(out=sg[:, 0:n], in_=o[:, 0:n], func=AF.Identity), reads=[ok], writes=[sk])
            P.dma("sp", sk, lambda e, sg=sg, g=g: [e.dma_start(out=io["CT"][g, :, t0:t0 + n], in_=sg[:, 0:n])], reads=[sk])
        xb, l1, l2, dd = sp_
        P.dma("sp", "sp0", lambda e: [e.dma_start(out=xb[0:64, 0:n], in_=pT[d0:d0 + 64, t0:t0 + n]),
                                      e.dma_start(out=xb[64:128, 0:n], in_=pT[d0:d0 + 64, t0:t0 + n])], writes=["sp0"], n=2)
        P.op("act", lambda e: e.activation(out=xb[:, 0:n], in_=xb[:, 0:n], func=AF.Identity, bias=dtb[:, 0:1]), reads=["sp0", "dtb"], writes=["sp0"])
        P.op("act", lambda e: e.activation(out=l1[:, 0:n], in_=xb[:, 0:n], func=AF.Abs), reads=["sp0"], writes=["sp1"])
        P.op("act", lambda e: e.activation(out=l1[:, 0:n], in_=l1[:, 0:n], func=AF.Exp, scale=-1.0), reads=["sp1"], writes=["sp1"])
        P.op("dve", lambda e: e.tensor_scalar(out=l1[:, 0:n], in0=l1[:, 0:n], scalar1=1.0, scalar2=None, op0=ALU.add), reads=["sp1"], writes=["sp1"])
        P.op("act", lambda e: e.activation(out=l2[:, 0:n], in_=l1[:, 0:n], func=AF.Ln), reads=["sp1"], writes=["sp2"])
        P.op("dve", lambda e: e.scalar_tensor_tensor(out=l2[:, 0:n], in0=xb[:, 0:n], scalar=0.0, in1=l2[:, 0:n], op0=ALU.max, op1=ALU.add),
             reads=["sp0", "sp2"], writes=["sp2"])
        P.op("dve", lambda e: e.tensor_scalar(out=dd[:, 0:n], in0=l2[:, 0:n], scalar1=mulc[:, 0:1], scalar2=None, op0=ALU.mult),
             reads=["sp2", "mulc"], writes=["sp3"])
        pb, pk = rot("pt", pt)
        for tt in range(ntile):
            P.op("pe", lambda e, pb=pb, tt=tt: e.transpose(pb[:, tt * 128:(tt + 1) * 128], dd[:, tt * 128:(tt + 1) * 128], identf[:]),
                 reads=["sp3", "identf1"], excl=["b_" + pk])
        sg, sk = rot("stf", stf)
        P.op("act", lambda e, sg=sg, pb=pb: e.activation(out=sg[:, 0:n], in_=pb[:, 0:n], func=AF.Identity), writes=[sk], excl=["b_" + pk])
        P.dma("sp", sk, lambda e, sg=sg: [e.dma_start(out=io["dtt"][t0:t0 + n, :].rearrange("(tt p) m -> p tt m", p=128),
                                                        in_=sg[:, 0:n].rearrange("p (tt m) -> p tt m", m=128))], reads=[sk])

    for (t0, n, s0, s1) in seq_blocks(nt):
        do_block(t0, n, s0, s1)
    P.pop()


def phase_S2(P, nc, io, nt, nsteps=None):
    P.push()
    nck = nt // 128
    nctx = NCTX // 128
    order = {0: list(range(nck)), 1: list(range(nctx - 1, -1, -1)) + list(range(nck - 1, nctx - 1, -1))}
    cm = {}
    for nm in ("TL", "TG", "SG", "SL", "onesf"):
        cm[nm] = P.sbuf("m_" + nm, [128, 128], F32)
        P.dma("sp", "m_" + nm, lambda e, nm=nm: [e.dma_start(out=cm[nm][:], in_=io[nm])], writes=["m_" + nm])
    TRI = {0: "TL", 1: "TG"}
    UU = {0: "SG", 1: "SL"}
    Sst = {}
    Sbf = {}
    for d in range(2):
        for g in range(NG):
            Sst[d, g] = P.sbuf("Sst%d%d" % (d, g), [128, GW], F32)
            Sbf[d, g] = P.sbuf("Sbf%d%d" % (d, g), [128, GW], BF16)
            P.op("pool", lambda e, d=d, g=g: e.memset(Sst[d, g][:], 0.0), writes=["Sst%d%d" % (d, g)])
            P.op("pool", lambda e, d=d, g=g: e.memset(Sbf[d, g][:], 0.0), writes=["Sbf%d%d" % (d, g)])
    NW = 3
    W = []
    for w in range(NW):
        W.append(dict(
            xs=P.sbuf("w%d_xs" % w, [128, GW], F32), xdt=P.sbuf("w%d_xdt" % w, [128, GW], BF16), xw=P.sbuf("w%d_xw" % w, [128, GW], BF16),
            R=P.sbuf("w%d_R" % w, [128, HPG, 128], F32), E=P.sbuf("w%d_E" % w, [128, HPG, 128], F32), MT=P.sbuf("w%d_MT" % w, [128, HPG, 128], BF16),
            mcb=P.sbuf("w%d_mcb" % w, [128, 128], F32), bt=P.sbuf("w%d_bt" % w, [128, 128], BF16), ct=P.sbuf("w%d_ct" % w, [128, 128], BF16),
            btm=P.sbuf("w%d_btm" % w, [128, 128], BF16), dtt=P.sbuf("w%d_dtt" % w, [128, 128], F32),
            ex3=P.sbuf("w%d_ex3" % w, [128, 48], F32), wv=P.sbuf("w%d_wv" % w, [128, 16], F32),
            ysb=P.sbuf("w%d_ysb" % w, [128, GW], F32), tmp=P.sbuf("w%d_tmp" % w, [128, GW], F32), yo=P.sbuf("w%d_yo" % w, [128, GW], F32),
        ))
    G = [P.psum("G%d" % i, [128, 512], F32) for i in range(2)]
    Y = [P.psum("Y%d" % i, [128, 512], F32) for i in range(2)]
    Z = [P.psum("Z%d" % i, [128, 512], F32) for i in range(2)]
    SM = P.psum("SM", [128, 512], F32)

    def step(si, d, g, ck):
        w = si % NW
        B = W[w]
        K = lambda s: "w%d_%s" % (w, s)
        tok = slice(ck * 128, (ck + 1) * 128)
        sk_st, sk_bf = "Sst%d%d" % (d, g), "Sbf%d%d" % (d, g)
        st, sb = Sst[d, g], Sbf[d, g]
        dcol = d * 32 + g * 16
        P.dma("sp", K("xs"), lambda e: [e.dma_start(out=B["xs"][:], in_=io["xs_tm"][tok, g * GW:(g + 1) * GW])], writes=[K("xs")])
        P.dma("sp", K("bt"), lambda e: [e.dma_start(out=B["bt"][:], in_=io["BT"][g, :, tok])], writes=[K("bt")])
        P.dma("sp", K("ct"), lambda e: [e.dma_start(out=B["ct"][:], in_=io["CT"][g, :, tok])], writes=[K("ct")])
        P.dma("sp", K("btm"), lambda e: [e.dma_start(out=B["btm"][:], in_=io["B_tm"][g, tok, :])], writes=[K("btm")])
        P.dma("sp", K("dtt"), lambda e: [e.dma_start(out=B["dtt"][:], in_=io["dtt"][tok, :])], writes=[K("dtt")])
        dt_ap = B["dtt"][:, dcol:dcol + 16]
        dta_ap = B["dtt"][:, 64 + dcol:64 + dcol + 16]
        tri, uu = cm[TRI[d]], cm[UU[d]]
        P.op("pool", lambda e: e.tensor_tensor(out=B["R"][:], in0=tri[:].unsqueeze(1).broadcast_to([128, HPG, 128]),
                                               in1=dta_ap.unsqueeze(2).broadcast_to([128, HPG, 128]), op=ALU.mult),
             reads=[K("dtt"), "m_" + TRI[d]], writes=[K("R")])
        P.op("pe", lambda e: e.matmul(SM[:, 0:128], B["bt"][:], B["ct"][:], start=True, stop=True), reads=[K("bt"), K("ct")], excl=["b_SM"])
        P.op("pe", lambda e: e.matmul(SM[:, 128:144], tri[:], dta_ap, start=True, stop=True), reads=[K("dtt"), "m_" + TRI[d]], excl=["b_SM"])
        P.op("pe", lambda e: e.matmul(SM[:, 144:160], uu[:], dta_ap, start=True, stop=True), reads=[K("dtt"), "m_" + UU[d]], excl=["b_SM"])
        P.op("pe", lambda e: e.matmul(SM[:, 160:176], cm["onesf"][:], dta_ap, start=True, stop=True), reads=[K("dtt"), "m_onesf"], excl=["b_SM"])
        P.op("dve", lambda e: e.tensor_tensor(out=B["mcb"][:], in0=SM[:, 0:128], in1=tri[:], op=ALU.mult), reads=["m_" + TRI[d]], writes=[K("mcb")], excl=["b_SM"])
        P.op("act", lambda e: e.activation(out=B["ex3"][:], in_=SM[:, 128:176], func=AF.Exp), writes=[K("ex3")], excl=["b_SM"])
        P.op("dve", lambda e: e.tensor_tensor(out=B["wv"][:], in0=B["ex3"][:, 16:32], in1=dt_ap, op=ALU.mult), reads=[K("ex3"), K("dtt")], writes=[K("wv")])
        for q in range(4):
            gb, gk = G[q % 2], "b_G%d" % (q % 2)
            P.op("pe", lambda e, gb=gb, q=q: e.matmul(gb[:, :].rearrange("p (h l) -> p h l", h=4), uu[:], B["R"][:, 4 * q:4 * q + 4, :], start=True, stop=True),
                 reads=[K("R"), "m_" + UU[d]], excl=[gk])
            P.op("act", lambda e, gb=gb, q=q: e.activation(out=B["E"][:, 4 * q:4 * q + 4, :], in_=gb[:, :].rearrange("p (h l) -> p h l", h=4), func=AF.Exp),
                 writes=[K("E")], excl=[gk])
        P.op("dve", lambda e: e.tensor_tensor(out=B["MT"][:], in0=B["E"][:], in1=B["mcb"][:].unsqueeze(1).broadcast_to([128, HPG, 128]), op=ALU.mult),
             reads=[K("E"), K("mcb")], writes=[K("MT")])
        xs3 = B["xs"][:, :].rearrange("p (h d) -> p h d", h=HPG)
        P.op("dve", lambda e: e.tensor_tensor(out=B["xdt"][:, :].rearrange("p (h d) -> p h d", h=HPG), in0=xs3,
                                              in1=dt_ap.unsqueeze(2).broadcast_to([128, HPG, HD]), op=ALU.mult),
             reads=[K("xs"), K("dtt")], writes=[K("xdt")])
        P.op("pool", lambda e: e.tensor_tensor(out=B["xw"][:, :].rearrange("p (h d) -> p h d", h=HPG), in0=xs3,
                                               in1=B["wv"][:].unsqueeze(2).broadcast_to([128, HPG, HD]), op=ALU.mult),
             reads=[K("xs"), K("wv")], writes=[K("xw")])
        for h in range(HPG):
            yb, yk = Y[h // 8], "b_Y%d" % (h // 8)
            P.op("pe", lambda e, yb=yb, h=h: e.matmul(yb[:, (h % 8) * HD:(h % 8 + 1) * HD], B["MT"][:, h, :], B["xdt"][:, h * HD:(h + 1) * HD], start=True, stop=True),
                 reads=[K("MT"), K("xdt")], excl=[yk])
        for hf in range(2):
            P.op("pe", lambda e, hf=hf: e.matmul(Z[hf][:, :], B["ct"][:], sb[:, hf * 512:(hf + 1) * 512], start=True, stop=True),
                 reads=[K("ct"), sk_bf], excl=["b_Z%d" % hf])
        for hf in range(2):
            P.op("act", lambda e, hf=hf: e.activation(out=B["ysb"][:, hf * 512:(hf + 1) * 512], in_=Y[hf][:, :], func=AF.Identity),
                 writes=[K("ysb") + str(hf)], excl=["b_Y%d" % hf])
            P.op("dve", lambda e, hf=hf: e.tensor_tensor(out=B["tmp"][:, hf * 512:(hf + 1) * 512].rearrange("p (h d) -> p h d", h=8),
                                                         in0=Z[hf][:, :].rearrange("p (h d) -> p h d", h=8),
                                                         in1=B["ex3"][:, 8 * hf:8 * hf + 8].unsqueeze(2).broadcast_to([128, 8, HD]), op=ALU.mult),
                 reads=[K("ex3")], writes=[K("tmp") + str(hf)], excl=["b_Z%d" % hf])
        P.op("pool", lambda e: e.tensor_tensor(out=B["yo"][:], in0=B["tmp"][:], in1=B["ysb"][:], op=ALU.add),
             reads=[K("tmp") + "0", K("tmp") + "1", K("ysb") + "0", K("ysb") + "1"], writes=[K("yo")])
        P.dma("pool", K("yo"), lambda e: [e.dma_start(out=io["yd"][d, tok, g * GW:(g + 1) * GW], in_=B["yo"][:])], reads=[K("yo")])
        for hf in range(2):
            P.op("pe", lambda e, hf=hf: e.matmul(Z[hf][:, :], B["btm"][:], B["xw"][:, hf * 512:(hf + 1) * 512], start=True, stop=True),
                 reads=[K("btm"), K("xw")], excl=["b_Z%d" % hf])
        P.op("dve", lambda e: e.tensor_tensor(out=st[:, :].rearrange("p (h d) -> p h d", h=HPG), in0=st[:, :].rearrange("p (h d) -> p h d", h=HPG),
                                              in1=B["ex3"][:, 32:48].unsqueeze(2).broadcast_to([128, HPG, HD]), op=ALU.mult),
             reads=[K("ex3"), sk_st], writes=[sk_st])
        for hf in range(2):
            P.op("dve", lambda e, hf=hf: e.tensor_tensor(out=st[:, hf * 512:(hf + 1) * 512], in0=st[:, hf * 512:(hf + 1) * 512], in1=Z[hf][:, :], op=ALU.add),
                 reads=[sk_st], writes=[sk_st], excl=["b_Z%d" % hf])
        P.op("act", lambda e: e.activation(out=sb[:], in_=st[:], func=AF.Identity), reads=[sk_st], writes=[sk_bf])

    si = 0
    ns = nck if nsteps is None else nsteps
    for i in range(ns):
        for d in range(2):
            for g in range(NG):
                step(si, d, g, order[d][i])
                si += 1
    P.pop()


def phase_S3(P, nc, io, nt):
    P.push()
    identf = P.sbuf("identf3", [128, 128], F32)
    onesf = P.sbuf("onesf3", [128, 128], F32)
    dsk = P.sbuf("dsk", [128, 2048], F32)
    P.dma("sp", "identf3", lambda e: [e.dma_start(out=identf[:], in_=io["identf"])], writes=["identf3"])
    P.dma("sp", "onesf3", lambda e: [e.dma_start(out=onesf[:], in_=io["onesf"])], writes=["onesf3"])
    P.dma("sp", "dsk", lambda e: [e.dma_start(out=dsk[:], in_=io["dskbc"])], writes=["dsk"])
    gssm = P.sbuf("gssm", [128, 16], F32)
    P.dma("sp", "gssm", lambda e: [e.dma_start(out=gssm[:], in_=io["gssmT"])], writes=["gssm"])
    yf = [P.sbuf("yf%d" % i, [128, 2048], F32) for i in range(2)]
    yb = [P.sbuf("yb%d" % i, [128, 2048], F32) for i in range(2)]
    xs = [P.sbuf("xs3_%d" % i, [128, 2048], F32) for i in range(2)]
    sz = [P.sbuf("sz%d" % i, [128, 16, 128], F32) for i in range(2)]
    yg = [P.sbuf("yg%d" % i, [128, 512], F32) for i in range(2)]
    sq = [P.sbuf("sq3_%d" % i, [128, 512], F32) for i in range(2)]
    ygs = [P.sbuf("ygs%d" % i, [128, 16, 512], BF16) for i in range(2)]
    ssr = P.sbuf("ssr", [1, 512], F32)
    pt = [P.psum("pt3_%d" % i, [128, 512], F32) for i in range(2)]
    pq = P.psum("pq", [128, 512], F32)
    z0 = SEG_OFF["z"] * 128
    nck = nt // 128
    qi = 0
    blocks = token_blocks(nt)
    for (t0, n) in blocks:
        bi = (t0 // 512) % 2 if t0 >= NCTX else 0
        gs, gsk = ygs[bi % 2], "ygs%d" % (bi % 2)
        for tt in range(n // 128):
            ck = (t0 + tt * 128) // 128
            i2 = ck % 2
            tok = slice(ck * 128, (ck + 1) * 128)
            a, b_, x_, s_ = yf[i2], yb[i2], xs[i2], sz[i2]
            ka, kb, kx, ks = "yf%d" % i2, "yb%d" % i2, "xs3_%d" % i2, "sz%d" % i2
            P.dma("sp", ka, lambda e, a=a, tok=tok: [e.dma_start(out=a[:], in_=io["yd"][0, tok, :])], writes=[ka])
            P.dma("sp", kb, lambda e, b_=b_, tok=tok: [e.dma_start(out=b_[:], in_=io["yd"][1, tok, :])], writes=[kb])
            P.dma("sp", kx, lambda e, x_=x_, tok=tok: [e.dma_start(out=x_[:], in_=io["xs_tm"][tok, :])], writes=[kx])
            P.dma("sp", ks, lambda e, s_=s_, tok=tok: [e.dma_start(out=s_[:], in_=io["pT"][z0:z0 + 2048, tok].rearrange("(c p) t -> p c t", p=128))], writes=[ks])
            P.op("dve", lambda e, a=a, b_=b_: e.tensor_tensor(out=a[:], in0=a[:], in1=b_[:], op=ALU.add), reads=[ka, kb], writes=[ka])
            P.op("pool", lambda e, x_=x_: e.tensor_tensor(out=x_[:], in0=x_[:], in1=dsk[:], op=ALU.mult), reads=[kx, "dsk"], writes=[kx])
            P.op("pool", lambda e, a=a, x_=x_: e.tensor_tensor(out=a[:], in0=a[:], in1=x_[:], op=ALU.add), reads=[ka, kx], writes=[ka])
            for q in range(4):
                pb, pk = pt[qi % 2], "b_pt3_%d" % (qi % 2)
                g_, gk = yg[qi % 2], "yg%d" % (qi % 2)
                s2, s2k = sq[qi % 2], "sq3_%d" % (qi % 2)
                qi += 1
                for c4 in range(4):
                    c = q * 4 + c4
                    P.op("pe", lambda e, pb=pb, a=a, c=c, c4=c4: e.transpose(pb[:, c4 * 128:(c4 + 1) * 128], a[:, c * 128:(c + 1) * 128], identf[:]),
                         reads=[ka, "identf3"], excl=[pk])
                P.op("dve", lambda e, pb=pb, g_=g_, s_=s_, q=q: e.tensor_tensor(out=g_[:, :].rearrange("p (c t) -> p c t", c=4),
                                                                              in0=pb[:, :].rearrange("p (c t) -> p c t", c=4),
                                                                              in1=s_[:, 4 * q:4 * q + 4, :], op=ALU.mult),
                     reads=[ks], writes=[gk], excl=[pk])
                for c4 in range(4):
                    c = q * 4 + c4
                    P.op("act", lambda e, g_=g_, gs=gs, c=c, c4=c4, tt=tt: e.activation(out=gs[:, c, tt * 128:(tt + 1) * 128],
                                                                                      in_=g_[:, c4 * 128:(c4 + 1) * 128], func=AF.Identity, scale=gssm[:, c:c + 1]),
                         reads=[gk, "gssm"], writes=[(gsk, tt, q, c4)])
                P.op("act", lambda e, g_=g_, s2=s2: e.activation(out=s2[:], in_=g_[:], func=AF.Square), reads=[gk], writes=[s2k])
                for c4 in range(4):
                    first = (q == 0 and c4 == 0)
                    last = (q == 3 and c4 == 3)
                    P.op("pe", lambda e, s2=s2, c4=c4, tt=tt, first=first, last=last: e.matmul(pq[0:1, tt * 128:(tt + 1) * 128], onesf[:, 0:1], s2[:, c4 * 128:(c4 + 1) * 128],
                                                                                             start=first, stop=last),
                         reads=[s2k, "onesf3"], excl=["b_pq"])
        P.op("dve", lambda e, n=n: e.tensor_copy(out=ssr[:, 0:n], in_=pq[0:1, 0:n]), writes=["ssr"], excl=["b_pq"])
        P.dma("sp", "ssr", lambda e, t0=t0, n=n: [e.dma_start(out=io["ssq"][:, t0:t0 + n], in_=ssr[:, 0:n])], reads=["ssr"])
        P.dma("sp", gsk, lambda e, gs=gs, t0=t0, n=n: [e.dma_start(out=io["ygT_parts"][hf][:, t0:t0 + n].rearrange("(c p) t -> p c t", p=128), in_=gs[:, 8 * hf:8 * hf + 8, 0:n]) for hf in range(2)],
              n=2, reads=[(gsk, tt, q, c4) for tt in range(n // 128) for q in range(4) for c4 in range(4)])
    P.pop()


TC = 2112
NLAT = 2048


def c_blocks(with_ctx):
    blks = [(t0, 256, 0) for t0 in range(0, NLAT, 256)]
    if with_ctx:
        blks.append((NLAT, 64, 1))
    return blks


def phase_Cmod(P, nc, io, wadag_fn):
    P.push()
    c2 = P.sbuf("cm_c2", [128, KC, 2], F32)
    sc = P.sbuf("cm_sc", [128, KC, 2], F32)
    scb = P.sbuf("cm_scb", [128, 2, KC, 128], F32)
    bb = P.sbuf("cm_bb", [128, D], F32)
    wg = [P.sbuf("cm_wg%d" % i, [128, KC, 256], F32) for i in range(2)]
    sg = [P.sbuf("cm_sg%d" % i, [128, 256], F32) for i in range(2)]
    pg = [P.psum("cm_pg%d" % i, [128, 512], F32) for i in range(2)]
    P.dma("sp", "cm_c2", lambda e: [e.dma_start(out=c2[:], in_=io["c2T"])], writes=["cm_c2"])
    P.dma("sp", "cm_bb", lambda e: [e.dma_start(out=bb[:], in_=io["badag"][0:1, :].broadcast_to([128, D]))], writes=["cm_bb"])
    P.op("act", lambda e: e.activation(out=sc[:], in_=c2[:], func=AF.Silu), reads=["cm_c2"], writes=["cm_sc"])
    for v in range(2):
        P.op("dve", lambda e, v=v: e.tensor_copy(out=scb[:, v], in_=sc[:, :, v].unsqueeze(2).broadcast_to([128, KC, 128])),
             reads=["cm_sc"], writes=["cm_scb"])
    si = 0
    for cb in range(16):
        w_, wk = wg[cb % 2], "cm_wg%d" % (cb % 2)
        wp_ = wadag_fn(cb)
        P.dma("sp", wk, lambda e, w_=w_, wp_=wp_: dma_pieces(e, w_, wp_, "p (k n) -> p k n", k=KC), writes=[wk], n=len(wp_))
        for v in range(2):
            for k in range(KC):
                P.op("pe", lambda e, v=v, k=k, w_=w_: e.matmul(pg[v][:, 0:256], scb[:, v, k, :], w_[:, k, :], start=(k == 0), stop=(k == KC - 1)),
                     reads=[wk, "cm_scb"], excl=["b_cm_pg%d" % v])
            s_, sk = sg[si % 2], "cm_sg%d" % (si % 2)
            si += 1
            P.op("dve", lambda e, v=v, s_=s_, cb=cb: e.tensor_tensor(out=s_[:], in0=pg[v][:, 0:256], in1=bb[:, cb * 256:(cb + 1) * 256], op=ALU.add),
                 reads=["cm_bb"], writes=[sk], excl=["b_cm_pg%d" % v])
            P.dma("sp", sk, lambda e, v=v, s_=s_, cb=cb: [e.dma_start(out=io["gsc"][v, :, cb * 256:(cb + 1) * 256], in_=s_[:])], reads=[sk])
    P.pop()


def phase_C(P, nc, io, wpa_fn, wpb_fn, wout_fn, with_ctx, final, dyn):
    P.push()
    mrg = P.sbuf("c_mrg", [128, KC, 256], BF16)
    ones4 = P.sbuf("c_ones4", [4, 128], F32)
    P.op("pool", lambda e: e.memset(ones4[:], 1.0), writes=["c_ones4"])
    for (t0, n, var) in c_blocks(with_ctx):
        P.push()
        ogb = P.sbuf("c_ogb", [128, KC, 256], BF16)
        ygb = P.sbuf("c_ygb", [128, 64, 256], BF16)
        ss4 = P.sbuf("c_ss4", [4, 256], F32)
        rs = P.sbuf("c_rs", [128, 3, 256], F32)
        wa = [P.sbuf("c_wa%d" % i, [128, KC, 128], BF16) for i in range(2)]
        wb = [P.sbuf("c_wb%d" % i, [128, 64, 128], BF16) for i in range(2)]
        sga = [P.sbuf("c_sga%d" % i, [128, 256], F32) for i in range(2)]
        sgb = [P.sbuf("c_sgb%d" % i, [128, 256], F32) for i in range(2)]
        m1 = [P.sbuf("c_m1%d" % i, [128, 256], F32) for i in range(2)]
        m2 = [P.sbuf("c_m2%d" % i, [128, 256], F32) for i in range(2)]
        pa = [P.psum("c_pa%d" % i, [128, 512], F32) for i in range(2)]
        pb = [P.psum("c_pb%d" % i, [128, 512], F32) for i in range(2)]
        pr = P.psum("c_pr", [128, 512], F32)

        def cols(ap, t0=t0, n=n):
            return ap[:, t0:t0 + n]

        P.dma("sp", "c_ogb", lambda e, n=n, cols=cols: [e.dma_start(out=ogb[:, :, 0:n], in_=cols(io["oga2"]).rearrange("(k p) t -> p k t", p=128))], writes=["c_ogb"])
        P.dma("sp", "c_ygb", lambda e, n=n, cols=cols: [e.dma_start(out=ygb[:, 32 * q:32 * q + 32, 0:n],
                                                                   in_=cols(io["yga_parts"][q]).rearrange("(k p) t -> p k t", p=128))
                                                       for q in range(2)], writes=["c_ygb"], n=2)
        P.dma("sp", "c_ss4", lambda e, n=n, cols=cols: [e.dma_start(out=ss4[:, 0:n], in_=cols(io["ssqa"]))], writes=["c_ss4"])
        P.op("pe", lambda e, n=n: e.matmul(pr[:, 0:n], ones4[:], ss4[:, 0:n], start=True, stop=True), reads=["c_ones4", "c_ss4"], excl=["b_c_pr"])
        P.op("dve", lambda e, n=n: e.tensor_scalar(out=rs[:, 0, 0:n], in0=pr[:, 0:n], scalar1=1.0 / 8192.0, scalar2=EPS, op0=ALU.mult, op1=ALU.add),
             writes=["c_rs0"], excl=["b_c_pr"])
        P.op("act", lambda e, n=n: e.activation(out=rs[:, 1, 0:n], in_=rs[:, 0, 0:n], func=AF.Sqrt), reads=["c_rs0"], writes=["c_rs1"])
        P.op("dve", lambda e, n=n: e.reciprocal(out=rs[:, 2, 0:n], in_=rs[:, 1, 0:n]), reads=["c_rs1"], writes=["c_rs2"])
        for oc in range(KC):
            i2 = oc % 2
            wpa_, wpb_ = wpa_fn(oc), wpb_fn(oc)
            P.dma("pool", "c_wa%d" % i2, lambda e, i2=i2, wpa_=wpa_: dma_pieces(e, wa[i2], wpa_, "p (k n) -> p k n", k=KC), writes=["c_wa%d" % i2], n=len(wpa_))
            P.dma("pool", "c_wb%d" % i2, lambda e, i2=i2, wpb_=wpb_: dma_pieces(e, wb[i2], wpb_, "p (k n) -> p k n", k=64), writes=["c_wb%d" % i2], n=len(wpb_))
            P.dma("sp", "c_sga%d" % i2, lambda e, oc=oc, i2=i2, t0=t0, n=n: [e.dma_start(out=sga[i2][:, 0:n], in_=io["sgT"][oc * 128:(oc + 1) * 128, t0:t0 + n])], writes=["c_sga%d" % i2])
            P.dma("sp", "c_sgb%d" % i2, lambda e, oc=oc, i2=i2, t0=t0, n=n: [e.dma_start(out=sgb[i2][:, 0:n], in_=io["sgT"][D + oc * 128:D + (oc + 1) * 128, t0:t0 + n])], writes=["c_sgb%d" % i2])
            for k in range(KC):
                P.op("pe", lambda e, k=k, i2=i2, n=n: e.matmul(pa[i2][:, 0:n], wa[i2][:, k, :], ogb[:, k, 0:n], start=(k == 0), stop=(k == KC - 1)),
                     reads=["c_wa%d" % i2, "c_ogb"], excl=["b_c_pa%d" % i2])
            for k in range(64):
                P.op("pe", lambda e, k=k, i2=i2, n=n: e.matmul(pb[i2][:, 0:n], wb[i2][:, k, :], ygb[:, k, 0:n], start=(k == 0), stop=(k == 63)),
                     reads=["c_wb%d" % i2, "c_ygb"], excl=["b_c_pb%d" % i2])
            P.op("dve", lambda e, i2=i2, n=n: e.tensor_tensor(out=m1[i2][:, 0:n], in0=pa[i2][:, 0:n], in1=sga[i2][:, 0:n], op=ALU.mult),
                 reads=["c_sga%d" % i2], writes=["c_m1%d" % i2], excl=["b_c_pa%d" % i2])
            P.op("dve", lambda e, i2=i2, n=n: e.tensor_tensor(out=m2[i2][:, 0:n], in0=pb[i2][:, 0:n], in1=rs[:, 2, 0:n], op=ALU.mult),
                 reads=["c_rs2"], writes=["c_m2%d" % i2], excl=["b_c_pb%d" % i2])
            P.op("pool", lambda e, i2=i2, n=n: e.tensor_tensor(out=m2[i2][:, 0:n], in0=m2[i2][:, 0:n], in1=sgb[i2][:, 0:n], op=ALU.mult),
                 reads=["c_sgb%d" % i2, "c_m2%d" % i2], writes=["c_m2%d" % i2])
            P.op("pool", lambda e, i2=i2, n=n, oc=oc: e.tensor_tensor(out=mrg[:, oc, 0:n], in0=m1[i2][:, 0:n], in1=m2[i2][:, 0:n], op=ALU.add),
                 reads=["c_m1%d" % i2, "c_m2%d" % i2], writes=["c_mrg"])
        P.pop()
        P.push()
        wo = [P.sbuf("c_wo%d" % i, [128, KC, 512], BF16) for i in range(2)]
        hrow = [P.sbuf("c_hrow%d" % i, [128, D], F32) for i in range(2)]
        gb = P.sbuf("c_gb", [128, D], F32)
        ht = [P.sbuf("c_ht%d" % i, [128, 512], F32) for i in range(2)]
        tmp = [P.sbuf("c_tmp%d" % i, [128, 512], F32) for i in range(2)]
        po = [P.psum("c_po%d" % i, [128, 512], F32) for i in range(2)]
        P.dma("sp", "c_gb", lambda e, var=var: [e.dma_start(out=gb[:], in_=io["gsc"][var])], writes=["c_gb"])
        if final:
            junk = P.sbuf("c_junk", [128, D], BF16)
            gf = P.sbuf("c_gf", [128, D], F32)
            st = P.sbuf("c_st", [128, 8], F32)
            P.dma("sp", "c_gf", lambda e: [e.dma_start(out=gf[:], in_=io["gfin"][0:1, :].broadcast_to([128, D]))], writes=["c_gf"])
        tiles = [(0, 128), (128, 128)] if n == 256 else [(0, n)]
        ci = 0
        for ob in range(8):
            w_, wk = wo[ob % 2], "c_wo%d" % (ob % 2)
            wo_ = wout_fn(ob)
            P.dma("pool", wk, lambda e, w_=w_, wo_=wo_: dma_pieces(e, w_, wo_, "p (k n) -> p k n", k=KC), writes=[wk], n=len(wo_))
            for ti, (c0, rows) in enumerate(tiles):
                p_, pk = po[ci % 2], "b_c_po%d" % (ci % 2)
                h_, hk = ht[ci % 2], "c_ht%d" % (ci % 2)
                t_, tk = tmp[ci % 2], "c_tmp%d" % (ci % 2)
                ci += 1
                P.dma("sp", hk, lambda e, h_=h_, t0=t0, c0=c0, rows=rows, ob=ob: [e.dma_start(out=h_[0:rows, :], in_=io["hin_rows"](t0 + c0, rows)[:, ob * 512:(ob + 1) * 512])], writes=[hk])
                for k in range(KC):
                    P.op("pe", lambda e, p_=p_, k=k, c0=c0, rows=rows, w_=w_: e.matmul(p_[0:rows, :], mrg[:, k, c0:c0 + rows], w_[:, k, :], start=(k == 0), stop=(k == KC - 1)),
                         reads=["c_mrg", wk], excl=[pk])
                P.op("dve", lambda e, p_=p_, t_=t_, rows=rows, ob=ob: e.tensor_tensor(out=t_[0:rows, :], in0=p_[0:rows, :], in1=gb[0:rows, ob * 512:(ob + 1) * 512], op=ALU.mult),
                     reads=["c_gb"], writes=[tk], excl=[pk])
                P.op("pool", lambda e, t_=t_, h_=h_, rows=rows, ob=ob, ti=ti: e.tensor_tensor(out=hrow[ti][0:rows, ob * 512:(ob + 1) * 512], in0=t_[0:rows, :], in1=h_[0:rows, :], op=ALU.add),
                     reads=[tk, hk], writes=[("c_hrow", ti, ob)])
        for ti, (c0, rows) in enumerate(tiles):
            hkeys = [("c_hrow", ti, ob) for ob in range(8)]
            if not final:
                P.dma("sp", "c_hst%d" % ti, lambda e, ti=ti, t0=t0, c0=c0, rows=rows: [e.dma_start(out=io["hout_rows"](t0 + c0, rows), in_=hrow[ti][0:rows, :])], reads=hkeys)
            else:
                assert var == 0
                P.op("act", lambda e, ti=ti: e.activation(out=junk[:], in_=hrow[ti][:], func=AF.Square, accum_out=st[:, 4 * ti:4 * ti + 1]), reads=hkeys, writes=["c_junk", "c_st%d" % ti])
                P.op("dve", lambda e, ti=ti: e.tensor_scalar(out=st[:, 4 * ti + 1:4 * ti + 2], in0=st[:, 4 * ti:4 * ti + 1], scalar1=1.0 / D, scalar2=EPS, op0=ALU.mult, op1=ALU.add),
                     reads=["c_st%d" % ti], writes=["c_st%db" % ti])
                P.op("act", lambda e, ti=ti: e.activation(out=st[:, 4 * ti + 3:4 * ti + 4], in_=st[:, 4 * ti + 1:4 * ti + 2], func=AF.Sqrt), reads=["c_st%db" % ti], writes=["c_st%dd" % ti])
                P.op("dve", lambda e, ti=ti: e.reciprocal(out=st[:, 4 * ti + 2:4 * ti + 3], in_=st[:, 4 * ti + 3:4 * ti + 4]), reads=["c_st%dd" % ti], writes=["c_st%dc" % ti])
                P.op("dve", lambda e, ti=ti: e.scalar_tensor_tensor(out=hrow[ti][:], in0=hrow[ti][:], scalar=st[:, 4 * ti + 2:4 * ti + 3], in1=gf[:], op0=ALU.mult, op1=ALU.mult),
                     reads=hkeys + ["c_st%dc" % ti, "c_gf"], writes=hkeys)
                P.dma("sp", "c_hst%d" % ti, lambda e, ti=ti, t0=t0, c0=c0: [e.dma_start(out=io["hnorm"][t0 + c0:t0 + c0 + 128, :], in_=hrow[ti][:])], reads=hkeys)
        P.pop()
    P.pop()


I32 = mybir.dt.int32
NT = 8448
DEPTH = 2
G8 = [list(range(8))]
G4 = [[0, 1, 2, 3], [4, 5, 6, 7]]
G2 = [[0, 4], [1, 5], [2, 6], [3, 7]]
FUNC = {"id": AF.Identity, "silu": AF.Silu, "sig": AF.Sigmoid}

LAYER_IN = {
    "wadal_sh": ([16, 128, 4096], F32), "badaT": ([128, 64], F32), "gpreT": ([128, 32], F32),
    "wl_sh": ([56, 128, 4096], F32), "wmg_sh": ([16, 128, 4096], F32),
    "gqT": ([128, 6], F32), "gkvT": ([128, 4], F32), "wuq": ([128, 9216], F32), "wuqsw": ([128, 3072], F32), "wukv": ([128, 8192], F32),
    "convw": ([128, 20, 5], F32), "convb": ([128, 20], F32), "dtbias2": ([128, 1], F32), "alog2": ([128, 1], F32),
    "dskbc": ([128, 2048], F32), "gssmT": ([128, 16], F32),
    "wadag_sh": ([4, 128, 8192], F32), "badag": ([1, 4096], F32),
    "wpa_sh": ([8, 128, 4096], F32), "wpb_sh": ([8, 128, 8192], F32), "wout_sh": ([2, 128, 16384], F32),
}
CONST_IN = {
    "xin_sh": ([TC, D], F32), "offs": ([1, 2], I32), "c2T": ([128, 32, 2], F32),
    "identf": ([128, 128], F32), "onesf": ([128, 128], F32), "TL": ([128, 128], F32), "TG": ([128, 128], F32),
    "SG": ([128, 128], F32), "SL": ([128, 128], F32), "cos2": ([64, NT], F32), "sin2": ([64, NT], F32), "gfin": ([1, D], F32),
}


def build_program(nlayers=DEPTH, probe=None):
    nc = bass.Bass("TRN2", target_bir_lowering=False)
    ext = {}
    for k, (shp, dt) in CONST_IN.items():
        ext[k] = nc.dram_tensor(k, shp, dt, kind="ExternalInput").ap()
    for i in range(nlayers):
        for k, (shp, dt) in LAYER_IN.items():
            ext["%s_%d" % (k, i)] = nc.dram_tensor("%s_%d" % (k, i), shp, dt, kind="ExternalInput").ap()
    hnorm = nc.dram_tensor("hnorm", [NLAT, D], F32, kind="ExternalOutput").ap()
    P = Prog(nc)

    def dram(name, shape, dt=F32):
        return nc.dram_tensor(name, list(shape), dt)

    dyn = {}
    rl = P.es.enter_context(nc.sync.register("r_lat"))
    rc = P.es.enter_context(nc.sync.register("r_ctx"))

    def ldregs(e):
        e.reg_load(rl, ext["offs"][0:1, 0:1])
        ins = e.reg_load(rc, ext["offs"][0:1, 1:2])
        dyn["lat"] = e.snap(rl)
        dyn["ctx"] = e.snap(rc)
        return ins
    P.op("sp", ldregs)

    class Gathered:
        def __init__(self, out, CR, W, unit):
            self.out, self.CR, self.W, self.unit = out, CR, W, unit

        def rows(self, rank, r0, n):
            res = []
            r = r0
            while r < r0 + n:
                c, i = r // self.CR, r % self.CR
                k = min(self.CR - i, r0 + n - r)
                res.append((self.out.ap()[c, rank * self.CR + i:rank * self.CR + i + k, :], r - r0, k))
                r += k
            return res

        def at(self, rank, a):
            return self.rows(rank, a * self.unit, self.unit)

    def gather_from(bnc, R, ncol, dt, name, esize):
        CR = 1
        while CR * 2 * ncol * esize <= (1 << 20) and R % (CR * 2) == 0:
            CR *= 2
        nch = R // CR
        out = dram("gat_%s" % name, [nch, 4 * CR, ncol], dt)
        return out, CR, nch

    def gather(src_ap, shape, dt, name, batch):
        unit = shape[1] if len(shape) == 3 else 1
        ncol = shape[-1]
        R = shape[0] * unit
        esize = 4 if dt == F32 else 2
        bnc = dram("bnc_%s" % name, [R, ncol], dt)
        if len(shape) == 3:
            pieces = [(bnc.ap()[a * unit:(a + 1) * unit, :], src_ap[a]) for a in range(shape[0])]
        else:
            step = 264
            pieces = [(bnc.ap()[r0:min(r0 + step, R), :], src_ap[r0:min(r0 + step, R), :]) for r0 in range(0, R, step)]
        P.dma("sp", "bnc", lambda e, pieces=pieces: [e.dma_start(out=o_, in_=i_) for (o_, i_) in pieces],
              writes=[("bnc", name)], n=len(pieces))
        out, CR, nch = gather_from(bnc, R, ncol, dt, name, esize)
        batch.append((bnc, out, CR, nch, ("bnc", name)))
        return Gathered(out, CR, 4, unit)

    def flush(batch, slot):
        items = list(batch)
        tot = sum(it[3] for it in items)
        P.coll(lambda e: [e.collective_compute("AllGather", ALU.bypass, replica_groups=G4, ins=[b_.ap()[c * CR:(c + 1) * CR, :]], outs=[o_.ap()[c]])
                          for (b_, o_, CR, nch, _k) in items for c in range(nch)], tot, reads=[it[4] for it in items], writes=[], slot=slot)
        del batch[:]

    def do_gather(bnc, out, CR, nch, reads, writes):
        P.coll(lambda e: [e.collective_compute("AllGather", ALU.bypass, replica_groups=G4, ins=[bnc.ap()[c * CR:(c + 1) * CR, :]], outs=[out.ap()[c]])
                          for c in range(nch)], nch, reads=reads, writes=writes)

    def precast(g, name):
        shp = g.out.ap().shape
        o2 = dram("bf_%s" % name, list(shp), BF16)
        P.dma("pool", "precast", lambda e: [e.dma_start(out=o2.ap()[c], in_=g.out.ap()[c]) for c in range(shp[0])], n=shp[0])
        return Gathered(o2, g.CR, g.W, g.unit)

    b0_ = []
    xg = gather(ext["xin_sh"], [TC, D], F32, "x0", b0_)
    W_ = [dict() for _ in range(nlayers)]
    W_[0]["wadal"] = gather(ext["wadal_sh_0"], [16, 128, 4096], F32, "wadal0", b0_)
    flush(b0_, "cc")
    for i in range(nlayers):
        w = W_[i]
        bt = []
        if i > 0:
            w["wadal"] = gather(ext["wadal_sh_%d" % i], [16, 128, 4096], F32, "wadal%d" % i, bt)
        w["wmg"] = gather(ext["wmg_sh_%d" % i], [16, 128, 4096], F32, "wmg%d" % i, bt)
        w["wadag"] = gather(ext["wadag_sh_%d" % i], [4, 128, 8192], F32, "wadag%d" % i, bt)
        w["wpa"] = gather(ext["wpa_sh_%d" % i], [8, 128, 4096], F32, "wpa%d" % i, bt)
        w["wpb"] = gather(ext["wpb_sh_%d" % i], [8, 128, 8192], F32, "wpb%d" % i, bt)
        w["wout"] = gather(ext["wout_sh_%d" % i], [2, 128, 16384], F32, "wout%d" % i, bt)
        P.detach("ccw%d" % i)
        flush(bt, "ccw%d" % i)
    P.barrier()
    if probe == "gathers":
        f0 = lambda pcs: pcs[0][0]
        pieces = [(hnorm[8 * r_:8 * r_ + 8, :], f0(xg.rows(r_, 0, 8))) for r_ in range(4)]
        pieces += [(hnorm[32:40, :], f0(xg.rows(2, NLAT + 8, 8)))]
        w = W_[0]
        pieces += [(hnorm[40:48, :], ext["wl_sh_0"][31, 0:8, :]), (hnorm[48:56, :], ext["wl_sh_0"][27, 8:16, :]),
                   (hnorm[56:64, :], f0(w["wadal"].at(2, 10))[0:8, :]), (hnorm[64:72, :], f0(w["wmg"].at(3, 15))[0:8, :]),
                   (hnorm[72:80, :], f0(w["wadag"].at(1, 3))[0:8, 0:4096]), (hnorm[80:88, :], f0(w["wpa"].at(3, 3))[0:8, :]),
                   (hnorm[88:96, :], f0(w["wpb"].at(1, 1))[0:8, 4096:8192]), (hnorm[96:104, :], f0(w["wout"].at(2, 0))[0:8, 0:4096])]
        P.dma("sp", "probe", lambda e: [e.dma_start(out=o_, in_=i_) for (o_, i_) in pieces], n=len(pieces))
        P.emit()
        return nc, P

    pT = dram("pT", [NPT * 128, NT]).ap()
    sc = {
        "qT": dram("qT", [8, 192, NT], BF16).ap(), "kT": dram("kT", [8, 128, NT], BF16).ap(), "kpeT": dram("kpeT", [64, NT], BF16).ap(),
        "vs": dram("vs", [8, 128, NT // 128, 128], BF16).ap(),
        "xs_tm": dram("xs_tm", [NT, 2048]).ap(), "B_tm": dram("B_tm", [2, NT, 128], BF16).ap(), "BT": dram("BT", [2, 128, NT], BF16).ap(),
        "CT": dram("CT", [2, 128, NT], BF16).ap(), "dtt": dram("dtt", [NT, 128]).ap(), "yd": dram("yd", [2, NT, 2048]).ap(),
        "sgT": dram("sgT", [2 * D, TC]).ap(), "gsc": dram("gsc", [2, 128, D]).ap(),
    }
    ogb_ = dram("og_b", [1024, NT], BF16)
    ygb_ = [dram("yg_b%d" % q, [1024, NT], BF16) for q in range(2)]
    ssb_ = dram("ss_b", [1, NT])
    oga, ogCR, ognch = gather_from(ogb_, 1024, NT, BF16, "og", 2)
    ygg = [gather_from(ygb_[q], 1024, NT, BF16, "yg%d" % q, 2) for q in range(2)]
    ssa, ssCR, ssnch = gather_from(ssb_, 1, NT, F32, "ss", 4)
    og_own = dram("og_own", [4 * 1024, TC], BF16)
    yg_own = [dram("yg_own%d" % q, [4 * 1024, TC], BF16) for q in range(2)]
    ss_own = dram("ss_own", [4, TC])
    hout = dram("hout", [TC, D])
    xg1o, hCR, hnch = gather_from(hout, TC, D, F32, "h1", 4)
    xg1 = Gathered(xg1o, hCR, 4, 1)

    def hout_rows(r0, n):
        return hout.ap()[r0:r0 + n, :]

    for i in range(nlayers):
        w = W_[i]
        L = lambda k, i=i: ext["%s_%d" % (k, i)]
        xga = xg if i == 0 else xg1
        own_rows = (lambda r0, n: ext["xin_sh"][r0:r0 + n, :]) if i == 0 else hout_rows
        modio = {"c2T": ext["c2T"], "badaT": L("badaT"), "gpreT": L("gpreT"), "identf": ext["identf"]}
        if i > 0:
            P.join("ccw%d" % i)
            P.barrier()
        wlb = dram("wlbf_%d" % i, [NPT * 128, 4096], BF16)
        P.dma("pool", "precast", lambda e, i=i, wlb=wlb: [e.dma_start(out=wlb.ap()[c * 128:(c + 1) * 128, :], in_=ext["wl_sh_%d" % i][c]) for c in range(NPT)], n=NPT)
        P.barrier()
        wada_fn = lambda oc, w=w: w["wadal"].at(oc // 16, oc % 16)
        blocks = [dict(n=256, mcol=1, tiles=[[(a_, o_, k_) for (a_, o_, k_) in xga.rows(2 * t, NLAT, 64)] + [(a_, 64 + o_, k_) for (a_, o_, k_) in xga.rows(2 * t + 1, NLAT, 64)] for t in range(2)])]
        for bl in range(16):
            tiles = []
            for tt in range(4):
                l0 = bl * 512 + tt * 128
                tiles.append(xga.rows(l0 // NLAT, l0 % NLAT, 128))
            blocks.append(dict(n=512, mcol=0, tiles=tiles))
        t0s = [0] + [256 + 512 * b_ for b_ in range(16)]
        chunks = []
        cc = 0
        for name, cnt, f in SEG:
            if name in ("mga", "mgb"):
                continue
            for q in range(cnt):
                w_ap = [(wlb.ap()[cc * 128:(cc + 1) * 128, :], 0, 128)]
                chunks.append((w_ap, FUNC[f], (lambda bi, cc=cc: pT[cc * 128:(cc + 1) * 128, t0s[bi]:t0s[bi] + blocks[bi]["n"]])))
                cc += 1
        assert cc == NPT
        phase_A(P, nc, modio, blocks, chunks, wada_fn)
        P.barrier()
        iob = {"pT": pT, "gqT": L("gqT"), "gkvT": L("gkvT"), "wuq": L("wuq"), "wuqsw": L("wuqsw"), "wukv": L("wukv"),
               "cos2": ext["cos2"], "sin2": ext["sin2"], "onesf": ext["onesf"], "ogT": ogb_.ap()}
        iob.update(sc)
        phase_B1(P, nc, iob, NT)
        P.barrier()
        phase_B2(P, nc, iob, NT)
        P.barrier()
        ios = {"pT": pT, "convw": L("convw"), "convb": L("convb"), "dtbias2": L("dtbias2"), "alog2": L("alog2"), "identf": ext["identf"],
               "onesf": ext["onesf"], "TL": ext["TL"], "TG": ext["TG"], "SG": ext["SG"], "SL": ext["SL"], "dskbc": L("dskbc"), "gssmT": L("gssmT"),
               "ygT_parts": [t_.ap() for t_ in ygb_], "ssq": ssb_.ap()}
        ios.update(sc)
        phase_S1(P, nc, ios, NT)
        P.barrier()
        phase_S2(P, nc, ios, NT)
        P.barrier()
        phase_S3(P, nc, ios, NT)
        P.barrier()
        xb_ = [(ogb_, oga, ogCR, ognch, ("x", "og"))] + [(ygb_[q], ygg[q][0], ygg[q][1], ygg[q][2], ("x", "yg%d" % q)) for q in range(2)]
        xb_.append((ssb_, ssa, ssCR, ssnch, ("x", "ss")))
        flush(xb_, "cc")
        if i == 0:
            P.join("ccw0")
            P.barrier()
        wmg_b = precast(w["wmg"], "wmg%d" % i)
        wpa_b = precast(w["wpa"], "wpa%d" % i)
        wpb_b = precast(w["wpb"], "wpb%d" % i)
        wout_b = precast(w["wout"], "wout%d" % i)
        P.barrier()
        mblocks = []
        for bl in range(4):
            mblocks.append(dict(n=512, mcol=0, tiles=[[(own_rows(bl * 512 + tt * 128, 128), 0, 128)] for tt in range(4)]))
        mt0 = [0, 512, 1024, 1536]
        if i < nlayers - 1:
            mblocks.append(dict(n=64, mcol=1, tiles=[[(own_rows(NLAT, 64), 0, 64)]]))
            mt0.append(NLAT)
        mchunks = []
        for cc in range(64):
            mchunks.append((wmg_b.at(cc // 16, cc % 16), AF.Sigmoid,
                            (lambda bi, cc=cc: sc["sgT"][cc * 128:(cc + 1) * 128, mt0[bi]:mt0[bi] + mblocks[bi]["n"]])))
        phase_A(P, nc, modio, mblocks, mchunks, wada_fn)
        P.barrier()
        def own_cols(dst, src, CR, nch):
            sv = src.ap().rearrange("c q t -> (c q) t")
            dv = dst.ap()
            P.dma("sp", "owncp", lambda e: [e.dma_start(out=dv[:, 0:NLAT], in_=sv[:, bass.ds(dyn["lat"], NLAT)]),
                                            e.dma_start(out=dv[:, NLAT:TC], in_=sv[:, bass.ds(dyn["ctx"], 64)])], n=2)
        own_cols(og_own, oga, ogCR, ognch)
        for q in range(2):
            own_cols(yg_own[q], ygg[q][0], ygg[q][1], ygg[q][2])
        own_cols(ss_own, ssa, ssCR, ssnch)
        P.barrier()
        ioc = {"c2T": ext["c2T"], "badag": L("badag"), "gsc": sc["gsc"], "sgT": sc["sgT"], "hin_rows": own_rows, "gfin": ext["gfin"],
               "oga2": og_own.ap(), "yga_parts": [t_.ap() for t_ in yg_own],
               "ssqa": ss_own.ap(), "hout_rows": hout_rows, "hnorm": hnorm}
        phase_Cmod(P, nc, ioc, lambda cb, w=w: w["wadag"].at(cb // 4, cb % 4))
        P.barrier()
        final = (i == nlayers - 1)
        phase_C(P, nc, ioc, lambda oc, g_=wpa_b: g_.at(oc // 8, oc % 8), lambda oc, g_=wpb_b: g_.at(oc // 8, oc % 8),
                lambda ob, g_=wout_b: g_.at(ob // 2, ob % 2), with_ctx=not final, final=final, dyn=dyn)
        P.barrier()
        if not final:
            do_gather(hout, xg1o, hCR, hnch, [], [])
            P.barrier()
    P.emit()
    return nc, P


def prep_core(inp, b, j, nlayers=DEPTH):
    r = 4 * b + j
    f32 = np.float32
    xin = np.concatenate([inp["x"][b][NLAT * j:NLAT * (j + 1)], inp["ctx"][b][64 * j:64 * (j + 1)]], axis=0)
    c2 = np.stack([inp["c"][b], inp["c_ctx"]], axis=1)
    cos2, sin2 = rope_tables(NT)
    rr = np.arange(128)
    m = {
        "xin_sh": np.ascontiguousarray(xin), "offs": np.array([[256 + NLAT * j, 64 * j]], np.int32),
        "c2T": np.ascontiguousarray(c2.reshape(32, 128, 2).transpose(1, 0, 2)),
        "identf": np.eye(128, dtype=f32), "onesf": np.ones((128, 128), f32),
        "TL": (rr[:, None] <= rr[None, :]).astype(f32), "TG": (rr[:, None] >= rr[None, :]).astype(f32),
        "SG": (rr[:, None] > rr[None, :]).astype(f32), "SL": (rr[:, None] < rr[None, :]).astype(f32),
        "cos2": cos2, "sin2": sin2, "gfin": inp["g_final"][None, :].astype(f32),
    }
    return m


def _gathered_perm():
    rho = np.arange(4096)
    c, r, ii = rho // 128, (rho % 128) // 32, rho % 32
    pa = r * 1024 + c * 32 + ii
    pb = np.concatenate([r * 2048 + q * 1024 + c * 32 + ii for q in range(2)])
    return pa, pb


PERM_A, PERM_B = _gathered_perm()


def prep_layer_shared(inp, i):
    wada = inp["w_ada"][i]
    out = {
        "wadal": chunked_weight(wada[:, :2 * D], np.arange(2 * D)),
        "wmg": chunked_weight(inp["w_in"][i], np.arange(MGA0, MGA0 + 2 * D)),
        "wadag": chunked_weight(wada[:, 2 * D:], np.arange(D), width=256),
        "wpa": chunked_weight(inp["w_proj_a"][i][PERM_A], np.arange(D)),
        "wpb": chunked_weight(inp["w_proj_b"][i][PERM_B], np.arange(D)),
        "wout": chunked_weight(inp["w_out"][i], np.arange(D), width=512),
    }
    return out


def prep_layer_core(inp, i, b, j, shared, wl_cache):
    r = 4 * b + j
    m = {}
    m["wadal_sh"] = shared["wadal"][16 * j:16 * j + 16]
    m["wmg_sh"] = shared["wmg"][16 * j:16 * j + 16]
    m["wadag_sh"] = shared["wadag"][4 * j:4 * j + 4]
    m["wpa_sh"] = shared["wpa"][8 * j:8 * j + 8]
    m["wpb_sh"] = shared["wpb"][8 * j:8 * j + 8]
    m["wout_sh"] = shared["wout"][2 * j:2 * j + 2]
    if j not in wl_cache:
        wl_cache[j] = chunked_weight(inp["w_in"][i], cols_for_core(j)[:NPT * 128])
    m["wl_sh"] = wl_cache[j]
    m["badaT"] = colvec(inp["b_ada"][i][:2 * D])
    m["gpreT"] = colvec(inp["g_pre"][i])
    m["badag"] = np.ascontiguousarray(inp["b_ada"][i][None, 2 * D:])
    pb = prep_B(inp, i, j, NT)
    for k in ("gqT", "gkvT", "wuq", "wuqsw", "wukv"):
        m[k] = pb[k]
    ps = prep_S(inp, i, j)
    for k in ("convw", "convb", "dtbias2", "alog2", "dskbc", "gssmT"):
        m[k] = ps[k]
    return m


def make_in_maps(inp, nlayers=DEPTH):
    maps = [prep_core(inp, r // 4, r % 4, nlayers) for r in range(8)]
    for i in range(nlayers):
        shared = prep_layer_shared(inp, i)
        wl_cache = {}
        for r in range(8):
            lm = prep_layer_core(inp, i, r // 4, r % 4, shared, wl_cache)
            for k, v in lm.items():
                maps[r]["%s_%d" % (k, i)] = np.ascontiguousarray(v, dtype=np.float32)
    return maps


_CACHE = {}


def kernel(**inputs):
    inp = {k: np.asarray(v) for k, v in inputs.items()}
    if "nc" not in _CACHE:
        _CACHE["nc"] = build_program()[0]
    nc = _CACHE["nc"]
    maps = make_in_maps(inp)
    res = run_bass_kernel_spmd(nc, maps, core_ids=list(range(8)))
    out = np.empty((2, 8192, D), np.float32)
    for r in range(8):
        b, j = r // 4, r % 4
        out[b, NLAT * j:NLAT * (j + 1)] = res.results[r]["hnorm"]
    return out
```
